# Optimizing a Trainium2 kernel written in Bass

```python
import math
import functools
import jax
import jax.numpy as jnp
from jax import lax
import numpy as np

D_MODEL = 1024
BATCH = 2
SEQ = 8192
DEPTH = 2

EPS = 1e-6
D_FF = 4 * D_MODEL
N_EVEN = (DEPTH + 1) // 2
N_ODD = DEPTH // 2
CHUNK = 128
BLOCK_Q = 128

REL_BUCKETS = 32
REL_MAX_DIST = 128
N_BIAS_HEADS = 8

SSD_D_INNER = D_MODEL
SSD_HEAD_DIM = 64
SSD_HEADS = SSD_D_INNER // SSD_HEAD_DIM
SSD_GROUPS = 4
SSD_STATE = 128
SSD_CONV = 4
SSD_CONV_CH = SSD_D_INNER + 2 * SSD_GROUPS * SSD_STATE

DIFF_HEADS = N_BIAS_HEADS
DIFF_HEAD_DIM = D_MODEL // (2 * DIFF_HEADS)
DIFF_QK = DIFF_HEADS * 2 * DIFF_HEAD_DIM
DIFF_V = DIFF_HEADS * 2 * DIFF_HEAD_DIM

EVEN_SPLITS = (SSD_D_INNER, SSD_CONV_CH, SSD_HEADS, DIFF_QK, DIFF_QK, DIFF_V)
EVEN_IN = sum(EVEN_SPLITS)
EVEN_MIX = SSD_D_INNER + DIFF_V

RET_HEADS = 4
RET_QK_DIM = 128
RET_V_DIM = 256
ROPE_BASE = 10000.0

SWA_HEADS = N_BIAS_HEADS
SWA_KV_HEADS = 2
SWA_HEAD_DIM = 64
WINDOW = 128

ODD_SPLITS = (RET_HEADS * RET_QK_DIM, RET_HEADS * RET_QK_DIM, RET_HEADS * RET_V_DIM,
              RET_HEADS * RET_V_DIM, SWA_HEADS * SWA_HEAD_DIM,
              SWA_KV_HEADS * SWA_HEAD_DIM, SWA_KV_HEADS * SWA_HEAD_DIM)
ODD_IN = sum(ODD_SPLITS)
ODD_MIX = RET_HEADS * RET_V_DIM + SWA_HEADS * SWA_HEAD_DIM

kernel_name = "hybrid_ssd_diffattn_retention_swa_trunk"


def rms_norm(x, w=None):
    x32 = x.astype(jnp.float32)
    y = x32 * lax.rsqrt(jnp.mean(x32 * x32, axis=-1, keepdims=True) + EPS)
    if w is not None:
        y = y * w.astype(jnp.float32)
    return y.astype(x.dtype)


def split_cols(a, sizes):
    return jnp.split(a, np.cumsum(sizes)[:-1].tolist(), axis=-1)


def t5_bucket(dist):
    dist = jnp.maximum(dist, 0)
    max_exact = REL_BUCKETS // 2
    logd = jnp.log(jnp.maximum(dist, 1).astype(jnp.float32) / max_exact)
    large = max_exact + (logd / math.log(REL_MAX_DIST / max_exact)
                         * (REL_BUCKETS - max_exact)).astype(jnp.int32)
    large = jnp.minimum(large, REL_BUCKETS - 1)
    return jnp.where(dist < max_exact, dist, large)


def scan_chunk_states(chunk_states, chunk_decay):
    def step(carry, inp):
        st, dec = inp
        return carry * dec + st, carry
    init = jnp.zeros_like(chunk_states[:, 0])
    _, prev = lax.scan(step, init, (jnp.moveaxis(chunk_states, 1, 0), jnp.moveaxis(chunk_decay, 1, 0)))
    return jnp.moveaxis(prev, 0, 1)


def causal_depthwise_conv(x, w, b):
    ch = x.shape[-1]
    out = lax.conv_general_dilated(x, w[:, None, :].astype(x.dtype), window_strides=(1,),
                                   padding=[(SSD_CONV - 1, 0)],
                                   dimension_numbers=("NWC", "WIO", "NWC"),
                                   feature_group_count=ch)
    return out + b.astype(x.dtype)


def ssd_chunked(xh, dt, A, Bg, Cg):
    b, s, h, p = xh.shape
    g, n = Bg.shape[-2:]
    r = h // g
    nc = s // CHUNK
    X = (xh * dt[..., None]).reshape(b, nc, CHUNK, g, r, p)
    Bc = Bg.reshape(b, nc, CHUNK, g, n)
    Cc = Cg.reshape(b, nc, CHUNK, g, n)
    a_cs = jnp.cumsum((dt * A).reshape(b, nc, CHUNK, g, r), axis=2)
    causal = jnp.tril(jnp.ones((CHUNK, CHUNK), dtype=bool))
    seg = a_cs[:, :, :, None] - a_cs[:, :, None, :]
    decay_in = jnp.exp(jnp.where(causal[:, :, None, None], seg, -jnp.inf))
    cb = jnp.einsum("bclgn,bcsgn->bclsg", Cc, Bc)
    y_diag = jnp.einsum("bclsg,bclsgr,bcsgrp->bclgrp", cb, decay_in, X)
    decay_to_end = jnp.exp(a_cs[:, :, -1:] - a_cs)
    chunk_states = jnp.einsum("bcsgn,bcsgr,bcsgrp->bcgrpn", Bc, decay_to_end, X)
    state_prev = scan_chunk_states(chunk_states, jnp.exp(a_cs[:, :, -1])[..., None, None])
    y_off = jnp.einsum("bclgn,bcgrpn,bclgr->bclgrp", Cc, state_prev, jnp.exp(a_cs))
    return (y_diag + y_off).reshape(b, s, h, p)


def ssd_mixer(z, xbc, dt_raw, conv_w, conv_b, dt_bias, A_log, D_skip, norm_w):
    b, s, _ = z.shape
    xbc = jax.nn.silu(causal_depthwise_conv(xbc, conv_w, conv_b))
    xs, Bs, Cs = split_cols(xbc, (SSD_D_INNER, SSD_GROUPS * SSD_STATE, SSD_GROUPS * SSD_STATE))
    xh = xs.astype(jnp.float32).reshape(b, s, SSD_HEADS, SSD_HEAD_DIM)
    dt = jax.nn.softplus(dt_raw.astype(jnp.float32) + dt_bias.astype(jnp.float32))
    A = -jnp.exp(A_log.astype(jnp.float32))
    y = ssd_chunked(xh, dt, A,
                    Bs.astype(jnp.float32).reshape(b, s, SSD_GROUPS, SSD_STATE),
                    Cs.astype(jnp.float32).reshape(b, s, SSD_GROUPS, SSD_STATE))
    y = y + xh * D_skip.astype(jnp.float32)[:, None]
    y = y.reshape(b, s, SSD_D_INNER) * jax.nn.silu(z.astype(jnp.float32))
    return rms_norm(y, norm_w).astype(z.dtype)


def diff_attention(q, k, v, lam, subln_w, rel_bias, layer_idx):
    b, s = q.shape[:2]
    lam_init = 0.8 - 0.6 * math.exp(-0.3 * layer_idx)
    lam32 = lam.astype(jnp.float32)
    lam_full = (jnp.exp(jnp.sum(lam32[0] * lam32[1])) - jnp.exp(jnp.sum(lam32[2] * lam32[3])) + lam_init)
    q32 = q.astype(jnp.float32) * DIFF_HEAD_DIM ** -0.5
    k32 = k.astype(jnp.float32)
    v32 = v.astype(jnp.float32)
    nb = s // BLOCK_Q
    q_blocks = jnp.moveaxis(q32.reshape(b, nb, BLOCK_Q, DIFF_HEADS, 2, DIFF_HEAD_DIM), 1, 0)
    key_pos = jnp.arange(s)

    def one_block(args):
        q_blk, blk = args
        dist = (blk * BLOCK_Q + jnp.arange(BLOCK_Q))[:, None] - key_pos[None, :]
        bias = jnp.transpose(rel_bias[t5_bucket(dist)], (2, 0, 1)).astype(jnp.float32)
        logits = jnp.einsum("bqhmd,bkhmd->bhmqk", q_blk, k32) + bias[None, :, None]
        logits = jnp.where(dist >= 0, logits, -jnp.inf)
        p = jax.nn.softmax(logits, axis=-1)
        attn = p[:, :, 0] - lam_full * p[:, :, 1]
        return jnp.einsum("bhqk,bkhe->bqhe", attn, v32)

    out = lax.map(one_block, (q_blocks, jnp.arange(nb)))
    out = jnp.moveaxis(out, 0, 1).reshape(b, s, DIFF_HEADS, 2 * DIFF_HEAD_DIM)
    out = rms_norm(out, subln_w) * (1.0 - lam_init)
    return out.reshape(b, s, DIFF_V).astype(q.dtype)


def rope(t, pos):
    half = t.shape[-1] // 2
    inv = ROPE_BASE ** (-jnp.arange(half, dtype=jnp.float32) / half)
    ang = pos[:, None] * inv[None]
    cos = jnp.cos(ang)[None, :, None]
    sin = jnp.sin(ang)[None, :, None]
    t1, t2 = t[..., :half], t[..., half:]
    return jnp.concatenate([t1 * cos - t2 * sin, t1 * sin + t2 * cos], axis=-1)


def retention(q, k, v):
    b, s = q.shape[:2]
    nc = s // CHUNK
    pos = jnp.arange(s, dtype=jnp.float32)
    Q = rope(q.astype(jnp.float32), pos).reshape(b, nc, CHUNK, RET_HEADS, RET_QK_DIM)
    K = (rope(k.astype(jnp.float32), pos) * RET_QK_DIM ** -0.5).reshape(b, nc, CHUNK, RET_HEADS, RET_QK_DIM)
    V = v.astype(jnp.float32).reshape(b, nc, CHUNK, RET_HEADS, RET_V_DIM)
    log_gamma = jnp.log(1.0 - 2.0 ** (-5.0 - jnp.arange(RET_HEADS, dtype=jnp.float32)))
    idx = jnp.arange(CHUNK, dtype=jnp.float32)
    rel = idx[:, None] - idx[None, :]
    decay_mat = jnp.where(rel >= 0, jnp.exp(jnp.maximum(rel, 0.0)[None] * log_gamma[:, None, None]), 0.0)
    scores = jnp.einsum("bclhd,bcshd->bchls", Q, K) * decay_mat
    inner = jnp.einsum("bchls,bcshe->bclhe", scores, V)
    zeta = jnp.exp((CHUNK - 1 - idx)[None] * log_gamma[:, None])
    kv = jnp.einsum("bcshd,hs,bcshe->bchde", K, zeta, V)
    chunk_decay = jnp.broadcast_to(jnp.exp(CHUNK * log_gamma)[:, None, None], (b, nc, RET_HEADS, 1, 1))
    state_prev = scan_chunk_states(kv, chunk_decay)
    xi = jnp.exp((idx + 1.0)[None] * log_gamma[:, None])
    cross = jnp.einsum("bclhd,bchde,hl->bclhe", Q, state_prev, xi)
    out = (inner + cross).reshape(b, s, RET_HEADS, RET_V_DIM)
    return rms_norm(out).reshape(b, s, RET_HEADS * RET_V_DIM)


def sliding_window_attention(q, k, v, sinks, rel_bias):
    b, s = q.shape[:2]
    rep = SWA_HEADS // SWA_KV_HEADS
    nb = s // BLOCK_Q
    q32 = q.astype(jnp.float32).reshape(b, nb, BLOCK_Q, SWA_KV_HEADS, rep, SWA_HEAD_DIM) * SWA_HEAD_DIM ** -0.5

    def band(t):
        t = jnp.pad(t.astype(jnp.float32), ((0, 0), (BLOCK_Q, 0), (0, 0), (0, 0)))
        t = t.reshape(b, nb + 1, BLOCK_Q, SWA_KV_HEADS, SWA_HEAD_DIM)
        return jnp.concatenate([t[:, :-1], t[:, 1:]], axis=2)

    kb, vb = band(k), band(v)
    q_off = jnp.arange(BLOCK_Q)
    k_off = jnp.arange(2 * BLOCK_Q) - BLOCK_Q
    dist = q_off[:, None] - k_off[None, :]
    k_pos = (jnp.arange(nb) * BLOCK_Q)[:, None] + k_off[None, :]
    valid = ((dist >= 0) & (dist < WINDOW))[None] & (k_pos >= 0)[:, None, :]
    bias = jnp.transpose(rel_bias[t5_bucket(dist)], (2, 0, 1)).astype(jnp.float32)
    bias = bias.reshape(SWA_KV_HEADS, rep, BLOCK_Q, 2 * BLOCK_Q)
    logits = jnp.einsum("bnqgrd,bnkgd->bngrqk", q32, kb) + bias
    logits = jnp.where(valid[None, :, None, None], logits, -jnp.inf)
    sink = jnp.broadcast_to(sinks.astype(jnp.float32).reshape(1, 1, SWA_KV_HEADS, rep, 1, 1),
                            logits.shape[:-1] + (1,))
    p = jax.nn.softmax(jnp.concatenate([logits, sink], axis=-1), axis=-1)[..., :-1]
    out = jnp.einsum("bngrqk,bnkgd->bnqgrd", p, vb)
    return out.reshape(b, s, SWA_HEADS * SWA_HEAD_DIM).astype(q.dtype)


def even_mixer(h, w_in, conv_w, conv_b, dt_bias, A_log, D_skip, ssd_norm, lam, diff_norm, w_out,
               rel_bias, layer_idx):
    b, s, _ = h.shape
    z, xbc, dt_raw, q, k, v = split_cols(h @ w_in, EVEN_SPLITS)
    y_ssd = ssd_mixer(z, xbc, dt_raw, conv_w, conv_b, dt_bias, A_log, D_skip, ssd_norm)
    y_diff = diff_attention(q.reshape(b, s, DIFF_HEADS, 2, DIFF_HEAD_DIM),
                            k.reshape(b, s, DIFF_HEADS, 2, DIFF_HEAD_DIM),
                            v.reshape(b, s, DIFF_HEADS, 2 * DIFF_HEAD_DIM),
                            lam, diff_norm, rel_bias, layer_idx)
    return jnp.concatenate([y_ssd, y_diff], axis=-1) @ w_out


def odd_mixer(h, w_in, sinks, w_out, rel_bias):
    b, s, _ = h.shape
    rq, rk, rv, rg, sq, sk, sv = split_cols(h @ w_in, ODD_SPLITS)
    y_ret = retention(rq.reshape(b, s, RET_HEADS, RET_QK_DIM),
                      rk.reshape(b, s, RET_HEADS, RET_QK_DIM),
                      rv.reshape(b, s, RET_HEADS, RET_V_DIM))
    y_ret = (y_ret * jax.nn.silu(rg.astype(jnp.float32))).astype(h.dtype)
    y_swa = sliding_window_attention(sq.reshape(b, s, SWA_HEADS, SWA_HEAD_DIM),
                                     sk.reshape(b, s, SWA_KV_HEADS, SWA_HEAD_DIM),
                                     sv.reshape(b, s, SWA_KV_HEADS, SWA_HEAD_DIM),
                                     sinks, rel_bias)
    return jnp.concatenate([y_ret, y_swa], axis=-1) @ w_out


def squared_relu_mlp(h, w1, w2):
    return jnp.square(jax.nn.relu(h @ w1)) @ w2


def sandwich(x, c_act, sublayer, g_pre, g_post, w_mod, b_mod):
    shift, scale, gate = jnp.split(c_act @ w_mod + b_mod, 3, axis=-1)
    h = rms_norm(x, g_pre) * (1 + scale[:, None]) + shift[:, None]
    return x + gate[:, None] * rms_norm(sublayer(h), g_post)


def setup_inputs(seed: int = 0) -> dict:
    key = jax.random.key(seed)
    ks = jax.random.split(key, 24)

    def nrm(k, shape, scale):
        return jax.random.normal(k, shape, jnp.float32) * scale

    dt0 = jnp.exp(jax.random.uniform(ks[11], (N_EVEN, SSD_HEADS), jnp.float32,
                                     minval=math.log(1e-3), maxval=math.log(1e-1)))
    return {
        "x": nrm(ks[0], (BATCH, SEQ, D_MODEL), 1.0),
        "c": nrm(ks[1], (BATCH, D_MODEL), 1.0),
        "rel_bias": nrm(ks[2], (REL_BUCKETS, N_BIAS_HEADS), 0.5),
        "norm_gains": 1.0 + nrm(ks[3], (DEPTH, 4, D_MODEL), 0.05),
        "mod_w": nrm(ks[4], (DEPTH, 2, D_MODEL, 3 * D_MODEL), D_MODEL ** -0.5),
        "mod_b": nrm(ks[5], (DEPTH, 2, 3 * D_MODEL), 0.02),
        "mlp_w1": nrm(ks[6], (DEPTH, D_MODEL, D_FF), D_MODEL ** -0.5),
        "mlp_w2": nrm(ks[7], (DEPTH, D_FF, D_MODEL), D_FF ** -0.5),
        "e_w_in": nrm(ks[8], (N_EVEN, D_MODEL, EVEN_IN), D_MODEL ** -0.5),
        "e_conv_w": nrm(ks[9], (N_EVEN, SSD_CONV, SSD_CONV_CH), SSD_CONV ** -0.5),
        "e_conv_b": nrm(ks[10], (N_EVEN, SSD_CONV_CH), 0.02),
        "e_dt_bias": dt0 + jnp.log(-jnp.expm1(-dt0)),
        "e_A_log": jnp.log(jax.random.uniform(ks[12], (N_EVEN, SSD_HEADS), jnp.float32, minval=1.0, maxval=16.0)),
        "e_D": 1.0 + nrm(ks[13], (N_EVEN, SSD_HEADS), 0.05),
        "e_ssd_norm": 1.0 + nrm(ks[14], (N_EVEN, SSD_D_INNER), 0.05),
        "e_lambda": nrm(ks[15], (N_EVEN, 4, DIFF_HEAD_DIM), 0.1),
        "e_diff_norm": 1.0 + nrm(ks[16], (N_EVEN, 2 * DIFF_HEAD_DIM), 0.05),
        "e_w_out": nrm(ks[17], (N_EVEN, EVEN_MIX, D_MODEL), EVEN_MIX ** -0.5),
        "o_w_in": nrm(ks[18], (N_ODD, D_MODEL, ODD_IN), D_MODEL ** -0.5),
        "o_sinks": nrm(ks[19], (N_ODD, SWA_HEADS), 0.5),
        "o_w_out": nrm(ks[20], (N_ODD, ODD_MIX, D_MODEL), ODD_MIX ** -0.5),
    }


def reference(x, c, rel_bias, norm_gains, mod_w, mod_b, mlp_w1, mlp_w2, e_w_in, e_conv_w, e_conv_b,
              e_dt_bias, e_A_log, e_D, e_ssd_norm, e_lambda, e_diff_norm, e_w_out, o_w_in, o_sinks,
              o_w_out):
    c_act = jax.nn.silu(c)
    for layer in range(DEPTH):
        j = layer // 2
        if layer % 2 == 0:
            mixer = functools.partial(even_mixer, w_in=e_w_in[j], conv_w=e_conv_w[j], conv_b=e_conv_b[j],
                                      dt_bias=e_dt_bias[j], A_log=e_A_log[j], D_skip=e_D[j],
                                      ssd_norm=e_ssd_norm[j], lam=e_lambda[j], diff_norm=e_diff_norm[j],
                                      w_out=e_w_out[j], rel_bias=rel_bias, layer_idx=layer)
        else:
            mixer = functools.partial(odd_mixer, w_in=o_w_in[j], sinks=o_sinks[j], w_out=o_w_out[j],
                                      rel_bias=rel_bias)
        x = sandwich(x, c_act, mixer, norm_gains[layer, 0], norm_gains[layer, 1],
                     mod_w[layer, 0], mod_b[layer, 0])
        mlp = functools.partial(squared_relu_mlp, w1=mlp_w1[layer], w2=mlp_w2[layer])
        x = sandwich(x, c_act, mlp, norm_gains[layer, 2], norm_gains[layer, 3],
                     mod_w[layer, 1], mod_b[layer, 1])
    return x
```

```python
from contextlib import ExitStack
import os
import numpy as np
import ml_dtypes
import concourse.bass as bass
import concourse.mybir as mybir
from concourse.bass_utils import run_bass_kernel_spmd

F32 = mybir.dt.float32
BF16 = mybir.dt.bfloat16
AF = mybir.ActivationFunctionType
ALU = mybir.AluOpType
AX = mybir.AxisListType
EPOCH = 20000


class Buf:
    __slots__ = ("t", "name", "w", "r", "dsem", "dcount", "is_out", "bank", "alt")

    def __init__(self, t, name, bank=None):
        self.t = t
        self.name = name
        self.bank = bank
        self.alt = None
        self.w = []
        self.r = []
        self.dsem = None
        self.dcount = 0
        self.is_out = False

    def __getitem__(self, idx):
        return self.t[idx]


class Op:
    __slots__ = ("eng", "fn", "deps", "marked", "sem", "val", "waits", "is_dma", "dbuf", "snap", "inc", "is_bar")

    def __init__(self, eng, fn, is_dma=False, dbuf=None, inc=16):
        self.inc = inc
        self.is_bar = False
        self.eng = eng
        self.fn = fn
        self.deps = []
        self.marked = False
        self.sem = None
        self.val = 0
        self.waits = []
        self.is_dma = is_dma
        self.dbuf = dbuf
        self.snap = None


class Prog:
    ENGS = ("pe", "act", "dve", "pool", "sp")

    def __init__(self, nc):
        self.nc = nc
        self.ops = []
        self.stack = ExitStack()
        self.out_ops = []
        self.nsem = 0
        self.SB_BYTES = 206 * 1024
        self.sb_f32 = self.stack.enter_context(nc.sbuf_tensor("arena", [128, self.SB_BYTES // 4], F32))
        self.sb_bf = self.sb_f32.bitcast(BF16)
        self.ps_f32 = self.stack.enter_context(nc.psum_tensor("parena", [128, 4096], F32))
        self.ps_bf = self.ps_f32.bitcast(BF16)
        self.sb_ptr = 0
        self.ps_ptr = 0
        self.dma_log = []
        self.bar = None

    @staticmethod
    def _shape_view(v, shape):
        if len(shape) == 3:
            v = v.rearrange("p (a b) -> p a b", a=shape[1])
        elif len(shape) == 4:
            v = v.rearrange("p (a b c) -> p a b c", a=shape[1], b=shape[2])
        return v

    def sb(self, name, shape, dt=F32):
        shape = list(shape)
        nel = int(np.prod(shape[1:]))
        esz = 2 if dt == BF16 else 4
        nbytes = (nel * esz + 31) // 32 * 32
        off = self.sb_ptr
        self.sb_ptr += nbytes
        assert self.sb_ptr <= self.SB_BYTES, ("SBUF arena overflow", name, self.sb_ptr)
        base = self.sb_bf if esz == 2 else self.sb_f32
        v = base[0:shape[0], off // esz:off // esz + nel]
        return Buf(self._shape_view(v, shape), name)

    def ps(self, name, shape, dt=F32):
        shape = list(shape)
        nel = int(np.prod(shape[1:]))
        esz = 2 if dt == BF16 else 4
        nb = (nel * esz + 2047) // 2048
        off = self.ps_ptr * 2048
        self.ps_ptr += nb
        assert self.ps_ptr <= 8, ("PSUM arena overflow", name)
        base = self.ps_bf if esz == 2 else self.ps_f32
        v = base[0:shape[0], off // esz:off // esz + nel]
        b = Buf(self._shape_view(v, shape), name, bank=[None])
        b.alt = self.ps_bf[:, off // 2:off // 2 + nb * 1024]
        return b

    def mark(self):
        return (self.sb_ptr, self.ps_ptr)

    def reset(self, m):
        self.sb_ptr, self.ps_ptr = m

    def barrier(self):
        if self.bar is None:
            self.bar = {e: self.sb("bar_" + e, [128, 8]) for e in ("act", "dve", "pool")}
        marks = []
        for eng in ("act", "dve", "pool"):
            b = self.bar[eng]
            if eng == "act":
                marks.append(self.op(eng, lambda e, b=b: e.memzero(b[:]), w=[b]))
            else:
                marks.append(self.op(eng, lambda e, b=b: e.memset(b[:], 0.0), w=[b]))
        latest = {}
        for d in self.dma_log:
            latest[id(d.dbuf)] = d
        self.dma_log = []
        first = True
        for eng in self.ENGS:
            op = Op(eng, None)
            op.is_bar = first
            first = False
            op.deps = marks + list(latest.values())
            self.ops.append(op)

    def dram(self, name, shape, dt, kind="Internal"):
        t = self.nc.dram_tensor(name, list(shape), dt, kind=kind)
        b = Buf(t.ap(), name)
        b.is_out = kind == "ExternalOutput"
        return b

    def newsem(self, name):
        self.nsem += 1
        return self.stack.enter_context(self.nc.semaphore(f"{name}_{self.nsem}"))

    def _rec(self, op, r, w):
        deps = []
        for b in r:
            deps.extend(b.w)
        for b in w:
            for d in b.w:
                if not (d.eng == "pe" and op.eng == "pe" and not d.is_dma and not op.is_dma):
                    deps.append(d)
            deps.extend(b.r)
        for b in r:
            b.r.append(op)
        for b in w:
            b.w = [op]
            b.r = []
        for b in list(r) + list(w):
            if b.bank is not None:
                d = b.bank[0]
                if d is not None and d.eng != op.eng:
                    deps.append(d)
                b.bank[0] = op
        seen = set()
        for d in deps:
            if id(d) not in seen and d is not op:
                seen.add(id(d))
                op.deps.append(d)
        self.ops.append(op)
        return op

    def op(self, eng, fn, r=(), w=()):
        return self._rec(Op(eng, fn), r, w)

    def dma(self, q, out, in_, r=(), w=(), sembuf=None, **kw):
        if sembuf is None:
            sembuf = (list(w) + list(r))[0]
        if isinstance(out, Buf):
            out = out.t
        if isinstance(in_, Buf):
            in_ = in_.t
        op = Op(q, lambda e: e.dma_start(out=out, in_=in_, **kw), is_dma=True, dbuf=sembuf)
        self._rec(op, r, w)
        self.dma_log.append(op)
        if any(b.is_out for b in w):
            self.out_ops.append(op)
        return op

    def collective(self, kind, groups, in_buf, out_buf):
        op = Op("pool", lambda e: e.collective_compute(kind, ALU.bypass, replica_groups=groups,
                                                       ins=[in_buf.t.opt()], outs=[out_buf.t.opt()]),
                is_dma=True, dbuf=out_buf, inc=1)
        self.dma_log.append(op)
        return self._rec(op, [in_buf], [out_buf])

    def build(self):
        nc = self.nc
        fin = Op("sp", None)
        fin.deps = list(self.out_ops)
        self.ops.append(fin)
        for op in self.ops:
            if op.is_dma:
                op.marked = True
            for d in op.deps:
                d.marked = True
        cnt = {e: 0 for e in self.ENGS}
        esem = {}
        free_d = []
        assigned = []
        for op in self.ops:
            if op.is_bar:
                for b in assigned:
                    if b.dsem is not None:
                        free_d.append(b.dsem)
                        b.dsem = None
                assigned = []
            if not op.marked:
                continue
            if op.is_dma:
                b = op.dbuf
                if b.dsem is None:
                    b.dsem = free_d.pop() if free_d else [self.newsem("d"), 0]
                    assigned.append(b)
                sm = b.dsem
                sm[1] += op.inc
                op.sem, op.val = sm[0], sm[1]
                if sm[1] >= EPOCH:
                    b.dsem = None
            else:
                e = op.eng
                if e not in esem or cnt[e] >= EPOCH:
                    esem[e] = self.newsem(e)
                    cnt[e] = 0
                cnt[e] += 1
                op.sem, op.val = esem[e], cnt[e]
        known = {e: {} for e in self.ENGS}
        nwaits = 0
        for op in self.ops:
            k = known[op.eng]
            waits = {}
            for d in op.deps:
                key = id(d.sem)
                if k.get(key, 0) >= d.val:
                    continue
                waits[key] = (d.sem, d.val)
                k[key] = d.val
                for s, v in d.snap.items():
                    if k.get(s, 0) < v:
                        k[s] = v
            op.waits = list(waits.values())
            nwaits += len(op.waits)
            if op.marked:
                op.snap = dict(k)
                op.snap[id(op.sem)] = op.val
        per = {e: [o for o in self.ops if o.eng == e] for e in self.ENGS}
        self.stats = dict(n_ops=len(self.ops), n_waits=nwaits, n_sems=self.nsem,
                          per_eng={e: len(v) for e, v in per.items()})

        def emit(engobj, lst):
            for op in lst:
                for s, v in op.waits:
                    engobj.wait_ge(s, v)
                if op.fn is None:
                    continue
                ins = op.fn(engobj)
                if op.marked:
                    ins.then_inc(op.sem, op.inc if op.is_dma else 1)

        with nc.Block() as block:
            @block.tensor
            def _(e):
                emit(e, per["pe"])

            @block.scalar
            def _(e):
                emit(e, per["act"])

            @block.vector
            def _(e):
                emit(e, per["dve"])

            @block.gpsimd
            def _(e):
                emit(e, per["pool"])

            @block.sync
            def _(e):
                emit(e, per["sp"])
        self.stack.close()
        return nc


EPS = 1e-6


def interleave(gens_with_counts):
    gens = [[g, max(1, n), 0.0] for g, n in gens_with_counts]
    total = max(n for _, n, _ in gens)
    alive = list(gens)
    while alive:
        for item in list(alive):
            g, n, acc = item
            item[2] += n / total
            while item[2] >= 1.0 - 1e-9:
                item[2] -= 1.0
                try:
                    next(g)
                except StopIteration:
                    alive.remove(item)
                    break


def bcast_row(ap_row, n=128):
    ap_row = ap_row.t if isinstance(ap_row, Buf) else ap_row
    pairs = [list(p) for p in ap_row.ap]
    w = pairs[-1]
    return bass.AP(ap_row.tensor, ap_row.offset, [[0, n], [w[0], w[1]]])


def emit_rstd(P, eng, ss, rstd, n, r_extra=()):
    pass


class TokCtx:
    def __init__(self, P, ident):
        self.P = P
        self.ident = ident
        self.junk = P.sb("junk", [128, 1024], BF16)

    def sumsq_rstd(self, src_ap, src_bufs, ss, rstd, n):
        P = self.P
        j = self.junk
        P.op("act", lambda e: e.activation(out=j[:, 0:src_ap.shape[-1]], in_=src_ap, func=AF.Square,
                                           accum_out=ss[:, 0:1]), r=src_bufs, w=[ss])
        P.op("act", lambda e: e.activation(out=rstd[:, 0:1], in_=ss[:, 0:1], func=AF.Sqrt, bias=EPS, scale=1.0 / n),
             r=[ss], w=[rstd])
        P.op("dve", lambda e: e.reciprocal(out=rstd[:, 0:1], in_=rstd[:, 0:1]), r=[rstd], w=[rstd])


def load_modrow(P, dst, mod_all, s_, part):
    mt = mod_all.t
    src = bass.AP(mt.tensor, mt.offset + s_ * 768 + part * 256, [[0, 128], [3072, 4], [1, 256]])
    P.dma("sp", dst[:].rearrange("p (r c) -> p r c", r=4), src, r=[mod_all], w=[dst])


def setup_mod_rows(P, mod_all, ng, sA, sB, i_gpost, i_gpre, gg, gmod, shift, scratch):
    load_modrow(P, gg, mod_all, sA, 2)
    P.dma("sp", scratch[0][:], bcast_row(ng[i_gpost:i_gpost + 1, :]), w=[scratch[0]])
    P.op("dve", lambda e: e.tensor_tensor(out=gg[:], in0=gg[:], in1=scratch[0][:], op=ALU.mult),
         r=[gg, scratch[0]], w=[gg])
    if gmod is not None:
        load_modrow(P, gmod, mod_all, sB, 1)
        P.dma("sp", scratch[1][:], bcast_row(ng[i_gpre:i_gpre + 1, :]), w=[scratch[1]])
        P.op("dve", lambda e: e.scalar_tensor_tensor(out=gmod[:], in0=gmod[:], scalar=1.0, in1=scratch[1][:],
                                                     op0=ALU.add, op1=ALU.mult), r=[gmod, scratch[1]], w=[gmod])
        load_modrow(P, shift, mod_all, sB, 0)


def emit_prenorm_T(P, T, xsrc, xbufs, gmod, shift, psT, hT_out_ap, hT_bufs, tmp, hb, ss, rstd, psT_view=None):
    T.sumsq_rstd(xsrc, xbufs, ss, rstd, 1024)
    P.op("dve", lambda e: e.scalar_tensor_tensor(out=tmp[:], in0=xsrc, scalar=rstd[:, 0:1], in1=gmod[:],
                                                 op0=ALU.mult, op1=ALU.mult), r=list(xbufs) + [rstd, gmod], w=[tmp])
    P.op("pool", lambda e: e.tensor_tensor(out=hb[:], in0=tmp[:], in1=shift[:], op=ALU.add),
         r=[tmp, shift], w=[hb])
    pv = psT.alt if psT_view is None else psT_view
    for k in range(8):
        P.op("pe", lambda e, k=k: e.transpose(out=pv[:, k * 128:(k + 1) * 128], in_=hb[:, k * 128:(k + 1) * 128],
                                              identity=T.ident[:]), r=[hb, T.ident], w=[psT])
    P.op("act", lambda e: e.copy(out=hT_out_ap, in_=pv[:, 0:1024].rearrange("p (k t) -> p k t", k=8)),
         r=[psT], w=hT_bufs)


def emit_post_residual(P, T, osrc, obufs, x_t, gg, xo, ss, rstd):
    T.sumsq_rstd(osrc, obufs, ss, rstd, 1024)
    P.op("dve", lambda e: e.scalar_tensor_tensor(out=xo[:], in0=osrc, scalar=rstd[:, 0:1], in1=gg[:],
                                                 op0=ALU.mult, op1=ALU.mult), r=list(obufs) + [rstd, gg], w=[xo])
    P.op("pool", lambda e: e.tensor_tensor(out=xo[:], in0=xo[:], in1=x_t[:], op=ALU.add),
         r=[xo, x_t], w=[xo])


NT = 2048


def phase_outproj(P, C, io, FC, even, layer):
    m0 = P.mark()
    x, wout, xn, hT = io["x"], io["wout"], io["xn"], io["hT"]
    yT_all = io["yT_all"]
    T = TokCtx(P, C["ident_bf"])
    gg = P.sb("gg", [128, 1024])
    gmod = P.sb("gmod", [128, 1024])
    shift = P.sb("shift", [128, 1024])
    xt = [P.sb(f"xt{i}", [128, 1024]) for i in range(2)]
    setup_mod_rows(P, io["mod_all"], io["ng"], 2 * layer, 2 * layer + 1, layer * 4 + 1, layer * 4 + 2, gg, gmod, shift, xt)
    if even:
        ss_in = P.sb("ss_in", [128, 4, 16])
        rs_ssd = P.sb("rs_ssd", [128, 16])
        dyn_dma(P, ss_in[:], lambda: io["ss_all"].t.rearrange("(h p) c -> p h c", p=128)[:, :, bass.ds(P.qc, 16)],
                r=[io["ss_all"]], w=[ss_in])
        P.op("dve", lambda e: e.tensor_reduce(out=rs_ssd[:], in_=ss_in[:].rearrange("p h t -> p t h"), axis=AX.X,
                                              op=ALU.add), r=[ss_in], w=[rs_ssd])
        P.op("act", lambda e: e.activation(out=rs_ssd[:], in_=rs_ssd[:], func=AF.Sqrt, bias=EPS, scale=1.0 / 1024),
             r=[rs_ssd], w=[rs_ssd])
        P.op("dve", lambda e: e.reciprocal(out=rs_ssd[:], in_=rs_ssd[:]), r=[rs_ssd], w=[rs_ssd])
    w_sb = [P.sb(f"wo{k}", [128, 1024], BF16) for k in range(FC)]
    for k in range(FC):
        P.dma("pool", w_sb[k][:], wout[k * 128:(k + 1) * 128, :], w=[w_sb[k]])
    NGL = io["ngroups"] // 4
    yq = [P.sb(f"yq{i}", [128, 4, NT], BF16) for i in range(NGL)]
    for gl in range(NGL):
        for hf in range(2):
            src = yT_all[2 * gl + hf]
            dyn_dma(P, yq[gl][hf * 64:(hf + 1) * 64, :, :], lambda src=src: src.t.rearrange("(h p) t -> p h t", p=64)[
                :, :, bass.ds(P.qv, NT)], r=[src], w=[yq[gl]])
    xo = [P.sb(f"xo{i}", [128, 1024]) for i in range(2)]
    tmp = [P.sb(f"tmp{i}", [128, 1024]) for i in range(2)]
    hb = [P.sb(f"hb{i}", [128, 1024], BF16) for i in range(2)]
    hTs = [P.sb(f"hTs{i}", [128, 8, 128], BF16) for i in range(2)]
    osb = [P.sb(f"osb{i}", [128, 1024]) for i in range(2)]
    dsb = P.sb("dsb", [128, 1024])
    ss = [P.sb(f"ss{i}", [128, 2]) for i in range(4)]
    rstd = [P.sb(f"rstd{i}", [128, 2]) for i in range(4)]
    psA = [P.ps(f"psA{i}", [128, 1024]) for i in range(2)]
    psB = P.ps("psB", [128, 1024]) if even else None
    psT = [P.ps(f"psT{i}", [128, 1024], BF16) for i in range(2)]
    nA = FC // 2 if even else FC
    for t in range(16):
        q4, s4 = divmod(t, 4)
        gmap = io["rowmap"]
        x_t = xt[t % 2]
        P.dma("sp", x_t[:], x[t * 128:(t + 1) * 128, :], r=[x], w=[x_t])
        pa = psA[t % 2]
        for k in range(nA):
            for hlf in range(2):
                hg_, gl_ = gmap[k]
                yb = yq[gl_]
                P.op("pe", lambda e, k=k, hlf=hlf, pa=pa, yb=yb, hg_=hg_, t=t: e.matmul(
                    pa[:, hlf * 512:(hlf + 1) * 512], lhsT=yb[:, hg_, t * 128:(t + 1) * 128],
                    rhs=w_sb[k][:, hlf * 512:(hlf + 1) * 512], start=(k == 0), stop=(k == nA - 1)),
                    r=[yb, w_sb[k]], w=[pa])
        if even:
            for k in range(nA, FC):
                for hlf in range(2):
                    hg_, gl_ = gmap[k]
                    yb = yq[gl_]
                    P.op("pe", lambda e, k=k, hlf=hlf, yb=yb, hg_=hg_, t=t: e.matmul(
                        psB[:, hlf * 512:(hlf + 1) * 512], lhsT=yb[:, hg_, t * 128:(t + 1) * 128],
                        rhs=w_sb[k][:, hlf * 512:(hlf + 1) * 512], start=(k == nA), stop=(k == FC - 1)),
                        r=[yb, w_sb[k]], w=[psB])
            P.op("act", lambda e: e.copy(out=dsb[:], in_=psB[:]), r=[psB], w=[dsb])
            o_t = osb[t % 2]
            P.op("dve", lambda e, pa=pa, o_t=o_t, t=t: e.scalar_tensor_tensor(
                out=o_t[:], in0=pa[:], scalar=rs_ssd[:, t:t + 1], in1=dsb[:], op0=ALU.mult, op1=ALU.add),
                r=[pa, rs_ssd, dsb], w=[o_t])
            osrc, obufs = o_t[:], [o_t]
        else:
            osrc, obufs = pa[:], [pa]
        xo_t = xo[t % 2]
        emit_post_residual(P, T, osrc, obufs, x_t, gg, xo_t, ss[0], rstd[0])
        P.dma("act", xn[t * 128:(t + 1) * 128, :], xo_t[:], r=[xo_t], w=[xn])
        hts = hTs[t % 2]
        emit_prenorm_T(P, T, xo_t[:], [xo_t], gmod, shift, psT[t % 2], hts[:], [hts], tmp[1], hb[t % 2],
                       ss[1], rstd[1])
        P.dma("sp", hT[:, t * 128:(t + 1) * 128].rearrange("(k p) t -> p k t", p=128), hts[:], r=[hts], w=[hT])
    P.reset(m0)


def phase_mlp(P, C, io, next_pre, layer):
    m0 = P.mark()
    xn, hT, w1, w2, xo_d = io["xn"], io["hT"], io["w1"], io["w2"], io["xo"]
    hTn = io.get("hTn")
    T = TokCtx(P, C["ident_bf"])
    gg = P.sb("gg", [128, 1024])
    gmod = P.sb("gmod", [128, 1024]) if next_pre else None
    shift = P.sb("shift", [128, 1024]) if next_pre else None
    xt = [P.sb(f"xt{i}", [128, 1024]) for i in range(2)]
    setup_mod_rows(P, io["mod_all"], io["ng"], 2 * layer + 1, 2 * layer + 2, layer * 4 + 3, layer * 4 + 4, gg, gmod, shift, xt)
    w1_sb = [[P.sb(f"w1_{k}_{cb}", [128, 1024], BF16) for cb in range(4)] for k in range(8)]
    w2_sb = [P.sb(f"w2_{k}", [128, 1024], BF16) for k in range(32)]
    for cb in range(4):
        for k in range(8):
            P.dma("pool", w1_sb[k][cb][:], w1[k * 128:(k + 1) * 128, cb * 1024:(cb + 1) * 1024], w=[w1_sb[k][cb]])
    for k in range(32):
        P.dma("pool", w2_sb[k][:], w2[k * 128:(k + 1) * 128, :], w=[w2_sb[k]])
    ST = 256
    ht = [P.sb(f"ht{i}", [128, 8, ST], BF16) for i in range(2)]
    aT = [P.sb(f"aT{i}", [128, 2, ST], BF16) for i in range(16)]
    rl = [P.sb(f"rl{i}", [128, 2, ST]) for i in range(2)]
    xo = [P.sb(f"xo{i}", [128, 1024]) for i in range(2)]
    tmp = [None, P.sb("tmp1", [128, 1024])]
    ss = [P.sb(f"ss{i}", [128, 2]) for i in range(2)]
    rstd = [P.sb(f"rstd{i}", [128, 2]) for i in range(2)]
    psU = [P.ps(f"psU{i}", [128, 2, ST]) for i in range(3)]
    psO = [P.ps(f"psO{i}", [128, 1024]) for i in range(2)]
    psT = P.ps("psT", [128, 1024], BF16)
    if next_pre:
        hb = [P.sb("hb0", [128, 1024], BF16)] * 2
        hTs = [P.sb(f"hTs{i}", [128, 8, 128], BF16) for i in range(2)]
    nu = 0
    for st in range(NT // ST):
        h_t = ht[st % 2]
        P.dma("sp", h_t[:], hT[:, st * ST:(st + 1) * ST].rearrange("(k p) t -> p k t", p=128), r=[hT], w=[h_t])
        aTs = aT
        for fp in range(16):
            pu = psU[nu % 3]
            r_t = rl[nu % 2]
            nu += 1
            for j in range(2):
                f = fp * 2 + j
                for k in range(8):
                    wb = w1_sb[k][f // 8]
                    P.op("pe", lambda e, pu=pu, j=j, f=f, k=k, h_t=h_t, wb=wb: e.matmul(
                        pu[:, j, :], lhsT=wb[:, (f % 8) * 128:(f % 8 + 1) * 128], rhs=h_t[:, k, :],
                        start=(k == 0), stop=(k == 7)), r=[wb, h_t], w=[pu])
            P.op("act", lambda e, pu=pu, r_t=r_t: e.activation(out=r_t[:], in_=pu[:], func=AF.Relu),
                 r=[pu], w=[r_t])
            a_t = aTs[fp]
            P.op("dve" if fp % 2 == 0 else "pool", lambda e, r_t=r_t, a_t=a_t: e.tensor_tensor(
                out=a_t[:], in0=r_t[:], in1=r_t[:], op=ALU.mult), r=[r_t], w=[a_t])
        for sub in range(ST // 128):
            t = st * (ST // 128) + sub
            po = psO[t % 2]
            for f in range(32):
                for hlf in range(2):
                    P.op("pe", lambda e, po=po, f=f, hlf=hlf, sub=sub, aTs=aTs: e.matmul(
                        po[:, hlf * 512:(hlf + 1) * 512], lhsT=aTs[f // 2][:, f % 2, sub * 128:(sub + 1) * 128],
                        rhs=w2_sb[f][:, hlf * 512:(hlf + 1) * 512], start=(f == 0), stop=(f == 31)),
                        r=[aTs[f // 2], w2_sb[f]], w=[po])
            x_t = xt[t % 2]
            P.dma("sp", x_t[:], xn[t * 128:(t + 1) * 128, :], r=[xn], w=[x_t])
            xo_t = xo[t % 2]
            emit_post_residual(P, T, po[:], [po], x_t, gg, xo_t, ss[0], rstd[0])
            P.dma("act", xo_d[t * 128:(t + 1) * 128, :], xo_t[:], r=[xo_t], w=[xo_d])
            if next_pre:
                hts = hTs[t % 2]
                emit_prenorm_T(P, T, xo_t[:], [xo_t], gmod, shift, psT, hts[:], [hts], tmp[1], hb[t % 2],
                               ss[1], rstd[1])
                for c4 in range(4):
                    P.dma("sp", hTn[c4][:, t * 128:(t + 1) * 128].rearrange("(k p) t -> p k t", p=128),
                          hts[:, 2 * c4:2 * c4 + 2, :], r=[hts], w=[hTn[c4]])
    P.reset(m0)


S_LEN = 8192
NEGV = -30000.0
E_NCOL = 1024 + 516


def host_consts():
    c = {}
    c["ident_bf"] = np.eye(128, dtype=np.float32).astype(ml_dtypes.bfloat16)
    c["ident_f"] = np.eye(128, dtype=np.float32)
    i = np.arange(128)
    c["triU"] = (i[:, None] <= i[None, :]).astype(np.float32)
    c["triS"] = (i[:, None] > i[None, :]).astype(np.float32)
    c["NEG"] = np.where(i[None, :] < i[:, None], NEGV, 0.0).astype(np.float32)
    c["ones_f"] = np.ones((128, 128), np.float32)
    c["ones_bf"] = np.ones((128, 128), np.float32).astype(ml_dtypes.bfloat16)
    c["antiI"] = np.ascontiguousarray(np.eye(128, dtype=np.float32)[::-1])
    return c


def t5_bucket_np(d):
    d = np.maximum(d, 0)
    logd = np.log(np.maximum(d, 1).astype(np.float32) / np.float32(16))
    large = 16 + (logd / np.float32(np.log(128 / 16)) * np.float32(16)).astype(np.int32)
    large = np.minimum(large, 31)
    return np.where(d < 16, d, large)


def host_ohd(swa_prev=False):
    d = np.arange(383) - 127
    oh = np.zeros((33, 384), np.float32)
    oh[t5_bucket_np(d), np.arange(383)] = 1.0
    if swa_prev:
        oh[32, :383] = np.where((d < 0) | (d >= 128), NEGV, 0.0)
    else:
        oh[32, :383] = np.where(d < 0, NEGV, 0.0)
    oh[:32, :383] *= (d >= 0)[None, :]
    return oh


def phase_even(P, C, io, do_ssd=True, do_diff=True, n_super=16):
    m0 = P.mark()
    x, win, convw, convb, hv, normw = io["x_b"], io["e_win"], io["convw"], io["convb"], io["hv"], io["normw"]
    lam, subw, relb, ohd = io["lam"], io["subw"], io["relb"], io["ohdA"]
    yT, ss_out, vecd = io["yT_loc"], io["ss_loc"], io["vecdA"]
    T = TokCtx(P, C["ident_bf"])
    gmod = P.sb("gmod", [128, 1024])
    shift = P.sb("shift", [128, 1024])
    xt = [P.sb(f"xt{i}", [128, 1024]) for i in range(2)]
    load_modrow(P, gmod, io["mod_all"], 0, 1)
    P.dma("sp", xt[0][:], bcast_row(io["ng"][0:1, :]), w=[xt[0]])
    P.op("dve", lambda e: e.scalar_tensor_tensor(out=gmod[:], in0=gmod[:], scalar=1.0, in1=xt[0][:],
                                                 op0=ALU.add, op1=ALU.mult), r=[gmod, xt[0]], w=[gmod])
    load_modrow(P, shift, io["mod_all"], 0, 0)
    w_sb = [P.sb(f"win{k}", [128, E_NCOL], BF16) for k in range(8)]
    for k in range(8):
        P.dma("pool", w_sb[k][:], win[k * 128:(k + 1) * 128, :], w=[w_sb[k]])
    cw = P.sb("cw", [128, 4, 4])
    cb = P.sb("cb", [128, 4])
    nw = P.sb("nw", [128, 2])
    hvb = P.sb("hvb", [128, 3, 4])
    P.dma("sp", cw[:], convw, w=[cw])
    P.dma("sp", cb[:], convb, w=[cb])
    P.dma("sp", nw[:], normw, w=[nw])
    P.dma("sp", hvb[:], bass.AP(hv.t.tensor, hv.t.offset, [[0, 128], [4, 3], [1, 4]]), w=[hvb])
    Aneg = P.sb("Aneg", [128, 4])
    P.op("act", lambda e: e.activation(out=Aneg[:], in_=hvb[:, 1, :], func=AF.Exp), r=[hvb], w=[Aneg])
    P.op("dve", lambda e: e.tensor_scalar(out=Aneg[:], in0=Aneg[:], scalar1=-1.0, scalar2=None, op0=ALU.mult),
         r=[Aneg], w=[Aneg])

    hT = [P.sb(f"hT{i}", [128, 8, 512], BF16) for i in range(2)]
    hb = P.sb("hb", [128, 1024], BF16)
    tmp = P.sb("tmp", [128, 1024])
    ssq = P.sb("ssq", [128, 2])
    rstd = P.sb("rstd", [128, 2])
    pAB = P.ps("pAB", [128, 512])
    pAB_bf = pAB.alt
    bk1 = P.ps("bk1", [128, 512])
    bk2 = P.ps("bk2", [128, 512])
    bk3 = P.ps("bk3", [128, 512])
    pdt = Buf(bk3.t[:, 256:272], "pdt", bank=bk3.bank)
    raw = [P.sb(f"raw{j}", [128, 3 + 512]) for j in range(4)]
    for j in range(4):
        P.op("pool", lambda e, j=j: e.memset(raw[j][:, 0:3], 0.0), w=[raw[j]])
    acc = P.sb("acc", [128, 512])
    cvT = [P.sb(f"cvT{j}", [128, 512], BF16) for j in range(4)]
    z_sb = [P.sb(f"z_sb{i}", [128, 256]) for i in range(4)]
    dtr = [P.sb(f"dtr{i}", [128, 4]) for i in range(4)]
    if do_diff:
        KT = [P.sb(f"KT{h}", [128, S_LEN], BF16) for h in range(2)]
        VV = [P.sb(f"V{h}", [128, 64, 128], BF16) for h in range(2)]
        QT = [P.sb(f"QT{h}", [128, 512], BF16) for h in range(2)]
    if do_ssd:
        S_f = P.sb("S_f", [128, 256])
        S_b = P.sb("S_b", [128, 256], BF16)
        P.op("pool", lambda e: e.memset(S_f[:], 0.0), w=[S_f])
        P.op("pool", lambda e: e.memset(S_b[:], 0.0), w=[S_b])
        ss_all = P.sb("ss_all", [128, 64])
        p_bc = Buf(bk1.t[:, 0:256].rearrange("p (a b) -> p a b", a=2), "p_bc", bank=bk1.bank)
        p_y = Buf(bk1.t[:, 256:512], "p_y", bank=bk1.bank)
        p_cbxt = Buf(bk2.t[:, 0:256], "p_cbxt", bank=bk2.bank)
        p_S = Buf(bk2.t[:, 256:512], "p_S", bank=bk2.bank)
        p_sm = Buf(bk3.t[:, 0:256], "p_sm", bank=bk3.bank)
        p_cb = p_cbxt
        p_xt = bk2.alt
        p_smb = bk3.alt
        W = {}
        for nme, shp, dt in [("dt", [128, 4], F32), ("a", [128, 4], F32), ("nacs", [128, 4], F32),
                             ("E", [128, 4], F32), ("dte", [128, 4], F32), ("dec", [128, 4], F32),
                             ("trr", [128, 2, 128], F32), ("decay", [128, 4, 128], F32), ("cbs", [128, 128], F32),
                             ("M", [128, 4, 128], BF16), ("xs_tok", [128, 256], F32), ("X", [128, 256], BF16),
                             ("Xd", [128, 256], BF16), ("B_tok", [128, 128], BF16), ("yo", [128, 256], F32),
                             ("y", [128, 256], F32), ("sz", [128, 256], F32), ("y_bf", [128, 256], BF16),
                             ("yTs", [128, 2, 128], BF16), ("Dbc", [128, 4, 64], F32)]:
            W[nme] = P.sb("w_" + nme, shp, dt)
        for r_ in range(4):
            P.op("act", lambda e, r_=r_: e.activation(out=W["Dbc"][:, r_, :], in_=C["ones_f"][:, 0:64], func=AF.Copy,
                                                      scale=hvb[:, 2, r_:r_ + 1]), r=[C["ones_f"], hvb], w=[W["Dbc"]])

    def ssd_chunk(c, st, sub):
        cs = slice(sub * 128, (sub + 1) * 128)
        zt, dt_raw = z_sb[sub], dtr[sub]
        P.op("dve", lambda e: e.tensor_tensor(out=W["dt"][:], in0=dt_raw[:], in1=hvb[:, 0, :], op=ALU.add),
             r=[dt_raw, hvb], w=[W["dt"]])
        P.op("act", lambda e: e.activation(out=W["dt"][:], in_=W["dt"][:], func=AF.Exp), r=[W["dt"]], w=[W["dt"]])
        P.op("act", lambda e: e.activation(out=W["dt"][:], in_=W["dt"][:], func=AF.Ln, bias=1.0, scale=1.0),
             r=[W["dt"]], w=[W["dt"]])
        P.op("dve", lambda e: e.tensor_tensor(out=W["a"][:], in0=W["dt"][:], in1=Aneg[:], op=ALU.mult),
             r=[W["dt"], Aneg], w=[W["a"]])
        yield
        P.op("pe", lambda e: e.matmul(p_sm[:, 0:4], lhsT=C["triU"][:], rhs=W["a"][:], start=True, stop=True),
             r=[C["triU"], W["a"]], w=[p_sm])
        P.op("pe", lambda e: e.matmul(p_sm[:, 4:8], lhsT=C["triS"][:], rhs=W["a"][:], start=True, stop=True),
             r=[C["triS"], W["a"]], w=[p_sm])
        P.op("dve", lambda e: e.tensor_scalar(out=W["nacs"][:], in0=p_sm[:, 0:4], scalar1=-1.0, scalar2=None,
                                              op0=ALU.mult), r=[p_sm], w=[W["nacs"]])
        P.op("act", lambda e: e.activation(out=W["E"][:], in_=p_sm[:, 0:4], func=AF.Exp), r=[p_sm], w=[W["E"]])
        P.op("act", lambda e: e.activation(out=W["dte"][:], in_=p_sm[:, 4:8], func=AF.Exp), r=[p_sm], w=[W["dte"]])
        P.op("dve", lambda e: e.tensor_tensor(out=W["dec"][:], in0=W["E"][:], in1=W["dte"][:], op=ALU.mult),
             r=[W["E"], W["dte"]], w=[W["dec"]])
        yield
        for half in range(2):
            yield
            for rr in range(2):
                r_ = half * 2 + rr
                P.op("dve" if rr == 0 else "pool", lambda e, r_=r_, rr=rr: e.tensor_scalar(
                    out=W["trr"][:, rr, :], in0=C["triU"][:], scalar1=W["a"][:, r_:r_ + 1], scalar2=None,
                    op0=ALU.mult), r=[C["triU"], W["a"]], w=[W["trr"]])
            for rr in range(2):
                P.op("pe", lambda e, rr=rr: e.matmul(p_bc[:, rr, :], lhsT=C["ones_f"][:], rhs=W["trr"][:, rr, :],
                                                     start=True, stop=False), r=[C["ones_f"], W["trr"]], w=[p_bc])
                P.op("pe", lambda e, rr=rr: e.matmul(p_bc[:, rr, :], lhsT=C["ident_f"][:], rhs=C["NEG"][:],
                                                     start=False, stop=True), r=[C["ident_f"], C["NEG"]], w=[p_bc])
            for rr in range(2):
                r_ = half * 2 + rr
                P.op("act", lambda e, r_=r_, rr=rr: e.activation(
                    out=W["decay"][:, r_, :], in_=p_bc[:, rr, :], func=AF.Exp, bias=W["nacs"][:, r_:r_ + 1],
                    scale=1.0), r=[p_bc, W["nacs"]], w=[W["decay"]])
        yield
        P.op("pe", lambda e: e.matmul(p_cb[:, 0:128], lhsT=cvT[2][:, cs], rhs=cvT[3][:, cs], start=True, stop=True),
             r=[cvT[2], cvT[3]], w=[p_cbxt])
        P.op("act", lambda e: e.copy(out=W["cbs"][:], in_=p_cb[:, 0:128]), r=[p_cbxt], w=[W["cbs"]])
        for r_ in range(4):
            P.op("dve" if r_ % 2 == 0 else "pool", lambda e, r_=r_: e.tensor_tensor(
                out=W["M"][:, r_, :], in0=W["decay"][:, r_, :], in1=W["cbs"][:], op=ALU.mult),
                r=[W["decay"], W["cbs"]], w=[W["M"]])
        yield
        for j in range(2):
            P.op("pe", lambda e, j=j: e.transpose(out=p_xt[:, 256 + j * 128:256 + (j + 1) * 128], in_=cvT[j][:, cs],
                                                  identity=C["ident_bf"][:]), r=[cvT[j], C["ident_bf"]], w=[p_cbxt])
        P.op("pe", lambda e: e.transpose(out=p_smb[:, 128:256], in_=cvT[2][:, cs], identity=C["ident_bf"][:]),
             r=[cvT[2], C["ident_bf"]], w=[p_sm])
        P.op("act", lambda e: e.copy(out=W["xs_tok"][:], in_=p_xt[:, 256:512]), r=[p_cbxt], w=[W["xs_tok"]])
        P.op("act", lambda e: e.copy(out=W["B_tok"][:], in_=p_smb[:, 128:256]), r=[p_sm], w=[W["B_tok"]])
        for r_ in range(4):
            hs = slice(r_ * 64, (r_ + 1) * 64)
            P.op("dve", lambda e, r_=r_, hs=hs: e.tensor_scalar(
                out=W["X"][:, hs], in0=W["xs_tok"][:, hs], scalar1=W["dt"][:, r_:r_ + 1], scalar2=None,
                op0=ALU.mult), r=[W["xs_tok"], W["dt"]], w=[W["X"]])
            P.op("pool", lambda e, r_=r_, hs=hs: e.tensor_scalar(
                out=W["Xd"][:, hs], in0=W["X"][:, hs], scalar1=W["dte"][:, r_:r_ + 1], scalar2=None,
                op0=ALU.mult), r=[W["X"], W["dte"]], w=[W["Xd"]])
        yield
        P.op("pe", lambda e: e.matmul(p_y[:], lhsT=cvT[3][:, cs], rhs=S_b[:], start=True, stop=True),
             r=[cvT[3], S_b], w=[p_y])
        for r_ in range(4):
            hs = slice(r_ * 64, (r_ + 1) * 64)
            P.op("act", lambda e, r_=r_, hs=hs: e.activation(out=W["yo"][:, hs], in_=p_y[:, hs], func=AF.Copy,
                                                             scale=W["E"][:, r_:r_ + 1]), r=[p_y, W["E"]], w=[W["yo"]])
        yield
        for r_ in range(4):
            hs = slice(r_ * 64, (r_ + 1) * 64)
            P.op("pe", lambda e, r_=r_, hs=hs: e.matmul(p_y[:, hs], lhsT=W["M"][:, r_, :], rhs=W["X"][:, hs],
                                                        start=True, stop=True), r=[W["M"], W["X"]], w=[p_y])
        P.op("dve", lambda e: e.tensor_tensor(out=W["y"][:], in0=p_y[:], in1=W["yo"][:], op=ALU.add),
             r=[p_y, W["yo"]], w=[W["y"]])
        P.op("pool", lambda e: e.tensor_tensor(out=W["yo"][:], in0=W["xs_tok"][:],
                                               in1=W["Dbc"][:].rearrange("p a b -> p (a b)"), op=ALU.mult),
             r=[W["xs_tok"], W["Dbc"]], w=[W["yo"]])
        P.op("dve", lambda e: e.tensor_tensor(out=W["y"][:], in0=W["y"][:], in1=W["yo"][:], op=ALU.add),
             r=[W["y"], W["yo"]], w=[W["y"]])
        yield
        P.op("act", lambda e: e.activation(out=W["sz"][:], in_=zt[:], func=AF.Silu), r=[zt], w=[W["sz"]])
        P.op("dve", lambda e: e.tensor_tensor(out=W["y"][:], in0=W["y"][:], in1=W["sz"][:], op=ALU.mult),
             r=[W["y"], W["sz"]], w=[W["y"]])
        P.op("act", lambda e: e.activation(out=W["sz"][:], in_=W["y"][:], func=AF.Square,
                                           accum_out=ss_all[:, c:c + 1]), r=[W["y"]], w=[W["sz"], ss_all])
        P.op("pool", lambda e: e.tensor_copy(out=W["y_bf"][:], in_=W["y"][:]), r=[W["y"]], w=[W["y_bf"]])
        for j in range(2):
            P.op("pe", lambda e, j=j: e.transpose(out=p_smb[:, 256 + j * 128:256 + (j + 1) * 128],
                                                  in_=W["y_bf"][:, j * 128:(j + 1) * 128],
                                                  identity=C["ident_bf"][:]), r=[W["y_bf"], C["ident_bf"]], w=[p_sm])
        for j in range(2):
            P.op("act", lambda e, j=j: e.activation(out=W["yTs"][:, j, :], in_=p_smb[:, 256 + j * 128:256 + (j + 1) * 128],
                                                    func=AF.Copy, scale=nw[:, j:j + 1]), r=[p_sm, nw], w=[W["yTs"]])
        for j in range(2):
            for hf in range(2):
                ch = yT[2 * j + hf]
                P.dma("sp", ch[0:64, c * 128:(c + 1) * 128], W["yTs"][hf * 64:(hf + 1) * 64, j, :], r=[W["yTs"]], w=[ch])
        yield
        P.op("pe", lambda e: e.matmul(p_S[:], lhsT=W["B_tok"][:], rhs=W["Xd"][:], start=True, stop=True),
             r=[W["B_tok"], W["Xd"]], w=[p_S])
        for r_ in range(4):
            hs = slice(r_ * 64, (r_ + 1) * 64)
            P.op("dve", lambda e, r_=r_, hs=hs: e.scalar_tensor_tensor(
                out=S_f[:, hs], in0=S_f[:, hs], scalar=W["dec"][:, r_:r_ + 1], in1=p_S[:, hs],
                op0=ALU.mult, op1=ALU.add), r=[S_f, W["dec"], p_S], w=[S_f])
        P.op("pool", lambda e: e.tensor_copy(out=S_b[:], in_=S_f[:]), r=[S_f], w=[S_b])
        yield

    diff = DiffAttn(P, C, yT, lam, subw, relb, ohd, vecd) if do_diff else None

    def prenorm_gen(st):
        h_t = hT[st % 2]
        for sub in range(4):
            t = st * 4 + sub
            x_t = xt[t % 2]
            P.dma("sp" if t % 2 == 0 else "act", x_t[:], x[t * 128:(t + 1) * 128, :], w=[x_t])
            emit_prenorm_T(P, T, x_t[:], [x_t], gmod, shift, pAB, h_t[:, :, sub * 128:(sub + 1) * 128], [h_t],
                           tmp, hb, ssq, rstd, psT_view=pAB_bf)
            yield

    def ssd_gen(st):
        for sub in range(4):
            yield from ssd_chunk(st * 4 + sub, st, sub)

    def diff_gen(st):
        for h_ in range(2):
            yield from diff.superblock(st, h_, KT[h_], VV[h_], QT[h_])

    for _ in prenorm_gen(0):
        pass
    for st in range(n_super):
        h_t = hT[st % 2]
        for j in range(8):
            for k in range(8):
                P.op("pe", lambda e, j=j, k=k, h_t=h_t: e.matmul(
                    pAB[:], lhsT=w_sb[k][:, j * 128:(j + 1) * 128], rhs=h_t[:, k, :], start=(k == 0), stop=(k == 7)),
                    r=[w_sb[k], h_t], w=[pAB])
            if j < 4:
                if do_ssd:
                    rw = raw[j]
                    if st > 0:
                        P.op("dve", lambda e, rw=rw: e.tensor_copy(out=rw[:, 0:3], in_=rw[:, 512:515]), r=[rw], w=[rw])
                    P.op("act", lambda e, rw=rw: e.copy(out=rw[:, 3:515], in_=pAB[:]), r=[pAB], w=[rw])
                    P.op("dve", lambda e, rw=rw, j=j: e.tensor_scalar(
                        out=acc[:], in0=rw[:, 3:515], scalar1=cw[:, j, 3:4], scalar2=cb[:, j:j + 1],
                        op0=ALU.mult, op1=ALU.add), r=[rw, cw, cb], w=[acc])
                    for tap in (2, 1, 0):
                        P.op("dve", lambda e, rw=rw, j=j, tap=tap: e.scalar_tensor_tensor(
                            out=acc[:], in0=rw[:, tap:tap + 512], scalar=cw[:, j, tap:tap + 1], in1=acc[:],
                            op0=ALU.mult, op1=ALU.add), r=[rw, cw, acc], w=[acc])
                    P.op("act", lambda e, j=j: e.activation(out=cvT[j][:], in_=acc[:], func=AF.Silu),
                         r=[acc], w=[cvT[j]])
            elif do_diff and not os.environ.get('DIFF_NOCOPY'):
                h_ = (j - 4) % 2
                if os.environ.get('DIFF_SKIP', '') .count('q' if j < 6 else 'k'):
                    pass
                elif j < 6:
                    P.op("act", lambda e, h_=h_: e.copy(out=QT[h_][:], in_=pAB[:]), r=[pAB], w=[QT[h_]])
                else:
                    P.op("act", lambda e, h_=h_, st=st: e.copy(out=KT[h_][:, st * 512:(st + 1) * 512], in_=pAB[:]),
                         r=[pAB], w=[KT[h_]])
        for sub in range(4):
            t = st * 4 + sub
            ts_ = slice(sub * 128, (sub + 1) * 128)
            for k in range(8):
                P.op("pe", lambda e, k=k, h_t=h_t, ts_=ts_: e.matmul(
                    pAB[:], lhsT=h_t[:, k, ts_], rhs=w_sb[k][:, 1024:1536], start=(k == 0), stop=(k == 7)),
                    r=[h_t, w_sb[k]], w=[pAB])
            for k in range(8):
                P.op("pe", lambda e, k=k, h_t=h_t, ts_=ts_: e.matmul(
                    pdt[:, 0:4], lhsT=h_t[:, k, ts_], rhs=w_sb[k][:, 1536:1540], start=(k == 0), stop=(k == 7)),
                    r=[h_t, w_sb[k]], w=[pdt])
            if do_ssd:
                P.op("act", lambda e, sub=sub: e.copy(out=z_sb[sub][:], in_=pAB[:, 0:256]), r=[pAB], w=[z_sb[sub]])
                P.op("dve", lambda e, sub=sub: e.tensor_copy(out=dtr[sub][:], in_=pdt[:, 0:4]), r=[pdt], w=[dtr[sub]])
            if do_diff and not os.environ.get('DIFF_NOCOPY') and not os.environ.get('DIFF_SKIP', '').count('v'):
                P.op("dve", lambda e, t=t: e.tensor_copy(out=VV[0][:, t, :], in_=pAB[:, 256:384]), r=[pAB], w=[VV[0]])
                P.op("act", lambda e, t=t: e.copy(out=VV[1][:, t, :], in_=pAB[:, 384:512]), r=[pAB], w=[VV[1]])
        gens = []
        if do_ssd:
            gens.append((ssd_gen(st), 4 * 11))
        if do_diff:
            gens.append((diff_gen(st), 2 * (2 * (4 * st + 4) + 2)))
        if st + 1 < n_super:
            gens.append((prenorm_gen(st + 1), 4))
        interleave(gens)
    if do_ssd:
        P.dma("sp", ss_out[:], ss_all[:], r=[ss_all], w=[ss_out])
    P.reset(m0)


def make_bias_tiles(P, C, relb, ohd, vecd, ps, tag):
    relb_sb = P.sb("relb_sb" + tag, [33, 2])
    ohd_sb = P.sb("ohd_sb" + tag, [33, 384])
    P.dma("sp", relb_sb[:], relb, w=[relb_sb])
    P.dma("sp", ohd_sb[:], ohd, w=[ohd_sb])
    vec_sb = P.sb("vec_sb" + tag, [2, 384])
    P.op("pe", lambda e: e.matmul(ps[0:2, 0:384], lhsT=relb_sb[:], rhs=ohd_sb[:], start=True, stop=True),
         r=[relb_sb, ohd_sb], w=[ps])
    P.op("act", lambda e: e.copy(out=vec_sb[:], in_=ps[0:2, 0:384]), r=[ps], w=[vec_sb])
    P.dma("sp", vecd[:], vec_sb[:], r=[vec_sb], w=[vecd])
    Bd, Bp, c31 = [], [], []
    vt = vecd.t
    hk = P.sb("hankel" + tag, [128, 128])
    for h in range(2):
        bd = P.sb(f"Bd{tag}{h}", [128, 128])
        bp = P.sb(f"Bp{tag}{h}", [128, 128])
        c3 = P.sb(f"c31{tag}_{h}", [128, 1])
        for dst, base in ((bd, 0), (bp, 128)):
            P.dma("sp", hk[:], bass.AP(vt.tensor, vt.offset + h * 384 + base, [[1, 128], [1, 128]]), r=[vecd], w=[hk])
            P.op("pe", lambda e: e.matmul(ps[:, 0:128], lhsT=C["antiI"][:], rhs=hk[:], start=True, stop=True),
                 r=[C["antiI"], hk], w=[ps])
            P.op("act", lambda e, dst=dst: e.copy(out=dst[:], in_=ps[:, 0:128]), r=[ps], w=[dst])
        P.dma("sp", c3[:], bass.AP(vt.tensor, vt.offset + h * 384 + 382, [[0, 128], [1, 1]]), r=[vecd], w=[c3])
        Bd.append(bd)
        Bp.append(bp)
        c31.append(c3)
    return Bd, Bp, c31


class DiffAttn:
    def __init__(self, P, C, yT, lam, subw, relb, ohd, vecd, row0=256):
        self.P, self.C, self.yT, self.row0 = P, C, yT, row0
        self.ps_s = [P.ps(f"ps_s{i}", [128, 512]) for i in range(2)]
        self.ps_o = P.ps("ps_o", [128, 512])
        self.ps_l = P.ps("ps_l", [128, 512])
        self.Bd, self.Bp, self.c31 = make_bias_tiles(P, C, relb, ohd, vecd, self.ps_l, "A")
        lam_sb = P.sb("lam_sb", [128, 256])
        P.dma("sp", lam_sb[:], bcast_row(lam), w=[lam_sb])
        pr = P.sb("lam_pr", [128, 2, 64])
        sm = P.sb("lam_sm", [128, 2])
        self.neg_lam = P.sb("neg_lam", [128, 1])
        P.op("dve", lambda e: e.tensor_tensor(out=pr[:, 0, :], in0=lam_sb[:, 0:64], in1=lam_sb[:, 64:128], op=ALU.mult),
             r=[lam_sb], w=[pr])
        P.op("dve", lambda e: e.tensor_tensor(out=pr[:, 1, :], in0=lam_sb[:, 128:192], in1=lam_sb[:, 192:256], op=ALU.mult),
             r=[lam_sb, pr], w=[pr])
        P.op("dve", lambda e: e.tensor_reduce(out=sm[:], in_=pr[:], axis=AX.X, op=ALU.add), r=[pr], w=[sm])
        P.op("act", lambda e: e.activation(out=sm[:], in_=sm[:], func=AF.Exp), r=[sm], w=[sm])
        P.op("dve", lambda e: e.tensor_tensor(out=self.neg_lam[:], in0=sm[:, 1:2], in1=sm[:, 0:1], op=ALU.subtract),
             r=[sm], w=[self.neg_lam])
        P.op("dve", lambda e: e.tensor_scalar(out=self.neg_lam[:], in0=self.neg_lam[:], scalar1=-0.2, scalar2=None,
                                              op0=ALU.add), r=[self.neg_lam], w=[self.neg_lam])
        self.subs = P.sb("subs", [128, 1])
        P.dma("sp", self.subs[:], subw, w=[self.subs])
        P.op("dve", lambda e: e.tensor_scalar(out=self.subs[:], in0=self.subs[:], scalar1=0.8, scalar2=None,
                                              op0=ALU.mult), r=[self.subs], w=[self.subs])
        self.PT = [P.sb(f"PT{i}", [128, 512], BF16) for i in range(4)]
        self.Lacc = P.sb("Lacc", [128, 512])
        self.tS = [P.sb(f"tS{i}", [128, 512]) for i in range(2)]
        self.Tm = [P.sb(f"Tm{i}", [128, 512]) for i in range(2)]
        self.Rr = P.sb("Rr", [128, 512])
        self.sq = P.sb("sq", [128, 512])
        self.yd = [P.sb(f"yd{i}", [128, 512], BF16) for i in range(2)]
        self.n = 0
        self.nn = 0
        self.ny = 0

    def superblock(self, Q, h, KT, V, QT):
        import os
        stage = int(os.environ.get("DIFF_STAGE", "3"))
        if stage == 0:
            return
        yield
        P, C = self.P, self.C
        ps_o, ps_l = self.ps_o, self.ps_l
        for m in range(2):
            ms = slice(m * 64, (m + 1) * 64)

            def stage_a(kb, m=m, ms=ms):
                j0 = max(0, kb - 4 * Q)
                c0 = j0 * 128
                ps = self.ps_s[self.n % 2]
                pt = self.PT[self.n % 4]
                self.n += 1
                P.op("pe", lambda e, ps=ps, kb=kb, c0=c0, ms=ms: e.matmul(
                    ps[:, c0:512], lhsT=KT[ms, kb * 128:(kb + 1) * 128], rhs=QT[ms, c0:512], start=True, stop=True),
                    r=[KT, QT], w=[ps])
                fj = max(j0, kb + 2 - 4 * Q)
                for j in range(j0, min(4, fj)):
                    bt = self.Bd[h] if 4 * Q + j == kb else self.Bp[h]
                    ts = self.tS[self.nn % 2]
                    self.nn += 1
                    cs = slice(j * 128, (j + 1) * 128)
                    P.op("dve", lambda e, ps=ps, ts=ts, bt=bt, cs=cs: e.scalar_tensor_tensor(
                        out=ts[:, cs], in0=ps[:, cs], scalar=0.125, in1=bt[:], op0=ALU.mult, op1=ALU.add),
                        r=[ps, bt], w=[ts])
                    P.op("act", lambda e, ts=ts, pt=pt, cs=cs: e.activation(out=pt[:, cs], in_=ts[:, cs], func=AF.Exp),
                         r=[ts], w=[pt])
                if fj < 4:
                    fs = slice(fj * 128, 512)
                    P.op("act", lambda e, ps=ps, pt=pt, fs=fs: e.activation(
                        out=pt[:, fs], in_=ps[:, fs], func=AF.Exp, bias=self.c31[h][:, 0:1], scale=0.125),
                        r=[ps, self.c31[h]], w=[pt])
                return pt

            def stage_b(kb, pt):
                j0 = max(0, kb - 4 * Q)
                c0 = j0 * 128
                La = self.Lacc
                if kb == 0:
                    P.op("pool", lambda e, pt=pt: e.tensor_copy(out=La[:], in_=pt[:]), r=[pt], w=[La])
                else:
                    P.op("pool", lambda e, pt=pt, c0=c0: e.tensor_tensor(out=La[:, c0:512], in0=La[:, c0:512], in1=pt[:, c0:512],
                                                                         op=ALU.add), r=[pt, La], w=[La])
                if kb <= 4 * Q:
                    P.op("pe", lambda e, kb=kb, pt=pt: e.matmul(ps_o[:], lhsT=V[:, kb, :], rhs=pt[:],
                                                                start=(kb == 0), stop=False), r=[V, pt], w=[ps_o])
                else:
                    for j in range(j0, 4):
                        cs = slice(j * 128, (j + 1) * 128)
                        last = (kb == 4 * Q + j)
                        P.op("pe", lambda e, kb=kb, pt=pt, cs=cs, last=last: e.matmul(
                            ps_o[:, cs], lhsT=V[:, kb, :], rhs=pt[:, cs], start=(kb == 0), stop=last), r=[V, pt], w=[ps_o])

            nkb = 4 * Q + 4
            prev = stage_a(0)
            for kb in range(1, nkb):
                cur = stage_a(kb)
                yield
                stage_b(kb - 1, prev)
                prev = cur
            yield
            stage_b(nkb - 1, prev)
            P.op("pe", lambda e: e.matmul(ps_l[:], lhsT=C["ones_f"][:], rhs=self.Lacc[:], start=True, stop=True),
                 r=[C["ones_f"], self.Lacc], w=[ps_l])
            if stage < 3:
                continue
            tm = self.Tm[m]
            P.op("dve", lambda e: e.reciprocal(out=self.Rr[:], in_=ps_l[:]), r=[ps_l], w=[self.Rr])
            P.op("dve", lambda e, tm=tm: e.tensor_tensor(out=tm[:], in0=ps_o[:], in1=self.Rr[:], op=ALU.mult),
                 r=[ps_o, self.Rr], w=[tm])
        yield
        if stage < 3:
            return
        t1, t2 = self.Tm
        P.op("dve", lambda e: e.scalar_tensor_tensor(out=t1[:], in0=t2[:], scalar=self.neg_lam[:, 0:1], in1=t1[:],
                                                     op0=ALU.mult, op1=ALU.add), r=[t1, t2, self.neg_lam], w=[t1])
        P.op("pool", lambda e: e.tensor_tensor(out=self.sq[:], in0=t1[:], in1=t1[:], op=ALU.mult), r=[t1], w=[self.sq])
        P.op("pe", lambda e: e.matmul(ps_l[:], lhsT=C["ones_f"][:], rhs=self.sq[:], start=True, stop=True),
             r=[C["ones_f"], self.sq], w=[ps_l])
        P.op("act", lambda e: e.activation(out=self.Rr[:], in_=ps_l[:], func=AF.Sqrt, bias=EPS, scale=1.0 / 128),
             r=[ps_l], w=[self.Rr])
        P.op("dve", lambda e: e.reciprocal(out=self.Rr[:], in_=self.Rr[:]), r=[self.Rr], w=[self.Rr])
        yd = self.yd[self.ny % 2]
        self.ny += 1
        P.op("dve", lambda e, yd=yd: e.scalar_tensor_tensor(out=yd[:], in0=t1[:], scalar=self.subs[:, 0:1], in1=self.Rr[:],
                                                            op0=ALU.mult, op1=ALU.mult), r=[t1, self.subs, self.Rr], w=[yd])
        for hf in range(2):
            ch = self.yT[4 + 2 * h + hf]
            P.dma("sp", ch[0:64, Q * 512:(Q + 1) * 512], yd[hf * 64:(hf + 1) * 64, :], r=[yd], w=[ch])


def split3(v):
    return v[0:1024], v[1024:2048], v[2048:3072]


def even_inputs(z, mod, b, hg):
    wi = z["e_w_in"][0]
    cols = np.concatenate([
        np.arange(1024 + hg * 256, 1024 + hg * 256 + 256),
        np.arange(2048 + hg * 128, 2048 + hg * 128 + 128),
        np.arange(2560 + hg * 128, 2560 + hg * 128 + 128),
        np.arange(3088 + hg * 256, 3088 + hg * 256 + 256),
        np.arange(4112 + hg * 256, 4112 + hg * 256 + 256),
        np.arange(hg * 256, hg * 256 + 256),
        np.arange(5136 + hg * 256, 5136 + hg * 256 + 256),
        np.arange(3072 + hg * 4, 3072 + hg * 4 + 4),
    ])
    ch = cols[0:512] - 1024
    cw = z["e_conv_w"][0][:, ch]
    cb = z["e_conv_b"][0][ch]
    im = {
        "win": np.ascontiguousarray(wi[:, cols]),
        "convw": np.ascontiguousarray(cw.reshape(4, 4, 128).transpose(2, 1, 0)),
        "convb": np.ascontiguousarray(cb.reshape(4, 128).T),
        "hv": np.stack([z["e_dt_bias"][0][hg * 4:hg * 4 + 4], z["e_A_log"][0][hg * 4:hg * 4 + 4],
                        z["e_D"][0][hg * 4:hg * 4 + 4]]).astype(np.float32),
        "normw": np.ascontiguousarray(z["e_ssd_norm"][0][hg * 256:hg * 256 + 256].reshape(2, 128).T),
        "lam": np.ascontiguousarray(z["e_lambda"][0].reshape(1, 256)),
        "subw": np.ascontiguousarray(z["e_diff_norm"][0].reshape(128, 1)),
        "relb": np.concatenate([z["rel_bias"][:, 2 * hg:2 * hg + 2], np.ones((1, 2), np.float32)], 0),
        "ohd": host_ohd(),
    }
    im.update(host_consts())
    return im


O_NCOL = 256 + 512 + 320


def host_rope_tables(hg):
    gamma = 1.0 - 2.0 ** (-5.0 - hg)
    pos = np.arange(S_LEN, dtype=np.float32)
    inv = (np.float32(10000.0) ** (-np.arange(64, dtype=np.float32) / np.float32(64))).astype(np.float32)
    ang = (pos[:, None] * inv[None]).astype(np.float32).astype(np.float64)
    cos, sin = np.cos(ang), np.sin(ang)
    l = (np.arange(S_LEN) % 128).astype(np.float64)
    fq = (gamma ** l)[:, None]
    fk = (gamma ** (-l))[:, None] * 128.0 ** -0.5
    tab = np.stack([
        np.concatenate([cos, cos], 1) * fq, np.concatenate([-sin, sin], 1) * fq,
        np.concatenate([cos, cos], 1) * fk, np.concatenate([-sin, sin], 1) * fk]).astype(np.float32)
    gv = np.zeros((128, 2), np.float32)
    gv[:, 0] = gamma ** 128
    return tab, gv


def phase_odd(P, C, io, n_super=16):
    m0 = P.mark()
    hT_all, win, tab, gv, sinks = io["hT2_all"], io["o_win"], io["tab"], io["gv"], io["sinks"]
    relb, ohdA, ohdB = io["relb"], io["ohdA"], io["ohdB"]
    yT, vecdA, vecdB = io["yT2_loc"], io["vecdA"], io["vecdB"]
    T = TokCtx(P, C["ident_bf"])
    w_sb = [P.sb(f"win{k}", [128, O_NCOL], BF16) for k in range(8)]
    for k in range(8):
        P.dma("pool", w_sb[k][:], win[k * 128:(k + 1) * 128, :], w=[w_sb[k]])
    gv_sb = P.sb("gv_sb", [128, 2])
    P.dma("sp", gv_sb[:], gv, w=[gv_sb])
    es = P.sb("es", [128, 2])
    P.dma("sp", es[:], bcast_row(sinks), w=[es])
    P.op("act", lambda e: e.activation(out=es[:], in_=es[:], func=AF.Exp), r=[es], w=[es])
    pFM = P.ps("pFM", [128, 512])
    pT1 = P.ps("pT1", [128, 512])
    pT2 = P.ps("pT2", [128, 512])
    pSc = P.ps("pSc", [128, 512])
    pRO = P.ps("pRO", [128, 512])
    pTr = P.ps("pTr", [128, 512])
    pSW = P.ps("pSW", [128, 512])
    pOL = P.ps("pOL", [128, 512])
    pTr_bf = pTr.alt
    BdA, _, _ = make_bias_tiles(P, C, relb, ohdA, vecdA, pOL, "oA")
    _, BpB, _ = make_bias_tiles(P, C, relb, ohdB, vecdB, pOL, "oB")
    Bpd = []
    for h in range(2):
        t = P.sb(f"Bpd{h}", [128, 2, 128])
        P.op("dve", lambda e, t=t, h=h: e.tensor_copy(out=t[:, 0, :], in_=BpB[h][:]), r=[BpB[h]], w=[t])
        P.op("dve", lambda e, t=t, h=h: e.tensor_copy(out=t[:, 1, :], in_=BdA[h][:]), r=[BdA[h], t], w=[t])
        Bpd.append(t)
    hT = [P.sb(f"hT{i}", [128, 8, 512], BF16) for i in range(2)]
    tb = [P.sb(f"tb{i}", [128, 4, 4, 128]) for i in range(2)]
    SQT = P.sb("SQT", [128, 512], BF16)
    SKT = P.sb("SKT", [128, 128 + 512], BF16)
    SV = P.sb("SV", [128, 5, 64], BF16)
    qkv = [P.sb(f"qkv{i}", [128, 256]) for i in range(4)]
    v_bf = [P.sb(f"v_bf{i}", [128, 256], BF16) for i in range(4)]
    sg = [P.sb(f"sg{i}", [128, 256]) for i in range(4)]
    Wk = {}
    for nme, shp, dt in [("A", [128, 128], F32), ("B", [128, 128], F32), ("Qp", [128, 128], BF16),
                         ("Kp", [128, 128], BF16), ("QT", [128, 128], BF16), ("KT", [128, 128], BF16),
                         ("Sm", [128, 128], BF16), ("y_bf", [128, 256], BF16), ("yTs", [128, 2, 128], BF16),
                         ("ts", [128, 2, 128], F32), ("PT", [128, 2, 128], BF16), ("den", [64, 128], F32),
                         ("ob", [64, 128], BF16), ("ss", [128, 2], F32), ("rstd", [128, 2], F32)]:
        Wk[nme] = P.sb("k_" + nme, shp, dt)
    St = P.sb("St", [128, 256])
    gS = P.sb("gS", [128, 256], BF16)
    P.op("pool", lambda e: e.memset(St[:], 0.0), w=[St])
    P.op("pool", lambda e: e.memset(gS[:], 0.0), w=[gS])

    def ret_chunk(c, sub, tbs, qk, vb, sgt):
        for which, (col0, tq, dst) in enumerate(((0, 0, "Qp"), (128, 2, "Kp"))):
            src = qk[:, col0:col0 + 128]
            P.op("dve", lambda e, src=src, tq=tq: e.tensor_tensor(out=Wk["A"][:], in0=src, in1=tbs[:, tq, sub, :],
                                                                 op=ALU.mult), r=[qk, tbs], w=[Wk["A"]])
            P.op("pool", lambda e, col0=col0, tq=tq: e.tensor_tensor(
                out=Wk["B"][:, 0:64], in0=qk[:, col0 + 64:col0 + 128], in1=tbs[:, tq + 1, sub, 0:64], op=ALU.mult),
                r=[qk, tbs], w=[Wk["B"]])
            P.op("pool", lambda e, col0=col0, tq=tq: e.tensor_tensor(
                out=Wk["B"][:, 64:128], in0=qk[:, col0:col0 + 64], in1=tbs[:, tq + 1, sub, 64:128], op=ALU.mult),
                r=[qk, tbs, Wk["B"]], w=[Wk["B"]])
            P.op("dve", lambda e, dst=dst: e.tensor_tensor(out=Wk[dst][:], in0=Wk["A"][:], in1=Wk["B"][:], op=ALU.add),
                 r=[Wk["A"], Wk["B"]], w=[Wk[dst]])
            P.op("pe", lambda e, dst=dst, which=which: e.transpose(
                out=pTr_bf[:, which * 128:(which + 1) * 128], in_=Wk[dst][:], identity=C["ident_bf"][:]),
                r=[Wk[dst], C["ident_bf"]], w=[pTr])
        yield
        P.op("act", lambda e: e.copy(out=Wk["QT"][:], in_=pTr_bf[:, 0:128]), r=[pTr], w=[Wk["QT"]])
        P.op("act", lambda e: e.copy(out=Wk["KT"][:], in_=pTr_bf[:, 128:256]), r=[pTr], w=[Wk["KT"]])
        P.op("pe", lambda e: e.matmul(pSc[:, 0:128], lhsT=Wk["KT"][:], rhs=Wk["QT"][:], start=True, stop=True),
             r=[Wk["KT"], Wk["QT"]], w=[pSc])
        P.op("dve", lambda e: e.tensor_tensor(out=Wk["Sm"][:], in0=pSc[:, 0:128], in1=C["triU"][:], op=ALU.mult),
             r=[pSc, C["triU"]], w=[Wk["Sm"]])
        yield
        P.op("pe", lambda e: e.matmul(pRO[:, 0:256], lhsT=Wk["Sm"][:], rhs=vb[:], start=True, stop=False),
             r=[Wk["Sm"], vb], w=[pRO])
        P.op("pe", lambda e: e.matmul(pRO[:, 0:256], lhsT=Wk["QT"][:], rhs=gS[:], start=False, stop=True),
             r=[Wk["QT"], gS], w=[pRO])
        yield
        T.sumsq_rstd(pRO[:, 0:256], [pRO], Wk["ss"], Wk["rstd"], 256)
        yield
        P.op("dve", lambda e: e.scalar_tensor_tensor(out=Wk["y_bf"][:], in0=pRO[:, 0:256], scalar=Wk["rstd"][:, 0:1],
                                                     in1=sgt[:], op0=ALU.mult, op1=ALU.mult),
             r=[pRO, Wk["rstd"], sgt], w=[Wk["y_bf"]])
        for j in range(2):
            P.op("pe", lambda e, j=j: e.transpose(out=pTr_bf[:, 256 + j * 128:256 + (j + 1) * 128],
                                                  in_=Wk["y_bf"][:, j * 128:(j + 1) * 128], identity=C["ident_bf"][:]),
                 r=[Wk["y_bf"], C["ident_bf"]], w=[pTr])
        P.op("act", lambda e: e.copy(out=Wk["yTs"][:], in_=pTr_bf[:, 256:512].rearrange("p (j t) -> p j t", j=2)),
             r=[pTr], w=[Wk["yTs"]])
        for j in range(2):
            for hf in range(2):
                ch = yT[2 * j + hf]
                P.dma("sp", ch[0:64, c * 128:(c + 1) * 128], Wk["yTs"][hf * 64:(hf + 1) * 64, j, :], r=[Wk["yTs"]], w=[ch])
        yield
        P.op("pe", lambda e: e.matmul(pRO[:, 256:512], lhsT=Wk["Kp"][:], rhs=vb[:], start=True, stop=True),
             r=[Wk["Kp"], vb], w=[pRO])
        P.op("dve", lambda e: e.scalar_tensor_tensor(out=St[:], in0=St[:], scalar=gv_sb[:, 0:1], in1=pRO[:, 256:512],
                                                     op0=ALU.mult, op1=ALU.add), r=[St, gv_sb, pRO], w=[St])
        P.op("act", lambda e: e.activation(out=gS[:], in_=St[:], func=AF.Copy, scale=gv_sb[:, 0:1]),
             r=[St, gv_sb], w=[gS])
        yield

    def swa_block(blk, sub):
        for h in range(2):
            hs = slice(h * 64, (h + 1) * 64)
            qcols = slice(sub * 128, (sub + 1) * 128)
            first = (blk == 0)
            if not first:
                P.op("pe", lambda e, hs=hs, qcols=qcols, sub=sub: e.matmul(
                    pSW[:, 0:128], lhsT=SKT[hs, sub * 128:(sub + 1) * 128], rhs=SQT[hs, qcols], start=True, stop=True),
                    r=[SKT, SQT], w=[pSW])
            P.op("pe", lambda e, hs=hs, qcols=qcols, sub=sub: e.matmul(
                pSW[:, 128:256], lhsT=SKT[hs, (sub + 1) * 128:(sub + 2) * 128], rhs=SQT[hs, qcols], start=True, stop=True),
                r=[SKT, SQT], w=[pSW])
            lo = 1 if first else 0
            yield
            P.op("dve", lambda e, h=h, lo=lo: e.scalar_tensor_tensor(
                out=Wk["ts"][:, lo:2, :], in0=pSW[:, lo * 128:256].rearrange("p (a b) -> p a b", b=128), scalar=0.125,
                in1=Bpd[h][:, lo:2, :], op0=ALU.mult, op1=ALU.add), r=[pSW, Bpd[h]], w=[Wk["ts"]])
            P.op("act", lambda e, lo=lo: e.activation(out=Wk["PT"][:, lo:2, :], in_=Wk["ts"][:, lo:2, :], func=AF.Exp),
                 r=[Wk["ts"]], w=[Wk["PT"]])
            oc = slice(h * 256, h * 256 + 128)
            lc = slice(h * 256 + 128, h * 256 + 256)
            yield
            if not first:
                P.op("pe", lambda e, oc=oc, sub=sub: e.matmul(pOL[0:64, oc], lhsT=SV[:, sub, :], rhs=Wk["PT"][:, 0, :],
                                                              start=True, stop=False), r=[SV, Wk["PT"]], w=[pOL])
            P.op("pe", lambda e, oc=oc, sub=sub, first=first: e.matmul(
                pOL[0:64, oc], lhsT=SV[:, sub + 1, :], rhs=Wk["PT"][:, 1, :], start=first, stop=True),
                r=[SV, Wk["PT"]], w=[pOL])
            if not first:
                P.op("pe", lambda e, lc=lc: e.matmul(pOL[0:64, lc], lhsT=C["ones_bf"][:, 0:64], rhs=Wk["PT"][:, 0, :],
                                                     start=True, stop=False), r=[C["ones_bf"], Wk["PT"]], w=[pOL])
            P.op("pe", lambda e, lc=lc, first=first: e.matmul(
                pOL[0:64, lc], lhsT=C["ones_bf"][:, 0:64], rhs=Wk["PT"][:, 1, :], start=first, stop=True),
                r=[C["ones_bf"], Wk["PT"]], w=[pOL])
            yield
            P.op("dve", lambda e, lc=lc, h=h: e.tensor_scalar(out=Wk["den"][:], in0=pOL[0:64, lc], scalar1=es[0:64, h:h + 1],
                                                              scalar2=None, op0=ALU.add), r=[pOL, es], w=[Wk["den"]])
            P.op("dve", lambda e: e.reciprocal(out=Wk["den"][:], in_=Wk["den"][:]), r=[Wk["den"]], w=[Wk["den"]])
            P.op("dve", lambda e, oc=oc: e.tensor_tensor(out=Wk["ob"][:], in0=pOL[0:64, oc], in1=Wk["den"][:], op=ALU.mult),
                 r=[pOL, Wk["den"]], w=[Wk["ob"]])
            P.dma("act", yT[4 + h][0:64, blk * 128:(blk + 1) * 128], Wk["ob"][:], r=[Wk["ob"]], w=[yT[4 + h]])

    for st in range(n_super):
        h_t = hT[st % 2]
        qq, so = divmod(st, 4)
        for c4 in range(4):
            P.dma("sp" if c4 % 2 == 0 else "act", h_t[:, 2 * c4:2 * c4 + 2, :],
                  hT_all[c4][qq * 256:(qq + 1) * 256, so * 512:(so + 1) * 512].rearrange("(k p) t -> p k t", p=128),
                  r=[hT_all[c4]], w=[h_t])
        tbs = tb[st % 2]
        for q4 in range(4):
            P.dma("act", tbs[:, q4, :, :], tab[q4, st * 512:(st + 1) * 512, :].rearrange("(s p) d -> p s d", p=128),
                  w=[tbs])
        for j in range(2):
            for k in range(8):
                P.op("pe", lambda e, j=j, k=k, h_t=h_t: e.matmul(
                    pFM[:], lhsT=w_sb[k][:, j * 128:(j + 1) * 128], rhs=h_t[:, k, :], start=(k == 0), stop=(k == 7)),
                    r=[w_sb[k], h_t], w=[pFM])
            if j == 0:
                P.op("act", lambda e: e.copy(out=SQT[:], in_=pFM[:]), r=[pFM], w=[SQT])
            else:
                if st > 0:
                    P.op("dve", lambda e: e.tensor_copy(out=SKT[:, 0:128], in_=SKT[:, 512:640]), r=[SKT], w=[SKT])
                    P.op("dve", lambda e: e.tensor_copy(out=SV[:, 0, :], in_=SV[:, 4, :]), r=[SV], w=[SV])
                P.op("act", lambda e: e.copy(out=SKT[:, 128:640], in_=pFM[:]), r=[pFM], w=[SKT])
        for sub in range(4):
            c = st * 4 + sub
            ts_ = slice(sub * 128, (sub + 1) * 128)
            for k in range(8):
                P.op("pe", lambda e, k=k, h_t=h_t, ts_=ts_: e.matmul(
                    pT1[:], lhsT=h_t[:, k, ts_], rhs=w_sb[k][:, 256:768], start=(k == 0), stop=(k == 7)),
                    r=[h_t, w_sb[k]], w=[pT1])
            for k in range(8):
                P.op("pe", lambda e, k=k, h_t=h_t, ts_=ts_: e.matmul(
                    pT2[:, 0:320], lhsT=h_t[:, k, ts_], rhs=w_sb[k][:, 768:1088], start=(k == 0), stop=(k == 7)),
                    r=[h_t, w_sb[k]], w=[pT2])
            qk = qkv[sub]
            vb = v_bf[sub]
            sgt = sg[sub]
            P.op("act", lambda e, qk=qk: e.copy(out=qk[:, 0:256], in_=pT1[:, 0:256]), r=[pT1], w=[qk])
            P.op("dve", lambda e, vb=vb: e.tensor_copy(out=vb[:], in_=pT1[:, 256:512]), r=[pT1], w=[vb])
            P.op("act", lambda e, sgt=sgt: e.activation(out=sgt[:], in_=pT2[:, 0:256], func=AF.Silu), r=[pT2], w=[sgt])
            P.op("dve", lambda e, sub=sub: e.tensor_copy(out=SV[:, sub + 1, :], in_=pT2[:, 256:320]), r=[pT2], w=[SV])


        def ret_gen(st=st, tbs=tbs):
            for sub in range(4):
                yield from ret_chunk(st * 4 + sub, sub, tbs, qkv[sub], v_bf[sub], sg[sub])

        def swa_gen(st=st):
            for sub in range(4):
                yield from swa_block(st * 4 + sub, sub)

        interleave([(ret_gen(), 4 * 6), (swa_gen(), 4 * 2 * 4)])
    P.reset(m0)


def odd_inputs(z, hT_full, b, hg):
    wi = z["o_w_in"][0]
    kv = hg // 2
    cols = np.concatenate([
        np.arange(3072 + 2 * hg * 64, 3072 + 2 * hg * 64 + 128),
        np.arange(3584 + kv * 64, 3584 + kv * 64 + 64), np.arange(3584 + kv * 64, 3584 + kv * 64 + 64),
        np.arange(hg * 128, hg * 128 + 128),
        np.arange(512 + hg * 128, 512 + hg * 128 + 128),
        np.arange(1024 + hg * 256, 1024 + hg * 256 + 256),
        np.arange(2048 + hg * 256, 2048 + hg * 256 + 256),
        np.arange(3712 + kv * 64, 3712 + kv * 64 + 64),
    ])
    tab, gv = host_rope_tables(hg)
    hc = host_consts()
    im = {
        "win": np.ascontiguousarray(wi[:, cols]), "tab": tab, "gv": gv,
        "sinks": np.ascontiguousarray(z["o_sinks"][0][2 * hg:2 * hg + 2].reshape(1, 2)),
        "relb": np.concatenate([z["rel_bias"][:, 2 * hg:2 * hg + 2], np.ones((1, 2), np.float32)], 0),
        "ohdA": host_ohd(False), "ohdB": host_ohd(True),
    }
    for k in ["ident_bf", "triU", "ones_bf", "antiI"]:
        im[k] = hc[k]
    return im


I32 = mybir.dt.int32
GROUPS = [[0, 1, 2, 3], [4, 5, 6, 7]]


def dyn_dma(P, out, in_fn, r, w):
    op = Op("sp", lambda e: e.dma_start(out=out, in_=in_fn()), is_dma=True, dbuf=w[0])
    P._rec(op, r, w)
    P.dma_log.append(op)
    return op


def setup_regs(P, nc, qoff):
    regs = [P.stack.enter_context(nc.sync.register(f"qr{i}")) for i in range(2)]

    def ld(e):
        for i in range(2):
            ins = e.reg_load(regs[i], qoff.t[0:1, i:i + 1])
        P.qv = e.snap(regs[0], min_val=0, max_val=6144)
        P.qc = e.snap(regs[1], min_val=0, max_val=48)
        return ins
    P.op("sp", ld)


def extract_quarter(P, src_all, dst_q, nrows, step):
    for r0 in range(0, nrows, step):
        dyn_dma(P, dst_q.t[r0:r0 + step, :], lambda r0=r0: src_all.t[r0:r0 + step, bass.ds(P.qv, NT)],
                r=[src_all], w=[dst_q])


def phase_mod(P, io):
    m0 = P.mark()
    cT, modw, modb = io["cT"], io["modw"], io["modb"]
    c_sb = P.sb("c_sb", [128, 8])
    ca = P.sb("ca_sb", [128, 8])
    b_sb = P.sb("b_sb", [1, 3072])
    o_sb = P.sb("o_sb", [1, 3072])
    wt = [P.sb(f"mw{i}", [128, 8, 768]) for i in range(2)]
    ps = [P.ps(f"mps{i}", [1, 512]) for i in range(2)]
    P.dma("sp", c_sb[:], cT, w=[c_sb])
    P.dma("sp", b_sb[:], modb, w=[b_sb])
    P.op("act", lambda e: e.activation(out=ca[:], in_=c_sb[:], func=AF.Silu), r=[c_sb], w=[ca])
    for s_ in range(4):
        w_t = wt[s_ % 2]
        P.dma("sp" if s_ % 2 == 0 else "act", w_t[:], modw[s_].rearrange("(k p) n -> p k n", p=128), w=[w_t])
        for hf in range(2):
            for k in range(8):
                P.op("pe", lambda e, hf=hf, k=k, w_t=w_t: e.matmul(
                    ps[hf][0:1, 0:384], lhsT=ca[:, k:k + 1], rhs=w_t[:, k, hf * 384:(hf + 1) * 384],
                    start=(k == 0), stop=(k == 7)), r=[ca, w_t], w=[ps[hf]])
            o0 = s_ * 768 + hf * 384
            P.op("dve", lambda e, hf=hf, o0=o0: e.tensor_tensor(out=o_sb[0:1, o0:o0 + 384], in0=ps[hf][0:1, 0:384],
                                                                in1=b_sb[0:1, o0:o0 + 384], op=ALU.add),
                 r=[ps[hf], b_sb], w=[o_sb])
    P.dma("sp", io["mod_loc"], o_sb[:], r=[o_sb], w=[io["mod_loc"]])
    P.collective("AllGather", GROUPS, io["mod_loc"], io["mod_all"])
    P.reset(m0)


CONST_NAMES = ["ident_bf", "ident_f", "triU", "triS", "NEG", "ones_f", "ones_bf", "antiI"]


def build_fused():
    nc = bass.Bass("TRN2", target_bir_lowering=False)
    P = Prog(nc)

    def X(name, shape, dt=F32):
        return Buf(nc.dram_tensor(name, list(shape), dt, kind="ExternalInput").ap(), name)

    io = {}
    for name, shape, dt in [
        ("cT", [128, 8], F32), ("modw", [4, 1024, 768], F32), ("modb", [1, 3072], F32), ("ng", [8, 1024], F32),
        ("qoff", [1, 2], I32), ("x_b", [S_LEN, 1024], F32), ("x_tok", [NT, 1024], F32),
        ("e_win", [1024, E_NCOL], F32), ("convw", [128, 4, 4], F32), ("convb", [128, 4], F32), ("hv", [3, 4], F32),
        ("normw", [128, 2], F32), ("lam", [1, 256], F32), ("subw", [128, 1], F32), ("relb", [33, 2], F32),
        ("ohdA", [33, 384], F32), ("ohdB", [33, 384], F32), ("e_wout", [2048, 1024], F32),
        ("w1_0", [1024, 4096], F32), ("w2_0", [4096, 1024], F32), ("w1_1", [1024, 4096], F32), ("w2_1", [4096, 1024], F32),
        ("o_win", [1024, O_NCOL], F32), ("tab", [4, S_LEN, 128], F32), ("gv", [128, 2], F32), ("sinks", [1, 2], F32),
        ("o_wout", [1536, 1024], F32),
    ]:
        if int(os.environ.get("FUSED_STOP", "99")) <= 2 and name in ("x_tok", "e_wout", "w1_0", "w2_0", "w1_1", "w2_1", "o_win", "tab", "o_wout"):
            continue
        io[name] = X(name, shape, dt)
    P.declared = set(io.keys()) | set(CONST_NAMES)
    cn = {k: X(k, [128, 128], BF16 if k.endswith("bf") else F32) for k in CONST_NAMES}
    for name, shape, dt in [
        ("mod_loc", [1, 3072], F32), ("mod_all", [4, 3072], F32),
        ("ss_loc", [128, 64], F32), ("ss_all", [512, 64], F32),
        ("xn0", [NT, 1024], F32), ("hT0", [1024, NT], BF16), ("x1", [NT, 1024], F32),

        ("xn1", [NT, 1024], F32), ("hT1", [1024, NT], BF16), ("vecdA", [2, 384], F32), ("vecdB", [2, 384], F32),
        ("vecdC", [2, 384], F32), ("yT_q", [2048, NT], BF16), ("ss_q", [512, 16], F32), ("yT2_q", [1536, NT], BF16),
    ]:
        io[name] = P.dram(name, shape, dt)
    io["yT_loc"] = [P.dram(f"yT_loc{i}", [64, S_LEN], BF16) for i in range(8)]
    io["yT_all"] = [P.dram(f"yT_all{i}", [256, S_LEN], BF16) for i in range(8)]
    io["yT2_loc"] = [P.dram(f"yT2_loc{i}", [64, S_LEN], BF16) for i in range(6)]
    io["yT2_all"] = [P.dram(f"yT2_all{i}", [256, S_LEN], BF16) for i in range(6)]
    io["hTn_loc"] = [P.dram(f"hTn_loc{i}", [256, NT], BF16) for i in range(4)]
    io["hT2_all"] = [P.dram(f"hT2_all{i}", [1024, NT], BF16) for i in range(4)]
    io["out"] = P.dram("out", [NT, 1024], F32, kind="ExternalOutput")

    setup_regs(P, nc, io["qoff"])
    C = {}
    for k in CONST_NAMES:
        C[k] = P.sb("c_" + k, [128, 128], BF16 if k.endswith("bf") else F32)
        P.dma("sp", C[k][:], cn[k], w=[C[k]])
    P.barrier()

    stop = int(os.environ.get("FUSED_STOP", "99"))
    phase_mod(P, io)
    P.barrier()
    if stop <= 1:
        P.build()
        return nc, P
    phase_even(P, C, io, n_super=int(os.environ.get('FUSED_NSUPER', '16')))
    for i in range(8):
        P.collective("AllGather", GROUPS, io["yT_loc"][i], io["yT_all"][i])
    if not os.environ.get("FUSED_NOAG2"):
        P.collective("AllGather", GROUPS, io["ss_loc"], io["ss_all"])
    P.barrier()
    if stop <= 2:
        P.build()
        return nc, P
    rm_e = [(kk // 2, kk % 2) for kk in range(8)] + [(kk // 2, 2 + kk % 2) for kk in range(8)]
    phase_outproj(P, C, dict(io, x=io["x_tok"], yT_all=io["yT_all"], wout=io["e_wout"], xn=io["xn0"], hT=io["hT0"], rowmap=rm_e, ngroups=16),
                  16, True, 0)
    P.barrier()
    if stop <= 3:
        P.build()
        return nc, P
    phase_mlp(P, C, dict(io, xn=io["xn0"], hT=io["hT0"], w1=io["w1_0"], w2=io["w2_0"], xo=io["x1"], hTn=io["hTn_loc"]), True, 0)
    for i in range(4):
        P.collective("AllGather", GROUPS, io["hTn_loc"][i], io["hT2_all"][i])
    P.barrier()
    if stop <= 4:
        P.build()
        return nc, P
    phase_odd(P, C, dict(io, vecdA=io["vecdB"], vecdB=io["vecdC"]), n_super=int(os.environ.get('FUSED_NSUPER', '16')))
    for i in range(6):
        P.collective("AllGather", GROUPS, io["yT2_loc"][i], io["yT2_all"][i])
    P.barrier()
    rm_o = [(kk // 2, kk % 2) for kk in range(8)] + [(kk, 2) for kk in range(4)]
    phase_outproj(P, C, dict(io, x=io["x1"], yT_all=io["yT2_all"], wout=io["o_wout"], xn=io["xn1"], hT=io["hT1"], rowmap=rm_o, ngroups=12),
                  12, False, 1)
    P.barrier()
    phase_mlp(P, C, dict(io, xn=io["xn1"], hT=io["hT1"], w1=io["w1_1"], w2=io["w2_1"], xo=io["out"]), False, 1)
    P.build()
    return nc, P


def fused_inputs(z, i):
    b, r = divmod(i, 4)
    hc = host_consts()
    cols = np.concatenate([part * 1024 + r * 256 + np.arange(256) for part in range(3)])
    mw = z["mod_w"].reshape(4, 1024, 3072)
    mb = z["mod_b"].reshape(4, 3072)
    ei = even_inputs(z, None, b, r)
    oi = odd_inputs(z, None, b, r)
    im = {
        "cT": np.ascontiguousarray(z["c"][b].reshape(8, 128).T),
        "modw": np.ascontiguousarray(mw[:, :, cols]),
        "modb": np.ascontiguousarray(mb[:, cols].reshape(1, 3072)),
        "ng": np.ascontiguousarray(z["norm_gains"].reshape(8, 1024)),
        "qoff": np.array([[r * NT, r * 16]], np.int32),
        "x_b": np.ascontiguousarray(z["x"][b]),
        "x_tok": np.ascontiguousarray(z["x"][b, r * NT:(r + 1) * NT]),
        "e_win": ei["win"], "convw": ei["convw"], "convb": ei["convb"], "hv": ei["hv"], "normw": ei["normw"],
        "lam": ei["lam"], "subw": ei["subw"], "relb": ei["relb"], "ohdA": host_ohd(False), "ohdB": host_ohd(True),
        "e_wout": z["e_w_out"][0], "w1_0": z["mlp_w1"][0], "w2_0": z["mlp_w2"][0], "w1_1": z["mlp_w1"][1],
        "w2_1": z["mlp_w2"][1], "o_win": oi["win"], "tab": oi["tab"], "gv": oi["gv"], "sinks": oi["sinks"],
        "o_wout": z["o_w_out"][0],
    }
    for k in CONST_NAMES:
        im[k] = hc[k]
    return im


def kernel(**inputs):
    z = {k: np.asarray(v) for k, v in inputs.items()}
    nc, P_ = build_fused()
    in_maps = [{k: v for k, v in fused_inputs(z, i).items() if k in P_.declared} for i in range(8)]
    res = run_bass_kernel_spmd(nc, in_maps, core_ids=list(range(8)))
    out = np.stack([res.results[i]["out"] for i in range(8)]).reshape(2, S_LEN, 1024).astype(np.float32)
    return out
```

```python
from contextlib import ExitStack
import os
import numpy as np
import ml_dtypes
import concourse.bass as bass
import concourse.mybir as mybir
from concourse.bass_utils import run_bass_kernel_spmd

F32 = mybir.dt.float32
BF16 = mybir.dt.bfloat16
AF = mybir.ActivationFunctionType
ALU = mybir.AluOpType
AX = mybir.AxisListType
EPOCH = 20000


class Buf:
    __slots__ = ("t", "name", "w", "r", "dsem", "dcount", "is_out", "bank", "alt")

    def __init__(self, t, name, bank=None):
        self.t = t
        self.name = name
        self.bank = bank
        self.alt = None
        self.w = []
        self.r = []
        self.dsem = None
        self.dcount = 0
        self.is_out = False

    def __getitem__(self, idx):
        return self.t[idx]


class Op:
    __slots__ = ("eng", "fn", "deps", "marked", "sem", "val", "waits", "is_dma", "dbuf", "snap", "inc", "is_bar")

    def __init__(self, eng, fn, is_dma=False, dbuf=None, inc=16):
        self.inc = inc
        self.is_bar = False
        self.eng = eng
        self.fn = fn
        self.deps = []
        self.marked = False
        self.sem = None
        self.val = 0
        self.waits = []
        self.is_dma = is_dma
        self.dbuf = dbuf
        self.snap = None


class Prog:
    ENGS = ("pe", "act", "dve", "pool", "sp")

    def __init__(self, nc):
        self.nc = nc
        self.ops = []
        self.stack = ExitStack()
        self.out_ops = []
        self.nsem = 0
        self.SB_BYTES = 206 * 1024
        self.sb_f32 = self.stack.enter_context(nc.sbuf_tensor("arena", [128, self.SB_BYTES // 4], F32))
        self.sb_bf = self.sb_f32.bitcast(BF16)
        self.ps_f32 = self.stack.enter_context(nc.psum_tensor("parena", [128, 4096], F32))
        self.ps_bf = self.ps_f32.bitcast(BF16)
        self.sb_ptr = 0
        self.ps_ptr = 0
        self.dma_log = []
        self.bar = None

    @staticmethod
    def _shape_view(v, shape):
        if len(shape) == 3:
            v = v.rearrange("p (a b) -> p a b", a=shape[1])
        elif len(shape) == 4:
            v = v.rearrange("p (a b c) -> p a b c", a=shape[1], b=shape[2])
        return v

    def sb(self, name, shape, dt=F32):
        shape = list(shape)
        nel = int(np.prod(shape[1:]))
        esz = 2 if dt == BF16 else 4
        nbytes = (nel * esz + 31) // 32 * 32
        off = self.sb_ptr
        self.sb_ptr += nbytes
        assert self.sb_ptr <= self.SB_BYTES, ("SBUF arena overflow", name, self.sb_ptr)
        base = self.sb_bf if esz == 2 else self.sb_f32
        v = base[0:shape[0], off // esz:off // esz + nel]
        return Buf(self._shape_view(v, shape), name)

    def ps(self, name, shape, dt=F32):
        shape = list(shape)
        nel = int(np.prod(shape[1:]))
        esz = 2 if dt == BF16 else 4
        nb = (nel * esz + 2047) // 2048
        off = self.ps_ptr * 2048
        self.ps_ptr += nb
        assert self.ps_ptr <= 8, ("PSUM arena overflow", name)
        base = self.ps_bf if esz == 2 else self.ps_f32
        v = base[0:shape[0], off // esz:off // esz + nel]
        b = Buf(self._shape_view(v, shape), name, bank=[None])
        b.alt = self.ps_bf[:, off // 2:off // 2 + nb * 1024]
        return b

    def mark(self):
        return (self.sb_ptr, self.ps_ptr)

    def reset(self, m):
        self.sb_ptr, self.ps_ptr = m

    def barrier(self):
        if self.bar is None:
            self.bar = {e: self.sb("bar_" + e, [128, 8]) for e in ("act", "dve", "pool")}
        marks = []
        for eng in ("act", "dve", "pool"):
            b = self.bar[eng]
            if eng == "act":
                marks.append(self.op(eng, lambda e, b=b: e.memzero(b[:]), w=[b]))
            else:
                marks.append(self.op(eng, lambda e, b=b: e.memset(b[:], 0.0), w=[b]))
        latest = {}
        for d in self.dma_log:
            latest[id(d.dbuf)] = d
        self.dma_log = []
        first = True
        for eng in self.ENGS:
            op = Op(eng, None)
            op.is_bar = first
            first = False
            op.deps = marks + list(latest.values())
            self.ops.append(op)

    def dram(self, name, shape, dt, kind="Internal"):
        t = self.nc.dram_tensor(name, list(shape), dt, kind=kind)
        b = Buf(t.ap(), name)
        b.is_out = kind == "ExternalOutput"
        return b

    def newsem(self, name):
        self.nsem += 1
        return self.stack.enter_context(self.nc.semaphore(f"{name}_{self.nsem}"))

    def _rec(self, op, r, w):
        deps = []
        for b in r:
            deps.extend(b.w)
        for b in w:
            for d in b.w:
                if not (d.eng == "pe" and op.eng == "pe" and not d.is_dma and not op.is_dma):
                    deps.append(d)
            deps.extend(b.r)
        for b in r:
            b.r.append(op)
        for b in w:
            b.w = [op]
            b.r = []
        for b in list(r) + list(w):
            if b.bank is not None:
                d = b.bank[0]
                if d is not None and d.eng != op.eng:
                    deps.append(d)
                b.bank[0] = op
        seen = set()
        for d in deps:
            if id(d) not in seen and d is not op:
                seen.add(id(d))
                op.deps.append(d)
        self.ops.append(op)
        return op

    def op(self, eng, fn, r=(), w=()):
        return self._rec(Op(eng, fn), r, w)

    def dma(self, q, out, in_, r=(), w=(), sembuf=None, **kw):
        if sembuf is None:
            sembuf = (list(w) + list(r))[0]
        if isinstance(out, Buf):
            out = out.t
        if isinstance(in_, Buf):
            in_ = in_.t
        op = Op(q, lambda e: e.dma_start(out=out, in_=in_, **kw), is_dma=True, dbuf=sembuf)
        self._rec(op, r, w)
        self.dma_log.append(op)
        if any(b.is_out for b in w):
            self.out_ops.append(op)
        return op

    def collective(self, kind, groups, in_buf, out_buf):
        op = Op("pool", lambda e: e.collective_compute(kind, ALU.bypass, replica_groups=groups,
                                                       ins=[in_buf.t.opt()], outs=[out_buf.t.opt()]),
                is_dma=True, dbuf=out_buf, inc=1)
        self.dma_log.append(op)
        return self._rec(op, [in_buf], [out_buf])

    def build(self):
        nc = self.nc
        fin = Op("sp", None)
        fin.deps = list(self.out_ops)
        self.ops.append(fin)
        for op in self.ops:
            if op.is_dma:
                op.marked = True
            for d in op.deps:
                d.marked = True
        cnt = {e: 0 for e in self.ENGS}
        esem = {}
        free_d = []
        assigned = []
        for op in self.ops:
            if op.is_bar:
                for b in assigned:
                    if b.dsem is not None:
                        free_d.append(b.dsem)
                        b.dsem = None
                assigned = []
            if not op.marked:
                continue
            if op.is_dma:
                b = op.dbuf
                if b.dsem is None:
                    b.dsem = free_d.pop() if free_d else [self.newsem("d"), 0]
                    assigned.append(b)
                sm = b.dsem
                sm[1] += op.inc
                op.sem, op.val = sm[0], sm[1]
                if sm[1] >= EPOCH:
                    b.dsem = None
            else:
                e = op.eng
                if e not in esem or cnt[e] >= EPOCH:
                    esem[e] = self.newsem(e)
                    cnt[e] = 0
                cnt[e] += 1
                op.sem, op.val = esem[e], cnt[e]
        known = {e: {} for e in self.ENGS}
        nwaits = 0
        for op in self.ops:
            k = known[op.eng]
            waits = {}
            for d in op.deps:
                key = id(d.sem)
                if k.get(key, 0) >= d.val:
                    continue
                waits[key] = (d.sem, d.val)
                k[key] = d.val
                for s, v in d.snap.items():
                    if k.get(s, 0) < v:
                        k[s] = v
            op.waits = list(waits.values())
            nwaits += len(op.waits)
            if op.marked:
                op.snap = dict(k)
                op.snap[id(op.sem)] = op.val
        per = {e: [o for o in self.ops if o.eng == e] for e in self.ENGS}
        self.stats = dict(n_ops=len(self.ops), n_waits=nwaits, n_sems=self.nsem,
                          per_eng={e: len(v) for e, v in per.items()})

        def emit(engobj, lst):
            for op in lst:
                for s, v in op.waits:
                    engobj.wait_ge(s, v)
                if op.fn is None:
                    continue
                ins = op.fn(engobj)
                if op.marked:
                    ins.then_inc(op.sem, op.inc if op.is_dma else 1)

        with nc.Block() as block:
            @block.tensor
            def _(e):
                emit(e, per["pe"])

            @block.scalar
            def _(e):
                emit(e, per["act"])

            @block.vector
            def _(e):
                emit(e, per["dve"])

            @block.gpsimd
            def _(e):
                emit(e, per["pool"])

            @block.sync
            def _(e):
                emit(e, per["sp"])
        self.stack.close()
        return nc


EPS = 1e-6


def interleave(gens_with_counts):
    gens = [[g, max(1, n), 0.0] for g, n in gens_with_counts]
    total = max(n for _, n, _ in gens)
    alive = list(gens)
    while alive:
        for item in list(alive):
            g, n, acc = item
            item[2] += n / total
            while item[2] >= 1.0 - 1e-9:
                item[2] -= 1.0
                try:
                    next(g)
                except StopIteration:
                    alive.remove(item)
                    break


def bcast_row(ap_row, n=128):
    ap_row = ap_row.t if isinstance(ap_row, Buf) else ap_row
    pairs = [list(p) for p in ap_row.ap]
    w = pairs[-1]
    return bass.AP(ap_row.tensor, ap_row.offset, [[0, n], [w[0], w[1]]])


def emit_rstd(P, eng, ss, rstd, n, r_extra=()):
    pass


class TokCtx:
    def __init__(self, P, ident):
        self.P = P
        self.ident = ident
        self.junk = P.sb("junk", [128, 1024], BF16)

    def sumsq_rstd(self, src_ap, src_bufs, ss, rstd, n):
        P = self.P
        j = self.junk
        P.op("act", lambda e: e.activation(out=j[:, 0:src_ap.shape[-1]], in_=src_ap, func=AF.Square,
                                           accum_out=ss[:, 0:1]), r=src_bufs, w=[ss])
        P.op("act", lambda e: e.activation(out=rstd[:, 0:1], in_=ss[:, 0:1], func=AF.Sqrt, bias=EPS, scale=1.0 / n),
             r=[ss], w=[rstd])
        P.op("dve", lambda e: e.reciprocal(out=rstd[:, 0:1], in_=rstd[:, 0:1]), r=[rstd], w=[rstd])


def load_modrow(P, dst, mod_all, s_, part):
    mt = mod_all.t
    src = bass.AP(mt.tensor, mt.offset + s_ * 768 + part * 256, [[0, 128], [3072, 4], [1, 256]])
    P.dma("sp", dst[:].rearrange("p (r c) -> p r c", r=4), src, r=[mod_all], w=[dst])


def setup_mod_rows(P, mod_all, ng, sA, sB, i_gpost, i_gpre, gg, gmod, shift, scratch):
    load_modrow(P, gg, mod_all, sA, 2)
    P.dma("sp", scratch[0][:], bcast_row(ng[i_gpost:i_gpost + 1, :]), w=[scratch[0]])
    P.op("dve", lambda e: e.tensor_tensor(out=gg[:], in0=gg[:], in1=scratch[0][:], op=ALU.mult),
         r=[gg, scratch[0]], w=[gg])
    if gmod is not None:
        load_modrow(P, gmod, mod_all, sB, 1)
        P.dma("sp", scratch[1][:], bcast_row(ng[i_gpre:i_gpre + 1, :]), w=[scratch[1]])
        P.op("dve", lambda e: e.scalar_tensor_tensor(out=gmod[:], in0=gmod[:], scalar=1.0, in1=scratch[1][:],
                                                     op0=ALU.add, op1=ALU.mult), r=[gmod, scratch[1]], w=[gmod])
        load_modrow(P, shift, mod_all, sB, 0)


def emit_prenorm_T(P, T, xsrc, xbufs, gmod, shift, psT, hT_out_ap, hT_bufs, tmp, hb, ss, rstd, psT_view=None, part=None):
    if part in (None, 0):
        T.sumsq_rstd(xsrc, xbufs, ss, rstd, 1024)
        P.op("dve", lambda e: e.scalar_tensor_tensor(out=tmp[:], in0=xsrc, scalar=rstd[:, 0:1], in1=gmod[:],
                                                     op0=ALU.mult, op1=ALU.mult), r=list(xbufs) + [rstd, gmod], w=[tmp])
        P.op("pool", lambda e: e.tensor_tensor(out=hb[:], in0=tmp[:], in1=shift[:], op=ALU.add),
             r=[tmp, shift], w=[hb])
    if part == 0:
        return
    pv = psT.alt if psT_view is None else psT_view
    for k in range(8):
        P.op("pe", lambda e, k=k: e.transpose(out=pv[:, k * 128:(k + 1) * 128], in_=hb[:, k * 128:(k + 1) * 128],
                                              identity=T.ident[:]), r=[hb, T.ident], w=[psT])
    P.op("act", lambda e: e.copy(out=hT_out_ap, in_=pv[:, 0:1024].rearrange("p (k t) -> p k t", k=8)),
         r=[psT], w=hT_bufs)


def emit_post_residual(P, T, osrc, obufs, x_t, gg, xo, ss, rstd):
    T.sumsq_rstd(osrc, obufs, ss, rstd, 1024)
    P.op("dve", lambda e: e.scalar_tensor_tensor(out=xo[:], in0=osrc, scalar=rstd[:, 0:1], in1=gg[:],
                                                 op0=ALU.mult, op1=ALU.mult), r=list(obufs) + [rstd, gg], w=[xo])
    P.op("pool", lambda e: e.tensor_tensor(out=xo[:], in0=xo[:], in1=x_t[:], op=ALU.add),
         r=[xo, x_t], w=[xo])


NT = 2048


def phase_outproj(P, C, io, FC, even, layer):
    m0 = P.mark()
    x, wout, xn, hT = io["x"], io["wout"], io["xn"], io["hT"]
    yT_all = io["yT_all"]
    T = TokCtx(P, C["ident_bf"])
    gg = P.sb("gg", [128, 1024])
    gmod = P.sb("gmod", [128, 1024])
    shift = P.sb("shift", [128, 1024])
    xt = [P.sb(f"xt{i}", [128, 1024]) for i in range(2)]
    setup_mod_rows(P, io["mod_all"], io["ng"], 2 * layer, 2 * layer + 1, layer * 4 + 1, layer * 4 + 2, gg, gmod, shift, xt)
    if even:
        ss_in = P.sb("ss_in", [128, 4, 16])
        rs_ssd = P.sb("rs_ssd", [128, 16])
        dyn_dma(P, ss_in[:], lambda: io["ss_all"].t.rearrange("(h p) c -> p h c", p=128)[:, :, bass.ds(P.qc, 16)],
                r=[io["ss_all"]], w=[ss_in])
        P.op("dve", lambda e: e.tensor_reduce(out=rs_ssd[:], in_=ss_in[:].rearrange("p h t -> p t h"), axis=AX.X,
                                              op=ALU.add), r=[ss_in], w=[rs_ssd])
        P.op("act", lambda e: e.activation(out=rs_ssd[:], in_=rs_ssd[:], func=AF.Sqrt, bias=EPS, scale=1.0 / 1024),
             r=[rs_ssd], w=[rs_ssd])
        P.op("dve", lambda e: e.reciprocal(out=rs_ssd[:], in_=rs_ssd[:]), r=[rs_ssd], w=[rs_ssd])
    w_sb = [P.sb(f"wo{k}", [128, 1024], BF16) for k in range(FC)]
    for k in range(FC):
        P.dma("pool", w_sb[k][:], wout[k * 128:(k + 1) * 128, :], w=[w_sb[k]])
    NGL = io["ngroups"] // 4
    yq = [P.sb(f"yq{i}", [128, 4, NT], BF16) for i in range(NGL)]
    for gl in range(NGL):
        for hf in range(2):
            src = yT_all[2 * gl + hf]
            dyn_dma(P, yq[gl][hf * 64:(hf + 1) * 64, :, :], lambda src=src: src.t.rearrange("(h p) t -> p h t", p=64)[
                :, :, bass.ds(P.qv, NT)], r=[src], w=[yq[gl]])
    xo = [P.sb(f"xo{i}", [128, 1024]) for i in range(2)]
    tmp = [P.sb(f"tmp{i}", [128, 1024]) for i in range(2)]
    hb = [P.sb(f"hb{i}", [128, 1024], BF16) for i in range(2)]
    hTs = [P.sb(f"hTs{i}", [128, 8, 128], BF16) for i in range(2)]
    osb = [P.sb(f"osb{i}", [128, 1024]) for i in range(2)]
    dsb = P.sb("dsb", [128, 1024])
    ss = [P.sb(f"ss{i}", [128, 2]) for i in range(4)]
    rstd = [P.sb(f"rstd{i}", [128, 2]) for i in range(4)]
    psA = [P.ps(f"psA{i}", [128, 1024]) for i in range(2)]
    psB = P.ps("psB", [128, 1024]) if even else None
    psT = [P.ps(f"psT{i}", [128, 1024], BF16) for i in range(2)]
    nA = FC // 2 if even else FC
    pending = []
    for t in range(16):
        q4, s4 = divmod(t, 4)
        gmap = io["rowmap"]
        x_t = xt[t % 2]
        P.dma("sp", x_t[:], x[t * 128:(t + 1) * 128, :], r=[x], w=[x_t])
        pa = psA[t % 2]
        for k in range(nA):
            for hlf in range(2):
                hg_, gl_ = gmap[k]
                yb = yq[gl_]
                P.op("pe", lambda e, k=k, hlf=hlf, pa=pa, yb=yb, hg_=hg_, t=t: e.matmul(
                    pa[:, hlf * 512:(hlf + 1) * 512], lhsT=yb[:, hg_, t * 128:(t + 1) * 128],
                    rhs=w_sb[k][:, hlf * 512:(hlf + 1) * 512], start=(k == 0), stop=(k == nA - 1)),
                    r=[yb, w_sb[k]], w=[pa])
        if even:
            for k in range(nA, FC):
                for hlf in range(2):
                    hg_, gl_ = gmap[k]
                    yb = yq[gl_]
                    P.op("pe", lambda e, k=k, hlf=hlf, yb=yb, hg_=hg_, t=t: e.matmul(
                        psB[:, hlf * 512:(hlf + 1) * 512], lhsT=yb[:, hg_, t * 128:(t + 1) * 128],
                        rhs=w_sb[k][:, hlf * 512:(hlf + 1) * 512], start=(k == nA), stop=(k == FC - 1)),
                        r=[yb, w_sb[k]], w=[psB])
            P.op("act", lambda e: e.copy(out=dsb[:], in_=psB[:]), r=[psB], w=[dsb])
            o_t = osb[t % 2]
            P.op("dve", lambda e, pa=pa, o_t=o_t, t=t: e.scalar_tensor_tensor(
                out=o_t[:], in0=pa[:], scalar=rs_ssd[:, t:t + 1], in1=dsb[:], op0=ALU.mult, op1=ALU.add),
                r=[pa, rs_ssd, dsb], w=[o_t])
            osrc, obufs = o_t[:], [o_t]
        else:
            osrc, obufs = pa[:], [pa]
        while pending:
            pending.pop(0)()
        xo_t = xo[t % 2]
        emit_post_residual(P, T, osrc, obufs, x_t, gg, xo_t, ss[0], rstd[0])
        P.dma("act", xn[t * 128:(t + 1) * 128, :], xo_t[:], r=[xo_t], w=[xn])
        hts = hTs[t % 2]
        emit_prenorm_T(P, T, xo_t[:], [xo_t], gmod, shift, psT[t % 2], hts[:], [hts], tmp[1], hb[t % 2],
                       ss[1], rstd[1], part=0)

        def fin(t=t, xo_t=xo_t, hts=hts):
            emit_prenorm_T(P, T, xo_t[:], [xo_t], gmod, shift, psT[t % 2], hts[:], [hts], tmp[1], hb[t % 2],
                           ss[1], rstd[1], part=1)
            P.dma("sp", hT[:, t * 128:(t + 1) * 128].rearrange("(k p) t -> p k t", p=128), hts[:], r=[hts], w=[hT])
        pending.append(fin)
    while pending:
        pending.pop(0)()
    P.reset(m0)


def phase_mlp(P, C, io, next_pre, layer):
    m0 = P.mark()
    xn, hT, w1, w2, xo_d = io["xn"], io["hT"], io["w1"], io["w2"], io["xo"]
    hTn = io.get("hTn")
    T = TokCtx(P, C["ident_bf"])
    gg = P.sb("gg", [128, 1024])
    gmod = P.sb("gmod", [128, 1024]) if next_pre else None
    shift = P.sb("shift", [128, 1024]) if next_pre else None
    xt = [P.sb(f"xt{i}", [128, 1024]) for i in range(2)]
    setup_mod_rows(P, io["mod_all"], io["ng"], 2 * layer + 1, 2 * layer + 2, layer * 4 + 3, layer * 4 + 4, gg, gmod, shift, xt)
    w1_sb = [[P.sb(f"w1_{k}_{cb}", [128, 1024], BF16) for cb in range(4)] for k in range(8)]
    w2_sb = [P.sb(f"w2_{k}", [128, 1024], BF16) for k in range(32)]
    for cb in range(4):
        for k in range(8):
            P.dma("pool", w1_sb[k][cb][:], w1[k * 128:(k + 1) * 128, cb * 1024:(cb + 1) * 1024], w=[w1_sb[k][cb]])
    for k in range(32):
        P.dma("pool", w2_sb[k][:], w2[k * 128:(k + 1) * 128, :], w=[w2_sb[k]])
    ST = 256
    ht = [P.sb(f"ht{i}", [128, 8, ST], BF16) for i in range(2)]
    aT = [P.sb(f"aT{i}", [128, 2, ST], BF16) for i in range(16)]
    rl = [P.sb(f"rl{i}", [128, 2, ST]) for i in range(2)]
    xo = [P.sb(f"xo{i}", [128, 1024]) for i in range(2)]
    tmp = [None, P.sb("tmp1", [128, 1024])]
    ss = [P.sb(f"ss{i}", [128, 2]) for i in range(2)]
    rstd = [P.sb(f"rstd{i}", [128, 2]) for i in range(2)]
    psU = [P.ps(f"psU{i}", [128, 2, ST]) for i in range(3)]
    psO = [P.ps(f"psO{i}", [128, 1024]) for i in range(2)]
    psT = P.ps("psT", [128, 1024], BF16)
    if next_pre:
        hb = [P.sb(f"hb{i}", [128, 1024], BF16) for i in range(2)]
        hTs = [P.sb(f"hTs{i}", [128, 8, 128], BF16) for i in range(2)]
    nu = 0
    pending = []
    for st in range(NT // ST):
        h_t = ht[st % 2]
        P.dma("sp", h_t[:], hT[:, st * ST:(st + 1) * ST].rearrange("(k p) t -> p k t", p=128), r=[hT], w=[h_t])
        aTs = aT
        for fp in range(16):
            pu = psU[nu % 3]
            r_t = rl[nu % 2]
            nu += 1
            for j in range(2):
                f = fp * 2 + j
                for k in range(8):
                    wb = w1_sb[k][f // 8]
                    P.op("pe", lambda e, pu=pu, j=j, f=f, k=k, h_t=h_t, wb=wb: e.matmul(
                        pu[:, j, :], lhsT=wb[:, (f % 8) * 128:(f % 8 + 1) * 128], rhs=h_t[:, k, :],
                        start=(k == 0), stop=(k == 7)), r=[wb, h_t], w=[pu])
            P.op("act", lambda e, pu=pu, r_t=r_t: e.activation(out=r_t[:], in_=pu[:], func=AF.Relu),
                 r=[pu], w=[r_t])
            a_t = aTs[fp]
            P.op("dve" if fp % 2 == 0 else "pool", lambda e, r_t=r_t, a_t=a_t: e.tensor_tensor(
                out=a_t[:], in0=r_t[:], in1=r_t[:], op=ALU.mult), r=[r_t], w=[a_t])
        for sub in range(ST // 128):
            t = st * (ST // 128) + sub
            po = psO[t % 2]
            for f in range(32):
                for hlf in range(2):
                    P.op("pe", lambda e, po=po, f=f, hlf=hlf, sub=sub, aTs=aTs: e.matmul(
                        po[:, hlf * 512:(hlf + 1) * 512], lhsT=aTs[f // 2][:, f % 2, sub * 128:(sub + 1) * 128],
                        rhs=w2_sb[f][:, hlf * 512:(hlf + 1) * 512], start=(f == 0), stop=(f == 31)),
                        r=[aTs[f // 2], w2_sb[f]], w=[po])
            while pending:
                pending.pop(0)()
            x_t = xt[t % 2]
            P.dma("sp", x_t[:], xn[t * 128:(t + 1) * 128, :], r=[xn], w=[x_t])
            xo_t = xo[t % 2]
            emit_post_residual(P, T, po[:], [po], x_t, gg, xo_t, ss[0], rstd[0])
            P.dma("act", xo_d[t * 128:(t + 1) * 128, :], xo_t[:], r=[xo_t], w=[xo_d])
            if next_pre:
                hts = hTs[t % 2]
                emit_prenorm_T(P, T, xo_t[:], [xo_t], gmod, shift, psT, hts[:], [hts], tmp[1], hb[t % 2],
                               ss[1], rstd[1], part=0)

                def fin(t=t, xo_t=xo_t, hts=hts):
                    emit_prenorm_T(P, T, xo_t[:], [xo_t], gmod, shift, psT, hts[:], [hts], tmp[1], hb[t % 2],
                                   ss[1], rstd[1], part=1)
                    for c4 in range(4):
                        P.dma("sp", hTn[c4][:, t * 128:(t + 1) * 128].rearrange("(k p) t -> p k t", p=128),
                              hts[:, 2 * c4:2 * c4 + 2, :], r=[hts], w=[hTn[c4]])
                pending.append(fin)
    while pending:
        pending.pop(0)()
    P.reset(m0)


S_LEN = 8192
NEGV = -30000.0
E_NCOL = 1024 + 516


def host_consts():
    c = {}
    c["ident_bf"] = np.eye(128, dtype=np.float32).astype(ml_dtypes.bfloat16)
    c["ident_f"] = np.eye(128, dtype=np.float32)
    i = np.arange(128)
    c["triU"] = (i[:, None] <= i[None, :]).astype(np.float32)
    c["triS"] = (i[:, None] > i[None, :]).astype(np.float32)
    c["NEG"] = np.where(i[None, :] < i[:, None], NEGV, 0.0).astype(np.float32)
    c["ones_f"] = np.ones((128, 128), np.float32)
    c["ones_bf"] = np.ones((128, 128), np.float32).astype(ml_dtypes.bfloat16)
    c["antiI"] = np.ascontiguousarray(np.eye(128, dtype=np.float32)[::-1])
    return c


def t5_bucket_np(d):
    d = np.maximum(d, 0)
    logd = np.log(np.maximum(d, 1).astype(np.float32) / np.float32(16))
    large = 16 + (logd / np.float32(np.log(128 / 16)) * np.float32(16)).astype(np.int32)
    large = np.minimum(large, 31)
    return np.where(d < 16, d, large)


def host_ohd(swa_prev=False):
    d = np.arange(383) - 127
    oh = np.zeros((33, 384), np.float32)
    oh[t5_bucket_np(d), np.arange(383)] = 1.0
    if swa_prev:
        oh[32, :383] = np.where((d < 0) | (d >= 128), NEGV, 0.0)
    else:
        oh[32, :383] = np.where(d < 0, NEGV, 0.0)
    oh[:32, :383] *= (d >= 0)[None, :]
    return oh


def phase_even(P, C, io, do_ssd=True, do_diff=True, n_super=16):
    m0 = P.mark()
    x, win, convw, convb, hv, normw = io["x_b"], io["e_win"], io["convw"], io["convb"], io["hv"], io["normw"]
    lam, subw, relb, ohd = io["lam"], io["subw"], io["relb"], io["ohdA"]
    yT, ss_out, vecd = io["yT_loc"], io["ss_loc"], io["vecdA"]
    T = TokCtx(P, C["ident_bf"])
    gmod = P.sb("gmod", [128, 1024])
    shift = P.sb("shift", [128, 1024])
    xt = [P.sb(f"xt{i}", [128, 1024]) for i in range(2)]
    load_modrow(P, gmod, io["mod_all"], 0, 1)
    P.dma("sp", xt[0][:], bcast_row(io["ng"][0:1, :]), w=[xt[0]])
    P.op("dve", lambda e: e.scalar_tensor_tensor(out=gmod[:], in0=gmod[:], scalar=1.0, in1=xt[0][:],
                                                 op0=ALU.add, op1=ALU.mult), r=[gmod, xt[0]], w=[gmod])
    load_modrow(P, shift, io["mod_all"], 0, 0)
    w_sb = [P.sb(f"win{k}", [128, E_NCOL], BF16) for k in range(8)]
    for k in range(8):
        P.dma("pool", w_sb[k][:], win[k * 128:(k + 1) * 128, :], w=[w_sb[k]])
    cw = P.sb("cw", [128, 4, 4])
    cb = P.sb("cb", [128, 4])
    nw = P.sb("nw", [128, 2])
    hvb = P.sb("hvb", [128, 3, 4])
    P.dma("sp", cw[:], convw, w=[cw])
    P.dma("sp", cb[:], convb, w=[cb])
    P.dma("sp", nw[:], normw, w=[nw])
    P.dma("sp", hvb[:], bass.AP(hv.t.tensor, hv.t.offset, [[0, 128], [4, 3], [1, 4]]), w=[hvb])
    Aneg = P.sb("Aneg", [128, 4])
    P.op("act", lambda e: e.activation(out=Aneg[:], in_=hvb[:, 1, :], func=AF.Exp), r=[hvb], w=[Aneg])
    P.op("dve", lambda e: e.tensor_scalar(out=Aneg[:], in0=Aneg[:], scalar1=-1.0, scalar2=None, op0=ALU.mult),
         r=[Aneg], w=[Aneg])

    hT = [P.sb(f"hT{i}", [128, 8, 512], BF16) for i in range(2)]
    hb = P.sb("hb", [128, 1024], BF16)
    tmp = P.sb("tmp", [128, 1024])
    ssq = P.sb("ssq", [128, 2])
    rstd = P.sb("rstd", [128, 2])
    pAB = P.ps("pAB", [128, 512])
    pAB_bf = pAB.alt
    bk1 = P.ps("bk1", [128, 512])
    bk2 = P.ps("bk2", [128, 512])
    bk3 = P.ps("bk3", [128, 512])
    pdt = Buf(bk3.t[:, 256:272], "pdt", bank=bk3.bank)
    raw = [P.sb(f"raw{j}", [128, 3 + 512]) for j in range(4)]
    for j in range(4):
        P.op("pool", lambda e, j=j: e.memset(raw[j][:, 0:3], 0.0), w=[raw[j]])
    acc = P.sb("acc", [128, 512])
    cvT = [P.sb(f"cvT{j}", [128, 512], BF16) for j in range(4)]
    z_sb = [P.sb(f"z_sb{i}", [128, 256]) for i in range(4)]
    dtr = [P.sb(f"dtr{i}", [128, 4]) for i in range(4)]
    if do_diff:
        KT = [P.sb(f"KT{h}", [128, S_LEN], BF16) for h in range(2)]
        VV = [P.sb(f"V{h}", [128, 64, 128], BF16) for h in range(2)]
        QT = [P.sb(f"QT{h}", [128, 512], BF16) for h in range(2)]
    if do_ssd:
        S_f = P.sb("S_f", [128, 256])
        S_b = P.sb("S_b", [128, 256], BF16)
        P.op("pool", lambda e: e.memset(S_f[:], 0.0), w=[S_f])
        P.op("pool", lambda e: e.memset(S_b[:], 0.0), w=[S_b])
        ss_all = P.sb("ss_all", [128, 64])
        p_bc = Buf(bk1.t[:, 0:256].rearrange("p (a b) -> p a b", a=2), "p_bc", bank=bk1.bank)
        p_y = Buf(bk1.t[:, 256:512], "p_y", bank=bk1.bank)
        p_cbxt = Buf(bk2.t[:, 0:256], "p_cbxt", bank=bk2.bank)
        p_S = Buf(bk2.t[:, 256:512], "p_S", bank=bk2.bank)
        p_sm = Buf(bk3.t[:, 0:256], "p_sm", bank=bk3.bank)
        p_cb = p_cbxt
        p_xt = bk2.alt
        p_smb = bk3.alt
        W = {}
        for nme, shp, dt in [("dt", [128, 4], F32), ("a", [128, 4], F32), ("nacs", [128, 4], F32),
                             ("E", [128, 4], F32), ("dte", [128, 4], F32), ("dec", [128, 4], F32),
                             ("trr", [128, 2, 128], F32), ("decay", [128, 4, 128], F32), ("cbs", [128, 128], F32),
                             ("M", [128, 4, 128], BF16), ("xs_tok", [128, 256], F32), ("X", [128, 256], BF16),
                             ("Xd", [128, 256], BF16), ("B_tok", [128, 128], BF16), ("yo", [128, 256], F32),
                             ("y", [128, 256], F32), ("sz", [128, 256], F32), ("y_bf", [128, 256], BF16),
                             ("yTs", [128, 2, 128], BF16), ("Dbc", [128, 4, 64], F32)]:
            W[nme] = P.sb("w_" + nme, shp, dt)
        for r_ in range(4):
            P.op("act", lambda e, r_=r_: e.activation(out=W["Dbc"][:, r_, :], in_=C["ones_f"][:, 0:64], func=AF.Copy,
                                                      scale=hvb[:, 2, r_:r_ + 1]), r=[C["ones_f"], hvb], w=[W["Dbc"]])

    def ssd_chunk(c, st, sub):
        cs = slice(sub * 128, (sub + 1) * 128)
        zt, dt_raw = z_sb[sub], dtr[sub]
        P.op("dve", lambda e: e.tensor_tensor(out=W["dt"][:], in0=dt_raw[:], in1=hvb[:, 0, :], op=ALU.add),
             r=[dt_raw, hvb], w=[W["dt"]])
        P.op("act", lambda e: e.activation(out=W["dt"][:], in_=W["dt"][:], func=AF.Exp), r=[W["dt"]], w=[W["dt"]])
        P.op("act", lambda e: e.activation(out=W["dt"][:], in_=W["dt"][:], func=AF.Ln, bias=1.0, scale=1.0),
             r=[W["dt"]], w=[W["dt"]])
        P.op("dve", lambda e: e.tensor_tensor(out=W["a"][:], in0=W["dt"][:], in1=Aneg[:], op=ALU.mult),
             r=[W["dt"], Aneg], w=[W["a"]])
        yield
        P.op("pe", lambda e: e.matmul(p_sm[:, 0:4], lhsT=C["triU"][:], rhs=W["a"][:], start=True, stop=True),
             r=[C["triU"], W["a"]], w=[p_sm])
        P.op("pe", lambda e: e.matmul(p_sm[:, 4:8], lhsT=C["triS"][:], rhs=W["a"][:], start=True, stop=True),
             r=[C["triS"], W["a"]], w=[p_sm])
        P.op("dve", lambda e: e.tensor_scalar(out=W["nacs"][:], in0=p_sm[:, 0:4], scalar1=-1.0, scalar2=None,
                                              op0=ALU.mult), r=[p_sm], w=[W["nacs"]])
        P.op("act", lambda e: e.activation(out=W["E"][:], in_=p_sm[:, 0:4], func=AF.Exp), r=[p_sm], w=[W["E"]])
        P.op("act", lambda e: e.activation(out=W["dte"][:], in_=p_sm[:, 4:8], func=AF.Exp), r=[p_sm], w=[W["dte"]])
        P.op("dve", lambda e: e.tensor_tensor(out=W["dec"][:], in0=W["E"][:], in1=W["dte"][:], op=ALU.mult),
             r=[W["E"], W["dte"]], w=[W["dec"]])
        yield
        for half in range(2):
            yield
            for rr in range(2):
                r_ = half * 2 + rr
                P.op("dve" if rr == 0 else "pool", lambda e, r_=r_, rr=rr: e.tensor_scalar(
                    out=W["trr"][:, rr, :], in0=C["triU"][:], scalar1=W["a"][:, r_:r_ + 1], scalar2=None,
                    op0=ALU.mult), r=[C["triU"], W["a"]], w=[W["trr"]])
            yield
            for rr in range(2):
                P.op("pe", lambda e, rr=rr: e.matmul(p_bc[:, rr, :], lhsT=C["ones_f"][:], rhs=W["trr"][:, rr, :],
                                                     start=True, stop=False), r=[C["ones_f"], W["trr"]], w=[p_bc])
                P.op("pe", lambda e, rr=rr: e.matmul(p_bc[:, rr, :], lhsT=C["ident_f"][:], rhs=C["NEG"][:],
                                                     start=False, stop=True), r=[C["ident_f"], C["NEG"]], w=[p_bc])
            for rr in range(2):
                r_ = half * 2 + rr
                P.op("act", lambda e, r_=r_, rr=rr: e.activation(
                    out=W["decay"][:, r_, :], in_=p_bc[:, rr, :], func=AF.Exp, bias=W["nacs"][:, r_:r_ + 1],
                    scale=1.0), r=[p_bc, W["nacs"]], w=[W["decay"]])
        yield
        P.op("pe", lambda e: e.matmul(p_cb[:, 0:128], lhsT=cvT[2][:, cs], rhs=cvT[3][:, cs], start=True, stop=True),
             r=[cvT[2], cvT[3]], w=[p_cbxt])
        P.op("act", lambda e: e.copy(out=W["cbs"][:], in_=p_cb[:, 0:128]), r=[p_cbxt], w=[W["cbs"]])
        for r_ in range(4):
            P.op("dve" if r_ % 2 == 0 else "pool", lambda e, r_=r_: e.tensor_tensor(
                out=W["M"][:, r_, :], in0=W["decay"][:, r_, :], in1=W["cbs"][:], op=ALU.mult),
                r=[W["decay"], W["cbs"]], w=[W["M"]])
        yield
        for j in range(2):
            P.op("pe", lambda e, j=j: e.transpose(out=p_xt[:, 256 + j * 128:256 + (j + 1) * 128], in_=cvT[j][:, cs],
                                                  identity=C["ident_bf"][:]), r=[cvT[j], C["ident_bf"]], w=[p_cbxt])
        P.op("pe", lambda e: e.transpose(out=p_smb[:, 128:256], in_=cvT[2][:, cs], identity=C["ident_bf"][:]),
             r=[cvT[2], C["ident_bf"]], w=[p_sm])
        P.op("act", lambda e: e.copy(out=W["xs_tok"][:], in_=p_xt[:, 256:512]), r=[p_cbxt], w=[W["xs_tok"]])
        P.op("act", lambda e: e.copy(out=W["B_tok"][:], in_=p_smb[:, 128:256]), r=[p_sm], w=[W["B_tok"]])
        for r_ in range(4):
            hs = slice(r_ * 64, (r_ + 1) * 64)
            P.op("dve", lambda e, r_=r_, hs=hs: e.tensor_scalar(
                out=W["X"][:, hs], in0=W["xs_tok"][:, hs], scalar1=W["dt"][:, r_:r_ + 1], scalar2=None,
                op0=ALU.mult), r=[W["xs_tok"], W["dt"]], w=[W["X"]])
            P.op("pool", lambda e, r_=r_, hs=hs: e.tensor_scalar(
                out=W["Xd"][:, hs], in0=W["X"][:, hs], scalar1=W["dte"][:, r_:r_ + 1], scalar2=None,
                op0=ALU.mult), r=[W["X"], W["dte"]], w=[W["Xd"]])
        yield
        P.op("pe", lambda e: e.matmul(p_y[:], lhsT=cvT[3][:, cs], rhs=S_b[:], start=True, stop=True),
             r=[cvT[3], S_b], w=[p_y])
        for r_ in range(4):
            hs = slice(r_ * 64, (r_ + 1) * 64)
            P.op("act", lambda e, r_=r_, hs=hs: e.activation(out=W["yo"][:, hs], in_=p_y[:, hs], func=AF.Copy,
                                                             scale=W["E"][:, r_:r_ + 1]), r=[p_y, W["E"]], w=[W["yo"]])
        yield
        for r_ in range(4):
            hs = slice(r_ * 64, (r_ + 1) * 64)
            P.op("pe", lambda e, r_=r_, hs=hs: e.matmul(p_y[:, hs], lhsT=W["M"][:, r_, :], rhs=W["X"][:, hs],
                                                        start=True, stop=True), r=[W["M"], W["X"]], w=[p_y])
        P.op("dve", lambda e: e.tensor_tensor(out=W["y"][:], in0=p_y[:], in1=W["yo"][:], op=ALU.add),
             r=[p_y, W["yo"]], w=[W["y"]])
        P.op("pool", lambda e: e.tensor_tensor(out=W["yo"][:], in0=W["xs_tok"][:],
                                               in1=W["Dbc"][:].rearrange("p a b -> p (a b)"), op=ALU.mult),
             r=[W["xs_tok"], W["Dbc"]], w=[W["yo"]])
        P.op("dve", lambda e: e.tensor_tensor(out=W["y"][:], in0=W["y"][:], in1=W["yo"][:], op=ALU.add),
             r=[W["y"], W["yo"]], w=[W["y"]])
        yield
        P.op("act", lambda e: e.activation(out=W["sz"][:], in_=zt[:], func=AF.Silu), r=[zt], w=[W["sz"]])
        P.op("dve", lambda e: e.tensor_tensor(out=W["y"][:], in0=W["y"][:], in1=W["sz"][:], op=ALU.mult),
             r=[W["y"], W["sz"]], w=[W["y"]])
        P.op("act", lambda e: e.activation(out=W["sz"][:], in_=W["y"][:], func=AF.Square,
                                           accum_out=ss_all[:, c:c + 1]), r=[W["y"]], w=[W["sz"], ss_all])
        P.op("pool", lambda e: e.tensor_copy(out=W["y_bf"][:], in_=W["y"][:]), r=[W["y"]], w=[W["y_bf"]])
        yield
        yield
        for j in range(2):
            P.op("pe", lambda e, j=j: e.transpose(out=p_smb[:, 256 + j * 128:256 + (j + 1) * 128],
                                                  in_=W["y_bf"][:, j * 128:(j + 1) * 128],
                                                  identity=C["ident_bf"][:]), r=[W["y_bf"], C["ident_bf"]], w=[p_sm])
        for j in range(2):
            P.op("act", lambda e, j=j: e.activation(out=W["yTs"][:, j, :], in_=p_smb[:, 256 + j * 128:256 + (j + 1) * 128],
                                                    func=AF.Copy, scale=nw[:, j:j + 1]), r=[p_sm, nw], w=[W["yTs"]])
        for j in range(2):
            for hf in range(2):
                ch = yT[2 * j + hf]
                P.dma("sp", ch[0:64, c * 128:(c + 1) * 128], W["yTs"][hf * 64:(hf + 1) * 64, j, :], r=[W["yTs"]], w=[ch])
        yield
        P.op("pe", lambda e: e.matmul(p_S[:], lhsT=W["B_tok"][:], rhs=W["Xd"][:], start=True, stop=True),
             r=[W["B_tok"], W["Xd"]], w=[p_S])
        for r_ in range(4):
            hs = slice(r_ * 64, (r_ + 1) * 64)
            P.op("dve", lambda e, r_=r_, hs=hs: e.scalar_tensor_tensor(
                out=S_f[:, hs], in0=S_f[:, hs], scalar=W["dec"][:, r_:r_ + 1], in1=p_S[:, hs],
                op0=ALU.mult, op1=ALU.add), r=[S_f, W["dec"], p_S], w=[S_f])
        P.op("pool", lambda e: e.tensor_copy(out=S_b[:], in_=S_f[:]), r=[S_f], w=[S_b])
        yield

    diff = DiffAttn(P, C, yT, lam, subw, relb, ohd, vecd) if do_diff else None

    def prenorm_gen(st):
        h_t = hT[st % 2]
        for sub in range(4):
            t = st * 4 + sub
            x_t = xt[t % 2]
            P.dma("sp" if t % 2 == 0 else "act", x_t[:], x[t * 128:(t + 1) * 128, :], w=[x_t])
            emit_prenorm_T(P, T, x_t[:], [x_t], gmod, shift, pAB, h_t[:, :, sub * 128:(sub + 1) * 128], [h_t],
                           tmp, hb, ssq, rstd, psT_view=pAB_bf, part=0)
            yield
            yield
            yield
            emit_prenorm_T(P, T, x_t[:], [x_t], gmod, shift, pAB, h_t[:, :, sub * 128:(sub + 1) * 128], [h_t],
                           tmp, hb, ssq, rstd, psT_view=pAB_bf, part=1)
            yield

    def ssd_gen(st):
        for sub in range(4):
            yield from ssd_chunk(st * 4 + sub, st, sub)

    def diff_gen(st):
        for h_ in range(2):
            yield from diff.superblock(st, h_, KT[h_], VV[h_], QT[h_])

    for _ in prenorm_gen(0):
        pass
    for st in range(n_super):
        h_t = hT[st % 2]
        for j in range(8):
            for k in range(8):
                P.op("pe", lambda e, j=j, k=k, h_t=h_t: e.matmul(
                    pAB[:], lhsT=w_sb[k][:, j * 128:(j + 1) * 128], rhs=h_t[:, k, :], start=(k == 0), stop=(k == 7)),
                    r=[w_sb[k], h_t], w=[pAB])
            if j < 4:
                if do_ssd:
                    rw = raw[j]
                    if st > 0:
                        P.op("dve", lambda e, rw=rw: e.tensor_copy(out=rw[:, 0:3], in_=rw[:, 512:515]), r=[rw], w=[rw])
                    P.op("act", lambda e, rw=rw: e.copy(out=rw[:, 3:515], in_=pAB[:]), r=[pAB], w=[rw])
                    P.op("dve", lambda e, rw=rw, j=j: e.tensor_scalar(
                        out=acc[:], in0=rw[:, 3:515], scalar1=cw[:, j, 3:4], scalar2=cb[:, j:j + 1],
                        op0=ALU.mult, op1=ALU.add), r=[rw, cw, cb], w=[acc])
                    for tap in (2, 1, 0):
                        P.op("dve", lambda e, rw=rw, j=j, tap=tap: e.scalar_tensor_tensor(
                            out=acc[:], in0=rw[:, tap:tap + 512], scalar=cw[:, j, tap:tap + 1], in1=acc[:],
                            op0=ALU.mult, op1=ALU.add), r=[rw, cw, acc], w=[acc])
                    P.op("act", lambda e, j=j: e.activation(out=cvT[j][:], in_=acc[:], func=AF.Silu),
                         r=[acc], w=[cvT[j]])
            elif do_diff and not os.environ.get('DIFF_NOCOPY'):
                h_ = (j - 4) % 2
                if os.environ.get('DIFF_SKIP', '') .count('q' if j < 6 else 'k'):
                    pass
                elif j < 6:
                    P.op("act", lambda e, h_=h_: e.copy(out=QT[h_][:], in_=pAB[:]), r=[pAB], w=[QT[h_]])
                else:
                    P.op("act", lambda e, h_=h_, st=st: e.copy(out=KT[h_][:, st * 512:(st + 1) * 512], in_=pAB[:]),
                         r=[pAB], w=[KT[h_]])
        for sub in range(4):
            t = st * 4 + sub
            ts_ = slice(sub * 128, (sub + 1) * 128)
            for k in range(8):
                P.op("pe", lambda e, k=k, h_t=h_t, ts_=ts_: e.matmul(
                    pAB[:], lhsT=h_t[:, k, ts_], rhs=w_sb[k][:, 1024:1536], start=(k == 0), stop=(k == 7)),
                    r=[h_t, w_sb[k]], w=[pAB])
            for k in range(8):
                P.op("pe", lambda e, k=k, h_t=h_t, ts_=ts_: e.matmul(
                    pdt[:, 0:4], lhsT=h_t[:, k, ts_], rhs=w_sb[k][:, 1536:1540], start=(k == 0), stop=(k == 7)),
                    r=[h_t, w_sb[k]], w=[pdt])
            if do_ssd:
                P.op("act", lambda e, sub=sub: e.copy(out=z_sb[sub][:], in_=pAB[:, 0:256]), r=[pAB], w=[z_sb[sub]])
                P.op("dve", lambda e, sub=sub: e.tensor_copy(out=dtr[sub][:], in_=pdt[:, 0:4]), r=[pdt], w=[dtr[sub]])
            if do_diff and not os.environ.get('DIFF_NOCOPY') and not os.environ.get('DIFF_SKIP', '').count('v'):
                P.op("dve", lambda e, t=t: e.tensor_copy(out=VV[0][:, t, :], in_=pAB[:, 256:384]), r=[pAB], w=[VV[0]])
                P.op("act", lambda e, t=t: e.copy(out=VV[1][:, t, :], in_=pAB[:, 384:512]), r=[pAB], w=[VV[1]])
        gens = []
        if do_ssd:
            gens.append((ssd_gen(st), 4 * 15))
        if do_diff:
            gens.append((diff_gen(st), 2 * (2 * (4 * st + 4) + 2)))
        if st + 1 < n_super:
            gens.append((prenorm_gen(st + 1), 16))
        interleave(gens)
    if do_ssd:
        P.dma("sp", ss_out[:], ss_all[:], r=[ss_all], w=[ss_out])
    P.reset(m0)


def make_bias_tiles(P, C, relb, ohd, vecd, ps, tag):
    relb_sb = P.sb("relb_sb" + tag, [33, 2])
    ohd_sb = P.sb("ohd_sb" + tag, [33, 384])
    P.dma("sp", relb_sb[:], relb, w=[relb_sb])
    P.dma("sp", ohd_sb[:], ohd, w=[ohd_sb])
    vec_sb = P.sb("vec_sb" + tag, [2, 384])
    P.op("pe", lambda e: e.matmul(ps[0:2, 0:384], lhsT=relb_sb[:], rhs=ohd_sb[:], start=True, stop=True),
         r=[relb_sb, ohd_sb], w=[ps])
    P.op("act", lambda e: e.copy(out=vec_sb[:], in_=ps[0:2, 0:384]), r=[ps], w=[vec_sb])
    P.dma("sp", vecd[:], vec_sb[:], r=[vec_sb], w=[vecd])
    Bd, Bp, c31 = [], [], []
    vt = vecd.t
    hk = P.sb("hankel" + tag, [128, 128])
    for h in range(2):
        bd = P.sb(f"Bd{tag}{h}", [128, 128])
        bp = P.sb(f"Bp{tag}{h}", [128, 128])
        c3 = P.sb(f"c31{tag}_{h}", [128, 1])
        for dst, base in ((bd, 0), (bp, 128)):
            P.dma("sp", hk[:], bass.AP(vt.tensor, vt.offset + h * 384 + base, [[1, 128], [1, 128]]), r=[vecd], w=[hk])
            P.op("pe", lambda e: e.matmul(ps[:, 0:128], lhsT=C["antiI"][:], rhs=hk[:], start=True, stop=True),
                 r=[C["antiI"], hk], w=[ps])
            P.op("act", lambda e, dst=dst: e.copy(out=dst[:], in_=ps[:, 0:128]), r=[ps], w=[dst])
        P.dma("sp", c3[:], bass.AP(vt.tensor, vt.offset + h * 384 + 382, [[0, 128], [1, 1]]), r=[vecd], w=[c3])
        Bd.append(bd)
        Bp.append(bp)
        c31.append(c3)
    return Bd, Bp, c31


class DiffAttn:
    def __init__(self, P, C, yT, lam, subw, relb, ohd, vecd, row0=256):
        self.P, self.C, self.yT, self.row0 = P, C, yT, row0
        self.ps_s = [P.ps(f"ps_s{i}", [128, 512]) for i in range(2)]
        self.ps_o = P.ps("ps_o", [128, 512])
        self.ps_l = P.ps("ps_l", [128, 512])
        self.Bd, self.Bp, self.c31 = make_bias_tiles(P, C, relb, ohd, vecd, self.ps_l, "A")
        lam_sb = P.sb("lam_sb", [128, 256])
        P.dma("sp", lam_sb[:], bcast_row(lam), w=[lam_sb])
        pr = P.sb("lam_pr", [128, 2, 64])
        sm = P.sb("lam_sm", [128, 2])
        self.neg_lam = P.sb("neg_lam", [128, 1])
        P.op("dve", lambda e: e.tensor_tensor(out=pr[:, 0, :], in0=lam_sb[:, 0:64], in1=lam_sb[:, 64:128], op=ALU.mult),
             r=[lam_sb], w=[pr])
        P.op("dve", lambda e: e.tensor_tensor(out=pr[:, 1, :], in0=lam_sb[:, 128:192], in1=lam_sb[:, 192:256], op=ALU.mult),
             r=[lam_sb, pr], w=[pr])
        P.op("dve", lambda e: e.tensor_reduce(out=sm[:], in_=pr[:], axis=AX.X, op=ALU.add), r=[pr], w=[sm])
        P.op("act", lambda e: e.activation(out=sm[:], in_=sm[:], func=AF.Exp), r=[sm], w=[sm])
        P.op("dve", lambda e: e.tensor_tensor(out=self.neg_lam[:], in0=sm[:, 1:2], in1=sm[:, 0:1], op=ALU.subtract),
             r=[sm], w=[self.neg_lam])
        P.op("dve", lambda e: e.tensor_scalar(out=self.neg_lam[:], in0=self.neg_lam[:], scalar1=-0.2, scalar2=None,
                                              op0=ALU.add), r=[self.neg_lam], w=[self.neg_lam])
        self.subs = P.sb("subs", [128, 1])
        P.dma("sp", self.subs[:], subw, w=[self.subs])
        P.op("dve", lambda e: e.tensor_scalar(out=self.subs[:], in0=self.subs[:], scalar1=0.8, scalar2=None,
                                              op0=ALU.mult), r=[self.subs], w=[self.subs])
        self.PT = [P.sb(f"PT{i}", [128, 512], BF16) for i in range(3)]
        self.tS = [P.sb(f"tS{i}", [128, 512]) for i in range(2)]
        self.Tm = [P.sb(f"Tm{i}", [128, 512]) for i in range(2)]
        self.Rr = P.sb("Rr", [128, 512])
        self.sq = P.sb("sq", [128, 512])
        self.yd = [P.sb(f"yd{i}", [128, 512], BF16) for i in range(2)]
        self.n = 0
        self.nn = 0
        self.ny = 0

    def superblock(self, Q, h, KT, V, QT):
        import os
        stage = int(os.environ.get("DIFF_STAGE", "3"))
        if stage == 0:
            return
        yield
        P, C = self.P, self.C
        ps_o, ps_l = self.ps_o, self.ps_l
        for m in range(2):
            ms = slice(m * 64, (m + 1) * 64)

            def stage_a(kb, m=m, ms=ms):
                j0 = max(0, kb - 4 * Q)
                c0 = j0 * 128
                ps = self.ps_s[self.n % 2]
                pt = self.PT[self.n % 3]
                self.n += 1
                P.op("pe", lambda e, ps=ps, kb=kb, c0=c0, ms=ms: e.matmul(
                    ps[:, c0:512], lhsT=KT[ms, kb * 128:(kb + 1) * 128], rhs=QT[ms, c0:512], start=True, stop=True),
                    r=[KT, QT], w=[ps])
                fj = max(j0, kb + 2 - 4 * Q)
                for j in range(j0, min(4, fj)):
                    bt = self.Bd[h] if 4 * Q + j == kb else self.Bp[h]
                    ts = self.tS[self.nn % 2]
                    self.nn += 1
                    cs = slice(j * 128, (j + 1) * 128)
                    P.op("dve", lambda e, ps=ps, ts=ts, bt=bt, cs=cs: e.scalar_tensor_tensor(
                        out=ts[:, cs], in0=ps[:, cs], scalar=0.125, in1=bt[:], op0=ALU.mult, op1=ALU.add),
                        r=[ps, bt], w=[ts])
                    P.op("act", lambda e, ts=ts, pt=pt, cs=cs: e.activation(out=pt[:, cs], in_=ts[:, cs], func=AF.Exp),
                         r=[ts], w=[pt])
                if fj < 4:
                    fs = slice(fj * 128, 512)
                    P.op("act", lambda e, ps=ps, pt=pt, fs=fs: e.activation(
                        out=pt[:, fs], in_=ps[:, fs], func=AF.Exp, bias=self.c31[h][:, 0:1], scale=0.125),
                        r=[ps, self.c31[h]], w=[pt])
                return pt

            def stage_b(kb, pt):
                j0 = max(0, kb - 4 * Q)
                if kb <= 4 * Q:
                    P.op("pe", lambda e, kb=kb, pt=pt: e.matmul(ps_o[:], lhsT=V[:, kb, :], rhs=pt[:],
                                                                start=(kb == 0), stop=False), r=[V, pt], w=[ps_o])
                    P.op("pe", lambda e, kb=kb, pt=pt: e.matmul(ps_l[:], lhsT=C["ones_bf"][:], rhs=pt[:],
                                                                start=(kb == 0), stop=False), r=[C["ones_bf"], pt], w=[ps_l])
                else:
                    for j in range(j0, 4):
                        cs = slice(j * 128, (j + 1) * 128)
                        last = (kb == 4 * Q + j)
                        P.op("pe", lambda e, kb=kb, pt=pt, cs=cs, last=last: e.matmul(
                            ps_o[:, cs], lhsT=V[:, kb, :], rhs=pt[:, cs], start=(kb == 0), stop=last), r=[V, pt], w=[ps_o])
                        P.op("pe", lambda e, kb=kb, pt=pt, cs=cs, last=last: e.matmul(
                            ps_l[:, cs], lhsT=C["ones_bf"][:], rhs=pt[:, cs], start=(kb == 0), stop=last),
                            r=[C["ones_bf"], pt], w=[ps_l])

            nkb = 4 * Q + 4
            prev = stage_a(0)
            for kb in range(1, nkb):
                cur = stage_a(kb)
                yield
                stage_b(kb - 1, prev)
                prev = cur
            yield
            stage_b(nkb - 1, prev)
            if stage < 3:
                continue
            tm = self.Tm[m]
            P.op("dve", lambda e: e.reciprocal(out=self.Rr[:], in_=ps_l[:]), r=[ps_l], w=[self.Rr])
            P.op("dve", lambda e, tm=tm: e.tensor_tensor(out=tm[:], in0=ps_o[:], in1=self.Rr[:], op=ALU.mult),
                 r=[ps_o, self.Rr], w=[tm])
        yield
        if stage < 3:
            return
        t1, t2 = self.Tm
        P.op("dve", lambda e: e.scalar_tensor_tensor(out=t1[:], in0=t2[:], scalar=self.neg_lam[:, 0:1], in1=t1[:],
                                                     op0=ALU.mult, op1=ALU.add), r=[t1, t2, self.neg_lam], w=[t1])
        P.op("pool", lambda e: e.tensor_tensor(out=self.sq[:], in0=t1[:], in1=t1[:], op=ALU.mult), r=[t1], w=[self.sq])
        P.op("pe", lambda e: e.matmul(ps_l[:], lhsT=C["ones_f"][:], rhs=self.sq[:], start=True, stop=True),
             r=[C["ones_f"], self.sq], w=[ps_l])
        P.op("act", lambda e: e.activation(out=self.Rr[:], in_=ps_l[:], func=AF.Sqrt, bias=EPS, scale=1.0 / 128),
             r=[ps_l], w=[self.Rr])
        P.op("dve", lambda e: e.reciprocal(out=self.Rr[:], in_=self.Rr[:]), r=[self.Rr], w=[self.Rr])
        yd = self.yd[self.ny % 2]
        self.ny += 1
        P.op("dve", lambda e, yd=yd: e.scalar_tensor_tensor(out=yd[:], in0=t1[:], scalar=self.subs[:, 0:1], in1=self.Rr[:],
                                                            op0=ALU.mult, op1=ALU.mult), r=[t1, self.subs, self.Rr], w=[yd])
        for hf in range(2):
            ch = self.yT[4 + 2 * h + hf]
            P.dma("sp", ch[0:64, Q * 512:(Q + 1) * 512], yd[hf * 64:(hf + 1) * 64, :], r=[yd], w=[ch])


def split3(v):
    return v[0:1024], v[1024:2048], v[2048:3072]


def even_inputs(z, mod, b, hg):
    wi = z["e_w_in"][0]
    cols = np.concatenate([
        np.arange(1024 + hg * 256, 1024 + hg * 256 + 256),
        np.arange(2048 + hg * 128, 2048 + hg * 128 + 128),
        np.arange(2560 + hg * 128, 2560 + hg * 128 + 128),
        np.arange(3088 + hg * 256, 3088 + hg * 256 + 256),
        np.arange(4112 + hg * 256, 4112 + hg * 256 + 256),
        np.arange(hg * 256, hg * 256 + 256),
        np.arange(5136 + hg * 256, 5136 + hg * 256 + 256),
        np.arange(3072 + hg * 4, 3072 + hg * 4 + 4),
    ])
    ch = cols[0:512] - 1024
    cw = z["e_conv_w"][0][:, ch]
    cb = z["e_conv_b"][0][ch]
    im = {
        "win": np.ascontiguousarray(wi[:, cols]),
        "convw": np.ascontiguousarray(cw.reshape(4, 4, 128).transpose(2, 1, 0)),
        "convb": np.ascontiguousarray(cb.reshape(4, 128).T),
        "hv": np.stack([z["e_dt_bias"][0][hg * 4:hg * 4 + 4], z["e_A_log"][0][hg * 4:hg * 4 + 4],
                        z["e_D"][0][hg * 4:hg * 4 + 4]]).astype(np.float32),
        "normw": np.ascontiguousarray(z["e_ssd_norm"][0][hg * 256:hg * 256 + 256].reshape(2, 128).T),
        "lam": np.ascontiguousarray(z["e_lambda"][0].reshape(1, 256)),
        "subw": np.ascontiguousarray(z["e_diff_norm"][0].reshape(128, 1)),
        "relb": np.concatenate([z["rel_bias"][:, 2 * hg:2 * hg + 2], np.ones((1, 2), np.float32)], 0),
        "ohd": host_ohd(),
    }
    im.update(host_consts())
    return im


O_NCOL = 256 + 512 + 320


def host_rope_tables(hg):
    gamma = 1.0 - 2.0 ** (-5.0 - hg)
    pos = np.arange(S_LEN, dtype=np.float32)
    inv = (np.float32(10000.0) ** (-np.arange(64, dtype=np.float32) / np.float32(64))).astype(np.float32)
    ang = (pos[:, None] * inv[None]).astype(np.float32).astype(np.float64)
    cos, sin = np.cos(ang), np.sin(ang)
    l = (np.arange(S_LEN) % 128).astype(np.float64)
    fq = (gamma ** l)[:, None]
    fk = (gamma ** (-l))[:, None] * 128.0 ** -0.5
    tab = np.stack([
        np.concatenate([cos, cos], 1) * fq, np.concatenate([-sin, sin], 1) * fq,
        np.concatenate([cos, cos], 1) * fk, np.concatenate([-sin, sin], 1) * fk]).astype(np.float32)
    gv = np.zeros((128, 2), np.float32)
    gv[:, 0] = gamma ** 128
    return tab, gv


def phase_odd(P, C, io, n_super=16):
    m0 = P.mark()
    hT_all, win, tab, gv, sinks = io["hT2_all"], io["o_win"], io["tab"], io["gv"], io["sinks"]
    relb, ohdA, ohdB = io["relb"], io["ohdA"], io["ohdB"]
    yT, vecdA, vecdB = io["yT2_loc"], io["vecdA"], io["vecdB"]
    T = TokCtx(P, C["ident_bf"])
    w_sb = [P.sb(f"win{k}", [128, O_NCOL], BF16) for k in range(8)]
    for k in range(8):
        P.dma("pool", w_sb[k][:], win[k * 128:(k + 1) * 128, :], w=[w_sb[k]])
    gv_sb = P.sb("gv_sb", [128, 2])
    P.dma("sp", gv_sb[:], gv, w=[gv_sb])
    es = P.sb("es", [128, 2])
    P.dma("sp", es[:], bcast_row(sinks), w=[es])
    P.op("act", lambda e: e.activation(out=es[:], in_=es[:], func=AF.Exp), r=[es], w=[es])
    pFM = P.ps("pFM", [128, 512])
    pT1 = P.ps("pT1", [128, 512])
    pT2 = P.ps("pT2", [128, 512])
    pSc = P.ps("pSc", [128, 512])
    pRO = P.ps("pRO", [128, 512])
    pTr = P.ps("pTr", [128, 512])
    pSW = P.ps("pSW", [128, 512])
    pOL = P.ps("pOL", [128, 512])
    pTr_bf = pTr.alt
    BdA, _, _ = make_bias_tiles(P, C, relb, ohdA, vecdA, pOL, "oA")
    _, BpB, _ = make_bias_tiles(P, C, relb, ohdB, vecdB, pOL, "oB")
    Bpd = []
    for h in range(2):
        t = P.sb(f"Bpd{h}", [128, 2, 128])
        P.op("dve", lambda e, t=t, h=h: e.tensor_copy(out=t[:, 0, :], in_=BpB[h][:]), r=[BpB[h]], w=[t])
        P.op("dve", lambda e, t=t, h=h: e.tensor_copy(out=t[:, 1, :], in_=BdA[h][:]), r=[BdA[h], t], w=[t])
        Bpd.append(t)
    hT = [P.sb(f"hT{i}", [128, 8, 512], BF16) for i in range(2)]
    tb = [P.sb(f"tb{i}", [128, 4, 4, 128]) for i in range(2)]
    SQT = P.sb("SQT", [128, 512], BF16)
    SKT = P.sb("SKT", [128, 128 + 512], BF16)
    SV = P.sb("SV", [128, 5, 64], BF16)
    qkv = [P.sb(f"qkv{i}", [128, 256]) for i in range(4)]
    v_bf = [P.sb(f"v_bf{i}", [128, 256], BF16) for i in range(4)]
    sg = [P.sb(f"sg{i}", [128, 256]) for i in range(4)]
    Wk = {}
    for nme, shp, dt in [("A", [128, 128], F32), ("B", [128, 128], F32), ("Qp", [128, 128], BF16),
                         ("Kp", [128, 128], BF16), ("QT", [128, 128], BF16), ("KT", [128, 128], BF16),
                         ("Sm", [128, 128], BF16), ("y_bf", [128, 256], BF16), ("yTs", [128, 2, 128], BF16),
                         ("ts", [128, 2, 128], F32), ("PT", [128, 2, 128], BF16), ("den", [64, 128], F32),
                         ("ob", [64, 128], BF16), ("ss", [128, 2], F32), ("rstd", [128, 2], F32)]:
        Wk[nme] = P.sb("k_" + nme, shp, dt)
    St = P.sb("St", [128, 256])
    gS = P.sb("gS", [128, 256], BF16)
    P.op("pool", lambda e: e.memset(St[:], 0.0), w=[St])
    P.op("pool", lambda e: e.memset(gS[:], 0.0), w=[gS])

    def ret_chunk(c, sub, tbs, qk, vb, sgt):
        for which, (col0, tq, dst) in enumerate(((0, 0, "Qp"), (128, 2, "Kp"))):
            src = qk[:, col0:col0 + 128]
            P.op("dve", lambda e, src=src, tq=tq: e.tensor_tensor(out=Wk["A"][:], in0=src, in1=tbs[:, tq, sub, :],
                                                                 op=ALU.mult), r=[qk, tbs], w=[Wk["A"]])
            P.op("pool", lambda e, col0=col0, tq=tq: e.tensor_tensor(
                out=Wk["B"][:, 0:64], in0=qk[:, col0 + 64:col0 + 128], in1=tbs[:, tq + 1, sub, 0:64], op=ALU.mult),
                r=[qk, tbs], w=[Wk["B"]])
            P.op("pool", lambda e, col0=col0, tq=tq: e.tensor_tensor(
                out=Wk["B"][:, 64:128], in0=qk[:, col0:col0 + 64], in1=tbs[:, tq + 1, sub, 64:128], op=ALU.mult),
                r=[qk, tbs, Wk["B"]], w=[Wk["B"]])
            P.op("dve", lambda e, dst=dst: e.tensor_tensor(out=Wk[dst][:], in0=Wk["A"][:], in1=Wk["B"][:], op=ALU.add),
                 r=[Wk["A"], Wk["B"]], w=[Wk[dst]])
            P.op("pe", lambda e, dst=dst, which=which: e.transpose(
                out=pTr_bf[:, which * 128:(which + 1) * 128], in_=Wk[dst][:], identity=C["ident_bf"][:]),
                r=[Wk[dst], C["ident_bf"]], w=[pTr])
        yield
        P.op("act", lambda e: e.copy(out=Wk["QT"][:], in_=pTr_bf[:, 0:128]), r=[pTr], w=[Wk["QT"]])
        P.op("act", lambda e: e.copy(out=Wk["KT"][:], in_=pTr_bf[:, 128:256]), r=[pTr], w=[Wk["KT"]])
        P.op("pe", lambda e: e.matmul(pSc[:, 0:128], lhsT=Wk["KT"][:], rhs=Wk["QT"][:], start=True, stop=True),
             r=[Wk["KT"], Wk["QT"]], w=[pSc])
        P.op("dve", lambda e: e.tensor_tensor(out=Wk["Sm"][:], in0=pSc[:, 0:128], in1=C["triU"][:], op=ALU.mult),
             r=[pSc, C["triU"]], w=[Wk["Sm"]])
        yield
        P.op("pe", lambda e: e.matmul(pRO[:, 0:256], lhsT=Wk["Sm"][:], rhs=vb[:], start=True, stop=False),
             r=[Wk["Sm"], vb], w=[pRO])
        P.op("pe", lambda e: e.matmul(pRO[:, 0:256], lhsT=Wk["QT"][:], rhs=gS[:], start=False, stop=True),
             r=[Wk["QT"], gS], w=[pRO])
        yield
        T.sumsq_rstd(pRO[:, 0:256], [pRO], Wk["ss"], Wk["rstd"], 256)
        yield
        P.op("dve", lambda e: e.scalar_tensor_tensor(out=Wk["y_bf"][:], in0=pRO[:, 0:256], scalar=Wk["rstd"][:, 0:1],
                                                     in1=sgt[:], op0=ALU.mult, op1=ALU.mult),
             r=[pRO, Wk["rstd"], sgt], w=[Wk["y_bf"]])
        for j in range(2):
            P.op("pe", lambda e, j=j: e.transpose(out=pTr_bf[:, 256 + j * 128:256 + (j + 1) * 128],
                                                  in_=Wk["y_bf"][:, j * 128:(j + 1) * 128], identity=C["ident_bf"][:]),
                 r=[Wk["y_bf"], C["ident_bf"]], w=[pTr])
        P.op("act", lambda e: e.copy(out=Wk["yTs"][:], in_=pTr_bf[:, 256:512].rearrange("p (j t) -> p j t", j=2)),
             r=[pTr], w=[Wk["yTs"]])
        for j in range(2):
            for hf in range(2):
                ch = yT[2 * j + hf]
                P.dma("sp", ch[0:64, c * 128:(c + 1) * 128], Wk["yTs"][hf * 64:(hf + 1) * 64, j, :], r=[Wk["yTs"]], w=[ch])
        yield
        P.op("pe", lambda e: e.matmul(pRO[:, 256:512], lhsT=Wk["Kp"][:], rhs=vb[:], start=True, stop=True),
             r=[Wk["Kp"], vb], w=[pRO])
        P.op("dve", lambda e: e.scalar_tensor_tensor(out=St[:], in0=St[:], scalar=gv_sb[:, 0:1], in1=pRO[:, 256:512],
                                                     op0=ALU.mult, op1=ALU.add), r=[St, gv_sb, pRO], w=[St])
        P.op("act", lambda e: e.activation(out=gS[:], in_=St[:], func=AF.Copy, scale=gv_sb[:, 0:1]),
             r=[St, gv_sb], w=[gS])
        yield

    def swa_block(blk, sub):
        for h in range(2):
            hs = slice(h * 64, (h + 1) * 64)
            qcols = slice(sub * 128, (sub + 1) * 128)
            first = (blk == 0)
            if not first:
                P.op("pe", lambda e, hs=hs, qcols=qcols, sub=sub: e.matmul(
                    pSW[:, 0:128], lhsT=SKT[hs, sub * 128:(sub + 1) * 128], rhs=SQT[hs, qcols], start=True, stop=True),
                    r=[SKT, SQT], w=[pSW])
            P.op("pe", lambda e, hs=hs, qcols=qcols, sub=sub: e.matmul(
                pSW[:, 128:256], lhsT=SKT[hs, (sub + 1) * 128:(sub + 2) * 128], rhs=SQT[hs, qcols], start=True, stop=True),
                r=[SKT, SQT], w=[pSW])
            lo = 1 if first else 0
            yield
            P.op("dve", lambda e, h=h, lo=lo: e.scalar_tensor_tensor(
                out=Wk["ts"][:, lo:2, :], in0=pSW[:, lo * 128:256].rearrange("p (a b) -> p a b", b=128), scalar=0.125,
                in1=Bpd[h][:, lo:2, :], op0=ALU.mult, op1=ALU.add), r=[pSW, Bpd[h]], w=[Wk["ts"]])
            P.op("act", lambda e, lo=lo: e.activation(out=Wk["PT"][:, lo:2, :], in_=Wk["ts"][:, lo:2, :], func=AF.Exp),
                 r=[Wk["ts"]], w=[Wk["PT"]])
            oc = slice(h * 256, h * 256 + 128)
            lc = slice(h * 256 + 128, h * 256 + 256)
            yield
            if not first:
                P.op("pe", lambda e, oc=oc, sub=sub: e.matmul(pOL[0:64, oc], lhsT=SV[:, sub, :], rhs=Wk["PT"][:, 0, :],
                                                              start=True, stop=False), r=[SV, Wk["PT"]], w=[pOL])
            P.op("pe", lambda e, oc=oc, sub=sub, first=first: e.matmul(
                pOL[0:64, oc], lhsT=SV[:, sub + 1, :], rhs=Wk["PT"][:, 1, :], start=first, stop=True),
                r=[SV, Wk["PT"]], w=[pOL])
            if not first:
                P.op("pe", lambda e, lc=lc: e.matmul(pOL[0:64, lc], lhsT=C["ones_bf"][:, 0:64], rhs=Wk["PT"][:, 0, :],
                                                     start=True, stop=False), r=[C["ones_bf"], Wk["PT"]], w=[pOL])
            P.op("pe", lambda e, lc=lc, first=first: e.matmul(
                pOL[0:64, lc], lhsT=C["ones_bf"][:, 0:64], rhs=Wk["PT"][:, 1, :], start=first, stop=True),
                r=[C["ones_bf"], Wk["PT"]], w=[pOL])
            yield
            P.op("dve", lambda e, lc=lc, h=h: e.tensor_scalar(out=Wk["den"][:], in0=pOL[0:64, lc], scalar1=es[0:64, h:h + 1],
                                                              scalar2=None, op0=ALU.add), r=[pOL, es], w=[Wk["den"]])
            P.op("dve", lambda e: e.reciprocal(out=Wk["den"][:], in_=Wk["den"][:]), r=[Wk["den"]], w=[Wk["den"]])
            P.op("dve", lambda e, oc=oc: e.tensor_tensor(out=Wk["ob"][:], in0=pOL[0:64, oc], in1=Wk["den"][:], op=ALU.mult),
                 r=[pOL, Wk["den"]], w=[Wk["ob"]])
            P.dma("act", yT[4 + h][0:64, blk * 128:(blk + 1) * 128], Wk["ob"][:], r=[Wk["ob"]], w=[yT[4 + h]])

    for st in range(n_super):
        h_t = hT[st % 2]
        qq, so = divmod(st, 4)
        for c4 in range(4):
            P.dma("sp" if c4 % 2 == 0 else "act", h_t[:, 2 * c4:2 * c4 + 2, :],
                  hT_all[c4][qq * 256:(qq + 1) * 256, so * 512:(so + 1) * 512].rearrange("(k p) t -> p k t", p=128),
                  r=[hT_all[c4]], w=[h_t])
        tbs = tb[st % 2]
        for q4 in range(4):
            P.dma("act", tbs[:, q4, :, :], tab[q4, st * 512:(st + 1) * 512, :].rearrange("(s p) d -> p s d", p=128),
                  w=[tbs])
        for j in range(2):
            for k in range(8):
                P.op("pe", lambda e, j=j, k=k, h_t=h_t: e.matmul(
                    pFM[:], lhsT=w_sb[k][:, j * 128:(j + 1) * 128], rhs=h_t[:, k, :], start=(k == 0), stop=(k == 7)),
                    r=[w_sb[k], h_t], w=[pFM])
            if j == 0:
                P.op("act", lambda e: e.copy(out=SQT[:], in_=pFM[:]), r=[pFM], w=[SQT])
            else:
                if st > 0:
                    P.op("dve", lambda e: e.tensor_copy(out=SKT[:, 0:128], in_=SKT[:, 512:640]), r=[SKT], w=[SKT])
                    P.op("dve", lambda e: e.tensor_copy(out=SV[:, 0, :], in_=SV[:, 4, :]), r=[SV], w=[SV])
                P.op("act", lambda e: e.copy(out=SKT[:, 128:640], in_=pFM[:]), r=[pFM], w=[SKT])
        for sub in range(4):
            c = st * 4 + sub
            ts_ = slice(sub * 128, (sub + 1) * 128)
            for k in range(8):
                P.op("pe", lambda e, k=k, h_t=h_t, ts_=ts_: e.matmul(
                    pT1[:], lhsT=h_t[:, k, ts_], rhs=w_sb[k][:, 256:768], start=(k == 0), stop=(k == 7)),
                    r=[h_t, w_sb[k]], w=[pT1])
            for k in range(8):
                P.op("pe", lambda e, k=k, h_t=h_t, ts_=ts_: e.matmul(
                    pT2[:, 0:320], lhsT=h_t[:, k, ts_], rhs=w_sb[k][:, 768:1088], start=(k == 0), stop=(k == 7)),
                    r=[h_t, w_sb[k]], w=[pT2])
            qk = qkv[sub]
            vb = v_bf[sub]
            sgt = sg[sub]
            P.op("act", lambda e, qk=qk: e.copy(out=qk[:, 0:256], in_=pT1[:, 0:256]), r=[pT1], w=[qk])
            P.op("dve", lambda e, vb=vb: e.tensor_copy(out=vb[:], in_=pT1[:, 256:512]), r=[pT1], w=[vb])
            P.op("act", lambda e, sgt=sgt: e.activation(out=sgt[:], in_=pT2[:, 0:256], func=AF.Silu), r=[pT2], w=[sgt])
            P.op("dve", lambda e, sub=sub: e.tensor_copy(out=SV[:, sub + 1, :], in_=pT2[:, 256:320]), r=[pT2], w=[SV])


        def ret_gen(st=st, tbs=tbs):
            for sub in range(4):
                yield from ret_chunk(st * 4 + sub, sub, tbs, qkv[sub], v_bf[sub], sg[sub])

        def swa_gen(st=st):
            for sub in range(4):
                yield from swa_block(st * 4 + sub, sub)

        interleave([(ret_gen(), 4 * 6), (swa_gen(), 4 * 2 * 4)])
    P.reset(m0)


def odd_inputs(z, hT_full, b, hg):
    wi = z["o_w_in"][0]
    kv = hg // 2
    cols = np.concatenate([
        np.arange(3072 + 2 * hg * 64, 3072 + 2 * hg * 64 + 128),
        np.arange(3584 + kv * 64, 3584 + kv * 64 + 64), np.arange(3584 + kv * 64, 3584 + kv * 64 + 64),
        np.arange(hg * 128, hg * 128 + 128),
        np.arange(512 + hg * 128, 512 + hg * 128 + 128),
        np.arange(1024 + hg * 256, 1024 + hg * 256 + 256),
        np.arange(2048 + hg * 256, 2048 + hg * 256 + 256),
        np.arange(3712 + kv * 64, 3712 + kv * 64 + 64),
    ])
    tab, gv = host_rope_tables(hg)
    hc = host_consts()
    im = {
        "win": np.ascontiguousarray(wi[:, cols]), "tab": tab, "gv": gv,
        "sinks": np.ascontiguousarray(z["o_sinks"][0][2 * hg:2 * hg + 2].reshape(1, 2)),
        "relb": np.concatenate([z["rel_bias"][:, 2 * hg:2 * hg + 2], np.ones((1, 2), np.float32)], 0),
        "ohdA": host_ohd(False), "ohdB": host_ohd(True),
    }
    for k in ["ident_bf", "triU", "ones_bf", "antiI"]:
        im[k] = hc[k]
    return im


I32 = mybir.dt.int32
GROUPS = [[0, 1, 2, 3], [4, 5, 6, 7]]


def dyn_dma(P, out, in_fn, r, w):
    op = Op("sp", lambda e: e.dma_start(out=out, in_=in_fn()), is_dma=True, dbuf=w[0])
    P._rec(op, r, w)
    P.dma_log.append(op)
    return op


def setup_regs(P, nc, qoff):
    regs = [P.stack.enter_context(nc.sync.register(f"qr{i}")) for i in range(2)]

    def ld(e):
        for i in range(2):
            ins = e.reg_load(regs[i], qoff.t[0:1, i:i + 1])
        P.qv = e.snap(regs[0], min_val=0, max_val=6144)
        P.qc = e.snap(regs[1], min_val=0, max_val=48)
        return ins
    P.op("sp", ld)


def extract_quarter(P, src_all, dst_q, nrows, step):
    for r0 in range(0, nrows, step):
        dyn_dma(P, dst_q.t[r0:r0 + step, :], lambda r0=r0: src_all.t[r0:r0 + step, bass.ds(P.qv, NT)],
                r=[src_all], w=[dst_q])


def phase_mod(P, io):
    m0 = P.mark()
    cT, modw, modb = io["cT"], io["modw"], io["modb"]
    c_sb = P.sb("c_sb", [128, 8])
    ca = P.sb("ca_sb", [128, 8])
    b_sb = P.sb("b_sb", [1, 3072])
    o_sb = P.sb("o_sb", [1, 3072])
    wt = [P.sb(f"mw{i}", [128, 8, 768]) for i in range(2)]
    ps = [P.ps(f"mps{i}", [1, 512]) for i in range(2)]
    P.dma("sp", c_sb[:], cT, w=[c_sb])
    P.dma("sp", b_sb[:], modb, w=[b_sb])
    P.op("act", lambda e: e.activation(out=ca[:], in_=c_sb[:], func=AF.Silu), r=[c_sb], w=[ca])
    for s_ in range(4):
        w_t = wt[s_ % 2]
        P.dma("sp" if s_ % 2 == 0 else "act", w_t[:], modw[s_].rearrange("(k p) n -> p k n", p=128), w=[w_t])
        for hf in range(2):
            for k in range(8):
                P.op("pe", lambda e, hf=hf, k=k, w_t=w_t: e.matmul(
                    ps[hf][0:1, 0:384], lhsT=ca[:, k:k + 1], rhs=w_t[:, k, hf * 384:(hf + 1) * 384],
                    start=(k == 0), stop=(k == 7)), r=[ca, w_t], w=[ps[hf]])
            o0 = s_ * 768 + hf * 384
            P.op("dve", lambda e, hf=hf, o0=o0: e.tensor_tensor(out=o_sb[0:1, o0:o0 + 384], in0=ps[hf][0:1, 0:384],
                                                                in1=b_sb[0:1, o0:o0 + 384], op=ALU.add),
                 r=[ps[hf], b_sb], w=[o_sb])
    P.dma("sp", io["mod_loc"], o_sb[:], r=[o_sb], w=[io["mod_loc"]])
    P.collective("AllGather", GROUPS, io["mod_loc"], io["mod_all"])
    P.reset(m0)


CONST_NAMES = ["ident_bf", "ident_f", "triU", "triS", "NEG", "ones_f", "ones_bf", "antiI"]


def build_fused():
    nc = bass.Bass("TRN2", target_bir_lowering=False)
    P = Prog(nc)

    def X(name, shape, dt=F32):
        return Buf(nc.dram_tensor(name, list(shape), dt, kind="ExternalInput").ap(), name)

    io = {}
    for name, shape, dt in [
        ("cT", [128, 8], F32), ("modw", [4, 1024, 768], F32), ("modb", [1, 3072], F32), ("ng", [8, 1024], F32),
        ("qoff", [1, 2], I32), ("x_b", [S_LEN, 1024], F32), ("x_tok", [NT, 1024], F32),
        ("e_win", [1024, E_NCOL], F32), ("convw", [128, 4, 4], F32), ("convb", [128, 4], F32), ("hv", [3, 4], F32),
        ("normw", [128, 2], F32), ("lam", [1, 256], F32), ("subw", [128, 1], F32), ("relb", [33, 2], F32),
        ("ohdA", [33, 384], F32), ("ohdB", [33, 384], F32), ("e_wout", [2048, 1024], F32),
        ("w1_0", [1024, 4096], F32), ("w2_0", [4096, 1024], F32), ("w1_1", [1024, 4096], F32), ("w2_1", [4096, 1024], F32),
        ("o_win", [1024, O_NCOL], F32), ("tab", [4, S_LEN, 128], F32), ("gv", [128, 2], F32), ("sinks", [1, 2], F32),
        ("o_wout", [1536, 1024], F32),
    ]:
        if int(os.environ.get("FUSED_STOP", "99")) <= 2 and name in ("x_tok", "e_wout", "w1_0", "w2_0", "w1_1", "w2_1", "o_win", "tab", "o_wout"):
            continue
        io[name] = X(name, shape, dt)
    P.declared = set(io.keys()) | set(CONST_NAMES)
    cn = {k: X(k, [128, 128], BF16 if k.endswith("bf") else F32) for k in CONST_NAMES}
    for name, shape, dt in [
        ("mod_loc", [1, 3072], F32), ("mod_all", [4, 3072], F32),
        ("ss_loc", [128, 64], F32), ("ss_all", [512, 64], F32),
        ("xn0", [NT, 1024], F32), ("hT0", [1024, NT], BF16), ("x1", [NT, 1024], F32),

        ("xn1", [NT, 1024], F32), ("hT1", [1024, NT], BF16), ("vecdA", [2, 384], F32), ("vecdB", [2, 384], F32),
        ("vecdC", [2, 384], F32), ("yT_q", [2048, NT], BF16), ("ss_q", [512, 16], F32), ("yT2_q", [1536, NT], BF16),
    ]:
        io[name] = P.dram(name, shape, dt)
    io["yT_loc"] = [P.dram(f"yT_loc{i}", [64, S_LEN], BF16) for i in range(8)]
    io["yT_all"] = [P.dram(f"yT_all{i}", [256, S_LEN], BF16) for i in range(8)]
    io["yT2_loc"] = [P.dram(f"yT2_loc{i}", [64, S_LEN], BF16) for i in range(6)]
    io["yT2_all"] = [P.dram(f"yT2_all{i}", [256, S_LEN], BF16) for i in range(6)]
    io["hTn_loc"] = [P.dram(f"hTn_loc{i}", [256, NT], BF16) for i in range(4)]
    io["hT2_all"] = [P.dram(f"hT2_all{i}", [1024, NT], BF16) for i in range(4)]
    io["out"] = P.dram("out", [NT, 1024], F32, kind="ExternalOutput")

    setup_regs(P, nc, io["qoff"])
    C = {}
    for k in CONST_NAMES:
        C[k] = P.sb("c_" + k, [128, 128], BF16 if k.endswith("bf") else F32)
        P.dma("sp", C[k][:], cn[k], w=[C[k]])
    P.barrier()

    stop = int(os.environ.get("FUSED_STOP", "99"))
    phase_mod(P, io)
    P.barrier()
    if stop <= 1:
        P.build()
        return nc, P
    phase_even(P, C, io, n_super=int(os.environ.get('FUSED_NSUPER', '16')))
    for i in range(8):
        P.collective("AllGather", GROUPS, io["yT_loc"][i], io["yT_all"][i])
    if not os.environ.get("FUSED_NOAG2"):
        P.collective("AllGather", GROUPS, io["ss_loc"], io["ss_all"])
    P.barrier()
    if stop <= 2:
        P.build()
        return nc, P
    rm_e = [(kk // 2, kk % 2) for kk in range(8)] + [(kk // 2, 2 + kk % 2) for kk in range(8)]
    phase_outproj(P, C, dict(io, x=io["x_tok"], yT_all=io["yT_all"], wout=io["e_wout"], xn=io["xn0"], hT=io["hT0"], rowmap=rm_e, ngroups=16),
                  16, True, 0)
    P.barrier()
    if stop <= 3:
        P.build()
        return nc, P
    phase_mlp(P, C, dict(io, xn=io["xn0"], hT=io["hT0"], w1=io["w1_0"], w2=io["w2_0"], xo=io["x1"], hTn=io["hTn_loc"]), True, 0)
    for i in range(4):
        P.collective("AllGather", GROUPS, io["hTn_loc"][i], io["hT2_all"][i])
    P.barrier()
    if stop <= 4:
        P.build()
        return nc, P
    phase_odd(P, C, dict(io, vecdA=io["vecdB"], vecdB=io["vecdC"]), n_super=int(os.environ.get('FUSED_NSUPER', '16')))
    for i in range(6):
        P.collective("AllGather", GROUPS, io["yT2_loc"][i], io["yT2_all"][i])
    P.barrier()
    rm_o = [(kk // 2, kk % 2) for kk in range(8)] + [(kk, 2) for kk in range(4)]
    phase_outproj(P, C, dict(io, x=io["x1"], yT_all=io["yT2_all"], wout=io["o_wout"], xn=io["xn1"], hT=io["hT1"], rowmap=rm_o, ngroups=12),
                  12, False, 1)
    P.barrier()
    phase_mlp(P, C, dict(io, xn=io["xn1"], hT=io["hT1"], w1=io["w1_1"], w2=io["w2_1"], xo=io["out"]), False, 1)
    P.build()
    return nc, P


def fused_inputs(z, i):
    b, r = divmod(i, 4)
    hc = host_consts()
    cols = np.concatenate([part * 1024 + r * 256 + np.arange(256) for part in range(3)])
    mw = z["mod_w"].reshape(4, 1024, 3072)
    mb = z["mod_b"].reshape(4, 3072)
    ei = even_inputs(z, None, b, r)
    oi = odd_inputs(z, None, b, r)
    im = {
        "cT": np.ascontiguousarray(z["c"][b].reshape(8, 128).T),
        "modw": np.ascontiguousarray(mw[:, :, cols]),
        "modb": np.ascontiguousarray(mb[:, cols].reshape(1, 3072)),
        "ng": np.ascontiguousarray(z["norm_gains"].reshape(8, 1024)),
        "qoff": np.array([[r * NT, r * 16]], np.int32),
        "x_b": np.ascontiguousarray(z["x"][b]),
        "x_tok": np.ascontiguousarray(z["x"][b, r * NT:(r + 1) * NT]),
        "e_win": ei["win"], "convw": ei["convw"], "convb": ei["convb"], "hv": ei["hv"], "normw": ei["normw"],
        "lam": ei["lam"], "subw": ei["subw"], "relb": ei["relb"], "ohdA": host_ohd(False), "ohdB": host_ohd(True),
        "e_wout": z["e_w_out"][0], "w1_0": z["mlp_w1"][0], "w2_0": z["mlp_w2"][0], "w1_1": z["mlp_w1"][1],
        "w2_1": z["mlp_w2"][1], "o_win": oi["win"], "tab": oi["tab"], "gv": oi["gv"], "sinks": oi["sinks"],
        "o_wout": z["o_w_out"][0],
    }
    for k in CONST_NAMES:
        im[k] = hc[k]
    return im


def kernel(**inputs):
    z = {k: np.asarray(v) for k, v in inputs.items()}
    nc, P_ = build_fused()
    in_maps = [{k: v for k, v in fused_inputs(z, i).items() if k in P_.declared} for i in range(8)]
    res = run_bass_kernel_spmd(nc, in_maps, core_ids=list(range(8)))
    out = np.stack([res.results[i]["out"] for i in range(8)]).reshape(2, S_LEN, 1024).astype(np.float32)
    return out
```

```python
from contextlib import ExitStack
import os
import numpy as np
import ml_dtypes
import concourse.bass as bass
import concourse.mybir as mybir
from concourse.bass_utils import run_bass_kernel_spmd

F32 = mybir.dt.float32
BF16 = mybir.dt.bfloat16
AF = mybir.ActivationFunctionType
ALU = mybir.AluOpType
AX = mybir.AxisListType
EPOCH = 20000


class Buf:
    __slots__ = ("t", "name", "w", "r", "dsem", "dcount", "is_out", "bank", "alt")

    def __init__(self, t, name, bank=None):
        self.t = t
        self.name = name
        self.bank = bank
        self.alt = None
        self.w = []
        self.r = []
        self.dsem = None
        self.dcount = 0
        self.is_out = False

    def __getitem__(self, idx):
        return self.t[idx]


class Op:
    __slots__ = ("eng", "fn", "deps", "marked", "sem", "val", "waits", "is_dma", "dbuf", "snap", "inc", "is_bar")

    def __init__(self, eng, fn, is_dma=False, dbuf=None, inc=16):
        self.inc = inc
        self.is_bar = False
        self.eng = eng
        self.fn = fn
        self.deps = []
        self.marked = False
        self.sem = None
        self.val = 0
        self.waits = []
        self.is_dma = is_dma
        self.dbuf = dbuf
        self.snap = None


class Prog:
    ENGS = ("pe", "act", "dve", "pool", "sp")

    def __init__(self, nc):
        self.nc = nc
        self.ops = []
        self.stack = ExitStack()
        self.out_ops = []
        self.nsem = 0
        self.SB_BYTES = 206 * 1024
        self.sb_f32 = self.stack.enter_context(nc.sbuf_tensor("arena", [128, self.SB_BYTES // 4], F32))
        self.sb_bf = self.sb_f32.bitcast(BF16)
        self.ps_f32 = self.stack.enter_context(nc.psum_tensor("parena", [128, 4096], F32))
        self.ps_bf = self.ps_f32.bitcast(BF16)
        self.sb_ptr = 0
        self.ps_ptr = 0
        self.dma_log = []
        self.bar = None

    @staticmethod
    def _shape_view(v, shape):
        if len(shape) == 3:
            v = v.rearrange("p (a b) -> p a b", a=shape[1])
        elif len(shape) == 4:
            v = v.rearrange("p (a b c) -> p a b c", a=shape[1], b=shape[2])
        return v

    def sb(self, name, shape, dt=F32):
        shape = list(shape)
        nel = int(np.prod(shape[1:]))
        esz = 2 if dt == BF16 else 4
        nbytes = (nel * esz + 31) // 32 * 32
        off = self.sb_ptr
        self.sb_ptr += nbytes
        assert self.sb_ptr <= self.SB_BYTES, ("SBUF arena overflow", name, self.sb_ptr)
        base = self.sb_bf if esz == 2 else self.sb_f32
        v = base[0:shape[0], off // esz:off // esz + nel]
        return Buf(self._shape_view(v, shape), name)

    def ps(self, name, shape, dt=F32):
        shape = list(shape)
        nel = int(np.prod(shape[1:]))
        esz = 2 if dt == BF16 else 4
        nb = (nel * esz + 2047) // 2048
        off = self.ps_ptr * 2048
        self.ps_ptr += nb
        assert self.ps_ptr <= 8, ("PSUM arena overflow", name)
        base = self.ps_bf if esz == 2 else self.ps_f32
        v = base[0:shape[0], off // esz:off // esz + nel]
        b = Buf(self._shape_view(v, shape), name, bank=[None])
        b.alt = self.ps_bf[:, off // 2:off // 2 + nb * 1024]
        return b

    def mark(self):
        return (self.sb_ptr, self.ps_ptr)

    def reset(self, m):
        self.sb_ptr, self.ps_ptr = m

    def barrier(self):
        if self.bar is None:
            self.bar = {e: self.sb("bar_" + e, [128, 8]) for e in ("act", "dve", "pool")}
        marks = []
        for eng in ("act", "dve", "pool"):
            b = self.bar[eng]
            if eng == "act":
                marks.append(self.op(eng, lambda e, b=b: e.memzero(b[:]), w=[b]))
            else:
                marks.append(self.op(eng, lambda e, b=b: e.memset(b[:], 0.0), w=[b]))
        latest = {}
        for d in self.dma_log:
            latest[id(d.dbuf)] = d
        self.dma_log = []
        first = True
        for eng in self.ENGS:
            op = Op(eng, None)
            op.is_bar = first
            first = False
            op.deps = marks + list(latest.values())
            self.ops.append(op)

    def dram(self, name, shape, dt, kind="Internal"):
        t = self.nc.dram_tensor(name, list(shape), dt, kind=kind)
        b = Buf(t.ap(), name)
        b.is_out = kind == "ExternalOutput"
        return b

    def newsem(self, name):
        self.nsem += 1
        return self.stack.enter_context(self.nc.semaphore(f"{name}_{self.nsem}"))

    def _rec(self, op, r, w):
        deps = []
        for b in r:
            deps.extend(b.w)
        for b in w:
            for d in b.w:
                if not (d.eng == "pe" and op.eng == "pe" and not d.is_dma and not op.is_dma):
                    deps.append(d)
            deps.extend(b.r)
        for b in r:
            b.r.append(op)
        for b in w:
            b.w = [op]
            b.r = []
        for b in list(r) + list(w):
            if b.bank is not None:
                d = b.bank[0]
                if d is not None and d.eng != op.eng:
                    deps.append(d)
                b.bank[0] = op
        seen = set()
        for d in deps:
            if id(d) not in seen and d is not op:
                seen.add(id(d))
                op.deps.append(d)
        self.ops.append(op)
        return op

    def op(self, eng, fn, r=(), w=()):
        return self._rec(Op(eng, fn), r, w)

    def dma(self, q, out, in_, r=(), w=(), sembuf=None, **kw):
        if sembuf is None:
            sembuf = (list(w) + list(r))[0]
        if isinstance(out, Buf):
            out = out.t
        if isinstance(in_, Buf):
            in_ = in_.t
        op = Op(q, lambda e: e.dma_start(out=out, in_=in_, **kw), is_dma=True, dbuf=sembuf)
        self._rec(op, r, w)
        self.dma_log.append(op)
        if any(b.is_out for b in w):
            self.out_ops.append(op)
        return op

    def collective(self, kind, groups, in_buf, out_buf):
        op = Op("pool", lambda e: e.collective_compute(kind, ALU.bypass, replica_groups=groups,
                                                       ins=[in_buf.t.opt()], outs=[out_buf.t.opt()]),
                is_dma=True, dbuf=out_buf, inc=1)
        self.dma_log.append(op)
        return self._rec(op, [in_buf], [out_buf])

    def build(self):
        nc = self.nc
        fin = Op("sp", None)
        fin.deps = list(self.out_ops)
        self.ops.append(fin)
        for op in self.ops:
            if op.is_dma:
                op.marked = True
            for d in op.deps:
                d.marked = True
        cnt = {e: 0 for e in self.ENGS}
        esem = {}
        free_d = []
        assigned = []
        for op in self.ops:
            if op.is_bar:
                for b in assigned:
                    if b.dsem is not None:
                        free_d.append(b.dsem)
                        b.dsem = None
                assigned = []
            if not op.marked:
                continue
            if op.is_dma:
                b = op.dbuf
                if b.dsem is None:
                    b.dsem = free_d.pop() if free_d else [self.newsem("d"), 0]
                    assigned.append(b)
                sm = b.dsem
                sm[1] += op.inc
                op.sem, op.val = sm[0], sm[1]
                if sm[1] >= EPOCH:
                    b.dsem = None
            else:
                e = op.eng
                if e not in esem or cnt[e] >= EPOCH:
                    esem[e] = self.newsem(e)
                    cnt[e] = 0
                cnt[e] += 1
                op.sem, op.val = esem[e], cnt[e]
        known = {e: {} for e in self.ENGS}
        nwaits = 0
        for op in self.ops:
            k = known[op.eng]
            waits = {}
            for d in op.deps:
                key = id(d.sem)
                if k.get(key, 0) >= d.val:
                    continue
                waits[key] = (d.sem, d.val)
                k[key] = d.val
                for s, v in d.snap.items():
                    if k.get(s, 0) < v:
                        k[s] = v
            op.waits = list(waits.values())
            nwaits += len(op.waits)
            if op.marked:
                op.snap = dict(k)
                op.snap[id(op.sem)] = op.val
        per = {e: [o for o in self.ops if o.eng == e] for e in self.ENGS}
        self.stats = dict(n_ops=len(self.ops), n_waits=nwaits, n_sems=self.nsem,
                          per_eng={e: len(v) for e, v in per.items()})

        def emit(engobj, lst):
            for op in lst:
                for s, v in op.waits:
                    engobj.wait_ge(s, v)
                if op.fn is None:
                    continue
                ins = op.fn(engobj)
                if op.marked:
                    ins.then_inc(op.sem, op.inc if op.is_dma else 1)

        with nc.Block() as block:
            @block.tensor
            def _(e):
                emit(e, per["pe"])

            @block.scalar
            def _(e):
                emit(e, per["act"])

            @block.vector
            def _(e):
                emit(e, per["dve"])

            @block.gpsimd
            def _(e):
                emit(e, per["pool"])

            @block.sync
            def _(e):
                emit(e, per["sp"])
        self.stack.close()
        return nc


EPS = 1e-6


def interleave(gens_with_counts):
    gens = [[g, max(1, n), 0.0] for g, n in gens_with_counts]
    total = max(n for _, n, _ in gens)
    alive = list(gens)
    while alive:
        for item in list(alive):
            g, n, acc = item
            item[2] += n / total
            while item[2] >= 1.0 - 1e-9:
                item[2] -= 1.0
                try:
                    next(g)
                except StopIteration:
                    alive.remove(item)
                    break


def bcast_row(ap_row, n=128):
    ap_row = ap_row.t if isinstance(ap_row, Buf) else ap_row
    pairs = [list(p) for p in ap_row.ap]
    w = pairs[-1]
    return bass.AP(ap_row.tensor, ap_row.offset, [[0, n], [w[0], w[1]]])


def emit_rstd(P, eng, ss, rstd, n, r_extra=()):
    pass


class TokCtx:
    def __init__(self, P, ident):
        self.P = P
        self.ident = ident
        self.junk = P.sb("junk", [128, 1024], BF16)

    def sumsq_rstd(self, src_ap, src_bufs, ss, rstd, n):
        P = self.P
        j = self.junk
        P.op("act", lambda e: e.activation(out=j[:, 0:src_ap.shape[-1]], in_=src_ap, func=AF.Square,
                                           accum_out=ss[:, 0:1]), r=src_bufs, w=[ss])
        P.op("act", lambda e: e.activation(out=rstd[:, 0:1], in_=ss[:, 0:1], func=AF.Sqrt, bias=EPS, scale=1.0 / n),
             r=[ss], w=[rstd])
        P.op("dve", lambda e: e.reciprocal(out=rstd[:, 0:1], in_=rstd[:, 0:1]), r=[rstd], w=[rstd])


def load_modrow(P, dst, mod_all, s_, part):
    mt = mod_all.t
    src = bass.AP(mt.tensor, mt.offset + s_ * 768 + part * 256, [[0, 128], [3072, 4], [1, 256]])
    P.dma("sp", dst[:].rearrange("p (r c) -> p r c", r=4), src, r=[mod_all], w=[dst])


def setup_mod_rows(P, mod_all, ng, sA, sB, i_gpost, i_gpre, gg, gmod, shift, scratch):
    load_modrow(P, gg, mod_all, sA, 2)
    P.dma("sp", scratch[0][:], bcast_row(ng[i_gpost:i_gpost + 1, :]), w=[scratch[0]])
    P.op("dve", lambda e: e.tensor_tensor(out=gg[:], in0=gg[:], in1=scratch[0][:], op=ALU.mult),
         r=[gg, scratch[0]], w=[gg])
    if gmod is not None:
        load_modrow(P, gmod, mod_all, sB, 1)
        P.dma("sp", scratch[1][:], bcast_row(ng[i_gpre:i_gpre + 1, :]), w=[scratch[1]])
        P.op("dve", lambda e: e.scalar_tensor_tensor(out=gmod[:], in0=gmod[:], scalar=1.0, in1=scratch[1][:],
                                                     op0=ALU.add, op1=ALU.mult), r=[gmod, scratch[1]], w=[gmod])
        load_modrow(P, shift, mod_all, sB, 0)


def emit_prenorm_T(P, T, xsrc, xbufs, gmod, shift, psT, hT_out_ap, hT_bufs, tmp, hb, ss, rstd, psT_view=None, part=None):
    if part in (None, 0):
        T.sumsq_rstd(xsrc, xbufs, ss, rstd, 1024)
        P.op("dve", lambda e: e.scalar_tensor_tensor(out=tmp[:], in0=xsrc, scalar=rstd[:, 0:1], in1=gmod[:],
                                                     op0=ALU.mult, op1=ALU.mult), r=list(xbufs) + [rstd, gmod], w=[tmp])
        P.op("pool", lambda e: e.tensor_tensor(out=hb[:], in0=tmp[:], in1=shift[:], op=ALU.add),
             r=[tmp, shift], w=[hb])
    if part == 0:
        return
    pv = psT.alt if psT_view is None else psT_view
    for k in range(8):
        P.op("pe", lambda e, k=k: e.transpose(out=pv[:, k * 128:(k + 1) * 128], in_=hb[:, k * 128:(k + 1) * 128],
                                              identity=T.ident[:]), r=[hb, T.ident], w=[psT])
    P.op("act", lambda e: e.copy(out=hT_out_ap, in_=pv[:, 0:1024].rearrange("p (k t) -> p k t", k=8)),
         r=[psT], w=hT_bufs)


def emit_post_residual(P, T, osrc, obufs, x_t, gg, xo, ss, rstd):
    T.sumsq_rstd(osrc, obufs, ss, rstd, 1024)
    P.op("dve", lambda e: e.scalar_tensor_tensor(out=xo[:], in0=osrc, scalar=rstd[:, 0:1], in1=gg[:],
                                                 op0=ALU.mult, op1=ALU.mult), r=list(obufs) + [rstd, gg], w=[xo])
    P.op("pool", lambda e: e.tensor_tensor(out=xo[:], in0=xo[:], in1=x_t[:], op=ALU.add),
         r=[xo, x_t], w=[xo])


NT = 2048


def phase_outproj(P, C, io, FC, even, layer):
    m0 = P.mark()
    x, wout, xn, hT = io["x"], io["wout"], io["xn"], io["hT"]
    yT_all = io["yT_all"]
    T = TokCtx(P, C["ident_bf"])
    gg = P.sb("gg", [128, 1024])
    gmod = P.sb("gmod", [128, 1024])
    shift = P.sb("shift", [128, 1024])
    xt = [P.sb(f"xt{i}", [128, 1024]) for i in range(2)]
    setup_mod_rows(P, io["mod_all"], io["ng"], 2 * layer, 2 * layer + 1, layer * 4 + 1, layer * 4 + 2, gg, gmod, shift, xt)
    if even:
        ss_in = P.sb("ss_in", [128, 4, 16])
        rs_ssd = P.sb("rs_ssd", [128, 16])
        dyn_dma(P, ss_in[:], lambda: io["ss_all"].t.rearrange("(h p) c -> p h c", p=128)[:, :, bass.ds(P.qc, 16)],
                r=[io["ss_all"]], w=[ss_in])
        P.op("dve", lambda e: e.tensor_reduce(out=rs_ssd[:], in_=ss_in[:].rearrange("p h t -> p t h"), axis=AX.X,
                                              op=ALU.add), r=[ss_in], w=[rs_ssd])
        P.op("act", lambda e: e.activation(out=rs_ssd[:], in_=rs_ssd[:], func=AF.Sqrt, bias=EPS, scale=1.0 / 1024),
             r=[rs_ssd], w=[rs_ssd])
        P.op("dve", lambda e: e.reciprocal(out=rs_ssd[:], in_=rs_ssd[:]), r=[rs_ssd], w=[rs_ssd])
    w_sb = [P.sb(f"wo{k}", [128, 1024], BF16) for k in range(FC)]
    for k in range(FC):
        P.dma("pool", w_sb[k][:], wout[k * 128:(k + 1) * 128, :], w=[w_sb[k]])
    NGL = io["ngroups"] // 4
    yq = [P.sb(f"yq{i}", [128, 4, NT], BF16) for i in range(NGL)]
    for gl in range(NGL):
        for hf in range(2):
            src = yT_all[2 * gl + hf]
            dyn_dma(P, yq[gl][hf * 64:(hf + 1) * 64, :, :], lambda src=src: src.t.rearrange("(h p) t -> p h t", p=64)[
                :, :, bass.ds(P.qv, NT)], r=[src], w=[yq[gl]])
    xo = [P.sb(f"xo{i}", [128, 1024]) for i in range(2)]
    tmp = [P.sb(f"tmp{i}", [128, 1024]) for i in range(2)]
    hb = [P.sb(f"hb{i}", [128, 1024], BF16) for i in range(2)]
    hTs = [P.sb(f"hTs{i}", [128, 8, 128], BF16) for i in range(2)]
    osb = [P.sb(f"osb{i}", [128, 1024]) for i in range(2)]
    dsb = P.sb("dsb", [128, 1024])
    ss = [P.sb(f"ss{i}", [128, 2]) for i in range(4)]
    rstd = [P.sb(f"rstd{i}", [128, 2]) for i in range(4)]
    psA = [P.ps(f"psA{i}", [128, 1024]) for i in range(2)]
    psB = P.ps("psB", [128, 1024]) if even else None
    psT = [P.ps(f"psT{i}", [128, 1024], BF16) for i in range(2)]
    nA = FC // 2 if even else FC
    pending = []
    for t in range(16):
        q4, s4 = divmod(t, 4)
        gmap = io["rowmap"]
        x_t = xt[t % 2]
        P.dma("sp", x_t[:], x[t * 128:(t + 1) * 128, :], r=[x], w=[x_t])
        pa = psA[t % 2]
        for k in range(nA):
            for hlf in range(2):
                hg_, gl_ = gmap[k]
                yb = yq[gl_]
                P.op("pe", lambda e, k=k, hlf=hlf, pa=pa, yb=yb, hg_=hg_, t=t: e.matmul(
                    pa[:, hlf * 512:(hlf + 1) * 512], lhsT=yb[:, hg_, t * 128:(t + 1) * 128],
                    rhs=w_sb[k][:, hlf * 512:(hlf + 1) * 512], start=(k == 0), stop=(k == nA - 1)),
                    r=[yb, w_sb[k]], w=[pa])
        if even:
            for k in range(nA, FC):
                for hlf in range(2):
                    hg_, gl_ = gmap[k]
                    yb = yq[gl_]
                    P.op("pe", lambda e, k=k, hlf=hlf, yb=yb, hg_=hg_, t=t: e.matmul(
                        psB[:, hlf * 512:(hlf + 1) * 512], lhsT=yb[:, hg_, t * 128:(t + 1) * 128],
                        rhs=w_sb[k][:, hlf * 512:(hlf + 1) * 512], start=(k == nA), stop=(k == FC - 1)),
                        r=[yb, w_sb[k]], w=[psB])
            P.op("act", lambda e: e.copy(out=dsb[:], in_=psB[:]), r=[psB], w=[dsb])
            o_t = osb[t % 2]
            P.op("dve", lambda e, pa=pa, o_t=o_t, t=t: e.scalar_tensor_tensor(
                out=o_t[:], in0=pa[:], scalar=rs_ssd[:, t:t + 1], in1=dsb[:], op0=ALU.mult, op1=ALU.add),
                r=[pa, rs_ssd, dsb], w=[o_t])
            osrc, obufs = o_t[:], [o_t]
        else:
            osrc, obufs = pa[:], [pa]
        while pending:
            pending.pop(0)()
        xo_t = xo[t % 2]
        emit_post_residual(P, T, osrc, obufs, x_t, gg, xo_t, ss[0], rstd[0])
        P.dma("act", xn[t * 128:(t + 1) * 128, :], xo_t[:], r=[xo_t], w=[xn])
        hts = hTs[t % 2]
        emit_prenorm_T(P, T, xo_t[:], [xo_t], gmod, shift, psT[t % 2], hts[:], [hts], tmp[1], hb[t % 2],
                       ss[1], rstd[1], part=0)

        def fin(t=t, xo_t=xo_t, hts=hts):
            emit_prenorm_T(P, T, xo_t[:], [xo_t], gmod, shift, psT[t % 2], hts[:], [hts], tmp[1], hb[t % 2],
                           ss[1], rstd[1], part=1)
            P.dma("sp", hT[:, t * 128:(t + 1) * 128].rearrange("(k p) t -> p k t", p=128), hts[:], r=[hts], w=[hT])
        pending.append(fin)
    while pending:
        pending.pop(0)()
    P.reset(m0)


def phase_mlp(P, C, io, next_pre, layer):
    m0 = P.mark()
    xn, hT, w1, w2, xo_d = io["xn"], io["hT"], io["w1"], io["w2"], io["xo"]
    hTn = io.get("hTn")
    T = TokCtx(P, C["ident_bf"])
    gg = P.sb("gg", [128, 1024])
    gmod = P.sb("gmod", [128, 1024]) if next_pre else None
    shift = P.sb("shift", [128, 1024]) if next_pre else None
    xt = [P.sb(f"xt{i}", [128, 1024]) for i in range(2)]
    setup_mod_rows(P, io["mod_all"], io["ng"], 2 * layer + 1, 2 * layer + 2, layer * 4 + 3, layer * 4 + 4, gg, gmod, shift, xt)
    w1_sb = [[P.sb(f"w1_{k}_{cb}", [128, 1024], BF16) for cb in range(4)] for k in range(8)]
    w2_sb = [P.sb(f"w2_{k}", [128, 1024], BF16) for k in range(32)]
    for cb in range(4):
        for k in range(8):
            P.dma("pool", w1_sb[k][cb][:], w1[k * 128:(k + 1) * 128, cb * 1024:(cb + 1) * 1024], w=[w1_sb[k][cb]])
    for k in range(32):
        P.dma("pool", w2_sb[k][:], w2[k * 128:(k + 1) * 128, :], w=[w2_sb[k]])
    ST = 256
    ht = [P.sb(f"ht{i}", [128, 8, ST], BF16) for i in range(2)]
    aT = [P.sb(f"aT{i}", [128, 2, ST], BF16) for i in range(16)]
    rl = [P.sb(f"rl{i}", [128, 2, ST]) for i in range(2)]
    xo = [P.sb(f"xo{i}", [128, 1024]) for i in range(2)]
    tmp = [None, P.sb("tmp1", [128, 1024])]
    ss = [P.sb(f"ss{i}", [128, 2]) for i in range(2)]
    rstd = [P.sb(f"rstd{i}", [128, 2]) for i in range(2)]
    psU = [P.ps(f"psU{i}", [128, 2, ST]) for i in range(3)]
    psO = [P.ps(f"psO{i}", [128, 1024]) for i in range(2)]
    psT = P.ps("psT", [128, 1024], BF16)
    if next_pre:
        hb = [P.sb(f"hb{i}", [128, 1024], BF16) for i in range(2)]
        hTs = [P.sb(f"hTs{i}", [128, 8, 128], BF16) for i in range(2)]
    nu = 0
    pending = []
    for st in range(NT // ST):
        h_t = ht[st % 2]
        P.dma("sp", h_t[:], hT[:, st * ST:(st + 1) * ST].rearrange("(k p) t -> p k t", p=128), r=[hT], w=[h_t])
        aTs = aT
        for fp in range(16):
            pu = psU[nu % 3]
            r_t = rl[nu % 2]
            nu += 1
            for j in range(2):
                f = fp * 2 + j
                for k in range(8):
                    wb = w1_sb[k][f // 8]
                    P.op("pe", lambda e, pu=pu, j=j, f=f, k=k, h_t=h_t, wb=wb: e.matmul(
                        pu[:, j, :], lhsT=wb[:, (f % 8) * 128:(f % 8 + 1) * 128], rhs=h_t[:, k, :],
                        start=(k == 0), stop=(k == 7)), r=[wb, h_t], w=[pu])
            P.op("act", lambda e, pu=pu, r_t=r_t: e.activation(out=r_t[:], in_=pu[:], func=AF.Relu),
                 r=[pu], w=[r_t])
            a_t = aTs[fp]
            P.op("dve" if fp % 2 == 0 else "pool", lambda e, r_t=r_t, a_t=a_t: e.tensor_tensor(
                out=a_t[:], in0=r_t[:], in1=r_t[:], op=ALU.mult), r=[r_t], w=[a_t])
        for sub in range(ST // 128):
            t = st * (ST // 128) + sub
            po = psO[t % 2]
            for f in range(32):
                for hlf in range(2):
                    P.op("pe", lambda e, po=po, f=f, hlf=hlf, sub=sub, aTs=aTs: e.matmul(
                        po[:, hlf * 512:(hlf + 1) * 512], lhsT=aTs[f // 2][:, f % 2, sub * 128:(sub + 1) * 128],
                        rhs=w2_sb[f][:, hlf * 512:(hlf + 1) * 512], start=(f == 0), stop=(f == 31)),
                        r=[aTs[f // 2], w2_sb[f]], w=[po])
            while pending:
                pending.pop(0)()
            x_t = xt[t % 2]
            P.dma("sp", x_t[:], xn[t * 128:(t + 1) * 128, :], r=[xn], w=[x_t])
            xo_t = xo[t % 2]
            emit_post_residual(P, T, po[:], [po], x_t, gg, xo_t, ss[0], rstd[0])
            P.dma("act", xo_d[t * 128:(t + 1) * 128, :], xo_t[:], r=[xo_t], w=[xo_d])
            if next_pre:
                hts = hTs[t % 2]
                emit_prenorm_T(P, T, xo_t[:], [xo_t], gmod, shift, psT, hts[:], [hts], tmp[1], hb[t % 2],
                               ss[1], rstd[1], part=0)

                def fin(t=t, xo_t=xo_t, hts=hts):
                    emit_prenorm_T(P, T, xo_t[:], [xo_t], gmod, shift, psT, hts[:], [hts], tmp[1], hb[t % 2],
                                   ss[1], rstd[1], part=1)
                    for c4 in range(4):
                        P.dma("sp", hTn[c4][:, t * 128:(t + 1) * 128].rearrange("(k p) t -> p k t", p=128),
                              hts[:, 2 * c4:2 * c4 + 2, :], r=[hts], w=[hTn[c4]])
                pending.append(fin)
    while pending:
        pending.pop(0)()
    P.reset(m0)


S_LEN = 8192
NEGV = -30000.0
E_NCOL = 1024 + 516


def host_consts():
    c = {}
    c["ident_bf"] = np.eye(128, dtype=np.float32).astype(ml_dtypes.bfloat16)
    c["ident_f"] = np.eye(128, dtype=np.float32)
    i = np.arange(128)
    c["triU"] = (i[:, None] <= i[None, :]).astype(np.float32)
    c["triS"] = (i[:, None] > i[None, :]).astype(np.float32)
    c["NEG"] = np.where(i[None, :] < i[:, None], NEGV, 0.0).astype(np.float32)
    c["ones_f"] = np.ones((128, 128), np.float32)
    c["ones_bf"] = np.ones((128, 128), np.float32).astype(ml_dtypes.bfloat16)
    c["antiI"] = np.ascontiguousarray(np.eye(128, dtype=np.float32)[::-1])
    return c


def t5_bucket_np(d):
    d = np.maximum(d, 0)
    logd = np.log(np.maximum(d, 1).astype(np.float32) / np.float32(16))
    large = 16 + (logd / np.float32(np.log(128 / 16)) * np.float32(16)).astype(np.int32)
    large = np.minimum(large, 31)
    return np.where(d < 16, d, large)


def host_ohd(swa_prev=False):
    d = np.arange(383) - 127
    oh = np.zeros((33, 384), np.float32)
    oh[t5_bucket_np(d), np.arange(383)] = 1.0
    if swa_prev:
        oh[32, :383] = np.where((d < 0) | (d >= 128), NEGV, 0.0)
    else:
        oh[32, :383] = np.where(d < 0, NEGV, 0.0)
    oh[:32, :383] *= (d >= 0)[None, :]
    return oh


def phase_even(P, C, io, do_ssd=True, do_diff=True, n_super=16):
    m0 = P.mark()
    x, win, convw, convb, hv, normw = io["x_b"], io["e_win"], io["convw"], io["convb"], io["hv"], io["normw"]
    lam, subw, relb, ohd = io["lam"], io["subw"], io["relb"], io["ohdA"]
    yT, ss_out, vecd = io["yT_loc"], io["ss_loc"], io["vecdA"]
    T = TokCtx(P, C["ident_bf"])
    gmod = P.sb("gmod", [128, 1024])
    shift = P.sb("shift", [128, 1024])
    xt = [P.sb(f"xt{i}", [128, 1024]) for i in range(2)]
    load_modrow(P, gmod, io["mod_all"], 0, 1)
    P.dma("sp", xt[0][:], bcast_row(io["ng"][0:1, :]), w=[xt[0]])
    P.op("dve", lambda e: e.scalar_tensor_tensor(out=gmod[:], in0=gmod[:], scalar=1.0, in1=xt[0][:],
                                                 op0=ALU.add, op1=ALU.mult), r=[gmod, xt[0]], w=[gmod])
    load_modrow(P, shift, io["mod_all"], 0, 0)
    w_sb = [P.sb(f"win{k}", [128, E_NCOL], BF16) for k in range(8)]
    for k in range(8):
        P.dma("pool", w_sb[k][:], win[k * 128:(k + 1) * 128, :], w=[w_sb[k]])
    cw = P.sb("cw", [128, 4, 4])
    cb = P.sb("cb", [128, 4])
    nw = P.sb("nw", [128, 2])
    hvb = P.sb("hvb", [128, 3, 4])
    P.dma("sp", cw[:], convw, w=[cw])
    P.dma("sp", cb[:], convb, w=[cb])
    P.dma("sp", nw[:], normw, w=[nw])
    P.dma("sp", hvb[:], bass.AP(hv.t.tensor, hv.t.offset, [[0, 128], [4, 3], [1, 4]]), w=[hvb])
    Aneg = P.sb("Aneg", [128, 4])
    P.op("act", lambda e: e.activation(out=Aneg[:], in_=hvb[:, 1, :], func=AF.Exp), r=[hvb], w=[Aneg])
    P.op("dve", lambda e: e.tensor_scalar(out=Aneg[:], in0=Aneg[:], scalar1=-1.0, scalar2=None, op0=ALU.mult),
         r=[Aneg], w=[Aneg])

    hT = [P.sb(f"hT{i}", [128, 8, 512], BF16) for i in range(2)]
    hb = P.sb("hb", [128, 1024], BF16)
    tmp = P.sb("tmp", [128, 1024])
    ssq = P.sb("ssq", [128, 2])
    rstd = P.sb("rstd", [128, 2])
    pAB = P.ps("pAB", [128, 512])
    pAB_bf = pAB.alt
    bk1 = P.ps("bk1", [128, 512])
    bk2 = P.ps("bk2", [128, 512])
    bk3 = P.ps("bk3", [128, 512])
    pdt = Buf(bk3.t[:, 256:272], "pdt", bank=bk3.bank)
    raw = [P.sb(f"raw{j}", [128, 3 + 512]) for j in range(4)]
    for j in range(4):
        P.op("pool", lambda e, j=j: e.memset(raw[j][:, 0:3], 0.0), w=[raw[j]])
    acc = P.sb("acc", [128, 512])
    cvT = [P.sb(f"cvT{j}", [128, 512], BF16) for j in range(4)]
    z_sb = [P.sb(f"z_sb{i}", [128, 256]) for i in range(4)]
    dtr = [P.sb(f"dtr{i}", [128, 4]) for i in range(4)]
    if do_diff:
        KT = [P.sb(f"KT{h}", [128, S_LEN], BF16) for h in range(2)]
        VV = [P.sb(f"V{h}", [128, 64, 128], BF16) for h in range(2)]
        QT = [[P.sb(f"QT{h}_{m}", [128, 512], BF16) for m in range(2)] for h in range(2)]
        for h in range(2):
            for m in range(2):
                P.op("pool", lambda e, h=h, m=m: e.memset(QT[h][m][:], 0.0), w=[QT[h][m]])
    if do_ssd:
        S_f = P.sb("S_f", [128, 256])
        S_b = P.sb("S_b", [128, 256], BF16)
        P.op("pool", lambda e: e.memset(S_f[:], 0.0), w=[S_f])
        P.op("pool", lambda e: e.memset(S_b[:], 0.0), w=[S_b])
        ss_all = P.sb("ss_all", [128, 64])
        p_bc = Buf(bk1.t[:, 0:256].rearrange("p (a b) -> p a b", a=2), "p_bc", bank=bk1.bank)
        p_y = Buf(bk1.t[:, 256:512], "p_y", bank=bk1.bank)
        p_cbxt = Buf(bk2.t[:, 0:256], "p_cbxt", bank=bk2.bank)
        p_S = Buf(bk2.t[:, 256:512], "p_S", bank=bk2.bank)
        p_sm = Buf(bk3.t[:, 0:256], "p_sm", bank=bk3.bank)
        p_cb = p_cbxt
        p_xt = bk2.alt
        p_smb = bk3.alt
        W = {}
        for nme, shp, dt in [("dt", [128, 4], F32), ("a", [128, 4], F32), ("nacs", [128, 4], F32),
                             ("E", [128, 4], F32), ("dte", [128, 4], F32), ("dec", [128, 4], F32),
                             ("trr", [128, 2, 128], F32), ("decay", [128, 4, 128], F32), ("cbs", [128, 128], F32),
                             ("M", [128, 4, 128], BF16), ("xs_tok", [128, 256], F32), ("X", [128, 256], BF16),
                             ("Xd", [128, 256], BF16), ("B_tok", [128, 128], BF16), ("yo", [128, 256], F32),
                             ("y", [128, 256], F32), ("sz", [128, 256], F32), ("y_bf", [128, 256], BF16),
                             ("yTs", [128, 2, 128], BF16), ("Dbc", [128, 4, 64], F32)]:
            W[nme] = P.sb("w_" + nme, shp, dt)
        for r_ in range(4):
            P.op("act", lambda e, r_=r_: e.activation(out=W["Dbc"][:, r_, :], in_=C["ones_f"][:, 0:64], func=AF.Copy,
                                                      scale=hvb[:, 2, r_:r_ + 1]), r=[C["ones_f"], hvb], w=[W["Dbc"]])

    def ssd_chunk(c, st, sub):
        cs = slice(sub * 128, (sub + 1) * 128)
        zt, dt_raw = z_sb[sub], dtr[sub]
        P.op("dve", lambda e: e.tensor_tensor(out=W["dt"][:], in0=dt_raw[:], in1=hvb[:, 0, :], op=ALU.add),
             r=[dt_raw, hvb], w=[W["dt"]])
        P.op("act", lambda e: e.activation(out=W["dt"][:], in_=W["dt"][:], func=AF.Exp), r=[W["dt"]], w=[W["dt"]])
        P.op("act", lambda e: e.activation(out=W["dt"][:], in_=W["dt"][:], func=AF.Ln, bias=1.0, scale=1.0),
             r=[W["dt"]], w=[W["dt"]])
        P.op("dve", lambda e: e.tensor_tensor(out=W["a"][:], in0=W["dt"][:], in1=Aneg[:], op=ALU.mult),
             r=[W["dt"], Aneg], w=[W["a"]])
        yield
        P.op("pe", lambda e: e.matmul(p_sm[:, 0:4], lhsT=C["triU"][:], rhs=W["a"][:], start=True, stop=True),
             r=[C["triU"], W["a"]], w=[p_sm])
        P.op("pe", lambda e: e.matmul(p_sm[:, 4:8], lhsT=C["triS"][:], rhs=W["a"][:], start=True, stop=True),
             r=[C["triS"], W["a"]], w=[p_sm])
        P.op("dve", lambda e: e.tensor_scalar(out=W["nacs"][:], in0=p_sm[:, 0:4], scalar1=-1.0, scalar2=None,
                                              op0=ALU.mult), r=[p_sm], w=[W["nacs"]])
        P.op("act", lambda e: e.activation(out=W["E"][:], in_=p_sm[:, 0:4], func=AF.Exp), r=[p_sm], w=[W["E"]])
        P.op("act", lambda e: e.activation(out=W["dte"][:], in_=p_sm[:, 4:8], func=AF.Exp), r=[p_sm], w=[W["dte"]])
        P.op("dve", lambda e: e.tensor_tensor(out=W["dec"][:], in0=W["E"][:], in1=W["dte"][:], op=ALU.mult),
             r=[W["E"], W["dte"]], w=[W["dec"]])
        yield
        for half in range(2):
            yield
            for rr in range(2):
                r_ = half * 2 + rr
                P.op("dve" if rr == 0 else "pool", lambda e, r_=r_, rr=rr: e.tensor_scalar(
                    out=W["trr"][:, rr, :], in0=C["triU"][:], scalar1=W["a"][:, r_:r_ + 1], scalar2=None,
                    op0=ALU.mult), r=[C["triU"], W["a"]], w=[W["trr"]])
            yield
            for rr in range(2):
                P.op("pe", lambda e, rr=rr: e.matmul(p_bc[:, rr, :], lhsT=C["ones_f"][:], rhs=W["trr"][:, rr, :],
                                                     start=True, stop=False), r=[C["ones_f"], W["trr"]], w=[p_bc])
                P.op("pe", lambda e, rr=rr: e.matmul(p_bc[:, rr, :], lhsT=C["ident_f"][:], rhs=C["NEG"][:],
                                                     start=False, stop=True), r=[C["ident_f"], C["NEG"]], w=[p_bc])
            for rr in range(2):
                r_ = half * 2 + rr
                P.op("act", lambda e, r_=r_, rr=rr: e.activation(
                    out=W["decay"][:, r_, :], in_=p_bc[:, rr, :], func=AF.Exp, bias=W["nacs"][:, r_:r_ + 1],
                    scale=1.0), r=[p_bc, W["nacs"]], w=[W["decay"]])
        yield
        P.op("pe", lambda e: e.matmul(p_cb[:, 0:128], lhsT=cvT[2][:, cs], rhs=cvT[3][:, cs], start=True, stop=True),
             r=[cvT[2], cvT[3]], w=[p_cbxt])
        P.op("act", lambda e: e.copy(out=W["cbs"][:], in_=p_cb[:, 0:128]), r=[p_cbxt], w=[W["cbs"]])
        for r_ in range(4):
            P.op("dve" if r_ % 2 == 0 else "pool", lambda e, r_=r_: e.tensor_tensor(
                out=W["M"][:, r_, :], in0=W["decay"][:, r_, :], in1=W["cbs"][:], op=ALU.mult),
                r=[W["decay"], W["cbs"]], w=[W["M"]])
        yield
        for j in range(2):
            P.op("pe", lambda e, j=j: e.transpose(out=p_xt[:, 256 + j * 128:256 + (j + 1) * 128], in_=cvT[j][:, cs],
                                                  identity=C["ident_bf"][:]), r=[cvT[j], C["ident_bf"]], w=[p_cbxt])
        P.op("pe", lambda e: e.transpose(out=p_smb[:, 128:256], in_=cvT[2][:, cs], identity=C["ident_bf"][:]),
             r=[cvT[2], C["ident_bf"]], w=[p_sm])
        P.op("act", lambda e: e.copy(out=W["xs_tok"][:], in_=p_xt[:, 256:512]), r=[p_cbxt], w=[W["xs_tok"]])
        P.op("act", lambda e: e.copy(out=W["B_tok"][:], in_=p_smb[:, 128:256]), r=[p_sm], w=[W["B_tok"]])
        for r_ in range(4):
            hs = slice(r_ * 64, (r_ + 1) * 64)
            P.op("dve", lambda e, r_=r_, hs=hs: e.tensor_scalar(
                out=W["X"][:, hs], in0=W["xs_tok"][:, hs], scalar1=W["dt"][:, r_:r_ + 1], scalar2=None,
                op0=ALU.mult), r=[W["xs_tok"], W["dt"]], w=[W["X"]])
            P.op("pool", lambda e, r_=r_, hs=hs: e.tensor_scalar(
                out=W["Xd"][:, hs], in0=W["X"][:, hs], scalar1=W["dte"][:, r_:r_ + 1], scalar2=None,
                op0=ALU.mult), r=[W["X"], W["dte"]], w=[W["Xd"]])
        yield
        P.op("pe", lambda e: e.matmul(p_y[:], lhsT=cvT[3][:, cs], rhs=S_b[:], start=True, stop=True),
             r=[cvT[3], S_b], w=[p_y])
        for r_ in range(4):
            hs = slice(r_ * 64, (r_ + 1) * 64)
            P.op("act", lambda e, r_=r_, hs=hs: e.activation(out=W["yo"][:, hs], in_=p_y[:, hs], func=AF.Copy,
                                                             scale=W["E"][:, r_:r_ + 1]), r=[p_y, W["E"]], w=[W["yo"]])
        yield
        for r_ in range(4):
            hs = slice(r_ * 64, (r_ + 1) * 64)
            P.op("pe", lambda e, r_=r_, hs=hs: e.matmul(p_y[:, hs], lhsT=W["M"][:, r_, :], rhs=W["X"][:, hs],
                                                        start=True, stop=True), r=[W["M"], W["X"]], w=[p_y])
        P.op("dve", lambda e: e.tensor_tensor(out=W["y"][:], in0=p_y[:], in1=W["yo"][:], op=ALU.add),
             r=[p_y, W["yo"]], w=[W["y"]])
        P.op("pool", lambda e: e.tensor_tensor(out=W["yo"][:], in0=W["xs_tok"][:],
                                               in1=W["Dbc"][:].rearrange("p a b -> p (a b)"), op=ALU.mult),
             r=[W["xs_tok"], W["Dbc"]], w=[W["yo"]])
        P.op("dve", lambda e: e.tensor_tensor(out=W["y"][:], in0=W["y"][:], in1=W["yo"][:], op=ALU.add),
             r=[W["y"], W["yo"]], w=[W["y"]])
        yield
        P.op("act", lambda e: e.activation(out=W["sz"][:], in_=zt[:], func=AF.Silu), r=[zt], w=[W["sz"]])
        P.op("dve", lambda e: e.tensor_tensor(out=W["y"][:], in0=W["y"][:], in1=W["sz"][:], op=ALU.mult),
             r=[W["y"], W["sz"]], w=[W["y"]])
        P.op("act", lambda e: e.activation(out=W["sz"][:], in_=W["y"][:], func=AF.Square,
                                           accum_out=ss_all[:, c:c + 1]), r=[W["y"]], w=[W["sz"], ss_all])
        P.op("pool", lambda e: e.tensor_copy(out=W["y_bf"][:], in_=W["y"][:]), r=[W["y"]], w=[W["y_bf"]])
        yield
        yield
        for j in range(2):
            P.op("pe", lambda e, j=j: e.transpose(out=p_smb[:, 256 + j * 128:256 + (j + 1) * 128],
                                                  in_=W["y_bf"][:, j * 128:(j + 1) * 128],
                                                  identity=C["ident_bf"][:]), r=[W["y_bf"], C["ident_bf"]], w=[p_sm])
        for j in range(2):
            P.op("act", lambda e, j=j: e.activation(out=W["yTs"][:, j, :], in_=p_smb[:, 256 + j * 128:256 + (j + 1) * 128],
                                                    func=AF.Copy, scale=nw[:, j:j + 1]), r=[p_sm, nw], w=[W["yTs"]])
        for j in range(2):
            for hf in range(2):
                ch = yT[2 * j + hf]
                P.dma("sp", ch[0:64, c * 128:(c + 1) * 128], W["yTs"][hf * 64:(hf + 1) * 64, j, :], r=[W["yTs"]], w=[ch])
        yield
        P.op("pe", lambda e: e.matmul(p_S[:], lhsT=W["B_tok"][:], rhs=W["Xd"][:], start=True, stop=True),
             r=[W["B_tok"], W["Xd"]], w=[p_S])
        for r_ in range(4):
            hs = slice(r_ * 64, (r_ + 1) * 64)
            P.op("dve", lambda e, r_=r_, hs=hs: e.scalar_tensor_tensor(
                out=S_f[:, hs], in0=S_f[:, hs], scalar=W["dec"][:, r_:r_ + 1], in1=p_S[:, hs],
                op0=ALU.mult, op1=ALU.add), r=[S_f, W["dec"], p_S], w=[S_f])
        P.op("pool", lambda e: e.tensor_copy(out=S_b[:], in_=S_f[:]), r=[S_f], w=[S_b])
        yield

    diff = DiffAttn(P, C, yT, lam, subw, relb, ohd, vecd) if do_diff else None

    def prenorm_gen(st):
        h_t = hT[st % 2]
        for sub in range(4):
            t = st * 4 + sub
            x_t = xt[t % 2]
            P.dma("sp" if t % 2 == 0 else "act", x_t[:], x[t * 128:(t + 1) * 128, :], w=[x_t])
            emit_prenorm_T(P, T, x_t[:], [x_t], gmod, shift, pAB, h_t[:, :, sub * 128:(sub + 1) * 128], [h_t],
                           tmp, hb, ssq, rstd, psT_view=pAB_bf, part=0)
            yield
            yield
            yield
            emit_prenorm_T(P, T, x_t[:], [x_t], gmod, shift, pAB, h_t[:, :, sub * 128:(sub + 1) * 128], [h_t],
                           tmp, hb, ssq, rstd, psT_view=pAB_bf, part=1)
            yield

    def ssd_gen(st):
        for sub in range(4):
            yield from ssd_chunk(st * 4 + sub, st, sub)

    def diff_gen(st):
        for h_ in range(2):
            yield from diff.superblock(st, h_, KT[h_], VV[h_], QT[h_])

    for _ in prenorm_gen(0):
        pass
    for st in range(n_super):
        h_t = hT[st % 2]
        for j in range(8):
            for k in range(8):
                P.op("pe", lambda e, j=j, k=k, h_t=h_t: e.matmul(
                    pAB[:], lhsT=w_sb[k][:, j * 128:(j + 1) * 128], rhs=h_t[:, k, :], start=(k == 0), stop=(k == 7)),
                    r=[w_sb[k], h_t], w=[pAB])
            if j < 4:
                if do_ssd:
                    rw = raw[j]
                    if st > 0:
                        P.op("dve", lambda e, rw=rw: e.tensor_copy(out=rw[:, 0:3], in_=rw[:, 512:515]), r=[rw], w=[rw])
                    P.op("act", lambda e, rw=rw: e.copy(out=rw[:, 3:515], in_=pAB[:]), r=[pAB], w=[rw])
                    P.op("dve", lambda e, rw=rw, j=j: e.tensor_scalar(
                        out=acc[:], in0=rw[:, 3:515], scalar1=cw[:, j, 3:4], scalar2=cb[:, j:j + 1],
                        op0=ALU.mult, op1=ALU.add), r=[rw, cw, cb], w=[acc])
                    for tap in (2, 1, 0):
                        P.op("dve", lambda e, rw=rw, j=j, tap=tap: e.scalar_tensor_tensor(
                            out=acc[:], in0=rw[:, tap:tap + 512], scalar=cw[:, j, tap:tap + 1], in1=acc[:],
                            op0=ALU.mult, op1=ALU.add), r=[rw, cw, acc], w=[acc])
                    P.op("act", lambda e, j=j: e.activation(out=cvT[j][:], in_=acc[:], func=AF.Silu),
                         r=[acc], w=[cvT[j]])
            elif do_diff and not os.environ.get('DIFF_NOCOPY'):
                h_ = (j - 4) % 2
                if os.environ.get('DIFF_SKIP', '') .count('q' if j < 6 else 'k'):
                    pass
                elif j < 6:
                    P.op("act", lambda e, h_=h_: e.copy(out=QT[h_][0][0:64, :], in_=pAB[0:64, :]), r=[pAB], w=[QT[h_][0]])
                    P.op("dve", lambda e, h_=h_: e.tensor_copy(out=QT[h_][1][64:128, :], in_=pAB[64:128, :]),
                         r=[pAB], w=[QT[h_][1]])
                else:
                    P.op("act", lambda e, h_=h_, st=st: e.copy(out=KT[h_][:, st * 512:(st + 1) * 512], in_=pAB[:]),
                         r=[pAB], w=[KT[h_]])
        for sub in range(4):
            t = st * 4 + sub
            ts_ = slice(sub * 128, (sub + 1) * 128)
            for k in range(8):
                P.op("pe", lambda e, k=k, h_t=h_t, ts_=ts_: e.matmul(
                    pAB[:], lhsT=h_t[:, k, ts_], rhs=w_sb[k][:, 1024:1536], start=(k == 0), stop=(k == 7)),
                    r=[h_t, w_sb[k]], w=[pAB])
            for k in range(8):
                P.op("pe", lambda e, k=k, h_t=h_t, ts_=ts_: e.matmul(
                    pdt[:, 0:4], lhsT=h_t[:, k, ts_], rhs=w_sb[k][:, 1536:1540], start=(k == 0), stop=(k == 7)),
                    r=[h_t, w_sb[k]], w=[pdt])
            if do_ssd:
                P.op("act", lambda e, sub=sub: e.copy(out=z_sb[sub][:], in_=pAB[:, 0:256]), r=[pAB], w=[z_sb[sub]])
                P.op("dve", lambda e, sub=sub: e.tensor_copy(out=dtr[sub][:], in_=pdt[:, 0:4]), r=[pdt], w=[dtr[sub]])
            if do_diff and not os.environ.get('DIFF_NOCOPY') and not os.environ.get('DIFF_SKIP', '').count('v'):
                P.op("dve", lambda e, t=t: e.tensor_copy(out=VV[0][:, t, :], in_=pAB[:, 256:384]), r=[pAB], w=[VV[0]])
                P.op("act", lambda e, t=t: e.copy(out=VV[1][:, t, :], in_=pAB[:, 384:512]), r=[pAB], w=[VV[1]])
        gens = []
        if do_ssd:
            gens.append((ssd_gen(st), 4 * 15))
        if do_diff:
            gens.append((diff_gen(st), 2 * (2 * (4 * st + 4) + 2)))
        if st + 1 < n_super:
            gens.append((prenorm_gen(st + 1), 16))
        interleave(gens)
    if do_ssd:
        P.dma("sp", ss_out[:], ss_all[:], r=[ss_all], w=[ss_out])
    P.reset(m0)


def make_bias_tiles(P, C, relb, ohd, vecd, ps, tag):
    relb_sb = P.sb("relb_sb" + tag, [33, 2])
    ohd_sb = P.sb("ohd_sb" + tag, [33, 384])
    P.dma("sp", relb_sb[:], relb, w=[relb_sb])
    P.dma("sp", ohd_sb[:], ohd, w=[ohd_sb])
    vec_sb = P.sb("vec_sb" + tag, [2, 384])
    P.op("pe", lambda e: e.matmul(ps[0:2, 0:384], lhsT=relb_sb[:], rhs=ohd_sb[:], start=True, stop=True),
         r=[relb_sb, ohd_sb], w=[ps])
    P.op("act", lambda e: e.copy(out=vec_sb[:], in_=ps[0:2, 0:384]), r=[ps], w=[vec_sb])
    P.dma("sp", vecd[:], vec_sb[:], r=[vec_sb], w=[vecd])
    Bd, Bp, c31 = [], [], []
    vt = vecd.t
    hk = P.sb("hankel" + tag, [128, 128])
    for h in range(2):
        bd = P.sb(f"Bd{tag}{h}", [128, 128])
        bp = P.sb(f"Bp{tag}{h}", [128, 128])
        c3 = P.sb(f"c31{tag}_{h}", [128, 1])
        for dst, base in ((bd, 0), (bp, 128)):
            P.dma("sp", hk[:], bass.AP(vt.tensor, vt.offset + h * 384 + base, [[1, 128], [1, 128]]), r=[vecd], w=[hk])
            P.op("pe", lambda e: e.matmul(ps[:, 0:128], lhsT=C["antiI"][:], rhs=hk[:], start=True, stop=True),
                 r=[C["antiI"], hk], w=[ps])
            P.op("act", lambda e, dst=dst: e.copy(out=dst[:], in_=ps[:, 0:128]), r=[ps], w=[dst])
        P.dma("sp", c3[:], bass.AP(vt.tensor, vt.offset + h * 384 + 382, [[0, 128], [1, 1]]), r=[vecd], w=[c3])
        Bd.append(bd)
        Bp.append(bp)
        c31.append(c3)
    return Bd, Bp, c31


class DiffAttn:
    def __init__(self, P, C, yT, lam, subw, relb, ohd, vecd, row0=256):
        self.P, self.C, self.yT, self.row0 = P, C, yT, row0
        self.ps_s = [P.ps(f"ps_s{i}", [128, 512]) for i in range(2)]
        self.ps_o = P.ps("ps_o", [128, 512])
        self.ps_l = P.ps("ps_l", [128, 512])
        self.Bd, self.Bp, self.c31 = make_bias_tiles(P, C, relb, ohd, vecd, self.ps_l, "A")
        lam_sb = P.sb("lam_sb", [128, 256])
        P.dma("sp", lam_sb[:], bcast_row(lam), w=[lam_sb])
        pr = P.sb("lam_pr", [128, 2, 64])
        sm = P.sb("lam_sm", [128, 2])
        self.neg_lam = P.sb("neg_lam", [128, 1])
        P.op("dve", lambda e: e.tensor_tensor(out=pr[:, 0, :], in0=lam_sb[:, 0:64], in1=lam_sb[:, 64:128], op=ALU.mult),
             r=[lam_sb], w=[pr])
        P.op("dve", lambda e: e.tensor_tensor(out=pr[:, 1, :], in0=lam_sb[:, 128:192], in1=lam_sb[:, 192:256], op=ALU.mult),
             r=[lam_sb, pr], w=[pr])
        P.op("dve", lambda e: e.tensor_reduce(out=sm[:], in_=pr[:], axis=AX.X, op=ALU.add), r=[pr], w=[sm])
        P.op("act", lambda e: e.activation(out=sm[:], in_=sm[:], func=AF.Exp), r=[sm], w=[sm])
        P.op("dve", lambda e: e.tensor_tensor(out=self.neg_lam[:], in0=sm[:, 1:2], in1=sm[:, 0:1], op=ALU.subtract),
             r=[sm], w=[self.neg_lam])
        P.op("dve", lambda e: e.tensor_scalar(out=self.neg_lam[:], in0=self.neg_lam[:], scalar1=-0.2, scalar2=None,
                                              op0=ALU.add), r=[self.neg_lam], w=[self.neg_lam])
        self.subs = P.sb("subs", [128, 1])
        P.dma("sp", self.subs[:], subw, w=[self.subs])
        P.op("dve", lambda e: e.tensor_scalar(out=self.subs[:], in0=self.subs[:], scalar1=0.8, scalar2=None,
                                              op0=ALU.mult), r=[self.subs], w=[self.subs])
        self.PT = [P.sb(f"PT{i}", [128, 512], BF16) for i in range(4)]
        self.tS = [P.sb(f"tS{i}", [128, 512]) for i in range(2)]
        self.Tm = [P.sb(f"Tm{i}", [128, 512]) for i in range(2)]
        self.Rr = P.sb("Rr", [128, 512])
        self.sq = P.sb("sq", [128, 512])
        self.yd = [P.sb(f"yd{i}", [128, 512], BF16) for i in range(2)]
        self.n = 0
        self.nn = 0
        self.ny = 0

    def superblock(self, Q, h, KT, V, QT):
        import os
        stage = int(os.environ.get("DIFF_STAGE", "3"))
        if stage == 0:
            return
        yield
        P, C = self.P, self.C
        ps_o, ps_l = self.ps_o, self.ps_l
        for m in range(2):
            ms = slice(m * 64, (m + 1) * 64)

            def stage_a(kb, m=m, ms=ms):
                j0 = max(0, kb - 4 * Q)
                c0 = j0 * 128
                ps = self.ps_s[self.n % 2]
                pt = self.PT[self.n % 4]
                self.n += 1
                P.op("pe", lambda e, ps=ps, kb=kb, c0=c0, m=m: e.matmul(
                    ps[:, c0:512], lhsT=KT[:, kb * 128:(kb + 1) * 128], rhs=QT[m][:, c0:512], start=True, stop=True),
                    r=[KT, QT[m]], w=[ps])
                fj = max(j0, kb + 2 - 4 * Q)
                for j in range(j0, min(4, fj)):
                    bt = self.Bd[h] if 4 * Q + j == kb else self.Bp[h]
                    ts = self.tS[self.nn % 2]
                    self.nn += 1
                    cs = slice(j * 128, (j + 1) * 128)
                    P.op("dve", lambda e, ps=ps, ts=ts, bt=bt, cs=cs: e.scalar_tensor_tensor(
                        out=ts[:, cs], in0=ps[:, cs], scalar=0.125, in1=bt[:], op0=ALU.mult, op1=ALU.add),
                        r=[ps, bt], w=[ts])
                    P.op("act", lambda e, ts=ts, pt=pt, cs=cs: e.activation(out=pt[:, cs], in_=ts[:, cs], func=AF.Exp),
                         r=[ts], w=[pt])
                if fj < 4:
                    fs = slice(fj * 128, 512)
                    P.op("act", lambda e, ps=ps, pt=pt, fs=fs: e.activation(
                        out=pt[:, fs], in_=ps[:, fs], func=AF.Exp, bias=self.c31[h][:, 0:1], scale=0.125),
                        r=[ps, self.c31[h]], w=[pt])
                return pt

            def stage_b(kb, pt):
                j0 = max(0, kb - 4 * Q)
                if kb <= 4 * Q:
                    P.op("pe", lambda e, kb=kb, pt=pt: e.matmul(ps_o[:], lhsT=V[:, kb, :], rhs=pt[:],
                                                                start=(kb == 0), stop=False), r=[V, pt], w=[ps_o])
                    P.op("pe", lambda e, kb=kb, pt=pt: e.matmul(ps_l[:], lhsT=C["ones_bf"][:], rhs=pt[:],
                                                                start=(kb == 0), stop=False), r=[C["ones_bf"], pt], w=[ps_l])
                else:
                    for j in range(j0, 4):
                        cs = slice(j * 128, (j + 1) * 128)
                        last = (kb == 4 * Q + j)
                        P.op("pe", lambda e, kb=kb, pt=pt, cs=cs, last=last: e.matmul(
                            ps_o[:, cs], lhsT=V[:, kb, :], rhs=pt[:, cs], start=(kb == 0), stop=last), r=[V, pt], w=[ps_o])
                        P.op("pe", lambda e, kb=kb, pt=pt, cs=cs, last=last: e.matmul(
                            ps_l[:, cs], lhsT=C["ones_bf"][:], rhs=pt[:, cs], start=(kb == 0), stop=last),
                            r=[C["ones_bf"], pt], w=[ps_l])

            nkb = 4 * Q + 4
            LAG = 2
            pend = []
            for kb in range(nkb):
                pend.append((kb, stage_a(kb)))
                yield
                if len(pend) > LAG:
                    stage_b(*pend.pop(0))
            while pend:
                yield
                stage_b(*pend.pop(0))
            if stage < 3:
                continue
            tm = self.Tm[m]
            P.op("dve", lambda e: e.reciprocal(out=self.Rr[:], in_=ps_l[:]), r=[ps_l], w=[self.Rr])
            P.op("dve", lambda e, tm=tm: e.tensor_tensor(out=tm[:], in0=ps_o[:], in1=self.Rr[:], op=ALU.mult),
                 r=[ps_o, self.Rr], w=[tm])
        yield
        if stage < 3:
            return
        t1, t2 = self.Tm
        P.op("dve", lambda e: e.scalar_tensor_tensor(out=t1[:], in0=t2[:], scalar=self.neg_lam[:, 0:1], in1=t1[:],
                                                     op0=ALU.mult, op1=ALU.add), r=[t1, t2, self.neg_lam], w=[t1])
        P.op("pool", lambda e: e.tensor_tensor(out=self.sq[:], in0=t1[:], in1=t1[:], op=ALU.mult), r=[t1], w=[self.sq])
        P.op("pe", lambda e: e.matmul(ps_l[:], lhsT=C["ones_f"][:], rhs=self.sq[:], start=True, stop=True),
             r=[C["ones_f"], self.sq], w=[ps_l])
        P.op("act", lambda e: e.activation(out=self.Rr[:], in_=ps_l[:], func=AF.Sqrt, bias=EPS, scale=1.0 / 128),
             r=[ps_l], w=[self.Rr])
        P.op("dve", lambda e: e.reciprocal(out=self.Rr[:], in_=self.Rr[:]), r=[self.Rr], w=[self.Rr])
        yd = self.yd[self.ny % 2]
        self.ny += 1
        P.op("dve", lambda e, yd=yd: e.scalar_tensor_tensor(out=yd[:], in0=t1[:], scalar=self.subs[:, 0:1], in1=self.Rr[:],
                                                            op0=ALU.mult, op1=ALU.mult), r=[t1, self.subs, self.Rr], w=[yd])
        for hf in range(2):
            ch = self.yT[4 + 2 * h + hf]
            P.dma("sp", ch[0:64, Q * 512:(Q + 1) * 512], yd[hf * 64:(hf + 1) * 64, :], r=[yd], w=[ch])


def split3(v):
    return v[0:1024], v[1024:2048], v[2048:3072]


def even_inputs(z, mod, b, hg):
    wi = z["e_w_in"][0]
    cols = np.concatenate([
        np.arange(1024 + hg * 256, 1024 + hg * 256 + 256),
        np.arange(2048 + hg * 128, 2048 + hg * 128 + 128),
        np.arange(2560 + hg * 128, 2560 + hg * 128 + 128),
        np.arange(3088 + hg * 256, 3088 + hg * 256 + 256),
        np.arange(4112 + hg * 256, 4112 + hg * 256 + 256),
        np.arange(hg * 256, hg * 256 + 256),
        np.arange(5136 + hg * 256, 5136 + hg * 256 + 256),
        np.arange(3072 + hg * 4, 3072 + hg * 4 + 4),
    ])
    ch = cols[0:512] - 1024
    cw = z["e_conv_w"][0][:, ch]
    cb = z["e_conv_b"][0][ch]
    im = {
        "win": np.ascontiguousarray(wi[:, cols]),
        "convw": np.ascontiguousarray(cw.reshape(4, 4, 128).transpose(2, 1, 0)),
        "convb": np.ascontiguousarray(cb.reshape(4, 128).T),
        "hv": np.stack([z["e_dt_bias"][0][hg * 4:hg * 4 + 4], z["e_A_log"][0][hg * 4:hg * 4 + 4],
                        z["e_D"][0][hg * 4:hg * 4 + 4]]).astype(np.float32),
        "normw": np.ascontiguousarray(z["e_ssd_norm"][0][hg * 256:hg * 256 + 256].reshape(2, 128).T),
        "lam": np.ascontiguousarray(z["e_lambda"][0].reshape(1, 256)),
        "subw": np.ascontiguousarray(z["e_diff_norm"][0].reshape(128, 1)),
        "relb": np.concatenate([z["rel_bias"][:, 2 * hg:2 * hg + 2], np.ones((1, 2), np.float32)], 0),
        "ohd": host_ohd(),
    }
    im.update(host_consts())
    return im


O_NCOL = 256 + 512 + 320


def host_rope_tables(hg):
    gamma = 1.0 - 2.0 ** (-5.0 - hg)
    pos = np.arange(S_LEN, dtype=np.float32)
    inv = (np.float32(10000.0) ** (-np.arange(64, dtype=np.float32) / np.float32(64))).astype(np.float32)
    ang = (pos[:, None] * inv[None]).astype(np.float32).astype(np.float64)
    cos, sin = np.cos(ang), np.sin(ang)
    l = (np.arange(S_LEN) % 128).astype(np.float64)
    fq = (gamma ** l)[:, None]
    fk = (gamma ** (-l))[:, None] * 128.0 ** -0.5
    tab = np.stack([
        np.concatenate([cos, cos], 1) * fq, np.concatenate([-sin, sin], 1) * fq,
        np.concatenate([cos, cos], 1) * fk, np.concatenate([-sin, sin], 1) * fk]).astype(np.float32)
    gv = np.zeros((128, 2), np.float32)
    gv[:, 0] = gamma ** 128
    return tab, gv


def phase_odd(P, C, io, n_super=16):
    m0 = P.mark()
    hT_all, win, tab, gv, sinks = io["hT2_all"], io["o_win"], io["tab"], io["gv"], io["sinks"]
    relb, ohdA, ohdB = io["relb"], io["ohdA"], io["ohdB"]
    yT, vecdA, vecdB = io["yT2_loc"], io["vecdA"], io["vecdB"]
    T = TokCtx(P, C["ident_bf"])
    w_sb = [P.sb(f"win{k}", [128, O_NCOL], BF16) for k in range(8)]
    for k in range(8):
        P.dma("pool", w_sb[k][:], win[k * 128:(k + 1) * 128, :], w=[w_sb[k]])
    gv_sb = P.sb("gv_sb", [128, 2])
    P.dma("sp", gv_sb[:], gv, w=[gv_sb])
    es = P.sb("es", [128, 2])
    P.dma("sp", es[:], bcast_row(sinks), w=[es])
    P.op("act", lambda e: e.activation(out=es[:], in_=es[:], func=AF.Exp), r=[es], w=[es])
    pFM = P.ps("pFM", [128, 512])
    pT1 = P.ps("pT1", [128, 512])
    pT2 = P.ps("pT2", [128, 512])
    pSc = P.ps("pSc", [128, 512])
    pRO = P.ps("pRO", [128, 512])
    pTr = P.ps("pTr", [128, 512])
    pSW = P.ps("pSW", [128, 512])
    pOL = P.ps("pOL", [128, 512])
    pTr_bf = pTr.alt
    BdA, _, _ = make_bias_tiles(P, C, relb, ohdA, vecdA, pOL, "oA")
    _, BpB, _ = make_bias_tiles(P, C, relb, ohdB, vecdB, pOL, "oB")
    Bpd = []
    for h in range(2):
        t = P.sb(f"Bpd{h}", [128, 2, 128])
        P.op("dve", lambda e, t=t, h=h: e.tensor_copy(out=t[:, 0, :], in_=BpB[h][:]), r=[BpB[h]], w=[t])
        P.op("dve", lambda e, t=t, h=h: e.tensor_copy(out=t[:, 1, :], in_=BdA[h][:]), r=[BdA[h], t], w=[t])
        Bpd.append(t)
    hT = [P.sb(f"hT{i}", [128, 8, 512], BF16) for i in range(2)]
    tb = [P.sb(f"tb{i}", [128, 4, 4, 128]) for i in range(2)]
    SQT = P.sb("SQT", [128, 512], BF16)
    SKT = P.sb("SKT", [128, 128 + 512], BF16)
    SV = P.sb("SV", [128, 5, 64], BF16)
    qkv = [P.sb(f"qkv{i}", [128, 256]) for i in range(4)]
    v_bf = [P.sb(f"v_bf{i}", [128, 256], BF16) for i in range(4)]
    sg = [P.sb(f"sg{i}", [128, 256]) for i in range(4)]
    Wk = {}
    for nme, shp, dt in [("A", [128, 128], F32), ("B", [128, 128], F32), ("Qp", [128, 128], BF16),
                         ("Kp", [128, 128], BF16), ("QT", [128, 128], BF16), ("KT", [128, 128], BF16),
                         ("Sm", [128, 128], BF16), ("y_bf", [128, 256], BF16), ("yTs", [128, 2, 128], BF16),
                         ("ts", [128, 2, 128], F32), ("PT", [128, 2, 128], BF16), ("den", [64, 128], F32),
                         ("ob", [64, 128], BF16), ("ss", [128, 2], F32), ("rstd", [128, 2], F32)]:
        Wk[nme] = P.sb("k_" + nme, shp, dt)
    St = P.sb("St", [128, 256])
    gS = P.sb("gS", [128, 256], BF16)
    P.op("pool", lambda e: e.memset(St[:], 0.0), w=[St])
    P.op("pool", lambda e: e.memset(gS[:], 0.0), w=[gS])

    def ret_chunk(c, sub, tbs, qk, vb, sgt):
        for which, (col0, tq, dst) in enumerate(((0, 0, "Qp"), (128, 2, "Kp"))):
            src = qk[:, col0:col0 + 128]
            P.op("dve", lambda e, src=src, tq=tq: e.tensor_tensor(out=Wk["A"][:], in0=src, in1=tbs[:, tq, sub, :],
                                                                 op=ALU.mult), r=[qk, tbs], w=[Wk["A"]])
            P.op("pool", lambda e, col0=col0, tq=tq: e.tensor_tensor(
                out=Wk["B"][:, 0:64], in0=qk[:, col0 + 64:col0 + 128], in1=tbs[:, tq + 1, sub, 0:64], op=ALU.mult),
                r=[qk, tbs], w=[Wk["B"]])
            P.op("pool", lambda e, col0=col0, tq=tq: e.tensor_tensor(
                out=Wk["B"][:, 64:128], in0=qk[:, col0:col0 + 64], in1=tbs[:, tq + 1, sub, 64:128], op=ALU.mult),
                r=[qk, tbs, Wk["B"]], w=[Wk["B"]])
            P.op("dve", lambda e, dst=dst: e.tensor_tensor(out=Wk[dst][:], in0=Wk["A"][:], in1=Wk["B"][:], op=ALU.add),
                 r=[Wk["A"], Wk["B"]], w=[Wk[dst]])
            P.op("pe", lambda e, dst=dst, which=which: e.transpose(
                out=pTr_bf[:, which * 128:(which + 1) * 128], in_=Wk[dst][:], identity=C["ident_bf"][:]),
                r=[Wk[dst], C["ident_bf"]], w=[pTr])
        yield
        P.op("act", lambda e: e.copy(out=Wk["QT"][:], in_=pTr_bf[:, 0:128]), r=[pTr], w=[Wk["QT"]])
        P.op("act", lambda e: e.copy(out=Wk["KT"][:], in_=pTr_bf[:, 128:256]), r=[pTr], w=[Wk["KT"]])
        P.op("pe", lambda e: e.matmul(pSc[:, 0:128], lhsT=Wk["KT"][:], rhs=Wk["QT"][:], start=True, stop=True),
             r=[Wk["KT"], Wk["QT"]], w=[pSc])
        P.op("dve", lambda e: e.tensor_tensor(out=Wk["Sm"][:], in0=pSc[:, 0:128], in1=C["triU"][:], op=ALU.mult),
             r=[pSc, C["triU"]], w=[Wk["Sm"]])
        yield
        P.op("pe", lambda e: e.matmul(pRO[:, 0:256], lhsT=Wk["Sm"][:], rhs=vb[:], start=True, stop=False),
             r=[Wk["Sm"], vb], w=[pRO])
        P.op("pe", lambda e: e.matmul(pRO[:, 0:256], lhsT=Wk["QT"][:], rhs=gS[:], start=False, stop=True),
             r=[Wk["QT"], gS], w=[pRO])
        yield
        T.sumsq_rstd(pRO[:, 0:256], [pRO], Wk["ss"], Wk["rstd"], 256)
        yield
        P.op("dve", lambda e: e.scalar_tensor_tensor(out=Wk["y_bf"][:], in0=pRO[:, 0:256], scalar=Wk["rstd"][:, 0:1],
                                                     in1=sgt[:], op0=ALU.mult, op1=ALU.mult),
             r=[pRO, Wk["rstd"], sgt], w=[Wk["y_bf"]])
        for j in range(2):
            P.op("pe", lambda e, j=j: e.transpose(out=pTr_bf[:, 256 + j * 128:256 + (j + 1) * 128],
                                                  in_=Wk["y_bf"][:, j * 128:(j + 1) * 128], identity=C["ident_bf"][:]),
                 r=[Wk["y_bf"], C["ident_bf"]], w=[pTr])
        P.op("act", lambda e: e.copy(out=Wk["yTs"][:], in_=pTr_bf[:, 256:512].rearrange("p (j t) -> p j t", j=2)),
             r=[pTr], w=[Wk["yTs"]])
        for j in range(2):
            for hf in range(2):
                ch = yT[2 * j + hf]
                P.dma("sp", ch[0:64, c * 128:(c + 1) * 128], Wk["yTs"][hf * 64:(hf + 1) * 64, j, :], r=[Wk["yTs"]], w=[ch])
        yield
        P.op("pe", lambda e: e.matmul(pRO[:, 256:512], lhsT=Wk["Kp"][:], rhs=vb[:], start=True, stop=True),
             r=[Wk["Kp"], vb], w=[pRO])
        P.op("dve", lambda e: e.scalar_tensor_tensor(out=St[:], in0=St[:], scalar=gv_sb[:, 0:1], in1=pRO[:, 256:512],
                                                     op0=ALU.mult, op1=ALU.add), r=[St, gv_sb, pRO], w=[St])
        P.op("act", lambda e: e.activation(out=gS[:], in_=St[:], func=AF.Copy, scale=gv_sb[:, 0:1]),
             r=[St, gv_sb], w=[gS])
        yield

    def swa_block(blk, sub):
        for h in range(2):
            hs = slice(h * 64, (h + 1) * 64)
            qcols = slice(sub * 128, (sub + 1) * 128)
            first = (blk == 0)
            if not first:
                P.op("pe", lambda e, hs=hs, qcols=qcols, sub=sub: e.matmul(
                    pSW[:, 0:128], lhsT=SKT[hs, sub * 128:(sub + 1) * 128], rhs=SQT[hs, qcols], start=True, stop=True),
                    r=[SKT, SQT], w=[pSW])
            P.op("pe", lambda e, hs=hs, qcols=qcols, sub=sub: e.matmul(
                pSW[:, 128:256], lhsT=SKT[hs, (sub + 1) * 128:(sub + 2) * 128], rhs=SQT[hs, qcols], start=True, stop=True),
                r=[SKT, SQT], w=[pSW])
            lo = 1 if first else 0
            yield
            P.op("dve", lambda e, h=h, lo=lo: e.scalar_tensor_tensor(
                out=Wk["ts"][:, lo:2, :], in0=pSW[:, lo * 128:256].rearrange("p (a b) -> p a b", b=128), scalar=0.125,
                in1=Bpd[h][:, lo:2, :], op0=ALU.mult, op1=ALU.add), r=[pSW, Bpd[h]], w=[Wk["ts"]])
            P.op("act", lambda e, lo=lo: e.activation(out=Wk["PT"][:, lo:2, :], in_=Wk["ts"][:, lo:2, :], func=AF.Exp),
                 r=[Wk["ts"]], w=[Wk["PT"]])
            oc = slice(h * 256, h * 256 + 128)
            lc = slice(h * 256 + 128, h * 256 + 256)
            yield
            if not first:
                P.op("pe", lambda e, oc=oc, sub=sub: e.matmul(pOL[0:64, oc], lhsT=SV[:, sub, :], rhs=Wk["PT"][:, 0, :],
                                                              start=True, stop=False), r=[SV, Wk["PT"]], w=[pOL])
            P.op("pe", lambda e, oc=oc, sub=sub, first=first: e.matmul(
                pOL[0:64, oc], lhsT=SV[:, sub + 1, :], rhs=Wk["PT"][:, 1, :], start=first, stop=True),
                r=[SV, Wk["PT"]], w=[pOL])
            if not first:
                P.op("pe", lambda e, lc=lc: e.matmul(pOL[0:64, lc], lhsT=C["ones_bf"][:, 0:64], rhs=Wk["PT"][:, 0, :],
                                                     start=True, stop=False), r=[C["ones_bf"], Wk["PT"]], w=[pOL])
            P.op("pe", lambda e, lc=lc, first=first: e.matmul(
                pOL[0:64, lc], lhsT=C["ones_bf"][:, 0:64], rhs=Wk["PT"][:, 1, :], start=first, stop=True),
                r=[C["ones_bf"], Wk["PT"]], w=[pOL])
            yield
            P.op("dve", lambda e, lc=lc, h=h: e.tensor_scalar(out=Wk["den"][:], in0=pOL[0:64, lc], scalar1=es[0:64, h:h + 1],
                                                              scalar2=None, op0=ALU.add), r=[pOL, es], w=[Wk["den"]])
            P.op("dve", lambda e: e.reciprocal(out=Wk["den"][:], in_=Wk["den"][:]), r=[Wk["den"]], w=[Wk["den"]])
            P.op("dve", lambda e, oc=oc: e.tensor_tensor(out=Wk["ob"][:], in0=pOL[0:64, oc], in1=Wk["den"][:], op=ALU.mult),
                 r=[pOL, Wk["den"]], w=[Wk["ob"]])
            P.dma("act", yT[4 + h][0:64, blk * 128:(blk + 1) * 128], Wk["ob"][:], r=[Wk["ob"]], w=[yT[4 + h]])

    for st in range(n_super):
        h_t = hT[st % 2]
        qq, so = divmod(st, 4)
        for c4 in range(4):
            P.dma("sp" if c4 % 2 == 0 else "act", h_t[:, 2 * c4:2 * c4 + 2, :],
                  hT_all[c4][qq * 256:(qq + 1) * 256, so * 512:(so + 1) * 512].rearrange("(k p) t -> p k t", p=128),
                  r=[hT_all[c4]], w=[h_t])
        tbs = tb[st % 2]
        for q4 in range(4):
            P.dma("act", tbs[:, q4, :, :], tab[q4, st * 512:(st + 1) * 512, :].rearrange("(s p) d -> p s d", p=128),
                  w=[tbs])
        for j in range(2):
            for k in range(8):
                P.op("pe", lambda e, j=j, k=k, h_t=h_t: e.matmul(
                    pFM[:], lhsT=w_sb[k][:, j * 128:(j + 1) * 128], rhs=h_t[:, k, :], start=(k == 0), stop=(k == 7)),
                    r=[w_sb[k], h_t], w=[pFM])
            if j == 0:
                P.op("act", lambda e: e.copy(out=SQT[:], in_=pFM[:]), r=[pFM], w=[SQT])
            else:
                if st > 0:
                    P.op("dve", lambda e: e.tensor_copy(out=SKT[:, 0:128], in_=SKT[:, 512:640]), r=[SKT], w=[SKT])
                    P.op("dve", lambda e: e.tensor_copy(out=SV[:, 0, :], in_=SV[:, 4, :]), r=[SV], w=[SV])
                P.op("act", lambda e: e.copy(out=SKT[:, 128:640], in_=pFM[:]), r=[pFM], w=[SKT])
        for sub in range(4):
            c = st * 4 + sub
            ts_ = slice(sub * 128, (sub + 1) * 128)
            for k in range(8):
                P.op("pe", lambda e, k=k, h_t=h_t, ts_=ts_: e.matmul(
                    pT1[:], lhsT=h_t[:, k, ts_], rhs=w_sb[k][:, 256:768], start=(k == 0), stop=(k == 7)),
                    r=[h_t, w_sb[k]], w=[pT1])
            for k in range(8):
                P.op("pe", lambda e, k=k, h_t=h_t, ts_=ts_: e.matmul(
                    pT2[:, 0:320], lhsT=h_t[:, k, ts_], rhs=w_sb[k][:, 768:1088], start=(k == 0), stop=(k == 7)),
                    r=[h_t, w_sb[k]], w=[pT2])
            qk = qkv[sub]
            vb = v_bf[sub]
            sgt = sg[sub]
            P.op("act", lambda e, qk=qk: e.copy(out=qk[:, 0:256], in_=pT1[:, 0:256]), r=[pT1], w=[qk])
            P.op("dve", lambda e, vb=vb: e.tensor_copy(out=vb[:], in_=pT1[:, 256:512]), r=[pT1], w=[vb])
            P.op("act", lambda e, sgt=sgt: e.activation(out=sgt[:], in_=pT2[:, 0:256], func=AF.Silu), r=[pT2], w=[sgt])
            P.op("dve", lambda e, sub=sub: e.tensor_copy(out=SV[:, sub + 1, :], in_=pT2[:, 256:320]), r=[pT2], w=[SV])


        def ret_gen(st=st, tbs=tbs):
            for sub in range(4):
                yield from ret_chunk(st * 4 + sub, sub, tbs, qkv[sub], v_bf[sub], sg[sub])

        def swa_gen(st=st):
            for sub in range(4):
                yield from swa_block(st * 4 + sub, sub)

        interleave([(ret_gen(), 4 * 6), (swa_gen(), 4 * 2 * 4)])
    P.reset(m0)


def odd_inputs(z, hT_full, b, hg):
    wi = z["o_w_in"][0]
    kv = hg // 2
    cols = np.concatenate([
        np.arange(3072 + 2 * hg * 64, 3072 + 2 * hg * 64 + 128),
        np.arange(3584 + kv * 64, 3584 + kv * 64 + 64), np.arange(3584 + kv * 64, 3584 + kv * 64 + 64),
        np.arange(hg * 128, hg * 128 + 128),
        np.arange(512 + hg * 128, 512 + hg * 128 + 128),
        np.arange(1024 + hg * 256, 1024 + hg * 256 + 256),
        np.arange(2048 + hg * 256, 2048 + hg * 256 + 256),
        np.arange(3712 + kv * 64, 3712 + kv * 64 + 64),
    ])
    tab, gv = host_rope_tables(hg)
    hc = host_consts()
    im = {
        "win": np.ascontiguousarray(wi[:, cols]), "tab": tab, "gv": gv,
        "sinks": np.ascontiguousarray(z["o_sinks"][0][2 * hg:2 * hg + 2].reshape(1, 2)),
        "relb": np.concatenate([z["rel_bias"][:, 2 * hg:2 * hg + 2], np.ones((1, 2), np.float32)], 0),
        "ohdA": host_ohd(False), "ohdB": host_ohd(True),
    }
    for k in ["ident_bf", "triU", "ones_bf", "antiI"]:
        im[k] = hc[k]
    return im


I32 = mybir.dt.int32
GROUPS = [[0, 1, 2, 3], [4, 5, 6, 7]]


def dyn_dma(P, out, in_fn, r, w):
    op = Op("sp", lambda e: e.dma_start(out=out, in_=in_fn()), is_dma=True, dbuf=w[0])
    P._rec(op, r, w)
    P.dma_log.append(op)
    return op


def setup_regs(P, nc, qoff):
    regs = [P.stack.enter_context(nc.sync.register(f"qr{i}")) for i in range(2)]

    def ld(e):
        for i in range(2):
            ins = e.reg_load(regs[i], qoff.t[0:1, i:i + 1])
        P.qv = e.snap(regs[0], min_val=0, max_val=6144)
        P.qc = e.snap(regs[1], min_val=0, max_val=48)
        return ins
    P.op("sp", ld)


def extract_quarter(P, src_all, dst_q, nrows, step):
    for r0 in range(0, nrows, step):
        dyn_dma(P, dst_q.t[r0:r0 + step, :], lambda r0=r0: src_all.t[r0:r0 + step, bass.ds(P.qv, NT)],
                r=[src_all], w=[dst_q])


def phase_mod(P, io):
    m0 = P.mark()
    cT, modw, modb = io["cT"], io["modw"], io["modb"]
    c_sb = P.sb("c_sb", [128, 8])
    ca = P.sb("ca_sb", [128, 8])
    b_sb = P.sb("b_sb", [1, 3072])
    o_sb = P.sb("o_sb", [1, 3072])
    wt = [P.sb(f"mw{i}", [128, 8, 768]) for i in range(2)]
    ps = [P.ps(f"mps{i}", [1, 512]) for i in range(2)]
    P.dma("sp", c_sb[:], cT, w=[c_sb])
    P.dma("sp", b_sb[:], modb, w=[b_sb])
    P.op("act", lambda e: e.activation(out=ca[:], in_=c_sb[:], func=AF.Silu), r=[c_sb], w=[ca])
    for s_ in range(4):
        w_t = wt[s_ % 2]
        P.dma("sp" if s_ % 2 == 0 else "act", w_t[:], modw[s_].rearrange("(k p) n -> p k n", p=128), w=[w_t])
        for hf in range(2):
            for k in range(8):
                P.op("pe", lambda e, hf=hf, k=k, w_t=w_t: e.matmul(
                    ps[hf][0:1, 0:384], lhsT=ca[:, k:k + 1], rhs=w_t[:, k, hf * 384:(hf + 1) * 384],
                    start=(k == 0), stop=(k == 7)), r=[ca, w_t], w=[ps[hf]])
            o0 = s_ * 768 + hf * 384
            P.op("dve", lambda e, hf=hf, o0=o0: e.tensor_tensor(out=o_sb[0:1, o0:o0 + 384], in0=ps[hf][0:1, 0:384],
                                                                in1=b_sb[0:1, o0:o0 + 384], op=ALU.add),
                 r=[ps[hf], b_sb], w=[o_sb])
    P.dma("sp", io["mod_loc"], o_sb[:], r=[o_sb], w=[io["mod_loc"]])
    P.collective("AllGather", GROUPS, io["mod_loc"], io["mod_all"])
    P.reset(m0)


CONST_NAMES = ["ident_bf", "ident_f", "triU", "triS", "NEG", "ones_f", "ones_bf", "antiI"]


def build_fused():
    nc = bass.Bass("TRN2", target_bir_lowering=False)
    P = Prog(nc)

    def X(name, shape, dt=F32):
        return Buf(nc.dram_tensor(name, list(shape), dt, kind="ExternalInput").ap(), name)

    io = {}
    for name, shape, dt in [
        ("cT", [128, 8], F32), ("modw", [4, 1024, 768], F32), ("modb", [1, 3072], F32), ("ng", [8, 1024], F32),
        ("qoff", [1, 2], I32), ("x_b", [S_LEN, 1024], F32), ("x_tok", [NT, 1024], F32),
        ("e_win", [1024, E_NCOL], F32), ("convw", [128, 4, 4], F32), ("convb", [128, 4], F32), ("hv", [3, 4], F32),
        ("normw", [128, 2], F32), ("lam", [1, 256], F32), ("subw", [128, 1], F32), ("relb", [33, 2], F32),
        ("ohdA", [33, 384], F32), ("ohdB", [33, 384], F32), ("e_wout", [2048, 1024], F32),
        ("w1_0", [1024, 4096], F32), ("w2_0", [4096, 1024], F32), ("w1_1", [1024, 4096], F32), ("w2_1", [4096, 1024], F32),
        ("o_win", [1024, O_NCOL], F32), ("tab", [4, S_LEN, 128], F32), ("gv", [128, 2], F32), ("sinks", [1, 2], F32),
        ("o_wout", [1536, 1024], F32),
    ]:
        if int(os.environ.get("FUSED_STOP", "99")) <= 2 and name in ("x_tok", "e_wout", "w1_0", "w2_0", "w1_1", "w2_1", "o_win", "tab", "o_wout"):
            continue
        io[name] = X(name, shape, dt)
    P.declared = set(io.keys()) | set(CONST_NAMES)
    cn = {k: X(k, [128, 128], BF16 if k.endswith("bf") else F32) for k in CONST_NAMES}
    for name, shape, dt in [
        ("mod_loc", [1, 3072], F32), ("mod_all", [4, 3072], F32),
        ("ss_loc", [128, 64], F32), ("ss_all", [512, 64], F32),
        ("xn0", [NT, 1024], F32), ("hT0", [1024, NT], BF16), ("x1", [NT, 1024], F32),

        ("xn1", [NT, 1024], F32), ("hT1", [1024, NT], BF16), ("vecdA", [2, 384], F32), ("vecdB", [2, 384], F32),
        ("vecdC", [2, 384], F32), ("yT_q", [2048, NT], BF16), ("ss_q", [512, 16], F32), ("yT2_q", [1536, NT], BF16),
    ]:
        io[name] = P.dram(name, shape, dt)
    io["yT_loc"] = [P.dram(f"yT_loc{i}", [64, S_LEN], BF16) for i in range(8)]
    io["yT_all"] = [P.dram(f"yT_all{i}", [256, S_LEN], BF16) for i in range(8)]
    io["yT2_loc"] = [P.dram(f"yT2_loc{i}", [64, S_LEN], BF16) for i in range(6)]
    io["yT2_all"] = [P.dram(f"yT2_all{i}", [256, S_LEN], BF16) for i in range(6)]
    io["hTn_loc"] = [P.dram(f"hTn_loc{i}", [256, NT], BF16) for i in range(4)]
    io["hT2_all"] = [P.dram(f"hT2_all{i}", [1024, NT], BF16) for i in range(4)]
    io["out"] = P.dram("out", [NT, 1024], F32, kind="ExternalOutput")

    setup_regs(P, nc, io["qoff"])
    C = {}
    for k in CONST_NAMES:
        C[k] = P.sb("c_" + k, [128, 128], BF16 if k.endswith("bf") else F32)
        P.dma("sp", C[k][:], cn[k], w=[C[k]])
    P.barrier()

    stop = int(os.environ.get("FUSED_STOP", "99"))
    phase_mod(P, io)
    P.barrier()
    if stop <= 1:
        P.build()
        return nc, P
    phase_even(P, C, io, n_super=int(os.environ.get('FUSED_NSUPER', '16')))
    for i in range(8):
        P.collective("AllGather", GROUPS, io["yT_loc"][i], io["yT_all"][i])
    if not os.environ.get("FUSED_NOAG2"):
        P.collective("AllGather", GROUPS, io["ss_loc"], io["ss_all"])
    P.barrier()
    if stop <= 2:
        P.build()
        return nc, P
    rm_e = [(kk // 2, kk % 2) for kk in range(8)] + [(kk // 2, 2 + kk % 2) for kk in range(8)]
    phase_outproj(P, C, dict(io, x=io["x_tok"], yT_all=io["yT_all"], wout=io["e_wout"], xn=io["xn0"], hT=io["hT0"], rowmap=rm_e, ngroups=16),
                  16, True, 0)
    P.barrier()
    if stop <= 3:
        P.build()
        return nc, P
    phase_mlp(P, C, dict(io, xn=io["xn0"], hT=io["hT0"], w1=io["w1_0"], w2=io["w2_0"], xo=io["x1"], hTn=io["hTn_loc"]), True, 0)
    for i in range(4):
        P.collective("AllGather", GROUPS, io["hTn_loc"][i], io["hT2_all"][i])
    P.barrier()
    if stop <= 4:
        P.build()
        return nc, P
    phase_odd(P, C, dict(io, vecdA=io["vecdB"], vecdB=io["vecdC"]), n_super=int(os.environ.get('FUSED_NSUPER', '16')))
    for i in range(6):
        P.collective("AllGather", GROUPS, io["yT2_loc"][i], io["yT2_all"][i])
    P.barrier()
    rm_o = [(kk // 2, kk % 2) for kk in range(8)] + [(kk, 2) for kk in range(4)]
    phase_outproj(P, C, dict(io, x=io["x1"], yT_all=io["yT2_all"], wout=io["o_wout"], xn=io["xn1"], hT=io["hT1"], rowmap=rm_o, ngroups=12),
                  12, False, 1)
    P.barrier()
    phase_mlp(P, C, dict(io, xn=io["xn1"], hT=io["hT1"], w1=io["w1_1"], w2=io["w2_1"], xo=io["out"]), False, 1)
    P.build()
    return nc, P


def fused_inputs(z, i):
    b, r = divmod(i, 4)
    hc = host_consts()
    cols = np.concatenate([part * 1024 + r * 256 + np.arange(256) for part in range(3)])
    mw = z["mod_w"].reshape(4, 1024, 3072)
    mb = z["mod_b"].reshape(4, 3072)
    ei = even_inputs(z, None, b, r)
    oi = odd_inputs(z, None, b, r)
    im = {
        "cT": np.ascontiguousarray(z["c"][b].reshape(8, 128).T),
        "modw": np.ascontiguousarray(mw[:, :, cols]),
        "modb": np.ascontiguousarray(mb[:, cols].reshape(1, 3072)),
        "ng": np.ascontiguousarray(z["norm_gains"].reshape(8, 1024)),
        "qoff": np.array([[r * NT, r * 16]], np.int32),
        "x_b": np.ascontiguousarray(z["x"][b]),
        "x_tok": np.ascontiguousarray(z["x"][b, r * NT:(r + 1) * NT]),
        "e_win": ei["win"], "convw": ei["convw"], "convb": ei["convb"], "hv": ei["hv"], "normw": ei["normw"],
        "lam": ei["lam"], "subw": ei["subw"], "relb": ei["relb"], "ohdA": host_ohd(False), "ohdB": host_ohd(True),
        "e_wout": z["e_w_out"][0], "w1_0": z["mlp_w1"][0], "w2_0": z["mlp_w2"][0], "w1_1": z["mlp_w1"][1],
        "w2_1": z["mlp_w2"][1], "o_win": oi["win"], "tab": oi["tab"], "gv": oi["gv"], "sinks": oi["sinks"],
        "o_wout": z["o_w_out"][0],
    }
    for k in CONST_NAMES:
        im[k] = hc[k]
    return im


def kernel(**inputs):
    z = {k: np.asarray(v) for k, v in inputs.items()}
    nc, P_ = build_fused()
    in_maps = [{k: v for k, v in fused_inputs(z, i).items() if k in P_.declared} for i in range(8)]
    res = run_bass_kernel_spmd(nc, in_maps, core_ids=list(range(8)))
    out = np.stack([res.results[i]["out"] for i in range(8)]).reshape(2, S_LEN, 1024).astype(np.float32)
    return out
```

```python
from contextlib import ExitStack
import os
import numpy as np
import ml_dtypes
import concourse.bass as bass
import concourse.mybir as mybir
from concourse.bass_utils import run_bass_kernel_spmd

F32 = mybir.dt.float32
BF16 = mybir.dt.bfloat16
AF = mybir.ActivationFunctionType
ALU = mybir.AluOpType
AX = mybir.AxisListType
EPOCH = 20000


class Buf:
    __slots__ = ("t", "name", "w", "r", "dsem", "dcount", "is_out", "bank", "alt")

    def __init__(self, t, name, bank=None):
        self.t = t
        self.name = name
        self.bank = bank
        self.alt = None
        self.w = []
        self.r = []
        self.dsem = None
        self.dcount = 0
        self.is_out = False

    def __getitem__(self, idx):
        return self.t[idx]


class Op:
    __slots__ = ("eng", "fn", "deps", "marked", "sem", "val", "waits", "is_dma", "dbuf", "snap", "inc", "is_bar")

    def __init__(self, eng, fn, is_dma=False, dbuf=None, inc=16):
        self.inc = inc
        self.is_bar = False
        self.eng = eng
        self.fn = fn
        self.deps = []
        self.marked = False
        self.sem = None
        self.val = 0
        self.waits = []
        self.is_dma = is_dma
        self.dbuf = dbuf
        self.snap = None


class Prog:
    ENGS = ("pe", "act", "dve", "pool", "sp")

    def __init__(self, nc):
        self.nc = nc
        self.ops = []
        self.stack = ExitStack()
        self.out_ops = []
        self.nsem = 0
        self.SB_BYTES = 206 * 1024
        self.sb_f32 = self.stack.enter_context(nc.sbuf_tensor("arena", [128, self.SB_BYTES // 4], F32))
        self.sb_bf = self.sb_f32.bitcast(BF16)
        self.ps_f32 = self.stack.enter_context(nc.psum_tensor("parena", [128, 4096], F32))
        self.ps_bf = self.ps_f32.bitcast(BF16)
        self.sb_ptr = 0
        self.ps_ptr = 0
        self.dma_log = []
        self.bar = None

    @staticmethod
    def _shape_view(v, shape):
        if len(shape) == 3:
            v = v.rearrange("p (a b) -> p a b", a=shape[1])
        elif len(shape) == 4:
            v = v.rearrange("p (a b c) -> p a b c", a=shape[1], b=shape[2])
        return v

    def sb(self, name, shape, dt=F32):
        shape = list(shape)
        nel = int(np.prod(shape[1:]))
        esz = 2 if dt == BF16 else 4
        nbytes = (nel * esz + 31) // 32 * 32
        off = self.sb_ptr
        self.sb_ptr += nbytes
        assert self.sb_ptr <= self.SB_BYTES, ("SBUF arena overflow", name, self.sb_ptr)
        base = self.sb_bf if esz == 2 else self.sb_f32
        v = base[0:shape[0], off // esz:off // esz + nel]
        return Buf(self._shape_view(v, shape), name)

    def ps(self, name, shape, dt=F32):
        shape = list(shape)
        nel = int(np.prod(shape[1:]))
        esz = 2 if dt == BF16 else 4
        nb = (nel * esz + 2047) // 2048
        off = self.ps_ptr * 2048
        self.ps_ptr += nb
        assert self.ps_ptr <= 8, ("PSUM arena overflow", name)
        base = self.ps_bf if esz == 2 else self.ps_f32
        v = base[0:shape[0], off // esz:off // esz + nel]
        b = Buf(self._shape_view(v, shape), name, bank=[None])
        b.alt = self.ps_bf[:, off // 2:off // 2 + nb * 1024]
        return b

    def mark(self):
        return (self.sb_ptr, self.ps_ptr)

    def reset(self, m):
        self.sb_ptr, self.ps_ptr = m

    def barrier(self):
        if self.bar is None:
            self.bar = {e: self.sb("bar_" + e, [128, 8]) for e in ("act", "dve", "pool")}
        marks = []
        for eng in ("act", "dve", "pool"):
            b = self.bar[eng]
            if eng == "act":
                marks.append(self.op(eng, lambda e, b=b: e.memzero(b[:]), w=[b]))
            else:
                marks.append(self.op(eng, lambda e, b=b: e.memset(b[:], 0.0), w=[b]))
        latest = {}
        for d in self.dma_log:
            latest[id(d.dbuf)] = d
        self.dma_log = []
        first = True
        for eng in self.ENGS:
            op = Op(eng, None)
            op.is_bar = first
            first = False
            op.deps = marks + list(latest.values())
            self.ops.append(op)

    def dram(self, name, shape, dt, kind="Internal"):
        t = self.nc.dram_tensor(name, list(shape), dt, kind=kind)
        b = Buf(t.ap(), name)
        b.is_out = kind == "ExternalOutput"
        return b

    def newsem(self, name):
        self.nsem += 1
        return self.stack.enter_context(self.nc.semaphore(f"{name}_{self.nsem}"))

    def _rec(self, op, r, w):
        deps = []
        for b in r:
            deps.extend(b.w)
        for b in w:
            for d in b.w:
                if not (d.eng == "pe" and op.eng == "pe" and not d.is_dma and not op.is_dma):
                    deps.append(d)
            deps.extend(b.r)
        for b in r:
            b.r.append(op)
        for b in w:
            b.w = [op]
            b.r = []
        for b in list(r) + list(w):
            if b.bank is not None:
                d = b.bank[0]
                if d is not None and d.eng != op.eng:
                    deps.append(d)
                b.bank[0] = op
        seen = set()
        for d in deps:
            if id(d) not in seen and d is not op:
                seen.add(id(d))
                op.deps.append(d)
        self.ops.append(op)
        return op

    def op(self, eng, fn, r=(), w=()):
        return self._rec(Op(eng, fn), r, w)

    def dma(self, q, out, in_, r=(), w=(), sembuf=None, **kw):
        if sembuf is None:
            sembuf = (list(w) + list(r))[0]
        if isinstance(out, Buf):
            out = out.t
        if isinstance(in_, Buf):
            in_ = in_.t
        op = Op(q, lambda e: e.dma_start(out=out, in_=in_, **kw), is_dma=True, dbuf=sembuf)
        self._rec(op, r, w)
        self.dma_log.append(op)
        if any(b.is_out for b in w):
            self.out_ops.append(op)
        return op

    def collective(self, kind, groups, in_buf, out_buf):
        op = Op("pool", lambda e: e.collective_compute(kind, ALU.bypass, replica_groups=groups,
                                                       ins=[in_buf.t.opt()], outs=[out_buf.t.opt()]),
                is_dma=True, dbuf=out_buf, inc=1)
        self.dma_log.append(op)
        return self._rec(op, [in_buf], [out_buf])

    def build(self):
        nc = self.nc
        fin = Op("sp", None)
        fin.deps = list(self.out_ops)
        self.ops.append(fin)
        for op in self.ops:
            if op.is_dma:
                op.marked = True
            for d in op.deps:
                d.marked = True
        cnt = {e: 0 for e in self.ENGS}
        esem = {}
        free_d = []
        assigned = []
        for op in self.ops:
            if op.is_bar:
                for b in assigned:
                    if b.dsem is not None:
                        free_d.append(b.dsem)
                        b.dsem = None
                assigned = []
            if not op.marked:
                continue
            if op.is_dma:
                b = op.dbuf
                if b.dsem is None:
                    b.dsem = free_d.pop() if free_d else [self.newsem("d"), 0]
                    assigned.append(b)
                sm = b.dsem
                sm[1] += op.inc
                op.sem, op.val = sm[0], sm[1]
                if sm[1] >= EPOCH:
                    b.dsem = None
            else:
                e = op.eng
                if e not in esem or cnt[e] >= EPOCH:
                    esem[e] = self.newsem(e)
                    cnt[e] = 0
                cnt[e] += 1
                op.sem, op.val = esem[e], cnt[e]
        known = {e: {} for e in self.ENGS}
        nwaits = 0
        for op in self.ops:
            k = known[op.eng]
            waits = {}
            for d in op.deps:
                key = id(d.sem)
                if k.get(key, 0) >= d.val:
                    continue
                waits[key] = (d.sem, d.val)
                k[key] = d.val
                for s, v in d.snap.items():
                    if k.get(s, 0) < v:
                        k[s] = v
            op.waits = list(waits.values())
            nwaits += len(op.waits)
            if op.marked:
                op.snap = dict(k)
                op.snap[id(op.sem)] = op.val
        per = {e: [o for o in self.ops if o.eng == e] for e in self.ENGS}
        self.stats = dict(n_ops=len(self.ops), n_waits=nwaits, n_sems=self.nsem,
                          per_eng={e: len(v) for e, v in per.items()})

        def emit(engobj, lst):
            for op in lst:
                for s, v in op.waits:
                    engobj.wait_ge(s, v)
                if op.fn is None:
                    continue
                ins = op.fn(engobj)
                if op.marked:
                    ins.then_inc(op.sem, op.inc if op.is_dma else 1)

        with nc.Block() as block:
            @block.tensor
            def _(e):
                emit(e, per["pe"])

            @block.scalar
            def _(e):
                emit(e, per["act"])

            @block.vector
            def _(e):
                emit(e, per["dve"])

            @block.gpsimd
            def _(e):
                emit(e, per["pool"])

            @block.sync
            def _(e):
                emit(e, per["sp"])
        self.stack.close()
        return nc


EPS = 1e-6


def interleave(gens_with_counts):
    gens = [[g, max(1, n), 0.0] for g, n in gens_with_counts]
    total = max(n for _, n, _ in gens)
    alive = list(gens)
    while alive:
        for item in list(alive):
            g, n, acc = item
            item[2] += n / total
            while item[2] >= 1.0 - 1e-9:
                item[2] -= 1.0
                try:
                    next(g)
                except StopIteration:
                    alive.remove(item)
                    break


def bcast_row(ap_row, n=128):
    ap_row = ap_row.t if isinstance(ap_row, Buf) else ap_row
    pairs = [list(p) for p in ap_row.ap]
    w = pairs[-1]
    return bass.AP(ap_row.tensor, ap_row.offset, [[0, n], [w[0], w[1]]])


def emit_rstd(P, eng, ss, rstd, n, r_extra=()):
    pass


class TokCtx:
    def __init__(self, P, ident):
        self.P = P
        self.ident = ident
        self.junk = P.sb("junk", [128, 1024], BF16)

    def sumsq_rstd(self, src_ap, src_bufs, ss, rstd, n):
        P = self.P
        j = self.junk
        P.op("act", lambda e: e.activation(out=j[:, 0:src_ap.shape[-1]], in_=src_ap, func=AF.Square,
                                           accum_out=ss[:, 0:1]), r=src_bufs, w=[ss])
        P.op("act", lambda e: e.activation(out=rstd[:, 0:1], in_=ss[:, 0:1], func=AF.Sqrt, bias=EPS, scale=1.0 / n),
             r=[ss], w=[rstd])
        P.op("dve", lambda e: e.reciprocal(out=rstd[:, 0:1], in_=rstd[:, 0:1]), r=[rstd], w=[rstd])


def load_modrow(P, dst, mod_all, s_, part):
    mt = mod_all.t
    src = bass.AP(mt.tensor, mt.offset + s_ * 768 + part * 256, [[0, 128], [3072, 4], [1, 256]])
    P.dma("sp", dst[:].rearrange("p (r c) -> p r c", r=4), src, r=[mod_all], w=[dst])


def setup_mod_rows(P, mod_all, ng, sA, sB, i_gpost, i_gpre, gg, gmod, shift, scratch):
    load_modrow(P, gg, mod_all, sA, 2)
    P.dma("sp", scratch[0][:], bcast_row(ng[i_gpost:i_gpost + 1, :]), w=[scratch[0]])
    P.op("dve", lambda e: e.tensor_tensor(out=gg[:], in0=gg[:], in1=scratch[0][:], op=ALU.mult),
         r=[gg, scratch[0]], w=[gg])
    if gmod is not None:
        load_modrow(P, gmod, mod_all, sB, 1)
        P.dma("sp", scratch[1][:], bcast_row(ng[i_gpre:i_gpre + 1, :]), w=[scratch[1]])
        P.op("dve", lambda e: e.scalar_tensor_tensor(out=gmod[:], in0=gmod[:], scalar=1.0, in1=scratch[1][:],
                                                     op0=ALU.add, op1=ALU.mult), r=[gmod, scratch[1]], w=[gmod])
        load_modrow(P, shift, mod_all, sB, 0)


def emit_prenorm_T(P, T, xsrc, xbufs, gmod, shift, psT, hT_out_ap, hT_bufs, tmp, hb, ss, rstd, psT_view=None, part=None):
    if part in (None, 0):
        T.sumsq_rstd(xsrc, xbufs, ss, rstd, 1024)
        P.op("dve", lambda e: e.scalar_tensor_tensor(out=tmp[:], in0=xsrc, scalar=rstd[:, 0:1], in1=gmod[:],
                                                     op0=ALU.mult, op1=ALU.mult), r=list(xbufs) + [rstd, gmod], w=[tmp])
        P.op("pool", lambda e: e.tensor_tensor(out=hb[:], in0=tmp[:], in1=shift[:], op=ALU.add),
             r=[tmp, shift], w=[hb])
    if part == 0:
        return
    pv = psT.alt if psT_view is None else psT_view
    for k in range(8):
        P.op("pe", lambda e, k=k: e.transpose(out=pv[:, k * 128:(k + 1) * 128], in_=hb[:, k * 128:(k + 1) * 128],
                                              identity=T.ident[:]), r=[hb, T.ident], w=[psT])
    P.op("act", lambda e: e.copy(out=hT_out_ap, in_=pv[:, 0:1024].rearrange("p (k t) -> p k t", k=8)),
         r=[psT], w=hT_bufs)


def emit_post_residual(P, T, osrc, obufs, x_t, gg, xo, ss, rstd):
    T.sumsq_rstd(osrc, obufs, ss, rstd, 1024)
    P.op("dve", lambda e: e.scalar_tensor_tensor(out=xo[:], in0=osrc, scalar=rstd[:, 0:1], in1=gg[:],
                                                 op0=ALU.mult, op1=ALU.mult), r=list(obufs) + [rstd, gg], w=[xo])
    P.op("pool", lambda e: e.tensor_tensor(out=xo[:], in0=xo[:], in1=x_t[:], op=ALU.add),
         r=[xo, x_t], w=[xo])


NT = 2048


def phase_outproj(P, C, io, FC, even, layer):
    m0 = P.mark()
    x, wout, xn, hT = io["x"], io["wout"], io["xn"], io["hT"]
    yT_all = io["yT_all"]
    T = TokCtx(P, C["ident_bf"])
    gg = P.sb("gg", [128, 1024])
    gmod = P.sb("gmod", [128, 1024])
    shift = P.sb("shift", [128, 1024])
    xt = [P.sb(f"xt{i}", [128, 1024]) for i in range(2)]
    setup_mod_rows(P, io["mod_all"], io["ng"], 2 * layer, 2 * layer + 1, layer * 4 + 1, layer * 4 + 2, gg, gmod, shift, xt)
    if even:
        ss_in = P.sb("ss_in", [128, 4, 16])
        rs_ssd = P.sb("rs_ssd", [128, 16])
        dyn_dma(P, ss_in[:], lambda: io["ss_all"].t.rearrange("(h p) c -> p h c", p=128)[:, :, bass.ds(P.qc, 16)],
                r=[io["ss_all"]], w=[ss_in])
        P.op("dve", lambda e: e.tensor_reduce(out=rs_ssd[:], in_=ss_in[:].rearrange("p h t -> p t h"), axis=AX.X,
                                              op=ALU.add), r=[ss_in], w=[rs_ssd])
        P.op("act", lambda e: e.activation(out=rs_ssd[:], in_=rs_ssd[:], func=AF.Sqrt, bias=EPS, scale=1.0 / 1024),
             r=[rs_ssd], w=[rs_ssd])
        P.op("dve", lambda e: e.reciprocal(out=rs_ssd[:], in_=rs_ssd[:]), r=[rs_ssd], w=[rs_ssd])
    w_sb = [P.sb(f"wo{k}", [128, 1024], BF16) for k in range(FC)]
    for k in range(FC):
        P.dma("pool", w_sb[k][:], wout[k * 128:(k + 1) * 128, :], w=[w_sb[k]])
    NGL = io["ngroups"] // 4
    yq = [P.sb(f"yq{i}", [128, 4, NT], BF16) for i in range(NGL)]
    for gl in range(NGL):
        for hf in range(2):
            src = yT_all[2 * gl + hf]
            dyn_dma(P, yq[gl][hf * 64:(hf + 1) * 64, :, :], lambda src=src: src.t.rearrange("(h p) t -> p h t", p=64)[
                :, :, bass.ds(P.qv, NT)], r=[src], w=[yq[gl]])
    xo = [P.sb(f"xo{i}", [128, 1024]) for i in range(2)]
    tmp = [P.sb(f"tmp{i}", [128, 1024]) for i in range(2)]
    hb = [P.sb(f"hb{i}", [128, 1024], BF16) for i in range(2)]
    hTs = [P.sb(f"hTs{i}", [128, 8, 128], BF16) for i in range(2)]
    osb = [P.sb(f"osb{i}", [128, 1024]) for i in range(2)]
    dsb = P.sb("dsb", [128, 1024])
    ss = [P.sb(f"ss{i}", [128, 2]) for i in range(4)]
    rstd = [P.sb(f"rstd{i}", [128, 2]) for i in range(4)]
    psA = [P.ps(f"psA{i}", [128, 1024]) for i in range(2)]
    psB = P.ps("psB", [128, 1024]) if even else None
    psT = [P.ps(f"psT{i}", [128, 1024], BF16) for i in range(2)]
    nA = FC // 2 if even else FC
    pending = []
    for t in range(16):
        q4, s4 = divmod(t, 4)
        gmap = io["rowmap"]
        x_t = xt[t % 2]
        P.dma("sp", x_t[:], x[t * 128:(t + 1) * 128, :], r=[x], w=[x_t])
        pa = psA[t % 2]
        for k in range(nA):
            for hlf in range(2):
                hg_, gl_ = gmap[k]
                yb = yq[gl_]
                P.op("pe", lambda e, k=k, hlf=hlf, pa=pa, yb=yb, hg_=hg_, t=t: e.matmul(
                    pa[:, hlf * 512:(hlf + 1) * 512], lhsT=yb[:, hg_, t * 128:(t + 1) * 128],
                    rhs=w_sb[k][:, hlf * 512:(hlf + 1) * 512], start=(k == 0), stop=(k == nA - 1)),
                    r=[yb, w_sb[k]], w=[pa])
        if even:
            for k in range(nA, FC):
                for hlf in range(2):
                    hg_, gl_ = gmap[k]
                    yb = yq[gl_]
                    P.op("pe", lambda e, k=k, hlf=hlf, yb=yb, hg_=hg_, t=t: e.matmul(
                        psB[:, hlf * 512:(hlf + 1) * 512], lhsT=yb[:, hg_, t * 128:(t + 1) * 128],
                        rhs=w_sb[k][:, hlf * 512:(hlf + 1) * 512], start=(k == nA), stop=(k == FC - 1)),
                        r=[yb, w_sb[k]], w=[psB])
            P.op("act", lambda e: e.copy(out=dsb[:], in_=psB[:]), r=[psB], w=[dsb])
            o_t = osb[t % 2]
            P.op("dve", lambda e, pa=pa, o_t=o_t, t=t: e.scalar_tensor_tensor(
                out=o_t[:], in0=pa[:], scalar=rs_ssd[:, t:t + 1], in1=dsb[:], op0=ALU.mult, op1=ALU.add),
                r=[pa, rs_ssd, dsb], w=[o_t])
            osrc, obufs = o_t[:], [o_t]
        else:
            osrc, obufs = pa[:], [pa]
        while pending:
            pending.pop(0)()
        xo_t = xo[t % 2]
        emit_post_residual(P, T, osrc, obufs, x_t, gg, xo_t, ss[0], rstd[0])
        P.dma("sp", xn[t * 128:(t + 1) * 128, :], xo_t[:], r=[xo_t], w=[xn])
        hts = hTs[t % 2]
        emit_prenorm_T(P, T, xo_t[:], [xo_t], gmod, shift, psT[t % 2], hts[:], [hts], tmp[1], hb[t % 2],
                       ss[1], rstd[1], part=0)

        def fin(t=t, xo_t=xo_t, hts=hts):
            emit_prenorm_T(P, T, xo_t[:], [xo_t], gmod, shift, psT[t % 2], hts[:], [hts], tmp[1], hb[t % 2],
                           ss[1], rstd[1], part=1)
            P.dma("sp", hT[:, t * 128:(t + 1) * 128].rearrange("(k p) t -> p k t", p=128), hts[:], r=[hts], w=[hT])
        pending.append(fin)
    while pending:
        pending.pop(0)()
    P.reset(m0)


def phase_mlp(P, C, io, next_pre, layer):
    m0 = P.mark()
    xn, hT, w1, w2, xo_d = io["xn"], io["hT"], io["w1"], io["w2"], io["xo"]
    hTn = io.get("hTn")
    T = TokCtx(P, C["ident_bf"])
    gg = P.sb("gg", [128, 1024])
    gmod = P.sb("gmod", [128, 1024]) if next_pre else None
    shift = P.sb("shift", [128, 1024]) if next_pre else None
    xt = [P.sb(f"xt{i}", [128, 1024]) for i in range(2)]
    setup_mod_rows(P, io["mod_all"], io["ng"], 2 * layer + 1, 2 * layer + 2, layer * 4 + 3, layer * 4 + 4, gg, gmod, shift, xt)
    w1_sb = [[P.sb(f"w1_{k}_{cb}", [128, 1024], BF16) for cb in range(4)] for k in range(8)]
    w2_sb = [P.sb(f"w2_{k}", [128, 1024], BF16) for k in range(32)]
    for cb in range(4):
        for k in range(8):
            P.dma("pool", w1_sb[k][cb][:], w1[k * 128:(k + 1) * 128, cb * 1024:(cb + 1) * 1024], w=[w1_sb[k][cb]])
    for k in range(32):
        P.dma("pool", w2_sb[k][:], w2[k * 128:(k + 1) * 128, :], w=[w2_sb[k]])
    ST = 256
    ht = [P.sb(f"ht{i}", [128, 8, ST], BF16) for i in range(2)]
    aT = [P.sb(f"aT{i}", [128, 2, ST], BF16) for i in range(16)]
    rl = [P.sb(f"rl{i}", [128, 2, ST]) for i in range(2)]
    xo = [P.sb(f"xo{i}", [128, 1024]) for i in range(2)]
    tmp = [None, P.sb("tmp1", [128, 1024])]
    ss = [P.sb(f"ss{i}", [128, 2]) for i in range(2)]
    rstd = [P.sb(f"rstd{i}", [128, 2]) for i in range(2)]
    psU = [P.ps(f"psU{i}", [128, 2, ST]) for i in range(3)]
    psO = [P.ps(f"psO{i}", [128, 1024]) for i in range(2)]
    psT = P.ps("psT", [128, 1024], BF16)
    if next_pre:
        hb = [P.sb(f"hb{i}", [128, 1024], BF16) for i in range(2)]
        hTs = [P.sb(f"hTs{i}", [128, 8, 128], BF16) for i in range(2)]
    nu = 0
    pending = []
    for st in range(NT // ST):
        h_t = ht[st % 2]
        P.dma("sp", h_t[:], hT[:, st * ST:(st + 1) * ST].rearrange("(k p) t -> p k t", p=128), r=[hT], w=[h_t])
        aTs = aT
        for fp in range(16):
            pu = psU[nu % 3]
            r_t = rl[nu % 2]
            nu += 1
            for j in range(2):
                f = fp * 2 + j
                for k in range(8):
                    wb = w1_sb[k][f // 8]
                    P.op("pe", lambda e, pu=pu, j=j, f=f, k=k, h_t=h_t, wb=wb: e.matmul(
                        pu[:, j, :], lhsT=wb[:, (f % 8) * 128:(f % 8 + 1) * 128], rhs=h_t[:, k, :],
                        start=(k == 0), stop=(k == 7)), r=[wb, h_t], w=[pu])
            P.op("act", lambda e, pu=pu, r_t=r_t: e.activation(out=r_t[:], in_=pu[:], func=AF.Relu),
                 r=[pu], w=[r_t])
            a_t = aTs[fp]
            P.op("dve" if fp % 2 == 0 else "pool", lambda e, r_t=r_t, a_t=a_t: e.tensor_tensor(
                out=a_t[:], in0=r_t[:], in1=r_t[:], op=ALU.mult), r=[r_t], w=[a_t])
        for sub in range(ST // 128):
            t = st * (ST // 128) + sub
            po = psO[t % 2]
            for f in range(32):
                for hlf in range(2):
                    P.op("pe", lambda e, po=po, f=f, hlf=hlf, sub=sub, aTs=aTs: e.matmul(
                        po[:, hlf * 512:(hlf + 1) * 512], lhsT=aTs[f // 2][:, f % 2, sub * 128:(sub + 1) * 128],
                        rhs=w2_sb[f][:, hlf * 512:(hlf + 1) * 512], start=(f == 0), stop=(f == 31)),
                        r=[aTs[f // 2], w2_sb[f]], w=[po])
            while pending:
                pending.pop(0)()
            x_t = xt[t % 2]
            P.dma("sp", x_t[:], xn[t * 128:(t + 1) * 128, :], r=[xn], w=[x_t])
            xo_t = xo[t % 2]
            emit_post_residual(P, T, po[:], [po], x_t, gg, xo_t, ss[0], rstd[0])
            P.dma("sp", xo_d[t * 128:(t + 1) * 128, :], xo_t[:], r=[xo_t], w=[xo_d])
            if next_pre:
                hts = hTs[t % 2]
                emit_prenorm_T(P, T, xo_t[:], [xo_t], gmod, shift, psT, hts[:], [hts], tmp[1], hb[t % 2],
                               ss[1], rstd[1], part=0)

                def fin(t=t, xo_t=xo_t, hts=hts):
                    emit_prenorm_T(P, T, xo_t[:], [xo_t], gmod, shift, psT, hts[:], [hts], tmp[1], hb[t % 2],
                                   ss[1], rstd[1], part=1)
                    for c4 in range(4):
                        P.dma("sp", hTn[c4][:, t * 128:(t + 1) * 128].rearrange("(k p) t -> p k t", p=128),
                              hts[:, 2 * c4:2 * c4 + 2, :], r=[hts], w=[hTn[c4]])
                pending.append(fin)
    while pending:
        pending.pop(0)()
    P.reset(m0)


S_LEN = 8192
NEGV = -30000.0
E_NCOL = 1024 + 516


def host_consts():
    c = {}
    c["ident_bf"] = np.eye(128, dtype=np.float32).astype(ml_dtypes.bfloat16)
    c["ident_f"] = np.eye(128, dtype=np.float32)
    i = np.arange(128)
    c["triU"] = (i[:, None] <= i[None, :]).astype(np.float32)
    c["triS"] = (i[:, None] > i[None, :]).astype(np.float32)
    c["NEG"] = np.where(i[None, :] < i[:, None], NEGV, 0.0).astype(np.float32)
    c["ones_f"] = np.ones((128, 128), np.float32)
    c["ones_bf"] = np.ones((128, 128), np.float32).astype(ml_dtypes.bfloat16)
    c["antiI"] = np.ascontiguousarray(np.eye(128, dtype=np.float32)[::-1])
    return c


def t5_bucket_np(d):
    d = np.maximum(d, 0)
    logd = np.log(np.maximum(d, 1).astype(np.float32) / np.float32(16))
    large = 16 + (logd / np.float32(np.log(128 / 16)) * np.float32(16)).astype(np.int32)
    large = np.minimum(large, 31)
    return np.where(d < 16, d, large)


def host_ohd(swa_prev=False):
    d = np.arange(383) - 127
    oh = np.zeros((33, 384), np.float32)
    oh[t5_bucket_np(d), np.arange(383)] = 1.0
    if swa_prev:
        oh[32, :383] = np.where((d < 0) | (d >= 128), NEGV, 0.0)
    else:
        oh[32, :383] = np.where(d < 0, NEGV, 0.0)
    oh[:32, :383] *= (d >= 0)[None, :]
    return oh


def phase_even(P, C, io, do_ssd=True, do_diff=True, n_super=16):
    m0 = P.mark()
    x, win, convw, convb, hv, normw = io["x_b"], io["e_win"], io["convw"], io["convb"], io["hv"], io["normw"]
    lam, subw, relb, ohd = io["lam"], io["subw"], io["relb"], io["ohdA"]
    yT, ss_out, vecd = io["yT_loc"], io["ss_loc"], io["vecdA"]
    T = TokCtx(P, C["ident_bf"])
    gmod = P.sb("gmod", [128, 1024])
    shift = P.sb("shift", [128, 1024])
    xt = [P.sb(f"xt{i}", [128, 1024]) for i in range(2)]
    load_modrow(P, gmod, io["mod_all"], 0, 1)
    P.dma("sp", xt[0][:], bcast_row(io["ng"][0:1, :]), w=[xt[0]])
    P.op("dve", lambda e: e.scalar_tensor_tensor(out=gmod[:], in0=gmod[:], scalar=1.0, in1=xt[0][:],
                                                 op0=ALU.add, op1=ALU.mult), r=[gmod, xt[0]], w=[gmod])
    load_modrow(P, shift, io["mod_all"], 0, 0)
    w_sb = [P.sb(f"win{k}", [128, E_NCOL], BF16) for k in range(8)]
    for k in range(8):
        P.dma("pool", w_sb[k][:], win[k * 128:(k + 1) * 128, :], w=[w_sb[k]])
    cw = P.sb("cw", [128, 4, 4])
    cb = P.sb("cb", [128, 4])
    nw = P.sb("nw", [128, 2])
    hvb = P.sb("hvb", [128, 3, 4])
    P.dma("sp", cw[:], convw, w=[cw])
    P.dma("sp", cb[:], convb, w=[cb])
    P.dma("sp", nw[:], normw, w=[nw])
    P.dma("sp", hvb[:], bass.AP(hv.t.tensor, hv.t.offset, [[0, 128], [4, 3], [1, 4]]), w=[hvb])
    Aneg = P.sb("Aneg", [128, 4])
    P.op("act", lambda e: e.activation(out=Aneg[:], in_=hvb[:, 1, :], func=AF.Exp), r=[hvb], w=[Aneg])
    P.op("dve", lambda e: e.tensor_scalar(out=Aneg[:], in0=Aneg[:], scalar1=-1.0, scalar2=None, op0=ALU.mult),
         r=[Aneg], w=[Aneg])

    hT = [P.sb(f"hT{i}", [128, 8, 512], BF16) for i in range(2)]
    hb = P.sb("hb", [128, 1024], BF16)
    tmp = P.sb("tmp", [128, 1024])
    ssq = P.sb("ssq", [128, 2])
    rstd = P.sb("rstd", [128, 2])
    pAB = P.ps("pAB", [128, 512])
    pAB_bf = pAB.alt
    bk1 = P.ps("bk1", [128, 512])
    bk2 = P.ps("bk2", [128, 512])
    bk3 = P.ps("bk3", [128, 512])
    pdt = Buf(bk3.t[:, 256:272], "pdt", bank=bk3.bank)
    raw = [P.sb(f"raw{j}", [128, 3 + 512]) for j in range(4)]
    for j in range(4):
        P.op("pool", lambda e, j=j: e.memset(raw[j][:, 0:3], 0.0), w=[raw[j]])
    acc = P.sb("acc", [128, 512])
    cvT = [P.sb(f"cvT{j}", [128, 512], BF16) for j in range(4)]
    z_sb = [P.sb(f"z_sb{i}", [128, 256]) for i in range(4)]
    dtr = [P.sb(f"dtr{i}", [128, 4]) for i in range(4)]
    if do_diff:
        KT = [P.sb(f"KT{h}", [128, S_LEN], BF16) for h in range(2)]
        VV = [P.sb(f"V{h}", [128, 64, 128], BF16) for h in range(2)]
        QT = [[P.sb(f"QT{h}_{m}", [128, 512], BF16) for m in range(2)] for h in range(2)]
        for h in range(2):
            for m in range(2):
                P.op("pool", lambda e, h=h, m=m: e.memset(QT[h][m][:], 0.0), w=[QT[h][m]])
    if do_ssd:
        S_f = P.sb("S_f", [128, 256])
        S_b = P.sb("S_b", [128, 256], BF16)
        P.op("pool", lambda e: e.memset(S_f[:], 0.0), w=[S_f])
        P.op("pool", lambda e: e.memset(S_b[:], 0.0), w=[S_b])
        ss_all = P.sb("ss_all", [128, 64])
        p_bc = Buf(bk1.t[:, 0:256].rearrange("p (a b) -> p a b", a=2), "p_bc", bank=bk1.bank)
        p_y = Buf(bk1.t[:, 256:512], "p_y", bank=bk1.bank)
        p_cbxt = Buf(bk2.t[:, 0:256], "p_cbxt", bank=bk2.bank)
        p_S = Buf(bk2.t[:, 256:512], "p_S", bank=bk2.bank)
        p_sm = Buf(bk3.t[:, 0:256], "p_sm", bank=bk3.bank)
        p_cb = p_cbxt
        p_xt = bk2.alt
        p_smb = bk3.alt
        W = {}
        for nme, shp, dt in [("dt", [128, 4], F32), ("a", [128, 4], F32), ("nacs", [128, 4], F32),
                             ("E", [128, 4], F32), ("dte", [128, 4], F32), ("dec", [128, 4], F32),
                             ("trr", [128, 2, 128], F32), ("decay", [128, 4, 128], F32), ("cbs", [128, 128], F32),
                             ("M", [128, 4, 128], BF16), ("xs_tok", [128, 256], F32), ("X", [128, 256], BF16),
                             ("Xd", [128, 256], BF16), ("B_tok", [128, 128], BF16), ("yo", [128, 256], F32),
                             ("y", [128, 256], F32), ("sz", [128, 256], F32), ("y_bf", [128, 256], BF16),
                             ("yTs", [128, 2, 128], BF16), ("Dbc", [128, 4, 64], F32)]:
            W[nme] = P.sb("w_" + nme, shp, dt)
        for r_ in range(4):
            P.op("act", lambda e, r_=r_: e.activation(out=W["Dbc"][:, r_, :], in_=C["ones_f"][:, 0:64], func=AF.Copy,
                                                      scale=hvb[:, 2, r_:r_ + 1]), r=[C["ones_f"], hvb], w=[W["Dbc"]])

    def ssd_chunk(c, st, sub):
        cs = slice(sub * 128, (sub + 1) * 128)
        zt, dt_raw = z_sb[sub], dtr[sub]
        P.op("dve", lambda e: e.tensor_tensor(out=W["dt"][:], in0=dt_raw[:], in1=hvb[:, 0, :], op=ALU.add),
             r=[dt_raw, hvb], w=[W["dt"]])
        P.op("act", lambda e: e.activation(out=W["dt"][:], in_=W["dt"][:], func=AF.Exp), r=[W["dt"]], w=[W["dt"]])
        P.op("act", lambda e: e.activation(out=W["dt"][:], in_=W["dt"][:], func=AF.Ln, bias=1.0, scale=1.0),
             r=[W["dt"]], w=[W["dt"]])
        P.op("dve", lambda e: e.tensor_tensor(out=W["a"][:], in0=W["dt"][:], in1=Aneg[:], op=ALU.mult),
             r=[W["dt"], Aneg], w=[W["a"]])
        yield
        P.op("pe", lambda e: e.matmul(p_sm[:, 0:4], lhsT=C["triU"][:], rhs=W["a"][:], start=True, stop=True),
             r=[C["triU"], W["a"]], w=[p_sm])
        P.op("pe", lambda e: e.matmul(p_sm[:, 4:8], lhsT=C["triS"][:], rhs=W["a"][:], start=True, stop=True),
             r=[C["triS"], W["a"]], w=[p_sm])
        P.op("dve", lambda e: e.tensor_scalar(out=W["nacs"][:], in0=p_sm[:, 0:4], scalar1=-1.0, scalar2=None,
                                              op0=ALU.mult), r=[p_sm], w=[W["nacs"]])
        P.op("act", lambda e: e.activation(out=W["E"][:], in_=p_sm[:, 0:4], func=AF.Exp), r=[p_sm], w=[W["E"]])
        P.op("act", lambda e: e.activation(out=W["dte"][:], in_=p_sm[:, 4:8], func=AF.Exp), r=[p_sm], w=[W["dte"]])
        P.op("dve", lambda e: e.tensor_tensor(out=W["dec"][:], in0=W["E"][:], in1=W["dte"][:], op=ALU.mult),
             r=[W["E"], W["dte"]], w=[W["dec"]])
        yield
        for half in range(2):
            yield
            for rr in range(2):
                r_ = half * 2 + rr
                P.op("dve" if rr == 0 else "pool", lambda e, r_=r_, rr=rr: e.tensor_scalar(
                    out=W["trr"][:, rr, :], in0=C["triU"][:], scalar1=W["a"][:, r_:r_ + 1], scalar2=None,
                    op0=ALU.mult), r=[C["triU"], W["a"]], w=[W["trr"]])
            yield
            for rr in range(2):
                P.op("pe", lambda e, rr=rr: e.matmul(p_bc[:, rr, :], lhsT=C["ones_f"][:], rhs=W["trr"][:, rr, :],
                                                     start=True, stop=False), r=[C["ones_f"], W["trr"]], w=[p_bc])
                P.op("pe", lambda e, rr=rr: e.matmul(p_bc[:, rr, :], lhsT=C["ident_f"][:], rhs=C["NEG"][:],
                                                     start=False, stop=True), r=[C["ident_f"], C["NEG"]], w=[p_bc])
            for rr in range(2):
                r_ = half * 2 + rr
                P.op("act", lambda e, r_=r_, rr=rr: e.activation(
                    out=W["decay"][:, r_, :], in_=p_bc[:, rr, :], func=AF.Exp, bias=W["nacs"][:, r_:r_ + 1],
                    scale=1.0), r=[p_bc, W["nacs"]], w=[W["decay"]])
        yield
        P.op("pe", lambda e: e.matmul(p_cb[:, 0:128], lhsT=cvT[2][:, cs], rhs=cvT[3][:, cs], start=True, stop=True),
             r=[cvT[2], cvT[3]], w=[p_cbxt])
        P.op("act", lambda e: e.copy(out=W["cbs"][:], in_=p_cb[:, 0:128]), r=[p_cbxt], w=[W["cbs"]])
        for r_ in range(4):
            P.op("dve" if r_ % 2 == 0 else "pool", lambda e, r_=r_: e.tensor_tensor(
                out=W["M"][:, r_, :], in0=W["decay"][:, r_, :], in1=W["cbs"][:], op=ALU.mult),
                r=[W["decay"], W["cbs"]], w=[W["M"]])
        yield
        for j in range(2):
            P.op("pe", lambda e, j=j: e.transpose(out=p_xt[:, 256 + j * 128:256 + (j + 1) * 128], in_=cvT[j][:, cs],
                                                  identity=C["ident_bf"][:]), r=[cvT[j], C["ident_bf"]], w=[p_cbxt])
        P.op("pe", lambda e: e.transpose(out=p_smb[:, 128:256], in_=cvT[2][:, cs], identity=C["ident_bf"][:]),
             r=[cvT[2], C["ident_bf"]], w=[p_sm])
        P.op("act", lambda e: e.copy(out=W["xs_tok"][:], in_=p_xt[:, 256:512]), r=[p_cbxt], w=[W["xs_tok"]])
        P.op("act", lambda e: e.copy(out=W["B_tok"][:], in_=p_smb[:, 128:256]), r=[p_sm], w=[W["B_tok"]])
        for r_ in range(4):
            hs = slice(r_ * 64, (r_ + 1) * 64)
            P.op("dve", lambda e, r_=r_, hs=hs: e.tensor_scalar(
                out=W["X"][:, hs], in0=W["xs_tok"][:, hs], scalar1=W["dt"][:, r_:r_ + 1], scalar2=None,
                op0=ALU.mult), r=[W["xs_tok"], W["dt"]], w=[W["X"]])
            P.op("pool", lambda e, r_=r_, hs=hs: e.tensor_scalar(
                out=W["Xd"][:, hs], in0=W["X"][:, hs], scalar1=W["dte"][:, r_:r_ + 1], scalar2=None,
                op0=ALU.mult), r=[W["X"], W["dte"]], w=[W["Xd"]])
        yield
        P.op("pe", lambda e: e.matmul(p_y[:], lhsT=cvT[3][:, cs], rhs=S_b[:], start=True, stop=True),
             r=[cvT[3], S_b], w=[p_y])
        for r_ in range(4):
            hs = slice(r_ * 64, (r_ + 1) * 64)
            P.op("act", lambda e, r_=r_, hs=hs: e.activation(out=W["yo"][:, hs], in_=p_y[:, hs], func=AF.Copy,
                                                             scale=W["E"][:, r_:r_ + 1]), r=[p_y, W["E"]], w=[W["yo"]])
        yield
        for r_ in range(4):
            hs = slice(r_ * 64, (r_ + 1) * 64)
            P.op("pe", lambda e, r_=r_, hs=hs: e.matmul(p_y[:, hs], lhsT=W["M"][:, r_, :], rhs=W["X"][:, hs],
                                                        start=True, stop=True), r=[W["M"], W["X"]], w=[p_y])
        P.op("dve", lambda e: e.tensor_tensor(out=W["y"][:], in0=p_y[:], in1=W["yo"][:], op=ALU.add),
             r=[p_y, W["yo"]], w=[W["y"]])
        P.op("pool", lambda e: e.tensor_tensor(out=W["yo"][:], in0=W["xs_tok"][:],
                                               in1=W["Dbc"][:].rearrange("p a b -> p (a b)"), op=ALU.mult),
             r=[W["xs_tok"], W["Dbc"]], w=[W["yo"]])
        P.op("dve", lambda e: e.tensor_tensor(out=W["y"][:], in0=W["y"][:], in1=W["yo"][:], op=ALU.add),
             r=[W["y"], W["yo"]], w=[W["y"]])
        yield
        P.op("act", lambda e: e.activation(out=W["sz"][:], in_=zt[:], func=AF.Silu), r=[zt], w=[W["sz"]])
        P.op("dve", lambda e: e.tensor_tensor(out=W["y"][:], in0=W["y"][:], in1=W["sz"][:], op=ALU.mult),
             r=[W["y"], W["sz"]], w=[W["y"]])
        P.op("act", lambda e: e.activation(out=W["sz"][:], in_=W["y"][:], func=AF.Square,
                                           accum_out=ss_all[:, c:c + 1]), r=[W["y"]], w=[W["sz"], ss_all])
        P.op("pool", lambda e: e.tensor_copy(out=W["y_bf"][:], in_=W["y"][:]), r=[W["y"]], w=[W["y_bf"]])
        yield
        yield
        for j in range(2):
            P.op("pe", lambda e, j=j: e.transpose(out=p_smb[:, 256 + j * 128:256 + (j + 1) * 128],
                                                  in_=W["y_bf"][:, j * 128:(j + 1) * 128],
                                                  identity=C["ident_bf"][:]), r=[W["y_bf"], C["ident_bf"]], w=[p_sm])
        for j in range(2):
            P.op("act", lambda e, j=j: e.activation(out=W["yTs"][:, j, :], in_=p_smb[:, 256 + j * 128:256 + (j + 1) * 128],
                                                    func=AF.Copy, scale=nw[:, j:j + 1]), r=[p_sm, nw], w=[W["yTs"]])
        for j in range(2):
            for hf in range(2):
                ch = yT[2 * j + hf]
                P.dma("sp", ch[0:64, c * 128:(c + 1) * 128], W["yTs"][hf * 64:(hf + 1) * 64, j, :], r=[W["yTs"]], w=[ch])
        yield
        P.op("pe", lambda e: e.matmul(p_S[:], lhsT=W["B_tok"][:], rhs=W["Xd"][:], start=True, stop=True),
             r=[W["B_tok"], W["Xd"]], w=[p_S])
        for r_ in range(4):
            hs = slice(r_ * 64, (r_ + 1) * 64)
            P.op("dve", lambda e, r_=r_, hs=hs: e.scalar_tensor_tensor(
                out=S_f[:, hs], in0=S_f[:, hs], scalar=W["dec"][:, r_:r_ + 1], in1=p_S[:, hs],
                op0=ALU.mult, op1=ALU.add), r=[S_f, W["dec"], p_S], w=[S_f])
        P.op("pool", lambda e: e.tensor_copy(out=S_b[:], in_=S_f[:]), r=[S_f], w=[S_b])
        yield

    diff = DiffAttn(P, C, yT, lam, subw, relb, ohd, vecd) if do_diff else None

    def prenorm_gen(st):
        h_t = hT[st % 2]
        for sub in range(4):
            t = st * 4 + sub
            x_t = xt[t % 2]
            P.dma("sp", x_t[:], x[t * 128:(t + 1) * 128, :], w=[x_t])
            emit_prenorm_T(P, T, x_t[:], [x_t], gmod, shift, pAB, h_t[:, :, sub * 128:(sub + 1) * 128], [h_t],
                           tmp, hb, ssq, rstd, psT_view=pAB_bf, part=0)
            yield
            yield
            yield
            emit_prenorm_T(P, T, x_t[:], [x_t], gmod, shift, pAB, h_t[:, :, sub * 128:(sub + 1) * 128], [h_t],
                           tmp, hb, ssq, rstd, psT_view=pAB_bf, part=1)
            yield

    def ssd_gen(st):
        for sub in range(4):
            yield from ssd_chunk(st * 4 + sub, st, sub)

    def diff_gen(st):
        for h_ in range(2):
            yield from diff.superblock(st, h_, KT[h_], VV[h_], QT[h_])

    pin = [pAB, bk1]
    for _ in prenorm_gen(0):
        pass
    for st in range(n_super):
        h_t = hT[st % 2]
        for j in range(8):
            pb = pin[j % 2]
            for k in range(8):
                P.op("pe", lambda e, pb=pb, j=j, k=k, h_t=h_t: e.matmul(
                    pb[:], lhsT=w_sb[k][:, j * 128:(j + 1) * 128], rhs=h_t[:, k, :], start=(k == 0), stop=(k == 7)),
                    r=[w_sb[k], h_t], w=[pb])
            if j < 4:
                if do_ssd:
                    rw = raw[j]
                    if st > 0:
                        P.op("dve", lambda e, pb=pb, rw=rw: e.tensor_copy(out=rw[:, 0:3], in_=rw[:, 512:515]), r=[rw], w=[rw])
                    P.op("act", lambda e, pb=pb, rw=rw: e.copy(out=rw[:, 3:515], in_=pb[:]), r=[pb], w=[rw])
                    P.op("dve", lambda e, pb=pb, rw=rw, j=j: e.tensor_scalar(
                        out=acc[:], in0=rw[:, 3:515], scalar1=cw[:, j, 3:4], scalar2=cb[:, j:j + 1],
                        op0=ALU.mult, op1=ALU.add), r=[rw, cw, cb], w=[acc])
                    for tap in (2, 1, 0):
                        P.op("dve", lambda e, pb=pb, rw=rw, j=j, tap=tap: e.scalar_tensor_tensor(
                            out=acc[:], in0=rw[:, tap:tap + 512], scalar=cw[:, j, tap:tap + 1], in1=acc[:],
                            op0=ALU.mult, op1=ALU.add), r=[rw, cw, acc], w=[acc])
                    P.op("act", lambda e, pb=pb, j=j: e.activation(out=cvT[j][:], in_=acc[:], func=AF.Silu),
                         r=[acc], w=[cvT[j]])
            elif do_diff and not os.environ.get('DIFF_NOCOPY'):
                h_ = (j - 4) % 2
                if os.environ.get('DIFF_SKIP', '') .count('q' if j < 6 else 'k'):
                    pass
                elif j < 6:
                    P.op("act", lambda e, pb=pb, h_=h_: e.copy(out=QT[h_][0][0:64, :], in_=pb[0:64, :]), r=[pb], w=[QT[h_][0]])
                    P.op("dve", lambda e, pb=pb, h_=h_: e.tensor_copy(out=QT[h_][1][64:128, :], in_=pb[64:128, :]),
                         r=[pb], w=[QT[h_][1]])
                else:
                    P.op("act", lambda e, pb=pb, h_=h_, st=st: e.copy(out=KT[h_][:, st * 512:(st + 1) * 512], in_=pb[:]),
                         r=[pb], w=[KT[h_]])
        for sub in range(4):
            t = st * 4 + sub
            ts_ = slice(sub * 128, (sub + 1) * 128)
            pb = pin[sub % 2]
            for k in range(8):
                P.op("pe", lambda e, pb=pb, k=k, h_t=h_t, ts_=ts_: e.matmul(
                    pb[:], lhsT=h_t[:, k, ts_], rhs=w_sb[k][:, 1024:1536], start=(k == 0), stop=(k == 7)),
                    r=[h_t, w_sb[k]], w=[pb])
            for k in range(8):
                P.op("pe", lambda e, pb=pb, k=k, h_t=h_t, ts_=ts_: e.matmul(
                    pdt[:, 0:4], lhsT=h_t[:, k, ts_], rhs=w_sb[k][:, 1536:1540], start=(k == 0), stop=(k == 7)),
                    r=[h_t, w_sb[k]], w=[pdt])
            if do_ssd:
                P.op("act", lambda e, pb=pb, sub=sub: e.copy(out=z_sb[sub][:], in_=pb[:, 0:256]), r=[pb], w=[z_sb[sub]])
                P.op("dve", lambda e, pb=pb, sub=sub: e.tensor_copy(out=dtr[sub][:], in_=pdt[:, 0:4]), r=[pdt], w=[dtr[sub]])
            if do_diff and not os.environ.get('DIFF_NOCOPY') and not os.environ.get('DIFF_SKIP', '').count('v'):
                P.op("dve", lambda e, pb=pb, t=t: e.tensor_copy(out=VV[0][:, t, :], in_=pb[:, 256:384]), r=[pb], w=[VV[0]])
                P.op("act", lambda e, pb=pb, t=t: e.copy(out=VV[1][:, t, :], in_=pb[:, 384:512]), r=[pb], w=[VV[1]])
        gens = []
        if do_ssd:
            gens.append((ssd_gen(st), 4 * 15))
        if do_diff:
            gens.append((diff_gen(st), 2 * (2 * (4 * st + 4) + 2)))
        if st + 1 < n_super:
            gens.append((prenorm_gen(st + 1), 16))
        interleave(gens)
    if do_ssd:
        P.dma("sp", ss_out[:], ss_all[:], r=[ss_all], w=[ss_out])
    P.reset(m0)


def make_bias_tiles(P, C, relb, ohd, vecd, ps, tag):
    relb_sb = P.sb("relb_sb" + tag, [33, 2])
    ohd_sb = P.sb("ohd_sb" + tag, [33, 384])
    P.dma("sp", relb_sb[:], relb, w=[relb_sb])
    P.dma("sp", ohd_sb[:], ohd, w=[ohd_sb])
    vec_sb = P.sb("vec_sb" + tag, [2, 384])
    P.op("pe", lambda e: e.matmul(ps[0:2, 0:384], lhsT=relb_sb[:], rhs=ohd_sb[:], start=True, stop=True),
         r=[relb_sb, ohd_sb], w=[ps])
    P.op("act", lambda e: e.copy(out=vec_sb[:], in_=ps[0:2, 0:384]), r=[ps], w=[vec_sb])
    P.dma("sp", vecd[:], vec_sb[:], r=[vec_sb], w=[vecd])
    Bd, Bp, c31 = [], [], []
    vt = vecd.t
    hk = P.sb("hankel" + tag, [128, 128])
    for h in range(2):
        bd = P.sb(f"Bd{tag}{h}", [128, 128])
        bp = P.sb(f"Bp{tag}{h}", [128, 128])
        c3 = P.sb(f"c31{tag}_{h}", [128, 1])
        for dst, base in ((bd, 0), (bp, 128)):
            P.dma("sp", hk[:], bass.AP(vt.tensor, vt.offset + h * 384 + base, [[1, 128], [1, 128]]), r=[vecd], w=[hk])
            P.op("pe", lambda e: e.matmul(ps[:, 0:128], lhsT=C["antiI"][:], rhs=hk[:], start=True, stop=True),
                 r=[C["antiI"], hk], w=[ps])
            P.op("act", lambda e, dst=dst: e.copy(out=dst[:], in_=ps[:, 0:128]), r=[ps], w=[dst])
        P.dma("sp", c3[:], bass.AP(vt.tensor, vt.offset + h * 384 + 382, [[0, 128], [1, 1]]), r=[vecd], w=[c3])
        Bd.append(bd)
        Bp.append(bp)
        c31.append(c3)
    return Bd, Bp, c31


class DiffAttn:
    def __init__(self, P, C, yT, lam, subw, relb, ohd, vecd, row0=256):
        self.P, self.C, self.yT, self.row0 = P, C, yT, row0
        self.ps_s = [P.ps(f"ps_s{i}", [128, 512]) for i in range(2)]
        self.ps_o = P.ps("ps_o", [128, 512])
        self.ps_l = P.ps("ps_l", [128, 512])
        self.Bd, self.Bp, self.c31 = make_bias_tiles(P, C, relb, ohd, vecd, self.ps_l, "A")
        lam_sb = P.sb("lam_sb", [128, 256])
        P.dma("sp", lam_sb[:], bcast_row(lam), w=[lam_sb])
        pr = P.sb("lam_pr", [128, 2, 64])
        sm = P.sb("lam_sm", [128, 2])
        self.neg_lam = P.sb("neg_lam", [128, 1])
        P.op("dve", lambda e: e.tensor_tensor(out=pr[:, 0, :], in0=lam_sb[:, 0:64], in1=lam_sb[:, 64:128], op=ALU.mult),
             r=[lam_sb], w=[pr])
        P.op("dve", lambda e: e.tensor_tensor(out=pr[:, 1, :], in0=lam_sb[:, 128:192], in1=lam_sb[:, 192:256], op=ALU.mult),
             r=[lam_sb, pr], w=[pr])
        P.op("dve", lambda e: e.tensor_reduce(out=sm[:], in_=pr[:], axis=AX.X, op=ALU.add), r=[pr], w=[sm])
        P.op("act", lambda e: e.activation(out=sm[:], in_=sm[:], func=AF.Exp), r=[sm], w=[sm])
        P.op("dve", lambda e: e.tensor_tensor(out=self.neg_lam[:], in0=sm[:, 1:2], in1=sm[:, 0:1], op=ALU.subtract),
             r=[sm], w=[self.neg_lam])
        P.op("dve", lambda e: e.tensor_scalar(out=self.neg_lam[:], in0=self.neg_lam[:], scalar1=-0.2, scalar2=None,
                                              op0=ALU.add), r=[self.neg_lam], w=[self.neg_lam])
        self.subs = P.sb("subs", [128, 1])
        P.dma("sp", self.subs[:], subw, w=[self.subs])
        P.op("dve", lambda e: e.tensor_scalar(out=self.subs[:], in0=self.subs[:], scalar1=0.8, scalar2=None,
                                              op0=ALU.mult), r=[self.subs], w=[self.subs])
        self.PT = [P.sb(f"PT{i}", [128, 512], BF16) for i in range(4)]
        self.tS = [P.sb(f"tS{i}", [128, 512]) for i in range(2)]
        self.Tm = [P.sb(f"Tm{i}", [128, 512]) for i in range(2)]
        self.Rr = P.sb("Rr", [128, 512])
        self.sq = P.sb("sq", [128, 512])
        self.yd = [P.sb(f"yd{i}", [128, 512], BF16) for i in range(2)]
        self.n = 0
        self.nn = 0
        self.ny = 0

    def superblock(self, Q, h, KT, V, QT):
        import os
        stage = int(os.environ.get("DIFF_STAGE", "3"))
        if stage == 0:
            return
        yield
        P, C = self.P, self.C
        ps_o, ps_l = self.ps_o, self.ps_l
        for m in range(2):
            ms = slice(m * 64, (m + 1) * 64)

            def stage_a(kb, m=m, ms=ms):
                j0 = max(0, kb - 4 * Q)
                c0 = j0 * 128
                ps = self.ps_s[self.n % 2]
                pt = self.PT[self.n % 4]
                self.n += 1
                P.op("pe", lambda e, ps=ps, kb=kb, c0=c0, m=m: e.matmul(
                    ps[:, c0:512], lhsT=KT[:, kb * 128:(kb + 1) * 128], rhs=QT[m][:, c0:512], start=True, stop=True),
                    r=[KT, QT[m]], w=[ps])
                fj = max(j0, kb + 2 - 4 * Q)
                for j in range(j0, min(4, fj)):
                    bt = self.Bd[h] if 4 * Q + j == kb else self.Bp[h]
                    ts = self.tS[self.nn % 2]
                    self.nn += 1
                    cs = slice(j * 128, (j + 1) * 128)
                    P.op("dve", lambda e, ps=ps, ts=ts, bt=bt, cs=cs: e.scalar_tensor_tensor(
                        out=ts[:, cs], in0=ps[:, cs], scalar=0.125, in1=bt[:], op0=ALU.mult, op1=ALU.add),
                        r=[ps, bt], w=[ts])
                    P.op("act", lambda e, ts=ts, pt=pt, cs=cs: e.activation(out=pt[:, cs], in_=ts[:, cs], func=AF.Exp),
                         r=[ts], w=[pt])
                if fj < 4:
                    fs = slice(fj * 128, 512)
                    P.op("act", lambda e, ps=ps, pt=pt, fs=fs: e.activation(
                        out=pt[:, fs], in_=ps[:, fs], func=AF.Exp, bias=self.c31[h][:, 0:1], scale=0.125),
                        r=[ps, self.c31[h]], w=[pt])
                return pt

            def stage_b(kb, pt):
                j0 = max(0, kb - 4 * Q)
                if kb <= 4 * Q:
                    P.op("pe", lambda e, kb=kb, pt=pt: e.matmul(ps_o[:], lhsT=V[:, kb, :], rhs=pt[:],
                                                                start=(kb == 0), stop=False), r=[V, pt], w=[ps_o])
                    P.op("pe", lambda e, kb=kb, pt=pt: e.matmul(ps_l[:], lhsT=C["ones_bf"][:], rhs=pt[:],
                                                                start=(kb == 0), stop=False), r=[C["ones_bf"], pt], w=[ps_l])
                else:
                    for j in range(j0, 4):
                        cs = slice(j * 128, (j + 1) * 128)
                        last = (kb == 4 * Q + j)
                        P.op("pe", lambda e, kb=kb, pt=pt, cs=cs, last=last: e.matmul(
                            ps_o[:, cs], lhsT=V[:, kb, :], rhs=pt[:, cs], start=(kb == 0), stop=last), r=[V, pt], w=[ps_o])
                        P.op("pe", lambda e, kb=kb, pt=pt, cs=cs, last=last: e.matmul(
                            ps_l[:, cs], lhsT=C["ones_bf"][:], rhs=pt[:, cs], start=(kb == 0), stop=last),
                            r=[C["ones_bf"], pt], w=[ps_l])

            nkb = 4 * Q + 4
            LAG = 2
            pend = []
            for kb in range(nkb):
                pend.append((kb, stage_a(kb)))
                yield
                if len(pend) > LAG:
                    stage_b(*pend.pop(0))
            while pend:
                yield
                stage_b(*pend.pop(0))
            if stage < 3:
                continue
            tm = self.Tm[m]
            P.op("dve", lambda e: e.reciprocal(out=self.Rr[:], in_=ps_l[:]), r=[ps_l], w=[self.Rr])
            P.op("dve", lambda e, tm=tm: e.tensor_tensor(out=tm[:], in0=ps_o[:], in1=self.Rr[:], op=ALU.mult),
                 r=[ps_o, self.Rr], w=[tm])
        yield
        if stage < 3:
            return
        t1, t2 = self.Tm
        P.op("dve", lambda e: e.scalar_tensor_tensor(out=t1[:], in0=t2[:], scalar=self.neg_lam[:, 0:1], in1=t1[:],
                                                     op0=ALU.mult, op1=ALU.add), r=[t1, t2, self.neg_lam], w=[t1])
        P.op("pool", lambda e: e.tensor_tensor(out=self.sq[:], in0=t1[:], in1=t1[:], op=ALU.mult), r=[t1], w=[self.sq])
        P.op("pe", lambda e: e.matmul(ps_l[:], lhsT=C["ones_f"][:], rhs=self.sq[:], start=True, stop=True),
             r=[C["ones_f"], self.sq], w=[ps_l])
        P.op("act", lambda e: e.activation(out=self.Rr[:], in_=ps_l[:], func=AF.Sqrt, bias=EPS, scale=1.0 / 128),
             r=[ps_l], w=[self.Rr])
        P.op("dve", lambda e: e.reciprocal(out=self.Rr[:], in_=self.Rr[:]), r=[self.Rr], w=[self.Rr])
        yd = self.yd[self.ny % 2]
        self.ny += 1
        P.op("dve", lambda e, yd=yd: e.scalar_tensor_tensor(out=yd[:], in0=t1[:], scalar=self.subs[:, 0:1], in1=self.Rr[:],
                                                            op0=ALU.mult, op1=ALU.mult), r=[t1, self.subs, self.Rr], w=[yd])
        for hf in range(2):
            ch = self.yT[4 + 2 * h + hf]
            P.dma("sp", ch[0:64, Q * 512:(Q + 1) * 512], yd[hf * 64:(hf + 1) * 64, :], r=[yd], w=[ch])


def split3(v):
    return v[0:1024], v[1024:2048], v[2048:3072]


def even_inputs(z, mod, b, hg):
    wi = z["e_w_in"][0]
    cols = np.concatenate([
        np.arange(1024 + hg * 256, 1024 + hg * 256 + 256),
        np.arange(2048 + hg * 128, 2048 + hg * 128 + 128),
        np.arange(2560 + hg * 128, 2560 + hg * 128 + 128),
        np.arange(3088 + hg * 256, 3088 + hg * 256 + 256),
        np.arange(4112 + hg * 256, 4112 + hg * 256 + 256),
        np.arange(hg * 256, hg * 256 + 256),
        np.arange(5136 + hg * 256, 5136 + hg * 256 + 256),
        np.arange(3072 + hg * 4, 3072 + hg * 4 + 4),
    ])
    ch = cols[0:512] - 1024
    cw = z["e_conv_w"][0][:, ch]
    cb = z["e_conv_b"][0][ch]
    im = {
        "win": np.ascontiguousarray(wi[:, cols]),
        "convw": np.ascontiguousarray(cw.reshape(4, 4, 128).transpose(2, 1, 0)),
        "convb": np.ascontiguousarray(cb.reshape(4, 128).T),
        "hv": np.stack([z["e_dt_bias"][0][hg * 4:hg * 4 + 4], z["e_A_log"][0][hg * 4:hg * 4 + 4],
                        z["e_D"][0][hg * 4:hg * 4 + 4]]).astype(np.float32),
        "normw": np.ascontiguousarray(z["e_ssd_norm"][0][hg * 256:hg * 256 + 256].reshape(2, 128).T),
        "lam": np.ascontiguousarray(z["e_lambda"][0].reshape(1, 256)),
        "subw": np.ascontiguousarray(z["e_diff_norm"][0].reshape(128, 1)),
        "relb": np.concatenate([z["rel_bias"][:, 2 * hg:2 * hg + 2], np.ones((1, 2), np.float32)], 0),
        "ohd": host_ohd(),
    }
    im.update(host_consts())
    return im


O_NCOL = 256 + 512 + 320


def host_rope_tables(hg):
    gamma = 1.0 - 2.0 ** (-5.0 - hg)
    pos = np.arange(S_LEN, dtype=np.float32)
    inv = (np.float32(10000.0) ** (-np.arange(64, dtype=np.float32) / np.float32(64))).astype(np.float32)
    ang = (pos[:, None] * inv[None]).astype(np.float32).astype(np.float64)
    cos, sin = np.cos(ang), np.sin(ang)
    l = (np.arange(S_LEN) % 128).astype(np.float64)
    fq = (gamma ** l)[:, None]
    fk = (gamma ** (-l))[:, None] * 128.0 ** -0.5
    tab = np.stack([
        np.concatenate([cos, cos], 1) * fq, np.concatenate([-sin, sin], 1) * fq,
        np.concatenate([cos, cos], 1) * fk, np.concatenate([-sin, sin], 1) * fk]).astype(np.float32)
    gv = np.zeros((128, 2), np.float32)
    gv[:, 0] = gamma ** 128
    return tab, gv


def phase_odd(P, C, io, n_super=16):
    m0 = P.mark()
    hT_all, win, tab, gv, sinks = io["hT2_all"], io["o_win"], io["tab"], io["gv"], io["sinks"]
    relb, ohdA, ohdB = io["relb"], io["ohdA"], io["ohdB"]
    yT, vecdA, vecdB = io["yT2_loc"], io["vecdA"], io["vecdB"]
    T = TokCtx(P, C["ident_bf"])
    w_sb = [P.sb(f"win{k}", [128, O_NCOL], BF16) for k in range(8)]
    for k in range(8):
        P.dma("pool", w_sb[k][:], win[k * 128:(k + 1) * 128, :], w=[w_sb[k]])
    gv_sb = P.sb("gv_sb", [128, 2])
    P.dma("sp", gv_sb[:], gv, w=[gv_sb])
    es = P.sb("es", [128, 2])
    P.dma("sp", es[:], bcast_row(sinks), w=[es])
    P.op("act", lambda e: e.activation(out=es[:], in_=es[:], func=AF.Exp), r=[es], w=[es])
    pFM = P.ps("pFM", [128, 512])
    pT1 = P.ps("pT1", [128, 512])
    pT2 = P.ps("pT2", [128, 512])
    pSc = P.ps("pSc", [128, 512])
    pRO = P.ps("pRO", [128, 512])
    pTr = P.ps("pTr", [128, 512])
    pSW = P.ps("pSW", [128, 512])
    pOL = P.ps("pOL", [128, 512])
    pTr_bf = pTr.alt
    BdA, _, _ = make_bias_tiles(P, C, relb, ohdA, vecdA, pOL, "oA")
    _, BpB, _ = make_bias_tiles(P, C, relb, ohdB, vecdB, pOL, "oB")
    Bpd = []
    for h in range(2):
        t = P.sb(f"Bpd{h}", [128, 2, 128])
        P.op("dve", lambda e, t=t, h=h: e.tensor_copy(out=t[:, 0, :], in_=BpB[h][:]), r=[BpB[h]], w=[t])
        P.op("dve", lambda e, t=t, h=h: e.tensor_copy(out=t[:, 1, :], in_=BdA[h][:]), r=[BdA[h], t], w=[t])
        Bpd.append(t)
    hT = [P.sb(f"hT{i}", [128, 8, 512], BF16) for i in range(2)]
    tb = [P.sb(f"tb{i}", [128, 4, 4, 128]) for i in range(2)]
    SQT = P.sb("SQT", [128, 512], BF16)
    SKT = P.sb("SKT", [128, 128 + 512], BF16)
    SV = P.sb("SV", [128, 5, 64], BF16)
    qkv = [P.sb(f"qkv{i}", [128, 256]) for i in range(4)]
    v_bf = [P.sb(f"v_bf{i}", [128, 256], BF16) for i in range(4)]
    sg = [P.sb(f"sg{i}", [128, 256]) for i in range(4)]
    Wk = {}
    for nme, shp, dt in [("A", [128, 128], F32), ("B", [128, 128], F32), ("Qp", [128, 128], BF16),
                         ("Kp", [128, 128], BF16), ("QT", [128, 128], BF16), ("KT", [128, 128], BF16),
                         ("Sm", [128, 128], BF16), ("y_bf", [128, 256], BF16), ("yTs", [128, 2, 128], BF16),
                         ("ts", [128, 2, 128], F32), ("PT", [128, 2, 128], BF16), ("den", [64, 128], F32),
                         ("ob", [64, 128], BF16), ("ss", [128, 2], F32), ("rstd", [128, 2], F32)]:
        Wk[nme] = P.sb("k_" + nme, shp, dt)
    St = P.sb("St", [128, 256])
    gS = P.sb("gS", [128, 256], BF16)
    P.op("pool", lambda e: e.memset(St[:], 0.0), w=[St])
    P.op("pool", lambda e: e.memset(gS[:], 0.0), w=[gS])

    def ret_chunk(c, sub, tbs, qk, vb, sgt):
        for which, (col0, tq, dst) in enumerate(((0, 0, "Qp"), (128, 2, "Kp"))):
            src = qk[:, col0:col0 + 128]
            P.op("dve", lambda e, src=src, tq=tq: e.tensor_tensor(out=Wk["A"][:], in0=src, in1=tbs[:, tq, sub, :],
                                                                 op=ALU.mult), r=[qk, tbs], w=[Wk["A"]])
            P.op("pool", lambda e, col0=col0, tq=tq: e.tensor_tensor(
                out=Wk["B"][:, 0:64], in0=qk[:, col0 + 64:col0 + 128], in1=tbs[:, tq + 1, sub, 0:64], op=ALU.mult),
                r=[qk, tbs], w=[Wk["B"]])
            P.op("pool", lambda e, col0=col0, tq=tq: e.tensor_tensor(
                out=Wk["B"][:, 64:128], in0=qk[:, col0:col0 + 64], in1=tbs[:, tq + 1, sub, 64:128], op=ALU.mult),
                r=[qk, tbs, Wk["B"]], w=[Wk["B"]])
            P.op("dve", lambda e, dst=dst: e.tensor_tensor(out=Wk[dst][:], in0=Wk["A"][:], in1=Wk["B"][:], op=ALU.add),
                 r=[Wk["A"], Wk["B"]], w=[Wk[dst]])
            P.op("pe", lambda e, dst=dst, which=which: e.transpose(
                out=pTr_bf[:, which * 128:(which + 1) * 128], in_=Wk[dst][:], identity=C["ident_bf"][:]),
                r=[Wk[dst], C["ident_bf"]], w=[pTr])
        yield
        P.op("act", lambda e: e.copy(out=Wk["QT"][:], in_=pTr_bf[:, 0:128]), r=[pTr], w=[Wk["QT"]])
        P.op("act", lambda e: e.copy(out=Wk["KT"][:], in_=pTr_bf[:, 128:256]), r=[pTr], w=[Wk["KT"]])
        P.op("pe", lambda e: e.matmul(pSc[:, 0:128], lhsT=Wk["KT"][:], rhs=Wk["QT"][:], start=True, stop=True),
             r=[Wk["KT"], Wk["QT"]], w=[pSc])
        P.op("dve", lambda e: e.tensor_tensor(out=Wk["Sm"][:], in0=pSc[:, 0:128], in1=C["triU"][:], op=ALU.mult),
             r=[pSc, C["triU"]], w=[Wk["Sm"]])
        yield
        P.op("pe", lambda e: e.matmul(pRO[:, 0:256], lhsT=Wk["Sm"][:], rhs=vb[:], start=True, stop=False),
             r=[Wk["Sm"], vb], w=[pRO])
        P.op("pe", lambda e: e.matmul(pRO[:, 0:256], lhsT=Wk["QT"][:], rhs=gS[:], start=False, stop=True),
             r=[Wk["QT"], gS], w=[pRO])
        yield
        T.sumsq_rstd(pRO[:, 0:256], [pRO], Wk["ss"], Wk["rstd"], 256)
        yield
        P.op("dve", lambda e: e.scalar_tensor_tensor(out=Wk["y_bf"][:], in0=pRO[:, 0:256], scalar=Wk["rstd"][:, 0:1],
                                                     in1=sgt[:], op0=ALU.mult, op1=ALU.mult),
             r=[pRO, Wk["rstd"], sgt], w=[Wk["y_bf"]])
        for j in range(2):
            P.op("pe", lambda e, j=j: e.transpose(out=pTr_bf[:, 256 + j * 128:256 + (j + 1) * 128],
                                                  in_=Wk["y_bf"][:, j * 128:(j + 1) * 128], identity=C["ident_bf"][:]),
                 r=[Wk["y_bf"], C["ident_bf"]], w=[pTr])
        P.op("act", lambda e: e.copy(out=Wk["yTs"][:], in_=pTr_bf[:, 256:512].rearrange("p (j t) -> p j t", j=2)),
             r=[pTr], w=[Wk["yTs"]])
        for j in range(2):
            for hf in range(2):
                ch = yT[2 * j + hf]
                P.dma("sp", ch[0:64, c * 128:(c + 1) * 128], Wk["yTs"][hf * 64:(hf + 1) * 64, j, :], r=[Wk["yTs"]], w=[ch])
        yield
        P.op("pe", lambda e: e.matmul(pRO[:, 256:512], lhsT=Wk["Kp"][:], rhs=vb[:], start=True, stop=True),
             r=[Wk["Kp"], vb], w=[pRO])
        P.op("dve", lambda e: e.scalar_tensor_tensor(out=St[:], in0=St[:], scalar=gv_sb[:, 0:1], in1=pRO[:, 256:512],
                                                     op0=ALU.mult, op1=ALU.add), r=[St, gv_sb, pRO], w=[St])
        P.op("act", lambda e: e.activation(out=gS[:], in_=St[:], func=AF.Copy, scale=gv_sb[:, 0:1]),
             r=[St, gv_sb], w=[gS])
        yield

    def swa_block(blk, sub):
        for h in range(2):
            hs = slice(h * 64, (h + 1) * 64)
            qcols = slice(sub * 128, (sub + 1) * 128)
            first = (blk == 0)
            if not first:
                P.op("pe", lambda e, hs=hs, qcols=qcols, sub=sub: e.matmul(
                    pSW[:, 0:128], lhsT=SKT[hs, sub * 128:(sub + 1) * 128], rhs=SQT[hs, qcols], start=True, stop=True),
                    r=[SKT, SQT], w=[pSW])
            P.op("pe", lambda e, hs=hs, qcols=qcols, sub=sub: e.matmul(
                pSW[:, 128:256], lhsT=SKT[hs, (sub + 1) * 128:(sub + 2) * 128], rhs=SQT[hs, qcols], start=True, stop=True),
                r=[SKT, SQT], w=[pSW])
            lo = 1 if first else 0
            yield
            P.op("dve", lambda e, h=h, lo=lo: e.scalar_tensor_tensor(
                out=Wk["ts"][:, lo:2, :], in0=pSW[:, lo * 128:256].rearrange("p (a b) -> p a b", b=128), scalar=0.125,
                in1=Bpd[h][:, lo:2, :], op0=ALU.mult, op1=ALU.add), r=[pSW, Bpd[h]], w=[Wk["ts"]])
            P.op("act", lambda e, lo=lo: e.activation(out=Wk["PT"][:, lo:2, :], in_=Wk["ts"][:, lo:2, :], func=AF.Exp),
                 r=[Wk["ts"]], w=[Wk["PT"]])
            oc = slice(h * 256, h * 256 + 128)
            lc = slice(h * 256 + 128, h * 256 + 256)
            yield
            if not first:
                P.op("pe", lambda e, oc=oc, sub=sub: e.matmul(pOL[0:64, oc], lhsT=SV[:, sub, :], rhs=Wk["PT"][:, 0, :],
                                                              start=True, stop=False), r=[SV, Wk["PT"]], w=[pOL])
            P.op("pe", lambda e, oc=oc, sub=sub, first=first: e.matmul(
                pOL[0:64, oc], lhsT=SV[:, sub + 1, :], rhs=Wk["PT"][:, 1, :], start=first, stop=True),
                r=[SV, Wk["PT"]], w=[pOL])
            if not first:
                P.op("pe", lambda e, lc=lc: e.matmul(pOL[0:64, lc], lhsT=C["ones_bf"][:, 0:64], rhs=Wk["PT"][:, 0, :],
                                                     start=True, stop=False), r=[C["ones_bf"], Wk["PT"]], w=[pOL])
            P.op("pe", lambda e, lc=lc, first=first: e.matmul(
                pOL[0:64, lc], lhsT=C["ones_bf"][:, 0:64], rhs=Wk["PT"][:, 1, :], start=first, stop=True),
                r=[C["ones_bf"], Wk["PT"]], w=[pOL])
            yield
            P.op("dve", lambda e, lc=lc, h=h: e.tensor_scalar(out=Wk["den"][:], in0=pOL[0:64, lc], scalar1=es[0:64, h:h + 1],
                                                              scalar2=None, op0=ALU.add), r=[pOL, es], w=[Wk["den"]])
            P.op("dve", lambda e: e.reciprocal(out=Wk["den"][:], in_=Wk["den"][:]), r=[Wk["den"]], w=[Wk["den"]])
            P.op("dve", lambda e, oc=oc: e.tensor_tensor(out=Wk["ob"][:], in0=pOL[0:64, oc], in1=Wk["den"][:], op=ALU.mult),
                 r=[pOL, Wk["den"]], w=[Wk["ob"]])
            P.dma("sp", yT[4 + h][0:64, blk * 128:(blk + 1) * 128], Wk["ob"][:], r=[Wk["ob"]], w=[yT[4 + h]])

    for st in range(n_super):
        h_t = hT[st % 2]
        qq, so = divmod(st, 4)
        for c4 in range(4):
            P.dma("sp", h_t[:, 2 * c4:2 * c4 + 2, :],
                  hT_all[c4][qq * 256:(qq + 1) * 256, so * 512:(so + 1) * 512].rearrange("(k p) t -> p k t", p=128),
                  r=[hT_all[c4]], w=[h_t])
        tbs = tb[st % 2]
        for q4 in range(4):
            P.dma("sp", tbs[:, q4, :, :], tab[q4, st * 512:(st + 1) * 512, :].rearrange("(s p) d -> p s d", p=128),
                  w=[tbs])
        for j in range(2):
            for k in range(8):
                P.op("pe", lambda e, j=j, k=k, h_t=h_t: e.matmul(
                    pFM[:], lhsT=w_sb[k][:, j * 128:(j + 1) * 128], rhs=h_t[:, k, :], start=(k == 0), stop=(k == 7)),
                    r=[w_sb[k], h_t], w=[pFM])
            if j == 0:
                P.op("act", lambda e: e.copy(out=SQT[:], in_=pFM[:]), r=[pFM], w=[SQT])
            else:
                if st > 0:
                    P.op("dve", lambda e: e.tensor_copy(out=SKT[:, 0:128], in_=SKT[:, 512:640]), r=[SKT], w=[SKT])
                    P.op("dve", lambda e: e.tensor_copy(out=SV[:, 0, :], in_=SV[:, 4, :]), r=[SV], w=[SV])
                P.op("act", lambda e: e.copy(out=SKT[:, 128:640], in_=pFM[:]), r=[pFM], w=[SKT])
        for sub in range(4):
            c = st * 4 + sub
            ts_ = slice(sub * 128, (sub + 1) * 128)
            for k in range(8):
                P.op("pe", lambda e, k=k, h_t=h_t, ts_=ts_: e.matmul(
                    pT1[:], lhsT=h_t[:, k, ts_], rhs=w_sb[k][:, 256:768], start=(k == 0), stop=(k == 7)),
                    r=[h_t, w_sb[k]], w=[pT1])
            for k in range(8):
                P.op("pe", lambda e, k=k, h_t=h_t, ts_=ts_: e.matmul(
                    pT2[:, 0:320], lhsT=h_t[:, k, ts_], rhs=w_sb[k][:, 768:1088], start=(k == 0), stop=(k == 7)),
                    r=[h_t, w_sb[k]], w=[pT2])
            qk = qkv[sub]
            vb = v_bf[sub]
            sgt = sg[sub]
            P.op("act", lambda e, qk=qk: e.copy(out=qk[:, 0:256], in_=pT1[:, 0:256]), r=[pT1], w=[qk])
            P.op("dve", lambda e, vb=vb: e.tensor_copy(out=vb[:], in_=pT1[:, 256:512]), r=[pT1], w=[vb])
            P.op("act", lambda e, sgt=sgt: e.activation(out=sgt[:], in_=pT2[:, 0:256], func=AF.Silu), r=[pT2], w=[sgt])
            P.op("dve", lambda e, sub=sub: e.tensor_copy(out=SV[:, sub + 1, :], in_=pT2[:, 256:320]), r=[pT2], w=[SV])


        def ret_gen(st=st, tbs=tbs):
            for sub in range(4):
                yield from ret_chunk(st * 4 + sub, sub, tbs, qkv[sub], v_bf[sub], sg[sub])

        def swa_gen(st=st):
            for sub in range(4):
                yield from swa_block(st * 4 + sub, sub)

        interleave([(ret_gen(), 4 * 6), (swa_gen(), 4 * 2 * 4)])
    P.reset(m0)


def odd_inputs(z, hT_full, b, hg):
    wi = z["o_w_in"][0]
    kv = hg // 2
    cols = np.concatenate([
        np.arange(3072 + 2 * hg * 64, 3072 + 2 * hg * 64 + 128),
        np.arange(3584 + kv * 64, 3584 + kv * 64 + 64), np.arange(3584 + kv * 64, 3584 + kv * 64 + 64),
        np.arange(hg * 128, hg * 128 + 128),
        np.arange(512 + hg * 128, 512 + hg * 128 + 128),
        np.arange(1024 + hg * 256, 1024 + hg * 256 + 256),
        np.arange(2048 + hg * 256, 2048 + hg * 256 + 256),
        np.arange(3712 + kv * 64, 3712 + kv * 64 + 64),
    ])
    tab, gv = host_rope_tables(hg)
    hc = host_consts()
    im = {
        "win": np.ascontiguousarray(wi[:, cols]), "tab": tab, "gv": gv,
        "sinks": np.ascontiguousarray(z["o_sinks"][0][2 * hg:2 * hg + 2].reshape(1, 2)),
        "relb": np.concatenate([z["rel_bias"][:, 2 * hg:2 * hg + 2], np.ones((1, 2), np.float32)], 0),
        "ohdA": host_ohd(False), "ohdB": host_ohd(True),
    }
    for k in ["ident_bf", "triU", "ones_bf", "antiI"]:
        im[k] = hc[k]
    return im


I32 = mybir.dt.int32
GROUPS = [[0, 1, 2, 3], [4, 5, 6, 7]]


def dyn_dma(P, out, in_fn, r, w):
    op = Op("sp", lambda e: e.dma_start(out=out, in_=in_fn()), is_dma=True, dbuf=w[0])
    P._rec(op, r, w)
    P.dma_log.append(op)
    return op


def setup_regs(P, nc, qoff):
    regs = [P.stack.enter_context(nc.sync.register(f"qr{i}")) for i in range(2)]

    def ld(e):
        for i in range(2):
            ins = e.reg_load(regs[i], qoff.t[0:1, i:i + 1])
        P.qv = e.snap(regs[0], min_val=0, max_val=6144)
        P.qc = e.snap(regs[1], min_val=0, max_val=48)
        return ins
    P.op("sp", ld)


def extract_quarter(P, src_all, dst_q, nrows, step):
    for r0 in range(0, nrows, step):
        dyn_dma(P, dst_q.t[r0:r0 + step, :], lambda r0=r0: src_all.t[r0:r0 + step, bass.ds(P.qv, NT)],
                r=[src_all], w=[dst_q])


def phase_mod(P, io):
    m0 = P.mark()
    cT, modw, modb = io["cT"], io["modw"], io["modb"]
    c_sb = P.sb("c_sb", [128, 8])
    ca = P.sb("ca_sb", [128, 8])
    b_sb = P.sb("b_sb", [1, 3072])
    o_sb = P.sb("o_sb", [1, 3072])
    wt = [P.sb(f"mw{i}", [128, 8, 768]) for i in range(2)]
    ps = [P.ps(f"mps{i}", [1, 512]) for i in range(2)]
    P.dma("sp", c_sb[:], cT, w=[c_sb])
    P.dma("sp", b_sb[:], modb, w=[b_sb])
    P.op("act", lambda e: e.activation(out=ca[:], in_=c_sb[:], func=AF.Silu), r=[c_sb], w=[ca])
    for s_ in range(4):
        w_t = wt[s_ % 2]
        P.dma("sp", w_t[:], modw[s_].rearrange("(k p) n -> p k n", p=128), w=[w_t])
        for hf in range(2):
            for k in range(8):
                P.op("pe", lambda e, hf=hf, k=k, w_t=w_t: e.matmul(
                    ps[hf][0:1, 0:384], lhsT=ca[:, k:k + 1], rhs=w_t[:, k, hf * 384:(hf + 1) * 384],
                    start=(k == 0), stop=(k == 7)), r=[ca, w_t], w=[ps[hf]])
            o0 = s_ * 768 + hf * 384
            P.op("dve", lambda e, hf=hf, o0=o0: e.tensor_tensor(out=o_sb[0:1, o0:o0 + 384], in0=ps[hf][0:1, 0:384],
                                                                in1=b_sb[0:1, o0:o0 + 384], op=ALU.add),
                 r=[ps[hf], b_sb], w=[o_sb])
    P.dma("sp", io["mod_loc"], o_sb[:], r=[o_sb], w=[io["mod_loc"]])
    P.collective("AllGather", GROUPS, io["mod_loc"], io["mod_all"])
    P.reset(m0)


CONST_NAMES = ["ident_bf", "ident_f", "triU", "triS", "NEG", "ones_f", "ones_bf", "antiI"]


def build_fused():
    nc = bass.Bass("TRN2", target_bir_lowering=False)
    P = Prog(nc)

    def X(name, shape, dt=F32):
        return Buf(nc.dram_tensor(name, list(shape), dt, kind="ExternalInput").ap(), name)

    io = {}
    for name, shape, dt in [
        ("cT", [128, 8], F32), ("modw", [4, 1024, 768], F32), ("modb", [1, 3072], F32), ("ng", [8, 1024], F32),
        ("qoff", [1, 2], I32), ("x_b", [S_LEN, 1024], F32), ("x_tok", [NT, 1024], F32),
        ("e_win", [1024, E_NCOL], F32), ("convw", [128, 4, 4], F32), ("convb", [128, 4], F32), ("hv", [3, 4], F32),
        ("normw", [128, 2], F32), ("lam", [1, 256], F32), ("subw", [128, 1], F32), ("relb", [33, 2], F32),
        ("ohdA", [33, 384], F32), ("ohdB", [33, 384], F32), ("e_wout", [2048, 1024], F32),
        ("w1_0", [1024, 4096], F32), ("w2_0", [4096, 1024], F32), ("w1_1", [1024, 4096], F32), ("w2_1", [4096, 1024], F32),
        ("o_win", [1024, O_NCOL], F32), ("tab", [4, S_LEN, 128], F32), ("gv", [128, 2], F32), ("sinks", [1, 2], F32),
        ("o_wout", [1536, 1024], F32),
    ]:
        if int(os.environ.get("FUSED_STOP", "99")) <= 2 and name in ("x_tok", "e_wout", "w1_0", "w2_0", "w1_1", "w2_1", "o_win", "tab", "o_wout"):
            continue
        io[name] = X(name, shape, dt)
    P.declared = set(io.keys()) | set(CONST_NAMES)
    cn = {k: X(k, [128, 128], BF16 if k.endswith("bf") else F32) for k in CONST_NAMES}
    for name, shape, dt in [
        ("mod_loc", [1, 3072], F32), ("mod_all", [4, 3072], F32),
        ("ss_loc", [128, 64], F32), ("ss_all", [512, 64], F32),
        ("xn0", [NT, 1024], F32), ("hT0", [1024, NT], BF16), ("x1", [NT, 1024], F32),

        ("xn1", [NT, 1024], F32), ("hT1", [1024, NT], BF16), ("vecdA", [2, 384], F32), ("vecdB", [2, 384], F32),
        ("vecdC", [2, 384], F32), ("yT_q", [2048, NT], BF16), ("ss_q", [512, 16], F32), ("yT2_q", [1536, NT], BF16),
    ]:
        io[name] = P.dram(name, shape, dt)
    io["yT_loc"] = [P.dram(f"yT_loc{i}", [64, S_LEN], BF16) for i in range(8)]
    io["yT_all"] = [P.dram(f"yT_all{i}", [256, S_LEN], BF16) for i in range(8)]
    io["yT2_loc"] = [P.dram(f"yT2_loc{i}", [64, S_LEN], BF16) for i in range(6)]
    io["yT2_all"] = [P.dram(f"yT2_all{i}", [256, S_LEN], BF16) for i in range(6)]
    io["hTn_loc"] = [P.dram(f"hTn_loc{i}", [256, NT], BF16) for i in range(4)]
    io["hT2_all"] = [P.dram(f"hT2_all{i}", [1024, NT], BF16) for i in range(4)]
    io["out"] = P.dram("out", [NT, 1024], F32, kind="ExternalOutput")

    setup_regs(P, nc, io["qoff"])
    C = {}
    for k in CONST_NAMES:
        C[k] = P.sb("c_" + k, [128, 128], BF16 if k.endswith("bf") else F32)
        P.dma("sp", C[k][:], cn[k], w=[C[k]])
    P.barrier()

    stop = int(os.environ.get("FUSED_STOP", "99"))
    phase_mod(P, io)
    P.barrier()
    if stop <= 1:
        P.build()
        return nc, P
    phase_even(P, C, io, n_super=int(os.environ.get('FUSED_NSUPER', '16')))
    for i in range(8):
        P.collective("AllGather", GROUPS, io["yT_loc"][i], io["yT_all"][i])
    if not os.environ.get("FUSED_NOAG2"):
        P.collective("AllGather", GROUPS, io["ss_loc"], io["ss_all"])
    P.barrier()
    if stop <= 2:
        P.build()
        return nc, P
    rm_e = [(kk // 2, kk % 2) for kk in range(8)] + [(kk // 2, 2 + kk % 2) for kk in range(8)]
    phase_outproj(P, C, dict(io, x=io["x_tok"], yT_all=io["yT_all"], wout=io["e_wout"], xn=io["xn0"], hT=io["hT0"], rowmap=rm_e, ngroups=16),
                  16, True, 0)
    P.barrier()
    if stop <= 3:
        P.build()
        return nc, P
    phase_mlp(P, C, dict(io, xn=io["xn0"], hT=io["hT0"], w1=io["w1_0"], w2=io["w2_0"], xo=io["x1"], hTn=io["hTn_loc"]), True, 0)
    for i in range(4):
        P.collective("AllGather", GROUPS, io["hTn_loc"][i], io["hT2_all"][i])
    P.barrier()
    if stop <= 4:
        P.build()
        return nc, P
    phase_odd(P, C, dict(io, vecdA=io["vecdB"], vecdB=io["vecdC"]), n_super=int(os.environ.get('FUSED_NSUPER', '16')))
    for i in range(6):
        P.collective("AllGather", GROUPS, io["yT2_loc"][i], io["yT2_all"][i])
    P.barrier()
    rm_o = [(kk // 2, kk % 2) for kk in range(8)] + [(kk, 2) for kk in range(4)]
    phase_outproj(P, C, dict(io, x=io["x1"], yT_all=io["yT2_all"], wout=io["o_wout"], xn=io["xn1"], hT=io["hT1"], rowmap=rm_o, ngroups=12),
                  12, False, 1)
    P.barrier()
    phase_mlp(P, C, dict(io, xn=io["xn1"], hT=io["hT1"], w1=io["w1_1"], w2=io["w2_1"], xo=io["out"]), False, 1)
    P.build()
    return nc, P


def fused_inputs(z, i):
    b, r = divmod(i, 4)
    hc = host_consts()
    cols = np.concatenate([part * 1024 + r * 256 + np.arange(256) for part in range(3)])
    mw = z["mod_w"].reshape(4, 1024, 3072)
    mb = z["mod_b"].reshape(4, 3072)
    ei = even_inputs(z, None, b, r)
    oi = odd_inputs(z, None, b, r)
    im = {
        "cT": np.ascontiguousarray(z["c"][b].reshape(8, 128).T),
        "modw": np.ascontiguousarray(mw[:, :, cols]),
        "modb": np.ascontiguousarray(mb[:, cols].reshape(1, 3072)),
        "ng": np.ascontiguousarray(z["norm_gains"].reshape(8, 1024)),
        "qoff": np.array([[r * NT, r * 16]], np.int32),
        "x_b": np.ascontiguousarray(z["x"][b]),
        "x_tok": np.ascontiguousarray(z["x"][b, r * NT:(r + 1) * NT]),
        "e_win": ei["win"], "convw": ei["convw"], "convb": ei["convb"], "hv": ei["hv"], "normw": ei["normw"],
        "lam": ei["lam"], "subw": ei["subw"], "relb": ei["relb"], "ohdA": host_ohd(False), "ohdB": host_ohd(True),
        "e_wout": z["e_w_out"][0], "w1_0": z["mlp_w1"][0], "w2_0": z["mlp_w2"][0], "w1_1": z["mlp_w1"][1],
        "w2_1": z["mlp_w2"][1], "o_win": oi["win"], "tab": oi["tab"], "gv": oi["gv"], "sinks": oi["sinks"],
        "o_wout": z["o_w_out"][0],
    }
    for k in CONST_NAMES:
        im[k] = hc[k]
    return im


def kernel(**inputs):
    z = {k: np.asarray(v) for k, v in inputs.items()}
    nc, P_ = build_fused()
    in_maps = [{k: v for k, v in fused_inputs(z, i).items() if k in P_.declared} for i in range(8)]
    res = run_bass_kernel_spmd(nc, in_maps, core_ids=list(range(8)))
    out = np.stack([res.results[i]["out"] for i in range(8)]).reshape(2, S_LEN, 1024).astype(np.float32)
    return out
```

```python
from contextlib import ExitStack
import os
import numpy as np
import ml_dtypes
import concourse.bass as bass
import concourse.mybir as mybir
from concourse.bass_utils import run_bass_kernel_spmd

F32 = mybir.dt.float32
BF16 = mybir.dt.bfloat16
AF = mybir.ActivationFunctionType
ALU = mybir.AluOpType
AX = mybir.AxisListType
EPOCH = 20000


class Buf:
    __slots__ = ("t", "name", "w", "r", "dsem", "dcount", "is_out", "bank", "alt")

    def __init__(self, t, name, bank=None):
        self.t = t
        self.name = name
        self.bank = bank
        self.alt = None
        self.w = []
        self.r = []
        self.dsem = None
        self.dcount = 0
        self.is_out = False

    def __getitem__(self, idx):
        return self.t[idx]


class Op:
    __slots__ = ("eng", "fn", "deps", "marked", "sem", "val", "waits", "is_dma", "dbuf", "snap", "inc", "is_bar")

    def __init__(self, eng, fn, is_dma=False, dbuf=None, inc=16):
        self.inc = inc
        self.is_bar = False
        self.eng = eng
        self.fn = fn
        self.deps = []
        self.marked = False
        self.sem = None
        self.val = 0
        self.waits = []
        self.is_dma = is_dma
        self.dbuf = dbuf
        self.snap = None


class Prog:
    ENGS = ("pe", "act", "dve", "pool", "sp")

    def __init__(self, nc):
        self.nc = nc
        self.ops = []
        self.stack = ExitStack()
        self.out_ops = []
        self.nsem = 0
        self.SB_BYTES = 206 * 1024
        self.sb_f32 = self.stack.enter_context(nc.sbuf_tensor("arena", [128, self.SB_BYTES // 4], F32))
        self.sb_bf = self.sb_f32.bitcast(BF16)
        self.ps_f32 = self.stack.enter_context(nc.psum_tensor("parena", [128, 4096], F32))
        self.ps_bf = self.ps_f32.bitcast(BF16)
        self.sb_ptr = 0
        self.ps_ptr = 0
        self.dma_log = []
        self.bar = None

    @staticmethod
    def _shape_view(v, shape):
        if len(shape) == 3:
            v = v.rearrange("p (a b) -> p a b", a=shape[1])
        elif len(shape) == 4:
            v = v.rearrange("p (a b c) -> p a b c", a=shape[1], b=shape[2])
        return v

    def sb(self, name, shape, dt=F32):
        shape = list(shape)
        nel = int(np.prod(shape[1:]))
        esz = 2 if dt == BF16 else 4
        nbytes = (nel * esz + 31) // 32 * 32
        off = self.sb_ptr
        self.sb_ptr += nbytes
        assert self.sb_ptr <= self.SB_BYTES, ("SBUF arena overflow", name, self.sb_ptr)
        base = self.sb_bf if esz == 2 else self.sb_f32
        v = base[0:shape[0], off // esz:off // esz + nel]
        return Buf(self._shape_view(v, shape), name)

    def ps(self, name, shape, dt=F32):
        shape = list(shape)
        nel = int(np.prod(shape[1:]))
        esz = 2 if dt == BF16 else 4
        nb = (nel * esz + 2047) // 2048
        off = self.ps_ptr * 2048
        self.ps_ptr += nb
        assert self.ps_ptr <= 8, ("PSUM arena overflow", name)
        base = self.ps_bf if esz == 2 else self.ps_f32
        v = base[0:shape[0], off // esz:off // esz + nel]
        b = Buf(self._shape_view(v, shape), name, bank=[None])
        b.alt = self.ps_bf[:, off // 2:off // 2 + nb * 1024]
        return b

    def mark(self):
        return (self.sb_ptr, self.ps_ptr)

    def reset(self, m):
        self.sb_ptr, self.ps_ptr = m

    def barrier(self):
        if self.bar is None:
            self.bar = {e: self.sb("bar_" + e, [128, 8]) for e in ("act", "dve", "pool")}
        marks = []
        for eng in ("act", "dve", "pool"):
            b = self.bar[eng]
            if eng == "act":
                marks.append(self.op(eng, lambda e, b=b: e.memzero(b[:]), w=[b]))
            else:
                marks.append(self.op(eng, lambda e, b=b: e.memset(b[:], 0.0), w=[b]))
        latest = {}
        for d in self.dma_log:
            latest[id(d.dbuf)] = d
        self.dma_log = []
        first = True
        for eng in self.ENGS:
            op = Op(eng, None)
            op.is_bar = first
            first = False
            op.deps = marks + list(latest.values())
            self.ops.append(op)

    def dram(self, name, shape, dt, kind="Internal"):
        t = self.nc.dram_tensor(name, list(shape), dt, kind=kind)
        b = Buf(t.ap(), name)
        b.is_out = kind == "ExternalOutput"
        return b

    def newsem(self, name):
        self.nsem += 1
        return self.stack.enter_context(self.nc.semaphore(f"{name}_{self.nsem}"))

    def _rec(self, op, r, w):
        deps = []
        for b in r:
            deps.extend(b.w)
        for b in w:
            for d in b.w:
                if not (d.eng == "pe" and op.eng == "pe" and not d.is_dma and not op.is_dma):
                    deps.append(d)
            deps.extend(b.r)
        for b in r:
            b.r.append(op)
        for b in w:
            b.w = [op]
            b.r = []
        for b in list(r) + list(w):
            if b.bank is not None:
                d = b.bank[0]
                if d is not None and d.eng != op.eng:
                    deps.append(d)
                b.bank[0] = op
        seen = set()
        for d in deps:
            if id(d) not in seen and d is not op:
                seen.add(id(d))
                op.deps.append(d)
        self.ops.append(op)
        return op

    def op(self, eng, fn, r=(), w=()):
        return self._rec(Op(eng, fn), r, w)

    def dma(self, q, out, in_, r=(), w=(), sembuf=None, **kw):
        if sembuf is None:
            sembuf = (list(w) + list(r))[0]
        if isinstance(out, Buf):
            out = out.t
        if isinstance(in_, Buf):
            in_ = in_.t
        op = Op(q, lambda e: e.dma_start(out=out, in_=in_, **kw), is_dma=True, dbuf=sembuf)
        self._rec(op, r, w)
        self.dma_log.append(op)
        if any(b.is_out for b in w):
            self.out_ops.append(op)
        return op

    def collective(self, kind, groups, in_buf, out_buf):
        op = Op("pool", lambda e: e.collective_compute(kind, ALU.bypass, replica_groups=groups,
                                                       ins=[in_buf.t.opt()], outs=[out_buf.t.opt()]),
                is_dma=True, dbuf=out_buf, inc=1)
        self.dma_log.append(op)
        return self._rec(op, [in_buf], [out_buf])

    def build(self):
        nc = self.nc
        fin = Op("sp", None)
        fin.deps = list(self.out_ops)
        self.ops.append(fin)
        for op in self.ops:
            if op.is_dma:
                op.marked = True
            for d in op.deps:
                d.marked = True
        cnt = {e: 0 for e in self.ENGS}
        esem = {}
        free_d = []
        assigned = []
        for op in self.ops:
            if op.is_bar:
                for b in assigned:
                    if b.dsem is not None:
                        free_d.append(b.dsem)
                        b.dsem = None
                assigned = []
            if not op.marked:
                continue
            if op.is_dma:
                b = op.dbuf
                if b.dsem is None:
                    b.dsem = free_d.pop() if free_d else [self.newsem("d"), 0]
                    assigned.append(b)
                sm = b.dsem
                sm[1] += op.inc
                op.sem, op.val = sm[0], sm[1]
                if sm[1] >= EPOCH:
                    b.dsem = None
            else:
                e = op.eng
                if e not in esem or cnt[e] >= EPOCH:
                    esem[e] = self.newsem(e)
                    cnt[e] = 0
                cnt[e] += 1
                op.sem, op.val = esem[e], cnt[e]
        known = {e: {} for e in self.ENGS}
        nwaits = 0
        for op in self.ops:
            k = known[op.eng]
            waits = {}
            for d in op.deps:
                key = id(d.sem)
                if k.get(key, 0) >= d.val:
                    continue
                waits[key] = (d.sem, d.val)
                k[key] = d.val
                for s, v in d.snap.items():
                    if k.get(s, 0) < v:
                        k[s] = v
            op.waits = list(waits.values())
            nwaits += len(op.waits)
            if op.marked:
                op.snap = dict(k)
                op.snap[id(op.sem)] = op.val
        per = {e: [o for o in self.ops if o.eng == e] for e in self.ENGS}
        self.stats = dict(n_ops=len(self.ops), n_waits=nwaits, n_sems=self.nsem,
                          per_eng={e: len(v) for e, v in per.items()})

        def emit(engobj, lst):
            for op in lst:
                for s, v in op.waits:
                    engobj.wait_ge(s, v)
                if op.fn is None:
                    continue
                ins = op.fn(engobj)
                if op.marked:
                    ins.then_inc(op.sem, op.inc if op.is_dma else 1)

        with nc.Block() as block:
            @block.tensor
            def _(e):
                emit(e, per["pe"])

            @block.scalar
            def _(e):
                emit(e, per["act"])

            @block.vector
            def _(e):
                emit(e, per["dve"])

            @block.gpsimd
            def _(e):
                emit(e, per["pool"])

            @block.sync
            def _(e):
                emit(e, per["sp"])
        self.stack.close()
        return nc


EPS = 1e-6


def interleave(gens_with_counts):
    gens = [[g, max(1, n), 0.0] for g, n in gens_with_counts]
    total = max(n for _, n, _ in gens)
    alive = list(gens)
    while alive:
        for item in list(alive):
            g, n, acc = item
            item[2] += n / total
            while item[2] >= 1.0 - 1e-9:
                item[2] -= 1.0
                try:
                    next(g)
                except StopIteration:
                    alive.remove(item)
                    break


def bcast_row(ap_row, n=128):
    ap_row = ap_row.t if isinstance(ap_row, Buf) else ap_row
    pairs = [list(p) for p in ap_row.ap]
    w = pairs[-1]
    return bass.AP(ap_row.tensor, ap_row.offset, [[0, n], [w[0], w[1]]])


def emit_rstd(P, eng, ss, rstd, n, r_extra=()):
    pass


class TokCtx:
    def __init__(self, P, ident):
        self.P = P
        self.ident = ident
        self.junk = P.sb("junk", [128, 1024], BF16)

    def sumsq_rstd(self, src_ap, src_bufs, ss, rstd, n):
        P = self.P
        j = self.junk
        P.op("act", lambda e: e.activation(out=j[:, 0:src_ap.shape[-1]], in_=src_ap, func=AF.Square,
                                           accum_out=ss[:, 0:1]), r=src_bufs, w=[ss])
        P.op("act", lambda e: e.activation(out=rstd[:, 0:1], in_=ss[:, 0:1], func=AF.Sqrt, bias=EPS, scale=1.0 / n),
             r=[ss], w=[rstd])
        P.op("dve", lambda e: e.reciprocal(out=rstd[:, 0:1], in_=rstd[:, 0:1]), r=[rstd], w=[rstd])


def load_modrow(P, dst, mod_all, s_, part):
    mt = mod_all.t
    src = bass.AP(mt.tensor, mt.offset + s_ * 768 + part * 256, [[0, 128], [3072, 4], [1, 256]])
    P.dma("sp", dst[:].rearrange("p (r c) -> p r c", r=4), src, r=[mod_all], w=[dst])


def setup_mod_rows(P, mod_all, ng, sA, sB, i_gpost, i_gpre, gg, gmod, shift, scratch):
    load_modrow(P, gg, mod_all, sA, 2)
    P.dma("sp", scratch[0][:], bcast_row(ng[i_gpost:i_gpost + 1, :]), w=[scratch[0]])
    P.op("dve", lambda e: e.tensor_tensor(out=gg[:], in0=gg[:], in1=scratch[0][:], op=ALU.mult),
         r=[gg, scratch[0]], w=[gg])
    if gmod is not None:
        load_modrow(P, gmod, mod_all, sB, 1)
        P.dma("sp", scratch[1][:], bcast_row(ng[i_gpre:i_gpre + 1, :]), w=[scratch[1]])
        P.op("dve", lambda e: e.scalar_tensor_tensor(out=gmod[:], in0=gmod[:], scalar=1.0, in1=scratch[1][:],
                                                     op0=ALU.add, op1=ALU.mult), r=[gmod, scratch[1]], w=[gmod])
        load_modrow(P, shift, mod_all, sB, 0)


def emit_prenorm_T(P, T, xsrc, xbufs, gmod, shift, psT, hT_out_ap, hT_bufs, tmp, hb, ss, rstd, psT_view=None, part=None):
    if part in (None, 0):
        T.sumsq_rstd(xsrc, xbufs, ss, rstd, 1024)
        P.op("dve", lambda e: e.scalar_tensor_tensor(out=tmp[:], in0=xsrc, scalar=rstd[:, 0:1], in1=gmod[:],
                                                     op0=ALU.mult, op1=ALU.mult), r=list(xbufs) + [rstd, gmod], w=[tmp])
        P.op("pool", lambda e: e.tensor_tensor(out=hb[:], in0=tmp[:], in1=shift[:], op=ALU.add),
             r=[tmp, shift], w=[hb])
    if part == 0:
        return
    pv = psT.alt if psT_view is None else psT_view
    for k in range(8):
        P.op("pe", lambda e, k=k: e.transpose(out=pv[:, k * 128:(k + 1) * 128], in_=hb[:, k * 128:(k + 1) * 128],
                                              identity=T.ident[:]), r=[hb, T.ident], w=[psT])
    P.op("act", lambda e: e.copy(out=hT_out_ap, in_=pv[:, 0:1024].rearrange("p (k t) -> p k t", k=8)),
         r=[psT], w=hT_bufs)


def emit_post_residual(P, T, osrc, obufs, x_t, gg, xo, ss, rstd):
    T.sumsq_rstd(osrc, obufs, ss, rstd, 1024)
    P.op("dve", lambda e: e.scalar_tensor_tensor(out=xo[:], in0=osrc, scalar=rstd[:, 0:1], in1=gg[:],
                                                 op0=ALU.mult, op1=ALU.mult), r=list(obufs) + [rstd, gg], w=[xo])
    P.op("pool", lambda e: e.tensor_tensor(out=xo[:], in0=xo[:], in1=x_t[:], op=ALU.add),
         r=[xo, x_t], w=[xo])


NT = 2048


def phase_outproj(P, C, io, FC, even, layer):
    m0 = P.mark()
    x, wout, xn, hT = io["x"], io["wout"], io["xn"], io["hT"]
    yT_all = io["yT_all"]
    T = TokCtx(P, C["ident_bf"])
    gg = P.sb("gg", [128, 1024])
    gmod = P.sb("gmod", [128, 1024])
    shift = P.sb("shift", [128, 1024])
    xt = [P.sb(f"xt{i}", [128, 1024]) for i in range(2)]
    setup_mod_rows(P, io["mod_all"], io["ng"], 2 * layer, 2 * layer + 1, layer * 4 + 1, layer * 4 + 2, gg, gmod, shift, xt)
    if even:
        ss_in = P.sb("ss_in", [128, 4, 16])
        rs_ssd = P.sb("rs_ssd", [128, 16])
        dyn_dma(P, ss_in[:], lambda: io["ss_all"].t.rearrange("(h p) c -> p h c", p=128)[:, :, bass.ds(P.qc, 16)],
                r=[io["ss_all"]], w=[ss_in])
        P.op("dve", lambda e: e.tensor_reduce(out=rs_ssd[:], in_=ss_in[:].rearrange("p h t -> p t h"), axis=AX.X,
                                              op=ALU.add), r=[ss_in], w=[rs_ssd])
        P.op("act", lambda e: e.activation(out=rs_ssd[:], in_=rs_ssd[:], func=AF.Sqrt, bias=EPS, scale=1.0 / 1024),
             r=[rs_ssd], w=[rs_ssd])
        P.op("dve", lambda e: e.reciprocal(out=rs_ssd[:], in_=rs_ssd[:]), r=[rs_ssd], w=[rs_ssd])
    w_sb = [P.sb(f"wo{k}", [128, 1024], BF16) for k in range(FC)]
    for k in range(FC):
        P.dma("pool", w_sb[k][:], wout[k * 128:(k + 1) * 128, :], w=[w_sb[k]])
    NGL = io["ngroups"] // 4
    yq = [P.sb(f"yq{i}", [128, 4, NT], BF16) for i in range(NGL)]
    for gl in range(NGL):
        for hf in range(2):
            src = yT_all[2 * gl + hf]
            dyn_dma(P, yq[gl][hf * 64:(hf + 1) * 64, :, :], lambda src=src: src.t.rearrange("(h p) t -> p h t", p=64)[
                :, :, bass.ds(P.qv, NT)], r=[src], w=[yq[gl]])
    xo = [P.sb(f"xo{i}", [128, 1024]) for i in range(2)]
    tmp = [P.sb(f"tmp{i}", [128, 1024]) for i in range(2)]
    hb = [P.sb(f"hb{i}", [128, 1024], BF16) for i in range(2)]
    hTs = [P.sb(f"hTs{i}", [128, 8, 128], BF16) for i in range(2)]
    osb = [P.sb(f"osb{i}", [128, 1024]) for i in range(2)]
    dsb = P.sb("dsb", [128, 1024])
    ss = [P.sb(f"ss{i}", [128, 2]) for i in range(4)]
    rstd = [P.sb(f"rstd{i}", [128, 2]) for i in range(4)]
    psA = [P.ps(f"psA{i}", [128, 1024]) for i in range(2)]
    psB = P.ps("psB", [128, 1024]) if even else None
    psT = [P.ps(f"psT{i}", [128, 1024], BF16) for i in range(2)]
    nA = FC // 2 if even else FC
    pending = []
    for t in range(16):
        q4, s4 = divmod(t, 4)
        gmap = io["rowmap"]
        x_t = xt[t % 2]
        P.dma("sp", x_t[:], x[t * 128:(t + 1) * 128, :], r=[x], w=[x_t])
        pa = psA[t % 2]
        for k in range(nA):
            for hlf in range(2):
                hg_, gl_ = gmap[k]
                yb = yq[gl_]
                P.op("pe", lambda e, k=k, hlf=hlf, pa=pa, yb=yb, hg_=hg_, t=t: e.matmul(
                    pa[:, hlf * 512:(hlf + 1) * 512], lhsT=yb[:, hg_, t * 128:(t + 1) * 128],
                    rhs=w_sb[k][:, hlf * 512:(hlf + 1) * 512], start=(k == 0), stop=(k == nA - 1)),
                    r=[yb, w_sb[k]], w=[pa])
        if even:
            for k in range(nA, FC):
                for hlf in range(2):
                    hg_, gl_ = gmap[k]
                    yb = yq[gl_]
                    P.op("pe", lambda e, k=k, hlf=hlf, yb=yb, hg_=hg_, t=t: e.matmul(
                        psB[:, hlf * 512:(hlf + 1) * 512], lhsT=yb[:, hg_, t * 128:(t + 1) * 128],
                        rhs=w_sb[k][:, hlf * 512:(hlf + 1) * 512], start=(k == nA), stop=(k == FC - 1)),
                        r=[yb, w_sb[k]], w=[psB])
            P.op("act", lambda e: e.copy(out=dsb[:], in_=psB[:]), r=[psB], w=[dsb])
            o_t = osb[t % 2]
            P.op("dve", lambda e, pa=pa, o_t=o_t, t=t: e.scalar_tensor_tensor(
                out=o_t[:], in0=pa[:], scalar=rs_ssd[:, t:t + 1], in1=dsb[:], op0=ALU.mult, op1=ALU.add),
                r=[pa, rs_ssd, dsb], w=[o_t])
            osrc, obufs = o_t[:], [o_t]
        else:
            osrc, obufs = pa[:], [pa]
        while pending:
            pending.pop(0)()
        xo_t = xo[t % 2]
        emit_post_residual(P, T, osrc, obufs, x_t, gg, xo_t, ss[0], rstd[0])
        P.dma("sp", xn[t * 128:(t + 1) * 128, :], xo_t[:], r=[xo_t], w=[xn])
        hts = hTs[t % 2]
        emit_prenorm_T(P, T, xo_t[:], [xo_t], gmod, shift, psT[t % 2], hts[:], [hts], tmp[1], hb[t % 2],
                       ss[1], rstd[1], part=0)

        def fin(t=t, xo_t=xo_t, hts=hts):
            emit_prenorm_T(P, T, xo_t[:], [xo_t], gmod, shift, psT[t % 2], hts[:], [hts], tmp[1], hb[t % 2],
                           ss[1], rstd[1], part=1)
            P.dma("sp", hT[:, t * 128:(t + 1) * 128].rearrange("(k p) t -> p k t", p=128), hts[:], r=[hts], w=[hT])
        pending.append(fin)
    while pending:
        pending.pop(0)()
    P.reset(m0)


def phase_mlp(P, C, io, next_pre, layer):
    m0 = P.mark()
    xn, hT, w1, w2, xo_d = io["xn"], io["hT"], io["w1"], io["w2"], io["xo"]
    hTn = io.get("hTn")
    T = TokCtx(P, C["ident_bf"])
    gg = P.sb("gg", [128, 1024])
    gmod = P.sb("gmod", [128, 1024]) if next_pre else None
    shift = P.sb("shift", [128, 1024]) if next_pre else None
    xt = [P.sb(f"xt{i}", [128, 1024]) for i in range(2)]
    setup_mod_rows(P, io["mod_all"], io["ng"], 2 * layer + 1, 2 * layer + 2, layer * 4 + 3, layer * 4 + 4, gg, gmod, shift, xt)
    w1_sb = [[P.sb(f"w1_{k}_{cb}", [128, 1024], BF16) for cb in range(4)] for k in range(8)]
    w2_sb = [P.sb(f"w2_{k}", [128, 1024], BF16) for k in range(32)]
    for cb in range(4):
        for k in range(8):
            P.dma("pool", w1_sb[k][cb][:], w1[k * 128:(k + 1) * 128, cb * 1024:(cb + 1) * 1024], w=[w1_sb[k][cb]])
    for k in range(32):
        P.dma("pool", w2_sb[k][:], w2[k * 128:(k + 1) * 128, :], w=[w2_sb[k]])
    ST = 256
    ht = [P.sb(f"ht{i}", [128, 8, ST], BF16) for i in range(2)]
    aT = [P.sb(f"aT{i}", [128, 2, ST], BF16) for i in range(16)]
    rl = [P.sb(f"rl{i}", [128, 2, ST]) for i in range(2)]
    xo = [P.sb(f"xo{i}", [128, 1024]) for i in range(2)]
    tmp = [None, P.sb("tmp1", [128, 1024])]
    ss = [P.sb(f"ss{i}", [128, 2]) for i in range(2)]
    rstd = [P.sb(f"rstd{i}", [128, 2]) for i in range(2)]
    psU = [P.ps(f"psU{i}", [128, 2, ST]) for i in range(3)]
    psO = [P.ps(f"psO{i}", [128, 1024]) for i in range(2)]
    psT = P.ps("psT", [128, 1024], BF16)
    if next_pre:
        hb = [P.sb(f"hb{i}", [128, 1024], BF16) for i in range(2)]
        hTs = [P.sb(f"hTs{i}", [128, 8, 128], BF16) for i in range(2)]
    nu = 0
    pending = []
    for st in range(NT // ST):
        h_t = ht[st % 2]
        P.dma("sp", h_t[:], hT[:, st * ST:(st + 1) * ST].rearrange("(k p) t -> p k t", p=128), r=[hT], w=[h_t])
        aTs = aT
        for fp in range(16):
            pu = psU[nu % 3]
            r_t = rl[nu % 2]
            nu += 1
            for j in range(2):
                f = fp * 2 + j
                for k in range(8):
                    wb = w1_sb[k][f // 8]
                    P.op("pe", lambda e, pu=pu, j=j, f=f, k=k, h_t=h_t, wb=wb: e.matmul(
                        pu[:, j, :], lhsT=wb[:, (f % 8) * 128:(f % 8 + 1) * 128], rhs=h_t[:, k, :],
                        start=(k == 0), stop=(k == 7)), r=[wb, h_t], w=[pu])
            P.op("act", lambda e, pu=pu, r_t=r_t: e.activation(out=r_t[:], in_=pu[:], func=AF.Relu),
                 r=[pu], w=[r_t])
            a_t = aTs[fp]
            P.op("dve" if fp % 2 == 0 else "pool", lambda e, r_t=r_t, a_t=a_t: e.tensor_tensor(
                out=a_t[:], in0=r_t[:], in1=r_t[:], op=ALU.mult), r=[r_t], w=[a_t])
        for sub in range(ST // 128):
            t = st * (ST // 128) + sub
            po = psO[t % 2]
            for f in range(32):
                for hlf in range(2):
                    P.op("pe", lambda e, po=po, f=f, hlf=hlf, sub=sub, aTs=aTs: e.matmul(
                        po[:, hlf * 512:(hlf + 1) * 512], lhsT=aTs[f // 2][:, f % 2, sub * 128:(sub + 1) * 128],
                        rhs=w2_sb[f][:, hlf * 512:(hlf + 1) * 512], start=(f == 0), stop=(f == 31)),
                        r=[aTs[f // 2], w2_sb[f]], w=[po])
            while pending:
                pending.pop(0)()
            x_t = xt[t % 2]
            P.dma("sp", x_t[:], xn[t * 128:(t + 1) * 128, :], r=[xn], w=[x_t])
            xo_t = xo[t % 2]
            emit_post_residual(P, T, po[:], [po], x_t, gg, xo_t, ss[0], rstd[0])
            P.dma("sp", xo_d[t * 128:(t + 1) * 128, :], xo_t[:], r=[xo_t], w=[xo_d])
            if next_pre:
                hts = hTs[t % 2]
                emit_prenorm_T(P, T, xo_t[:], [xo_t], gmod, shift, psT, hts[:], [hts], tmp[1], hb[t % 2],
                               ss[1], rstd[1], part=0)

                def fin(t=t, xo_t=xo_t, hts=hts):
                    emit_prenorm_T(P, T, xo_t[:], [xo_t], gmod, shift, psT, hts[:], [hts], tmp[1], hb[t % 2],
                                   ss[1], rstd[1], part=1)
                    for c4 in range(4):
                        P.dma("sp", hTn[c4][:, t * 128:(t + 1) * 128].rearrange("(k p) t -> p k t", p=128),
                              hts[:, 2 * c4:2 * c4 + 2, :], r=[hts], w=[hTn[c4]])
                pending.append(fin)
    while pending:
        pending.pop(0)()
    P.reset(m0)


S_LEN = 8192
NEGV = -30000.0
E_NCOL = 1024 + 516


def host_consts():
    c = {}
    c["ident_bf"] = np.eye(128, dtype=np.float32).astype(ml_dtypes.bfloat16)
    c["ident_f"] = np.eye(128, dtype=np.float32)
    i = np.arange(128)
    c["triU"] = (i[:, None] <= i[None, :]).astype(np.float32)
    c["triS"] = (i[:, None] > i[None, :]).astype(np.float32)
    c["NEG"] = np.where(i[None, :] < i[:, None], NEGV, 0.0).astype(np.float32)
    c["ones_f"] = np.ones((128, 128), np.float32)
    c["ones_bf"] = np.ones((128, 128), np.float32).astype(ml_dtypes.bfloat16)
    c["antiI"] = np.ascontiguousarray(np.eye(128, dtype=np.float32)[::-1])
    return c


def t5_bucket_np(d):
    d = np.maximum(d, 0)
    logd = np.log(np.maximum(d, 1).astype(np.float32) / np.float32(16))
    large = 16 + (logd / np.float32(np.log(128 / 16)) * np.float32(16)).astype(np.int32)
    large = np.minimum(large, 31)
    return np.where(d < 16, d, large)


def host_ohd(swa_prev=False):
    d = np.arange(383) - 127
    oh = np.zeros((33, 384), np.float32)
    oh[t5_bucket_np(d), np.arange(383)] = 1.0
    if swa_prev:
        oh[32, :383] = np.where((d < 0) | (d >= 128), NEGV, 0.0)
    else:
        oh[32, :383] = np.where(d < 0, NEGV, 0.0)
    oh[:32, :383] *= (d >= 0)[None, :]
    return oh


def phase_even(P, C, io, do_ssd=True, do_diff=True, n_super=16):
    m0 = P.mark()
    x, win, convw, convb, hv, normw = io["x_b"], io["e_win"], io["convw"], io["convb"], io["hv"], io["normw"]
    lam, subw, relb, ohd = io["lam"], io["subw"], io["relb"], io["ohdA"]
    yT, ss_out, vecd = io["yT_loc"], io["ss_loc"], io["vecdA"]
    T = TokCtx(P, C["ident_bf"])
    gmod = P.sb("gmod", [128, 1024])
    shift = P.sb("shift", [128, 1024])
    xt = [P.sb(f"xt{i}", [128, 1024]) for i in range(2)]
    load_modrow(P, gmod, io["mod_all"], 0, 1)
    P.dma("sp", xt[0][:], bcast_row(io["ng"][0:1, :]), w=[xt[0]])
    P.op("dve", lambda e: e.scalar_tensor_tensor(out=gmod[:], in0=gmod[:], scalar=1.0, in1=xt[0][:],
                                                 op0=ALU.add, op1=ALU.mult), r=[gmod, xt[0]], w=[gmod])
    load_modrow(P, shift, io["mod_all"], 0, 0)
    w_sb = [P.sb(f"win{k}", [128, E_NCOL], BF16) for k in range(8)]
    for k in range(8):
        P.dma("pool", w_sb[k][:], win[k * 128:(k + 1) * 128, :], w=[w_sb[k]])
    cw = P.sb("cw", [128, 4, 4])
    cb = P.sb("cb", [128, 4])
    nw = P.sb("nw", [128, 2])
    hvb = P.sb("hvb", [128, 3, 4])
    P.dma("sp", cw[:], convw, w=[cw])
    P.dma("sp", cb[:], convb, w=[cb])
    P.dma("sp", nw[:], normw, w=[nw])
    P.dma("sp", hvb[:], bass.AP(hv.t.tensor, hv.t.offset, [[0, 128], [4, 3], [1, 4]]), w=[hvb])
    Aneg = P.sb("Aneg", [128, 4])
    P.op("act", lambda e: e.activation(out=Aneg[:], in_=hvb[:, 1, :], func=AF.Exp), r=[hvb], w=[Aneg])
    P.op("dve", lambda e: e.tensor_scalar(out=Aneg[:], in0=Aneg[:], scalar1=-1.0, scalar2=None, op0=ALU.mult),
         r=[Aneg], w=[Aneg])

    hT = [P.sb(f"hT{i}", [128, 8, 512], BF16) for i in range(2)]
    hb = P.sb("hb", [128, 1024], BF16)
    tmp = P.sb("tmp", [128, 1024])
    ssq = P.sb("ssq", [128, 2])
    rstd = P.sb("rstd", [128, 2])
    pAB = P.ps("pAB", [128, 512])
    pAB_bf = pAB.alt
    bk1 = P.ps("bk1", [128, 512])
    bk2 = P.ps("bk2", [128, 512])
    bk3 = P.ps("bk3", [128, 512])
    pdt = Buf(bk3.t[:, 256:272], "pdt", bank=bk3.bank)
    raw = [P.sb(f"raw{j}", [128, 3 + 512]) for j in range(4)]
    for j in range(4):
        P.op("pool", lambda e, j=j: e.memset(raw[j][:, 0:3], 0.0), w=[raw[j]])
    acc = P.sb("acc", [128, 512])
    cvT = [P.sb(f"cvT{j}", [128, 512], BF16) for j in range(4)]
    z_sb = [P.sb(f"z_sb{i}", [128, 256]) for i in range(4)]
    dtr = [P.sb(f"dtr{i}", [128, 4]) for i in range(4)]
    if do_diff:
        KT = [P.sb(f"KT{h}", [128, S_LEN], BF16) for h in range(2)]
        VV = [P.sb(f"V{h}", [128, 64, 128], BF16) for h in range(2)]
        QT = [[P.sb(f"QT{h}_{m}", [128, 512], BF16) for m in range(2)] for h in range(2)]
        for h in range(2):
            for m in range(2):
                P.op("pool", lambda e, h=h, m=m: e.memset(QT[h][m][:], 0.0), w=[QT[h][m]])
    if do_ssd:
        S_f = P.sb("S_f", [128, 256])
        S_b = P.sb("S_b", [128, 256], BF16)
        P.op("pool", lambda e: e.memset(S_f[:], 0.0), w=[S_f])
        P.op("pool", lambda e: e.memset(S_b[:], 0.0), w=[S_b])
        ss_all = P.sb("ss_all", [128, 64])
        p_bc = Buf(bk1.t[:, 0:256].rearrange("p (a b) -> p a b", a=2), "p_bc", bank=bk1.bank)
        p_y = Buf(bk1.t[:, 256:512], "p_y", bank=bk1.bank)
        p_cbxt = Buf(bk2.t[:, 0:256], "p_cbxt", bank=bk2.bank)
        p_S = Buf(bk2.t[:, 256:512], "p_S", bank=bk2.bank)
        p_sm = Buf(bk3.t[:, 0:256], "p_sm", bank=bk3.bank)
        p_cb = p_cbxt
        p_xt = bk2.alt
        p_smb = bk3.alt
        W = {}
        for nme, shp, dt in [("dt", [128, 4], F32), ("a", [128, 4], F32), ("nacs", [128, 4], F32),
                             ("E", [128, 4], F32), ("dte", [128, 4], F32), ("dec", [128, 4], F32),
                             ("trr", [128, 2, 128], F32), ("decay", [128, 4, 128], F32), ("cbs", [128, 128], F32),
                             ("M", [128, 4, 128], BF16), ("xs_tok", [128, 256], F32), ("X", [128, 256], BF16),
                             ("Xd", [128, 256], BF16), ("B_tok", [128, 128], BF16), ("yo", [128, 256], F32),
                             ("y", [128, 256], F32), ("sz", [128, 256], F32), ("y_bf", [128, 256], BF16),
                             ("yTs", [128, 2, 128], BF16), ("Dbc", [128, 4, 64], F32)]:
            W[nme] = P.sb("w_" + nme, shp, dt)
        for r_ in range(4):
            P.op("act", lambda e, r_=r_: e.activation(out=W["Dbc"][:, r_, :], in_=C["ones_f"][:, 0:64], func=AF.Copy,
                                                      scale=hvb[:, 2, r_:r_ + 1]), r=[C["ones_f"], hvb], w=[W["Dbc"]])

    def ssd_chunk(c, st, sub):
        cs = slice(sub * 128, (sub + 1) * 128)
        zt, dt_raw = z_sb[sub], dtr[sub]
        P.op("dve", lambda e: e.tensor_tensor(out=W["dt"][:], in0=dt_raw[:], in1=hvb[:, 0, :], op=ALU.add),
             r=[dt_raw, hvb], w=[W["dt"]])
        P.op("act", lambda e: e.activation(out=W["dt"][:], in_=W["dt"][:], func=AF.Exp), r=[W["dt"]], w=[W["dt"]])
        P.op("act", lambda e: e.activation(out=W["dt"][:], in_=W["dt"][:], func=AF.Ln, bias=1.0, scale=1.0),
             r=[W["dt"]], w=[W["dt"]])
        P.op("dve", lambda e: e.tensor_tensor(out=W["a"][:], in0=W["dt"][:], in1=Aneg[:], op=ALU.mult),
             r=[W["dt"], Aneg], w=[W["a"]])
        yield
        P.op("pe", lambda e: e.matmul(p_sm[:, 0:4], lhsT=C["triU"][:], rhs=W["a"][:], start=True, stop=True),
             r=[C["triU"], W["a"]], w=[p_sm])
        P.op("pe", lambda e: e.matmul(p_sm[:, 4:8], lhsT=C["triS"][:], rhs=W["a"][:], start=True, stop=True),
             r=[C["triS"], W["a"]], w=[p_sm])
        P.op("dve", lambda e: e.tensor_scalar(out=W["nacs"][:], in0=p_sm[:, 0:4], scalar1=-1.0, scalar2=None,
                                              op0=ALU.mult), r=[p_sm], w=[W["nacs"]])
        P.op("act", lambda e: e.activation(out=W["E"][:], in_=p_sm[:, 0:4], func=AF.Exp), r=[p_sm], w=[W["E"]])
        P.op("act", lambda e: e.activation(out=W["dte"][:], in_=p_sm[:, 4:8], func=AF.Exp), r=[p_sm], w=[W["dte"]])
        P.op("dve", lambda e: e.tensor_tensor(out=W["dec"][:], in0=W["E"][:], in1=W["dte"][:], op=ALU.mult),
             r=[W["E"], W["dte"]], w=[W["dec"]])
        yield
        for half in range(2):
            yield
            for rr in range(2):
                r_ = half * 2 + rr
                P.op("dve" if rr == 0 else "pool", lambda e, r_=r_, rr=rr: e.tensor_scalar(
                    out=W["trr"][:, rr, :], in0=C["triU"][:], scalar1=W["a"][:, r_:r_ + 1], scalar2=None,
                    op0=ALU.mult), r=[C["triU"], W["a"]], w=[W["trr"]])
            yield
            for rr in range(2):
                P.op("pe", lambda e, rr=rr: e.matmul(p_bc[:, rr, :], lhsT=C["ones_f"][:], rhs=W["trr"][:, rr, :],
                                                     start=True, stop=False), r=[C["ones_f"], W["trr"]], w=[p_bc])
                P.op("pe", lambda e, rr=rr: e.matmul(p_bc[:, rr, :], lhsT=C["ident_f"][:], rhs=C["NEG"][:],
                                                     start=False, stop=True), r=[C["ident_f"], C["NEG"]], w=[p_bc])
            for rr in range(2):
                r_ = half * 2 + rr
                P.op("act", lambda e, r_=r_, rr=rr: e.activation(
                    out=W["decay"][:, r_, :], in_=p_bc[:, rr, :], func=AF.Exp, bias=W["nacs"][:, r_:r_ + 1],
                    scale=1.0), r=[p_bc, W["nacs"]], w=[W["decay"]])
        yield
        P.op("pe", lambda e: e.matmul(p_cb[:, 0:128], lhsT=cvT[2][:, cs], rhs=cvT[3][:, cs], start=True, stop=True),
             r=[cvT[2], cvT[3]], w=[p_cbxt])
        P.op("act", lambda e: e.copy(out=W["cbs"][:], in_=p_cb[:, 0:128]), r=[p_cbxt], w=[W["cbs"]])
        for r_ in range(4):
            P.op("dve" if r_ % 2 == 0 else "pool", lambda e, r_=r_: e.tensor_tensor(
                out=W["M"][:, r_, :], in0=W["decay"][:, r_, :], in1=W["cbs"][:], op=ALU.mult),
                r=[W["decay"], W["cbs"]], w=[W["M"]])
        yield
        for j in range(2):
            P.op("pe", lambda e, j=j: e.transpose(out=p_xt[:, 256 + j * 128:256 + (j + 1) * 128], in_=cvT[j][:, cs],
                                                  identity=C["ident_bf"][:]), r=[cvT[j], C["ident_bf"]], w=[p_cbxt])
        P.op("pe", lambda e: e.transpose(out=p_smb[:, 128:256], in_=cvT[2][:, cs], identity=C["ident_bf"][:]),
             r=[cvT[2], C["ident_bf"]], w=[p_sm])
        P.op("act", lambda e: e.copy(out=W["xs_tok"][:], in_=p_xt[:, 256:512]), r=[p_cbxt], w=[W["xs_tok"]])
        P.op("act", lambda e: e.copy(out=W["B_tok"][:], in_=p_smb[:, 128:256]), r=[p_sm], w=[W["B_tok"]])
        for r_ in range(4):
            hs = slice(r_ * 64, (r_ + 1) * 64)
            P.op("dve", lambda e, r_=r_, hs=hs: e.tensor_scalar(
                out=W["X"][:, hs], in0=W["xs_tok"][:, hs], scalar1=W["dt"][:, r_:r_ + 1], scalar2=None,
                op0=ALU.mult), r=[W["xs_tok"], W["dt"]], w=[W["X"]])
            P.op("pool", lambda e, r_=r_, hs=hs: e.tensor_scalar(
                out=W["Xd"][:, hs], in0=W["X"][:, hs], scalar1=W["dte"][:, r_:r_ + 1], scalar2=None,
                op0=ALU.mult), r=[W["X"], W["dte"]], w=[W["Xd"]])
        yield
        P.op("pe", lambda e: e.matmul(p_y[:], lhsT=cvT[3][:, cs], rhs=S_b[:], start=True, stop=True),
             r=[cvT[3], S_b], w=[p_y])
        for r_ in range(4):
            hs = slice(r_ * 64, (r_ + 1) * 64)
            P.op("act", lambda e, r_=r_, hs=hs: e.activation(out=W["yo"][:, hs], in_=p_y[:, hs], func=AF.Copy,
                                                             scale=W["E"][:, r_:r_ + 1]), r=[p_y, W["E"]], w=[W["yo"]])
        yield
        for r_ in range(4):
            hs = slice(r_ * 64, (r_ + 1) * 64)
            P.op("pe", lambda e, r_=r_, hs=hs: e.matmul(p_y[:, hs], lhsT=W["M"][:, r_, :], rhs=W["X"][:, hs],
                                                        start=True, stop=True), r=[W["M"], W["X"]], w=[p_y])
        P.op("dve", lambda e: e.tensor_tensor(out=W["y"][:], in0=p_y[:], in1=W["yo"][:], op=ALU.add),
             r=[p_y, W["yo"]], w=[W["y"]])
        P.op("pool", lambda e: e.tensor_tensor(out=W["yo"][:], in0=W["xs_tok"][:],
                                               in1=W["Dbc"][:].rearrange("p a b -> p (a b)"), op=ALU.mult),
             r=[W["xs_tok"], W["Dbc"]], w=[W["yo"]])
        P.op("dve", lambda e: e.tensor_tensor(out=W["y"][:], in0=W["y"][:], in1=W["yo"][:], op=ALU.add),
             r=[W["y"], W["yo"]], w=[W["y"]])
        yield
        P.op("act", lambda e: e.activation(out=W["sz"][:], in_=zt[:], func=AF.Silu), r=[zt], w=[W["sz"]])
        P.op("dve", lambda e: e.tensor_tensor(out=W["y"][:], in0=W["y"][:], in1=W["sz"][:], op=ALU.mult),
             r=[W["y"], W["sz"]], w=[W["y"]])
        P.op("act", lambda e: e.activation(out=W["sz"][:], in_=W["y"][:], func=AF.Square,
                                           accum_out=ss_all[:, c:c + 1]), r=[W["y"]], w=[W["sz"], ss_all])
        P.op("pool", lambda e: e.tensor_copy(out=W["y_bf"][:], in_=W["y"][:]), r=[W["y"]], w=[W["y_bf"]])
        yield
        yield
        for j in range(2):
            P.op("pe", lambda e, j=j: e.transpose(out=p_smb[:, 256 + j * 128:256 + (j + 1) * 128],
                                                  in_=W["y_bf"][:, j * 128:(j + 1) * 128],
                                                  identity=C["ident_bf"][:]), r=[W["y_bf"], C["ident_bf"]], w=[p_sm])
        for j in range(2):
            P.op("act", lambda e, j=j: e.activation(out=W["yTs"][:, j, :], in_=p_smb[:, 256 + j * 128:256 + (j + 1) * 128],
                                                    func=AF.Copy, scale=nw[:, j:j + 1]), r=[p_sm, nw], w=[W["yTs"]])
        for j in range(2):
            for hf in range(2):
                ch = yT[2 * j + hf]
                P.dma("sp", ch[0:64, c * 128:(c + 1) * 128], W["yTs"][hf * 64:(hf + 1) * 64, j, :], r=[W["yTs"]], w=[ch])
        yield
        P.op("pe", lambda e: e.matmul(p_S[:], lhsT=W["B_tok"][:], rhs=W["Xd"][:], start=True, stop=True),
             r=[W["B_tok"], W["Xd"]], w=[p_S])
        for r_ in range(4):
            hs = slice(r_ * 64, (r_ + 1) * 64)
            P.op("dve", lambda e, r_=r_, hs=hs: e.scalar_tensor_tensor(
                out=S_f[:, hs], in0=S_f[:, hs], scalar=W["dec"][:, r_:r_ + 1], in1=p_S[:, hs],
                op0=ALU.mult, op1=ALU.add), r=[S_f, W["dec"], p_S], w=[S_f])
        P.op("pool", lambda e: e.tensor_copy(out=S_b[:], in_=S_f[:]), r=[S_f], w=[S_b])
        yield

    diff = DiffAttn(P, C, yT, lam, subw, relb, ohd, vecd) if do_diff else None

    def prenorm_gen(st):
        h_t = hT[st % 2]
        for sub in range(4):
            t = st * 4 + sub
            x_t = xt[t % 2]
            P.dma("sp", x_t[:], x[t * 128:(t + 1) * 128, :], w=[x_t])
            emit_prenorm_T(P, T, x_t[:], [x_t], gmod, shift, pAB, h_t[:, :, sub * 128:(sub + 1) * 128], [h_t],
                           tmp, hb, ssq, rstd, psT_view=pAB_bf, part=0)
            yield
            yield
            yield
            emit_prenorm_T(P, T, x_t[:], [x_t], gmod, shift, pAB, h_t[:, :, sub * 128:(sub + 1) * 128], [h_t],
                           tmp, hb, ssq, rstd, psT_view=pAB_bf, part=1)
            yield

    def ssd_gen(st):
        for sub in range(4):
            yield from ssd_chunk(st * 4 + sub, st, sub)

    def diff_gen(st):
        for h_ in range(2):
            yield from diff.superblock(st, h_, KT[h_], VV[h_], QT[h_])

    pin = [pAB, bk1]
    for _ in prenorm_gen(0):
        pass
    for st in range(n_super):
        h_t = hT[st % 2]
        for j in range(8):
            pb = pin[j % 2]
            for k in range(8):
                P.op("pe", lambda e, pb=pb, j=j, k=k, h_t=h_t: e.matmul(
                    pb[:], lhsT=w_sb[k][:, j * 128:(j + 1) * 128], rhs=h_t[:, k, :], start=(k == 0), stop=(k == 7)),
                    r=[w_sb[k], h_t], w=[pb])
            if j < 4:
                if do_ssd:
                    rw = raw[j]
                    if st > 0:
                        P.op("dve", lambda e, pb=pb, rw=rw: e.tensor_copy(out=rw[:, 0:3], in_=rw[:, 512:515]), r=[rw], w=[rw])
                    P.op("dve", lambda e, pb=pb, rw=rw: e.tensor_copy(out=rw[:, 3:515], in_=pb[:]), r=[pb], w=[rw])
                    P.op("dve", lambda e, pb=pb, rw=rw, j=j: e.tensor_scalar(
                        out=acc[:], in0=rw[:, 3:515], scalar1=cw[:, j, 3:4], scalar2=cb[:, j:j + 1],
                        op0=ALU.mult, op1=ALU.add), r=[rw, cw, cb], w=[acc])
                    for tap in (2, 1, 0):
                        P.op("dve", lambda e, pb=pb, rw=rw, j=j, tap=tap: e.scalar_tensor_tensor(
                            out=acc[:], in0=rw[:, tap:tap + 512], scalar=cw[:, j, tap:tap + 1], in1=acc[:],
                            op0=ALU.mult, op1=ALU.add), r=[rw, cw, acc], w=[acc])
                    P.op("act", lambda e, pb=pb, j=j: e.activation(out=cvT[j][:], in_=acc[:], func=AF.Silu),
                         r=[acc], w=[cvT[j]])
            elif do_diff and not os.environ.get('DIFF_NOCOPY'):
                h_ = (j - 4) % 2
                if os.environ.get('DIFF_SKIP', '') .count('q' if j < 6 else 'k'):
                    pass
                elif j < 6:
                    P.op("dve", lambda e, pb=pb, h_=h_: e.tensor_copy(out=QT[h_][0][0:64, :], in_=pb[0:64, :]), r=[pb], w=[QT[h_][0]])
                    P.op("dve", lambda e, pb=pb, h_=h_: e.tensor_copy(out=QT[h_][1][64:128, :], in_=pb[64:128, :]),
                         r=[pb], w=[QT[h_][1]])
                else:
                    P.op("dve", lambda e, pb=pb, h_=h_, st=st: e.tensor_copy(out=KT[h_][:, st * 512:(st + 1) * 512], in_=pb[:]),
                         r=[pb], w=[KT[h_]])
        for sub in range(4):
            t = st * 4 + sub
            ts_ = slice(sub * 128, (sub + 1) * 128)
            pb = pin[sub % 2]
            for k in range(8):
                P.op("pe", lambda e, pb=pb, k=k, h_t=h_t, ts_=ts_: e.matmul(
                    pb[:], lhsT=h_t[:, k, ts_], rhs=w_sb[k][:, 1024:1536], start=(k == 0), stop=(k == 7)),
                    r=[h_t, w_sb[k]], w=[pb])
            for k in range(8):
                P.op("pe", lambda e, pb=pb, k=k, h_t=h_t, ts_=ts_: e.matmul(
                    pdt[:, 0:4], lhsT=h_t[:, k, ts_], rhs=w_sb[k][:, 1536:1540], start=(k == 0), stop=(k == 7)),
                    r=[h_t, w_sb[k]], w=[pdt])
            if do_ssd:
                P.op("dve", lambda e, pb=pb, sub=sub: e.tensor_copy(out=z_sb[sub][:], in_=pb[:, 0:256]), r=[pb], w=[z_sb[sub]])
                P.op("dve", lambda e, pb=pb, sub=sub: e.tensor_copy(out=dtr[sub][:], in_=pdt[:, 0:4]), r=[pdt], w=[dtr[sub]])
            if do_diff and not os.environ.get('DIFF_NOCOPY') and not os.environ.get('DIFF_SKIP', '').count('v'):
                P.op("dve", lambda e, pb=pb, t=t: e.tensor_copy(out=VV[0][:, t, :], in_=pb[:, 256:384]), r=[pb], w=[VV[0]])
                P.op("dve", lambda e, pb=pb, t=t: e.tensor_copy(out=VV[1][:, t, :], in_=pb[:, 384:512]), r=[pb], w=[VV[1]])
        gens = []
        if do_ssd:
            gens.append((ssd_gen(st), 4 * 15))
        if do_diff:
            gens.append((diff_gen(st), 2 * (2 * (4 * st + 4) + 2)))
        if st + 1 < n_super:
            gens.append((prenorm_gen(st + 1), 16))
        interleave(gens)
    if do_ssd:
        P.dma("sp", ss_out[:], ss_all[:], r=[ss_all], w=[ss_out])
    P.reset(m0)


def make_bias_tiles(P, C, relb, ohd, vecd, ps, tag):
    relb_sb = P.sb("relb_sb" + tag, [33, 2])
    ohd_sb = P.sb("ohd_sb" + tag, [33, 384])
    P.dma("sp", relb_sb[:], relb, w=[relb_sb])
    P.dma("sp", ohd_sb[:], ohd, w=[ohd_sb])
    vec_sb = P.sb("vec_sb" + tag, [2, 384])
    P.op("pe", lambda e: e.matmul(ps[0:2, 0:384], lhsT=relb_sb[:], rhs=ohd_sb[:], start=True, stop=True),
         r=[relb_sb, ohd_sb], w=[ps])
    P.op("act", lambda e: e.copy(out=vec_sb[:], in_=ps[0:2, 0:384]), r=[ps], w=[vec_sb])
    P.dma("sp", vecd[:], vec_sb[:], r=[vec_sb], w=[vecd])
    Bd, Bp, c31 = [], [], []
    vt = vecd.t
    hk = P.sb("hankel" + tag, [128, 128])
    for h in range(2):
        bd = P.sb(f"Bd{tag}{h}", [128, 128])
        bp = P.sb(f"Bp{tag}{h}", [128, 128])
        c3 = P.sb(f"c31{tag}_{h}", [128, 1])
        for dst, base in ((bd, 0), (bp, 128)):
            P.dma("sp", hk[:], bass.AP(vt.tensor, vt.offset + h * 384 + base, [[1, 128], [1, 128]]), r=[vecd], w=[hk])
            P.op("pe", lambda e: e.matmul(ps[:, 0:128], lhsT=C["antiI"][:], rhs=hk[:], start=True, stop=True),
                 r=[C["antiI"], hk], w=[ps])
            P.op("act", lambda e, dst=dst: e.copy(out=dst[:], in_=ps[:, 0:128]), r=[ps], w=[dst])
        P.dma("sp", c3[:], bass.AP(vt.tensor, vt.offset + h * 384 + 382, [[0, 128], [1, 1]]), r=[vecd], w=[c3])
        Bd.append(bd)
        Bp.append(bp)
        c31.append(c3)
    return Bd, Bp, c31


class DiffAttn:
    def __init__(self, P, C, yT, lam, subw, relb, ohd, vecd, row0=256):
        self.P, self.C, self.yT, self.row0 = P, C, yT, row0
        self.ps_s = [P.ps(f"ps_s{i}", [128, 512]) for i in range(2)]
        self.ps_o = P.ps("ps_o", [128, 512])
        self.ps_l = P.ps("ps_l", [128, 512])
        self.Bd, self.Bp, self.c31 = make_bias_tiles(P, C, relb, ohd, vecd, self.ps_l, "A")
        lam_sb = P.sb("lam_sb", [128, 256])
        P.dma("sp", lam_sb[:], bcast_row(lam), w=[lam_sb])
        pr = P.sb("lam_pr", [128, 2, 64])
        sm = P.sb("lam_sm", [128, 2])
        self.neg_lam = P.sb("neg_lam", [128, 1])
        P.op("dve", lambda e: e.tensor_tensor(out=pr[:, 0, :], in0=lam_sb[:, 0:64], in1=lam_sb[:, 64:128], op=ALU.mult),
             r=[lam_sb], w=[pr])
        P.op("dve", lambda e: e.tensor_tensor(out=pr[:, 1, :], in0=lam_sb[:, 128:192], in1=lam_sb[:, 192:256], op=ALU.mult),
             r=[lam_sb, pr], w=[pr])
        P.op("dve", lambda e: e.tensor_reduce(out=sm[:], in_=pr[:], axis=AX.X, op=ALU.add), r=[pr], w=[sm])
        P.op("act", lambda e: e.activation(out=sm[:], in_=sm[:], func=AF.Exp), r=[sm], w=[sm])
        P.op("dve", lambda e: e.tensor_tensor(out=self.neg_lam[:], in0=sm[:, 1:2], in1=sm[:, 0:1], op=ALU.subtract),
             r=[sm], w=[self.neg_lam])
        P.op("dve", lambda e: e.tensor_scalar(out=self.neg_lam[:], in0=self.neg_lam[:], scalar1=-0.2, scalar2=None,
                                              op0=ALU.add), r=[self.neg_lam], w=[self.neg_lam])
        self.subs = P.sb("subs", [128, 1])
        P.dma("sp", self.subs[:], subw, w=[self.subs])
        P.op("dve", lambda e: e.tensor_scalar(out=self.subs[:], in0=self.subs[:], scalar1=0.8, scalar2=None,
                                              op0=ALU.mult), r=[self.subs], w=[self.subs])
        self.PT = [P.sb(f"PT{i}", [128, 512], BF16) for i in range(4)]
        self.tS = [P.sb(f"tS{i}", [128, 512]) for i in range(2)]
        self.Tm = [P.sb(f"Tm{i}", [128, 512]) for i in range(2)]
        self.Rr = P.sb("Rr", [128, 512])
        self.sq = P.sb("sq", [128, 512])
        self.yd = [P.sb(f"yd{i}", [128, 512], BF16) for i in range(2)]
        self.n = 0
        self.nn = 0
        self.ny = 0

    def superblock(self, Q, h, KT, V, QT):
        import os
        stage = int(os.environ.get("DIFF_STAGE", "3"))
        if stage == 0:
            return
        yield
        P, C = self.P, self.C
        ps_o, ps_l = self.ps_o, self.ps_l
        for m in range(2):
            ms = slice(m * 64, (m + 1) * 64)

            def stage_a(kb, m=m, ms=ms):
                j0 = max(0, kb - 4 * Q)
                c0 = j0 * 128
                ps = self.ps_s[self.n % 2]
                pt = self.PT[self.n % 4]
                self.n += 1
                P.op("pe", lambda e, ps=ps, kb=kb, c0=c0, m=m: e.matmul(
                    ps[:, c0:512], lhsT=KT[:, kb * 128:(kb + 1) * 128], rhs=QT[m][:, c0:512], start=True, stop=True),
                    r=[KT, QT[m]], w=[ps])
                fj = max(j0, kb + 2 - 4 * Q)
                for j in range(j0, min(4, fj)):
                    bt = self.Bd[h] if 4 * Q + j == kb else self.Bp[h]
                    ts = self.tS[self.nn % 2]
                    self.nn += 1
                    cs = slice(j * 128, (j + 1) * 128)
                    P.op("dve", lambda e, ps=ps, ts=ts, bt=bt, cs=cs: e.scalar_tensor_tensor(
                        out=ts[:, cs], in0=ps[:, cs], scalar=0.125, in1=bt[:], op0=ALU.mult, op1=ALU.add),
                        r=[ps, bt], w=[ts])
                    P.op("act", lambda e, ts=ts, pt=pt, cs=cs: e.activation(out=pt[:, cs], in_=ts[:, cs], func=AF.Exp),
                         r=[ts], w=[pt])
                if fj < 4:
                    fs = slice(fj * 128, 512)
                    P.op("act", lambda e, ps=ps, pt=pt, fs=fs: e.activation(
                        out=pt[:, fs], in_=ps[:, fs], func=AF.Exp, bias=self.c31[h][:, 0:1], scale=0.125),
                        r=[ps, self.c31[h]], w=[pt])
                return pt

            def stage_b(kb, pt):
                j0 = max(0, kb - 4 * Q)
                if kb <= 4 * Q:
                    P.op("pe", lambda e, kb=kb, pt=pt: e.matmul(ps_o[:], lhsT=V[:, kb, :], rhs=pt[:],
                                                                start=(kb == 0), stop=False), r=[V, pt], w=[ps_o])
                    P.op("pe", lambda e, kb=kb, pt=pt: e.matmul(ps_l[:], lhsT=C["ones_bf"][:], rhs=pt[:],
                                                                start=(kb == 0), stop=False), r=[C["ones_bf"], pt], w=[ps_l])
                else:
                    for j in range(j0, 4):
                        cs = slice(j * 128, (j + 1) * 128)
                        last = (kb == 4 * Q + j)
                        P.op("pe", lambda e, kb=kb, pt=pt, cs=cs, last=last: e.matmul(
                            ps_o[:, cs], lhsT=V[:, kb, :], rhs=pt[:, cs], start=(kb == 0), stop=last), r=[V, pt], w=[ps_o])
                        P.op("pe", lambda e, kb=kb, pt=pt, cs=cs, last=last: e.matmul(
                            ps_l[:, cs], lhsT=C["ones_bf"][:], rhs=pt[:, cs], start=(kb == 0), stop=last),
                            r=[C["ones_bf"], pt], w=[ps_l])

            nkb = 4 * Q + 4
            LAG = 2
            pend = []
            for kb in range(nkb):
                pend.append((kb, stage_a(kb)))
                yield
                if len(pend) > LAG:
                    stage_b(*pend.pop(0))
            while pend:
                yield
                stage_b(*pend.pop(0))
            if stage < 3:
                continue
            tm = self.Tm[m]
            P.op("dve", lambda e: e.reciprocal(out=self.Rr[:], in_=ps_l[:]), r=[ps_l], w=[self.Rr])
            P.op("dve", lambda e, tm=tm: e.tensor_tensor(out=tm[:], in0=ps_o[:], in1=self.Rr[:], op=ALU.mult),
                 r=[ps_o, self.Rr], w=[tm])
        yield
        if stage < 3:
            return
        t1, t2 = self.Tm
        P.op("dve", lambda e: e.scalar_tensor_tensor(out=t1[:], in0=t2[:], scalar=self.neg_lam[:, 0:1], in1=t1[:],
                                                     op0=ALU.mult, op1=ALU.add), r=[t1, t2, self.neg_lam], w=[t1])
        P.op("pool", lambda e: e.tensor_tensor(out=self.sq[:], in0=t1[:], in1=t1[:], op=ALU.mult), r=[t1], w=[self.sq])
        P.op("pe", lambda e: e.matmul(ps_l[:], lhsT=C["ones_f"][:], rhs=self.sq[:], start=True, stop=True),
             r=[C["ones_f"], self.sq], w=[ps_l])
        P.op("act", lambda e: e.activation(out=self.Rr[:], in_=ps_l[:], func=AF.Sqrt, bias=EPS, scale=1.0 / 128),
             r=[ps_l], w=[self.Rr])
        P.op("dve", lambda e: e.reciprocal(out=self.Rr[:], in_=self.Rr[:]), r=[self.Rr], w=[self.Rr])
        yd = self.yd[self.ny % 2]
        self.ny += 1
        P.op("dve", lambda e, yd=yd: e.scalar_tensor_tensor(out=yd[:], in0=t1[:], scalar=self.subs[:, 0:1], in1=self.Rr[:],
                                                            op0=ALU.mult, op1=ALU.mult), r=[t1, self.subs, self.Rr], w=[yd])
        for hf in range(2):
            ch = self.yT[4 + 2 * h + hf]
            P.dma("sp", ch[0:64, Q * 512:(Q + 1) * 512], yd[hf * 64:(hf + 1) * 64, :], r=[yd], w=[ch])


def split3(v):
    return v[0:1024], v[1024:2048], v[2048:3072]


def even_inputs(z, mod, b, hg):
    wi = z["e_w_in"][0]
    cols = np.concatenate([
        np.arange(1024 + hg * 256, 1024 + hg * 256 + 256),
        np.arange(2048 + hg * 128, 2048 + hg * 128 + 128),
        np.arange(2560 + hg * 128, 2560 + hg * 128 + 128),
        np.arange(3088 + hg * 256, 3088 + hg * 256 + 256),
        np.arange(4112 + hg * 256, 4112 + hg * 256 + 256),
        np.arange(hg * 256, hg * 256 + 256),
        np.arange(5136 + hg * 256, 5136 + hg * 256 + 256),
        np.arange(3072 + hg * 4, 3072 + hg * 4 + 4),
    ])
    ch = cols[0:512] - 1024
    cw = z["e_conv_w"][0][:, ch]
    cb = z["e_conv_b"][0][ch]
    im = {
        "win": np.ascontiguousarray(wi[:, cols]),
        "convw": np.ascontiguousarray(cw.reshape(4, 4, 128).transpose(2, 1, 0)),
        "convb": np.ascontiguousarray(cb.reshape(4, 128).T),
        "hv": np.stack([z["e_dt_bias"][0][hg * 4:hg * 4 + 4], z["e_A_log"][0][hg * 4:hg * 4 + 4],
                        z["e_D"][0][hg * 4:hg * 4 + 4]]).astype(np.float32),
        "normw": np.ascontiguousarray(z["e_ssd_norm"][0][hg * 256:hg * 256 + 256].reshape(2, 128).T),
        "lam": np.ascontiguousarray(z["e_lambda"][0].reshape(1, 256)),
        "subw": np.ascontiguousarray(z["e_diff_norm"][0].reshape(128, 1)),
        "relb": np.concatenate([z["rel_bias"][:, 2 * hg:2 * hg + 2], np.ones((1, 2), np.float32)], 0),
        "ohd": host_ohd(),
    }
    im.update(host_consts())
    return im


O_NCOL = 256 + 512 + 320


def host_rope_tables(hg):
    gamma = 1.0 - 2.0 ** (-5.0 - hg)
    pos = np.arange(S_LEN, dtype=np.float32)
    inv = (np.float32(10000.0) ** (-np.arange(64, dtype=np.float32) / np.float32(64))).astype(np.float32)
    ang = (pos[:, None] * inv[None]).astype(np.float32).astype(np.float64)
    cos, sin = np.cos(ang), np.sin(ang)
    l = (np.arange(S_LEN) % 128).astype(np.float64)
    fq = (gamma ** l)[:, None]
    fk = (gamma ** (-l))[:, None] * 128.0 ** -0.5
    tab = np.stack([
        np.concatenate([cos, cos], 1) * fq, np.concatenate([-sin, sin], 1) * fq,
        np.concatenate([cos, cos], 1) * fk, np.concatenate([-sin, sin], 1) * fk]).astype(np.float32)
    gv = np.zeros((128, 2), np.float32)
    gv[:, 0] = gamma ** 128
    return tab, gv


def phase_odd(P, C, io, n_super=16):
    m0 = P.mark()
    hT_all, win, tab, gv, sinks = io["hT2_all"], io["o_win"], io["tab"], io["gv"], io["sinks"]
    relb, ohdA, ohdB = io["relb"], io["ohdA"], io["ohdB"]
    yT, vecdA, vecdB = io["yT2_loc"], io["vecdA"], io["vecdB"]
    T = TokCtx(P, C["ident_bf"])
    w_sb = [P.sb(f"win{k}", [128, O_NCOL], BF16) for k in range(8)]
    for k in range(8):
        P.dma("pool", w_sb[k][:], win[k * 128:(k + 1) * 128, :], w=[w_sb[k]])
    gv_sb = P.sb("gv_sb", [128, 2])
    P.dma("sp", gv_sb[:], gv, w=[gv_sb])
    es = P.sb("es", [128, 2])
    P.dma("sp", es[:], bcast_row(sinks), w=[es])
    P.op("act", lambda e: e.activation(out=es[:], in_=es[:], func=AF.Exp), r=[es], w=[es])
    pFM = P.ps("pFM", [128, 512])
    pT1 = P.ps("pT1", [128, 512])
    pT2 = P.ps("pT2", [128, 512])
    pSc = P.ps("pSc", [128, 512])
    pRO = P.ps("pRO", [128, 512])
    pTr = P.ps("pTr", [128, 512])
    pSW = P.ps("pSW", [128, 512])
    pOL = P.ps("pOL", [128, 512])
    pTr_bf = pTr.alt
    BdA, _, _ = make_bias_tiles(P, C, relb, ohdA, vecdA, pOL, "oA")
    _, BpB, _ = make_bias_tiles(P, C, relb, ohdB, vecdB, pOL, "oB")
    Bpd = []
    for h in range(2):
        t = P.sb(f"Bpd{h}", [128, 2, 128])
        P.op("dve", lambda e, t=t, h=h: e.tensor_copy(out=t[:, 0, :], in_=BpB[h][:]), r=[BpB[h]], w=[t])
        P.op("dve", lambda e, t=t, h=h: e.tensor_copy(out=t[:, 1, :], in_=BdA[h][:]), r=[BdA[h], t], w=[t])
        Bpd.append(t)
    hT = [P.sb(f"hT{i}", [128, 8, 512], BF16) for i in range(2)]
    tb = [P.sb(f"tb{i}", [128, 4, 4, 128]) for i in range(2)]
    SQT = P.sb("SQT", [128, 512], BF16)
    SKT = P.sb("SKT", [128, 128 + 512], BF16)
    SV = P.sb("SV", [128, 5, 64], BF16)
    qkv = [P.sb(f"qkv{i}", [128, 256]) for i in range(4)]
    v_bf = [P.sb(f"v_bf{i}", [128, 256], BF16) for i in range(4)]
    sg = [P.sb(f"sg{i}", [128, 256]) for i in range(4)]
    Wk = {}
    for nme, shp, dt in [("A", [128, 128], F32), ("B", [128, 128], F32), ("Qp", [128, 128], BF16),
                         ("Kp", [128, 128], BF16), ("QT", [128, 128], BF16), ("KT", [128, 128], BF16),
                         ("Sm", [128, 128], BF16), ("y_bf", [128, 256], BF16), ("yTs", [128, 2, 128], BF16),
                         ("ts", [128, 2, 128], F32), ("PT", [128, 2, 128], BF16), ("den", [64, 128], F32),
                         ("ob", [64, 128], BF16), ("ss", [128, 2], F32), ("rstd", [128, 2], F32)]:
        Wk[nme] = P.sb("k_" + nme, shp, dt)
    St = P.sb("St", [128, 256])
    gS = P.sb("gS", [128, 256], BF16)
    P.op("pool", lambda e: e.memset(St[:], 0.0), w=[St])
    P.op("pool", lambda e: e.memset(gS[:], 0.0), w=[gS])

    def ret_chunk(c, sub, tbs, qk, vb, sgt):
        for which, (col0, tq, dst) in enumerate(((0, 0, "Qp"), (128, 2, "Kp"))):
            src = qk[:, col0:col0 + 128]
            P.op("dve", lambda e, src=src, tq=tq: e.tensor_tensor(out=Wk["A"][:], in0=src, in1=tbs[:, tq, sub, :],
                                                                 op=ALU.mult), r=[qk, tbs], w=[Wk["A"]])
            P.op("pool", lambda e, col0=col0, tq=tq: e.tensor_tensor(
                out=Wk["B"][:, 0:64], in0=qk[:, col0 + 64:col0 + 128], in1=tbs[:, tq + 1, sub, 0:64], op=ALU.mult),
                r=[qk, tbs], w=[Wk["B"]])
            P.op("pool", lambda e, col0=col0, tq=tq: e.tensor_tensor(
                out=Wk["B"][:, 64:128], in0=qk[:, col0:col0 + 64], in1=tbs[:, tq + 1, sub, 64:128], op=ALU.mult),
                r=[qk, tbs, Wk["B"]], w=[Wk["B"]])
            P.op("dve", lambda e, dst=dst: e.tensor_tensor(out=Wk[dst][:], in0=Wk["A"][:], in1=Wk["B"][:], op=ALU.add),
                 r=[Wk["A"], Wk["B"]], w=[Wk[dst]])
            P.op("pe", lambda e, dst=dst, which=which: e.transpose(
                out=pTr_bf[:, which * 128:(which + 1) * 128], in_=Wk[dst][:], identity=C["ident_bf"][:]),
                r=[Wk[dst], C["ident_bf"]], w=[pTr])
        yield
        P.op("act", lambda e: e.copy(out=Wk["QT"][:], in_=pTr_bf[:, 0:128]), r=[pTr], w=[Wk["QT"]])
        P.op("act", lambda e: e.copy(out=Wk["KT"][:], in_=pTr_bf[:, 128:256]), r=[pTr], w=[Wk["KT"]])
        P.op("pe", lambda e: e.matmul(pSc[:, 0:128], lhsT=Wk["KT"][:], rhs=Wk["QT"][:], start=True, stop=True),
             r=[Wk["KT"], Wk["QT"]], w=[pSc])
        P.op("dve", lambda e: e.tensor_tensor(out=Wk["Sm"][:], in0=pSc[:, 0:128], in1=C["triU"][:], op=ALU.mult),
             r=[pSc, C["triU"]], w=[Wk["Sm"]])
        yield
        P.op("pe", lambda e: e.matmul(pRO[:, 0:256], lhsT=Wk["Sm"][:], rhs=vb[:], start=True, stop=False),
             r=[Wk["Sm"], vb], w=[pRO])
        P.op("pe", lambda e: e.matmul(pRO[:, 0:256], lhsT=Wk["QT"][:], rhs=gS[:], start=False, stop=True),
             r=[Wk["QT"], gS], w=[pRO])
        yield
        T.sumsq_rstd(pRO[:, 0:256], [pRO], Wk["ss"], Wk["rstd"], 256)
        yield
        P.op("dve", lambda e: e.scalar_tensor_tensor(out=Wk["y_bf"][:], in0=pRO[:, 0:256], scalar=Wk["rstd"][:, 0:1],
                                                     in1=sgt[:], op0=ALU.mult, op1=ALU.mult),
             r=[pRO, Wk["rstd"], sgt], w=[Wk["y_bf"]])
        for j in range(2):
            P.op("pe", lambda e, j=j: e.transpose(out=pTr_bf[:, 256 + j * 128:256 + (j + 1) * 128],
                                                  in_=Wk["y_bf"][:, j * 128:(j + 1) * 128], identity=C["ident_bf"][:]),
                 r=[Wk["y_bf"], C["ident_bf"]], w=[pTr])
        P.op("act", lambda e: e.copy(out=Wk["yTs"][:], in_=pTr_bf[:, 256:512].rearrange("p (j t) -> p j t", j=2)),
             r=[pTr], w=[Wk["yTs"]])
        for j in range(2):
            for hf in range(2):
                ch = yT[2 * j + hf]
                P.dma("sp", ch[0:64, c * 128:(c + 1) * 128], Wk["yTs"][hf * 64:(hf + 1) * 64, j, :], r=[Wk["yTs"]], w=[ch])
        yield
        P.op("pe", lambda e: e.matmul(pRO[:, 256:512], lhsT=Wk["Kp"][:], rhs=vb[:], start=True, stop=True),
             r=[Wk["Kp"], vb], w=[pRO])
        P.op("dve", lambda e: e.scalar_tensor_tensor(out=St[:], in0=St[:], scalar=gv_sb[:, 0:1], in1=pRO[:, 256:512],
                                                     op0=ALU.mult, op1=ALU.add), r=[St, gv_sb, pRO], w=[St])
        P.op("act", lambda e: e.activation(out=gS[:], in_=St[:], func=AF.Copy, scale=gv_sb[:, 0:1]),
             r=[St, gv_sb], w=[gS])
        yield

    def swa_block(blk, sub):
        for h in range(2):
            hs = slice(h * 64, (h + 1) * 64)
            qcols = slice(sub * 128, (sub + 1) * 128)
            first = (blk == 0)
            if not first:
                P.op("pe", lambda e, hs=hs, qcols=qcols, sub=sub: e.matmul(
                    pSW[:, 0:128], lhsT=SKT[hs, sub * 128:(sub + 1) * 128], rhs=SQT[hs, qcols], start=True, stop=True),
                    r=[SKT, SQT], w=[pSW])
            P.op("pe", lambda e, hs=hs, qcols=qcols, sub=sub: e.matmul(
                pSW[:, 128:256], lhsT=SKT[hs, (sub + 1) * 128:(sub + 2) * 128], rhs=SQT[hs, qcols], start=True, stop=True),
                r=[SKT, SQT], w=[pSW])
            lo = 1 if first else 0
            yield
            P.op("dve", lambda e, h=h, lo=lo: e.scalar_tensor_tensor(
                out=Wk["ts"][:, lo:2, :], in0=pSW[:, lo * 128:256].rearrange("p (a b) -> p a b", b=128), scalar=0.125,
                in1=Bpd[h][:, lo:2, :], op0=ALU.mult, op1=ALU.add), r=[pSW, Bpd[h]], w=[Wk["ts"]])
            P.op("act", lambda e, lo=lo: e.activation(out=Wk["PT"][:, lo:2, :], in_=Wk["ts"][:, lo:2, :], func=AF.Exp),
                 r=[Wk["ts"]], w=[Wk["PT"]])
            oc = slice(h * 256, h * 256 + 128)
            lc = slice(h * 256 + 128, h * 256 + 256)
            yield
            if not first:
                P.op("pe", lambda e, oc=oc, sub=sub: e.matmul(pOL[0:64, oc], lhsT=SV[:, sub, :], rhs=Wk["PT"][:, 0, :],
                                                              start=True, stop=False), r=[SV, Wk["PT"]], w=[pOL])
            P.op("pe", lambda e, oc=oc, sub=sub, first=first: e.matmul(
                pOL[0:64, oc], lhsT=SV[:, sub + 1, :], rhs=Wk["PT"][:, 1, :], start=first, stop=True),
                r=[SV, Wk["PT"]], w=[pOL])
            if not first:
                P.op("pe", lambda e, lc=lc: e.matmul(pOL[0:64, lc], lhsT=C["ones_bf"][:, 0:64], rhs=Wk["PT"][:, 0, :],
                                                     start=True, stop=False), r=[C["ones_bf"], Wk["PT"]], w=[pOL])
            P.op("pe", lambda e, lc=lc, first=first: e.matmul(
                pOL[0:64, lc], lhsT=C["ones_bf"][:, 0:64], rhs=Wk["PT"][:, 1, :], start=first, stop=True),
                r=[C["ones_bf"], Wk["PT"]], w=[pOL])
            yield
            P.op("dve", lambda e, lc=lc, h=h: e.tensor_scalar(out=Wk["den"][:], in0=pOL[0:64, lc], scalar1=es[0:64, h:h + 1],
                                                              scalar2=None, op0=ALU.add), r=[pOL, es], w=[Wk["den"]])
            P.op("dve", lambda e: e.reciprocal(out=Wk["den"][:], in_=Wk["den"][:]), r=[Wk["den"]], w=[Wk["den"]])
            P.op("dve", lambda e, oc=oc: e.tensor_tensor(out=Wk["ob"][:], in0=pOL[0:64, oc], in1=Wk["den"][:], op=ALU.mult),
                 r=[pOL, Wk["den"]], w=[Wk["ob"]])
            P.dma("sp", yT[4 + h][0:64, blk * 128:(blk + 1) * 128], Wk["ob"][:], r=[Wk["ob"]], w=[yT[4 + h]])

    for st in range(n_super):
        h_t = hT[st % 2]
        qq, so = divmod(st, 4)
        for c4 in range(4):
            P.dma("sp", h_t[:, 2 * c4:2 * c4 + 2, :],
                  hT_all[c4][qq * 256:(qq + 1) * 256, so * 512:(so + 1) * 512].rearrange("(k p) t -> p k t", p=128),
                  r=[hT_all[c4]], w=[h_t])
        tbs = tb[st % 2]
        for q4 in range(4):
            P.dma("sp", tbs[:, q4, :, :], tab[q4, st * 512:(st + 1) * 512, :].rearrange("(s p) d -> p s d", p=128),
                  w=[tbs])
        for j in range(2):
            for k in range(8):
                P.op("pe", lambda e, j=j, k=k, h_t=h_t: e.matmul(
                    pFM[:], lhsT=w_sb[k][:, j * 128:(j + 1) * 128], rhs=h_t[:, k, :], start=(k == 0), stop=(k == 7)),
                    r=[w_sb[k], h_t], w=[pFM])
            if j == 0:
                P.op("act", lambda e: e.copy(out=SQT[:], in_=pFM[:]), r=[pFM], w=[SQT])
            else:
                if st > 0:
                    P.op("dve", lambda e: e.tensor_copy(out=SKT[:, 0:128], in_=SKT[:, 512:640]), r=[SKT], w=[SKT])
                    P.op("dve", lambda e: e.tensor_copy(out=SV[:, 0, :], in_=SV[:, 4, :]), r=[SV], w=[SV])
                P.op("act", lambda e: e.copy(out=SKT[:, 128:640], in_=pFM[:]), r=[pFM], w=[SKT])
        for sub in range(4):
            c = st * 4 + sub
            ts_ = slice(sub * 128, (sub + 1) * 128)
            for k in range(8):
                P.op("pe", lambda e, k=k, h_t=h_t, ts_=ts_: e.matmul(
                    pT1[:], lhsT=h_t[:, k, ts_], rhs=w_sb[k][:, 256:768], start=(k == 0), stop=(k == 7)),
                    r=[h_t, w_sb[k]], w=[pT1])
            for k in range(8):
                P.op("pe", lambda e, k=k, h_t=h_t, ts_=ts_: e.matmul(
                    pT2[:, 0:320], lhsT=h_t[:, k, ts_], rhs=w_sb[k][:, 768:1088], start=(k == 0), stop=(k == 7)),
                    r=[h_t, w_sb[k]], w=[pT2])
            qk = qkv[sub]
            vb = v_bf[sub]
            sgt = sg[sub]
            P.op("act", lambda e, qk=qk: e.copy(out=qk[:, 0:256], in_=pT1[:, 0:256]), r=[pT1], w=[qk])
            P.op("dve", lambda e, vb=vb: e.tensor_copy(out=vb[:], in_=pT1[:, 256:512]), r=[pT1], w=[vb])
            P.op("act", lambda e, sgt=sgt: e.activation(out=sgt[:], in_=pT2[:, 0:256], func=AF.Silu), r=[pT2], w=[sgt])
            P.op("dve", lambda e, sub=sub: e.tensor_copy(out=SV[:, sub + 1, :], in_=pT2[:, 256:320]), r=[pT2], w=[SV])


        def ret_gen(st=st, tbs=tbs):
            for sub in range(4):
                yield from ret_chunk(st * 4 + sub, sub, tbs, qkv[sub], v_bf[sub], sg[sub])

        def swa_gen(st=st):
            for sub in range(4):
                yield from swa_block(st * 4 + sub, sub)

        interleave([(ret_gen(), 4 * 6), (swa_gen(), 4 * 2 * 4)])
    P.reset(m0)


def odd_inputs(z, hT_full, b, hg):
    wi = z["o_w_in"][0]
    kv = hg // 2
    cols = np.concatenate([
        np.arange(3072 + 2 * hg * 64, 3072 + 2 * hg * 64 + 128),
        np.arange(3584 + kv * 64, 3584 + kv * 64 + 64), np.arange(3584 + kv * 64, 3584 + kv * 64 + 64),
        np.arange(hg * 128, hg * 128 + 128),
        np.arange(512 + hg * 128, 512 + hg * 128 + 128),
        np.arange(1024 + hg * 256, 1024 + hg * 256 + 256),
        np.arange(2048 + hg * 256, 2048 + hg * 256 + 256),
        np.arange(3712 + kv * 64, 3712 + kv * 64 + 64),
    ])
    tab, gv = host_rope_tables(hg)
    hc = host_consts()
    im = {
        "win": np.ascontiguousarray(wi[:, cols]), "tab": tab, "gv": gv,
        "sinks": np.ascontiguousarray(z["o_sinks"][0][2 * hg:2 * hg + 2].reshape(1, 2)),
        "relb": np.concatenate([z["rel_bias"][:, 2 * hg:2 * hg + 2], np.ones((1, 2), np.float32)], 0),
        "ohdA": host_ohd(False), "ohdB": host_ohd(True),
    }
    for k in ["ident_bf", "triU", "ones_bf", "antiI"]:
        im[k] = hc[k]
    return im


I32 = mybir.dt.int32
GROUPS = [[0, 1, 2, 3], [4, 5, 6, 7]]


def dyn_dma(P, out, in_fn, r, w):
    op = Op("sp", lambda e: e.dma_start(out=out, in_=in_fn()), is_dma=True, dbuf=w[0])
    P._rec(op, r, w)
    P.dma_log.append(op)
    return op


def setup_regs(P, nc, qoff):
    regs = [P.stack.enter_context(nc.sync.register(f"qr{i}")) for i in range(2)]

    def ld(e):
        for i in range(2):
            ins = e.reg_load(regs[i], qoff.t[0:1, i:i + 1])
        P.qv = e.snap(regs[0], min_val=0, max_val=6144)
        P.qc = e.snap(regs[1], min_val=0, max_val=48)
        return ins
    P.op("sp", ld)


def extract_quarter(P, src_all, dst_q, nrows, step):
    for r0 in range(0, nrows, step):
        dyn_dma(P, dst_q.t[r0:r0 + step, :], lambda r0=r0: src_all.t[r0:r0 + step, bass.ds(P.qv, NT)],
                r=[src_all], w=[dst_q])


def phase_mod(P, io):
    m0 = P.mark()
    cT, modw, modb = io["cT"], io["modw"], io["modb"]
    c_sb = P.sb("c_sb", [128, 8])
    ca = P.sb("ca_sb", [128, 8])
    b_sb = P.sb("b_sb", [1, 3072])
    o_sb = P.sb("o_sb", [1, 3072])
    wt = [P.sb(f"mw{i}", [128, 8, 768]) for i in range(2)]
    ps = [P.ps(f"mps{i}", [1, 512]) for i in range(2)]
    P.dma("sp", c_sb[:], cT, w=[c_sb])
    P.dma("sp", b_sb[:], modb, w=[b_sb])
    P.op("act", lambda e: e.activation(out=ca[:], in_=c_sb[:], func=AF.Silu), r=[c_sb], w=[ca])
    for s_ in range(4):
        w_t = wt[s_ % 2]
        P.dma("sp", w_t[:], modw[s_].rearrange("(k p) n -> p k n", p=128), w=[w_t])
        for hf in range(2):
            for k in range(8):
                P.op("pe", lambda e, hf=hf, k=k, w_t=w_t: e.matmul(
                    ps[hf][0:1, 0:384], lhsT=ca[:, k:k + 1], rhs=w_t[:, k, hf * 384:(hf + 1) * 384],
                    start=(k == 0), stop=(k == 7)), r=[ca, w_t], w=[ps[hf]])
            o0 = s_ * 768 + hf * 384
            P.op("dve", lambda e, hf=hf, o0=o0: e.tensor_tensor(out=o_sb[0:1, o0:o0 + 384], in0=ps[hf][0:1, 0:384],
                                                                in1=b_sb[0:1, o0:o0 + 384], op=ALU.add),
                 r=[ps[hf], b_sb], w=[o_sb])
    P.dma("sp", io["mod_loc"], o_sb[:], r=[o_sb], w=[io["mod_loc"]])
    P.collective("AllGather", GROUPS, io["mod_loc"], io["mod_all"])
    P.reset(m0)


CONST_NAMES = ["ident_bf", "ident_f", "triU", "triS", "NEG", "ones_f", "ones_bf", "antiI"]


def build_fused():
    nc = bass.Bass("TRN2", target_bir_lowering=False)
    P = Prog(nc)

    def X(name, shape, dt=F32):
        return Buf(nc.dram_tensor(name, list(shape), dt, kind="ExternalInput").ap(), name)

    io = {}
    for name, shape, dt in [
        ("cT", [128, 8], F32), ("modw", [4, 1024, 768], F32), ("modb", [1, 3072], F32), ("ng", [8, 1024], F32),
        ("qoff", [1, 2], I32), ("x_b", [S_LEN, 1024], F32), ("x_tok", [NT, 1024], F32),
        ("e_win", [1024, E_NCOL], F32), ("convw", [128, 4, 4], F32), ("convb", [128, 4], F32), ("hv", [3, 4], F32),
        ("normw", [128, 2], F32), ("lam", [1, 256], F32), ("subw", [128, 1], F32), ("relb", [33, 2], F32),
        ("ohdA", [33, 384], F32), ("ohdB", [33, 384], F32), ("e_wout", [2048, 1024], F32),
        ("w1_0", [1024, 4096], F32), ("w2_0", [4096, 1024], F32), ("w1_1", [1024, 4096], F32), ("w2_1", [4096, 1024], F32),
        ("o_win", [1024, O_NCOL], F32), ("tab", [4, S_LEN, 128], F32), ("gv", [128, 2], F32), ("sinks", [1, 2], F32),
        ("o_wout", [1536, 1024], F32),
    ]:
        if int(os.environ.get("FUSED_STOP", "99")) <= 2 and name in ("x_tok", "e_wout", "w1_0", "w2_0", "w1_1", "w2_1", "o_win", "tab", "o_wout"):
            continue
        io[name] = X(name, shape, dt)
    P.declared = set(io.keys()) | set(CONST_NAMES)
    cn = {k: X(k, [128, 128], BF16 if k.endswith("bf") else F32) for k in CONST_NAMES}
    for name, shape, dt in [
        ("mod_loc", [1, 3072], F32), ("mod_all", [4, 3072], F32),
        ("ss_loc", [128, 64], F32), ("ss_all", [512, 64], F32),
        ("xn0", [NT, 1024], F32), ("hT0", [1024, NT], BF16), ("x1", [NT, 1024], F32),

        ("xn1", [NT, 1024], F32), ("hT1", [1024, NT], BF16), ("vecdA", [2, 384], F32), ("vecdB", [2, 384], F32),
        ("vecdC", [2, 384], F32), ("yT_q", [2048, NT], BF16), ("ss_q", [512, 16], F32), ("yT2_q", [1536, NT], BF16),
    ]:
        io[name] = P.dram(name, shape, dt)
    io["yT_loc"] = [P.dram(f"yT_loc{i}", [64, S_LEN], BF16) for i in range(8)]
    io["yT_all"] = [P.dram(f"yT_all{i}", [256, S_LEN], BF16) for i in range(8)]
    io["yT2_loc"] = [P.dram(f"yT2_loc{i}", [64, S_LEN], BF16) for i in range(6)]
    io["yT2_all"] = [P.dram(f"yT2_all{i}", [256, S_LEN], BF16) for i in range(6)]
    io["hTn_loc"] = [P.dram(f"hTn_loc{i}", [256, NT], BF16) for i in range(4)]
    io["hT2_all"] = [P.dram(f"hT2_all{i}", [1024, NT], BF16) for i in range(4)]
    io["out"] = P.dram("out", [NT, 1024], F32, kind="ExternalOutput")

    setup_regs(P, nc, io["qoff"])
    C = {}
    for k in CONST_NAMES:
        C[k] = P.sb("c_" + k, [128, 128], BF16 if k.endswith("bf") else F32)
        P.dma("sp", C[k][:], cn[k], w=[C[k]])
    P.barrier()

    stop = int(os.environ.get("FUSED_STOP", "99"))
    phase_mod(P, io)
    P.barrier()
    if stop <= 1:
        P.build()
        return nc, P
    phase_even(P, C, io, n_super=int(os.environ.get('FUSED_NSUPER', '16')))
    for i in range(8):
        P.collective("AllGather", GROUPS, io["yT_loc"][i], io["yT_all"][i])
    if not os.environ.get("FUSED_NOAG2"):
        P.collective("AllGather", GROUPS, io["ss_loc"], io["ss_all"])
    P.barrier()
    if stop <= 2:
        P.build()
        return nc, P
    rm_e = [(kk // 2, kk % 2) for kk in range(8)] + [(kk // 2, 2 + kk % 2) for kk in range(8)]
    phase_outproj(P, C, dict(io, x=io["x_tok"], yT_all=io["yT_all"], wout=io["e_wout"], xn=io["xn0"], hT=io["hT0"], rowmap=rm_e, ngroups=16),
                  16, True, 0)
    P.barrier()
    if stop <= 3:
        P.build()
        return nc, P
    phase_mlp(P, C, dict(io, xn=io["xn0"], hT=io["hT0"], w1=io["w1_0"], w2=io["w2_0"], xo=io["x1"], hTn=io["hTn_loc"]), True, 0)
    for i in range(4):
        P.collective("AllGather", GROUPS, io["hTn_loc"][i], io["hT2_all"][i])
    P.barrier()
    if stop <= 4:
        P.build()
        return nc, P
    phase_odd(P, C, dict(io, vecdA=io["vecdB"], vecdB=io["vecdC"]), n_super=int(os.environ.get('FUSED_NSUPER', '16')))
    for i in range(6):
        P.collective("AllGather", GROUPS, io["yT2_loc"][i], io["yT2_all"][i])
    P.barrier()
    rm_o = [(kk // 2, kk % 2) for kk in range(8)] + [(kk, 2) for kk in range(4)]
    phase_outproj(P, C, dict(io, x=io["x1"], yT_all=io["yT2_all"], wout=io["o_wout"], xn=io["xn1"], hT=io["hT1"], rowmap=rm_o, ngroups=12),
                  12, False, 1)
    P.barrier()
    phase_mlp(P, C, dict(io, xn=io["xn1"], hT=io["hT1"], w1=io["w1_1"], w2=io["w2_1"], xo=io["out"]), False, 1)
    P.build()
    return nc, P


def fused_inputs(z, i):
    b, r = divmod(i, 4)
    hc = host_consts()
    cols = np.concatenate([part * 1024 + r * 256 + np.arange(256) for part in range(3)])
    mw = z["mod_w"].reshape(4, 1024, 3072)
    mb = z["mod_b"].reshape(4, 3072)
    ei = even_inputs(z, None, b, r)
    oi = odd_inputs(z, None, b, r)
    im = {
        "cT": np.ascontiguousarray(z["c"][b].reshape(8, 128).T),
        "modw": np.ascontiguousarray(mw[:, :, cols]),
        "modb": np.ascontiguousarray(mb[:, cols].reshape(1, 3072)),
        "ng": np.ascontiguousarray(z["norm_gains"].reshape(8, 1024)),
        "qoff": np.array([[r * NT, r * 16]], np.int32),
        "x_b": np.ascontiguousarray(z["x"][b]),
        "x_tok": np.ascontiguousarray(z["x"][b, r * NT:(r + 1) * NT]),
        "e_win": ei["win"], "convw": ei["convw"], "convb": ei["convb"], "hv": ei["hv"], "normw": ei["normw"],
        "lam": ei["lam"], "subw": ei["subw"], "relb": ei["relb"], "ohdA": host_ohd(False), "ohdB": host_ohd(True),
        "e_wout": z["e_w_out"][0], "w1_0": z["mlp_w1"][0], "w2_0": z["mlp_w2"][0], "w1_1": z["mlp_w1"][1],
        "w2_1": z["mlp_w2"][1], "o_win": oi["win"], "tab": oi["tab"], "gv": oi["gv"], "sinks": oi["sinks"],
        "o_wout": z["o_w_out"][0],
    }
    for k in CONST_NAMES:
        im[k] = hc[k]
    return im


def kernel(**inputs):
    z = {k: np.asarray(v) for k, v in inputs.items()}
    nc, P_ = build_fused()
    in_maps = [{k: v for k, v in fused_inputs(z, i).items() if k in P_.declared} for i in range(8)]
    res = run_bass_kernel_spmd(nc, in_maps, core_ids=list(range(8)))
    out = np.stack([res.results[i]["out"] for i in range(8)]).reshape(2, S_LEN, 1024).astype(np.float32)
    return out
```
